# Optimizing a Trainium2 kernel written in Bass

```python
import math
import jax, jax.numpy as jnp
from jax import lax
import numpy as np

D_MODEL = 1024
BATCH = 16
SEQ = 2048
DEPTH = 4
DEC_BATCH = 16
DEC_SEQ = 4096
PAST_LEN = 128

N_MEM = 256
CHUNK = 64
RMS_EPS = 1e-6
ROPE_BASE = 10000.0
RET_HEADS = 4
RET_DK = 128
RET_DV = 128
HG_HEADS = 4
HG_DK = 128
HG_DV = 128
GDN_HEADS = 8
GDN_DK = 128
GDN_DV = 128
CONV_W = 5
XA_HEADS = 4
XA_DH = D_MODEL // XA_HEADS
D_FF = int(math.ceil(8 * D_MODEL / 3 / 256)) * 256

N_EVEN = (DEPTH + 1) // 2
N_ODD = DEPTH // 2
RET_QK = RET_HEADS * RET_DK
RET_V = RET_HEADS * RET_DV
HG_K = HG_HEADS * HG_DK
HG_V = HG_HEADS * HG_DV
EVEN_IN = 2 * RET_QK + 2 * RET_V + 3 * HG_K + 2 * HG_V
EVEN_MIX = RET_V + HG_V
GDN_QK = GDN_HEADS * GDN_DK
GDN_V = GDN_HEADS * GDN_DV
GDN_CONV_CH = 2 * GDN_QK + GDN_V
ODD_IN = 2 * GDN_QK + 2 * GDN_V + 4 * GDN_HEADS

kernel_name = "hybrid_bidir_retention_hgrn2_gdn_encoder"

F32 = jnp.float32


def rmsnorm(x, g):
    xf = x.astype(F32)
    r = lax.rsqrt(jnp.mean(xf * xf, axis=-1, keepdims=True) + RMS_EPS)
    return (xf * r).astype(x.dtype) * g


def head_rmsnorm(x, g):
    xf = x.astype(F32)
    return xf * lax.rsqrt(jnp.mean(xf * xf, axis=-1, keepdims=True) + RMS_EPS) * g


def head_layernorm(x, g):
    xf = x.astype(F32)
    mu = jnp.mean(xf, axis=-1, keepdims=True)
    xc = xf - mu
    return xc * lax.rsqrt(jnp.mean(xc * xc, axis=-1, keepdims=True) + RMS_EPS) * g


def l2norm(x):
    return x * lax.rsqrt(jnp.sum(x * x, axis=-1, keepdims=True) + RMS_EPS)


def flip(t):
    return jnp.flip(t, axis=1)


def rope_tables(L, d):
    inv = 1.0 / (ROPE_BASE ** (jnp.arange(d // 2, dtype=F32) / (d // 2)))
    ang = jnp.arange(L, dtype=F32)[:, None] * inv[None, :]
    return jnp.cos(ang), jnp.sin(ang)


def apply_rope(x, cos, sin):
    x1, x2 = jnp.split(x, 2, axis=-1)
    c = cos[None, :, None, :]
    s = sin[None, :, None, :]
    return jnp.concatenate([x1 * c - x2 * s, x1 * s + x2 * c], axis=-1)


def to_chunks(x):
    B, L, H = x.shape[:3]
    x = x.reshape((B, L // CHUNK, CHUNK, H) + x.shape[3:])
    return jnp.moveaxis(x, (1, 3), (0, 2))


def from_chunks(x):
    x = jnp.moveaxis(x, (0, 2), (1, 3))
    B, N, C, H = x.shape[:4]
    return x.reshape((B, N * C, H) + x.shape[4:])


def chunk_gla(q, k, v, log_f):
    B, L, H, dk = q.shape
    dv = v.shape[-1]
    scalar = log_f.shape[-1] == 1
    qc, kc, vc = to_chunks(q), to_chunks(k), to_chunks(v)
    bc = jnp.cumsum(to_chunks(log_f), axis=3)
    causal = jnp.tril(jnp.ones((CHUNK, CHUNK), dtype=bool))

    def step(S, xs):
        q_, k_, v_, b_ = xs
        b_last = b_[:, :, -1:, :]
        o = jnp.einsum('bhck,bhkv->bhcv', q_ * jnp.exp(b_), S)
        diff = b_[:, :, :, None, :] - b_[:, :, None, :, :]
        decay = jnp.exp(jnp.where(causal[:, :, None], diff, -jnp.inf))
        if scalar:
            a = jnp.einsum('bhik,bhjk->bhij', q_, k_) * decay[..., 0]
        else:
            a = jnp.einsum('bhik,bhjk,bhijk->bhij', q_, k_, decay)
        o = o + jnp.einsum('bhij,bhjv->bhiv', a, v_)
        S = jnp.exp(b_last[:, :, 0, :, None]) * S + jnp.einsum(
            'bhck,bhcv->bhkv', k_ * jnp.exp(b_last - b_), v_)
        return S, o

    S0 = jnp.zeros((B, H, dk, dv), q.dtype)
    _, o = lax.scan(step, S0, (qc, kc, vc, bc))
    return from_chunks(o)


def chunk_gated_delta(q, k, v, log_a, beta):
    B, L, H, dk = q.shape
    dv = v.shape[-1]
    qc, kc, vc = to_chunks(q), to_chunks(k), to_chunks(v)
    g = jnp.cumsum(to_chunks(log_a), axis=-1)
    bt = to_chunks(beta)[..., None]
    incl = jnp.tril(jnp.ones((CHUNK, CHUNK), dtype=bool))
    strict = jnp.tril(jnp.ones((CHUNK, CHUNK), dtype=bool), -1)
    decay = jnp.exp(jnp.where(incl, g[..., :, None] - g[..., None, :], -jnp.inf))
    kb = kc * bt
    a = jnp.where(strict, jnp.einsum('...ik,...jk->...ij', kb, kc) * decay, 0.0)
    eye = jnp.eye(CHUNK, dtype=q.dtype)
    t = lax.linalg.triangular_solve(a + eye, jnp.broadcast_to(eye, a.shape),
                                    left_side=True, lower=True)
    u = jnp.einsum('...ij,...jv->...iv', t, vc * bt)
    w = jnp.einsum('...ij,...jk->...ik', t, kb * jnp.exp(g)[..., None])
    qk = jnp.where(incl, jnp.einsum('...ik,...jk->...ij', qc, kc) * decay, 0.0)

    def step(S, xs):
        q_, k_, u_, w_, qk_, g_ = xs
        g_last = g_[..., -1:]
        v_new = u_ - jnp.einsum('bhck,bhkv->bhcv', w_, S)
        o = jnp.einsum('bhck,bhkv->bhcv', q_ * jnp.exp(g_)[..., None], S) + \
            jnp.einsum('bhij,bhjv->bhiv', qk_, v_new)
        S = jnp.exp(g_last)[..., None] * S + jnp.einsum(
            'bhck,bhcv->bhkv', k_ * jnp.exp(g_last - g_)[..., None], v_new)
        return S, o

    S0 = jnp.zeros((B, H, dk, dv), q.dtype)
    _, o = lax.scan(step, S0, (qc, kc, u, w, qk, g))
    return from_chunks(o)


def centred_dwconv(x, w):
    return lax.conv_general_dilated(
        x, w[:, None, :], window_strides=(1,), padding=[(CONV_W // 2, CONV_W // 2)],
        dimension_numbers=('NWC', 'WIO', 'NWC'), feature_group_count=x.shape[-1])


def even_mixer(h, w_in, w_out, lb, ret_g, hg_g, cos, sin):
    B, L, _ = h.shape
    p = (h @ w_in).astype(F32)
    sizes = [RET_QK, RET_QK, RET_V, RET_V, HG_K, HG_K, HG_K, HG_V, HG_V]
    idx = [int(s) for s in np.cumsum(sizes)[:-1]]
    rq, rk, rv, rg, hq, hff, hfb, hi, hgt = jnp.split(p, idx, axis=-1)

    q = apply_rope(rq.reshape(B, L, RET_HEADS, RET_DK), cos, sin)
    k = apply_rope(rk.reshape(B, L, RET_HEADS, RET_DK), cos, sin) * (RET_DK ** -0.5)
    v = rv.reshape(B, L, RET_HEADS, RET_DV)
    log_gamma = jnp.log1p(-jnp.exp2(-5.0 - jnp.arange(RET_HEADS, dtype=F32)))
    lg = jnp.broadcast_to(log_gamma[:, None], (B, L, RET_HEADS, 1))
    ret = chunk_gla(q, k, v, lg) + flip(chunk_gla(flip(q), flip(k), flip(v), lg))
    ret = head_layernorm(ret, ret_g) * jax.nn.silu(rg.reshape(B, L, RET_HEADS, RET_DV))

    lbh = lb.astype(F32).reshape(HG_HEADS, HG_DK)

    def gate(z):
        z = z.reshape(B, L, HG_HEADS, HG_DK)
        log_f = jnp.logaddexp(jnp.log(lbh), jnp.log1p(-lbh) + jax.nn.log_sigmoid(z))
        kk = (1.0 - lbh) * jax.nn.sigmoid(-z)
        return log_f, kk

    hqh = hq.reshape(B, L, HG_HEADS, HG_DK)
    hih = hi.reshape(B, L, HG_HEADS, HG_DV)
    lf_f, k_f = gate(hff)
    lf_b, k_b = gate(hfb)
    hg = chunk_gla(hqh, k_f, hih, lf_f) + \
        flip(chunk_gla(flip(hqh), flip(k_b), flip(hih), flip(lf_b)))
    hg = head_rmsnorm(hg, hg_g) * jax.nn.silu(hgt.reshape(B, L, HG_HEADS, HG_DV))

    o = jnp.concatenate([ret.reshape(B, L, RET_V), hg.reshape(B, L, HG_V)], axis=-1)
    return o.astype(h.dtype) @ w_out


def odd_mixer(h, w_in, conv_w, a_log, dt_bias, norm_g, w_out):
    B, L, _ = h.shape
    p = (h @ w_in).astype(F32)
    qkv, g, a, b = jnp.split(p, [GDN_CONV_CH, GDN_CONV_CH + GDN_V,
                                 GDN_CONV_CH + GDN_V + 2 * GDN_HEADS], axis=-1)
    qkv = jax.nn.silu(centred_dwconv(qkv, conv_w.astype(F32)))
    q, k, v = jnp.split(qkv, [GDN_QK, 2 * GDN_QK], axis=-1)
    q = l2norm(q.reshape(B, L, GDN_HEADS, GDN_DK)) * (GDN_DK ** -0.5)
    k = l2norm(k.reshape(B, L, GDN_HEADS, GDN_DK))
    v = v.reshape(B, L, GDN_HEADS, GDN_DV)
    a = a.reshape(B, L, 2, GDN_HEADS)
    b = b.reshape(B, L, 2, GDN_HEADS)
    log_a = -jnp.exp(a_log.astype(F32)) * jax.nn.softplus(a + dt_bias.astype(F32))
    beta = jax.nn.sigmoid(b)
    o = chunk_gated_delta(q, k, v, log_a[:, :, 0], beta[:, :, 0]) + \
        flip(chunk_gated_delta(flip(q), flip(k), flip(v), flip(log_a[:, :, 1]), flip(beta[:, :, 1])))
    o = head_rmsnorm(o, norm_g) * jax.nn.silu(g.reshape(B, L, GDN_HEADS, GDN_DV))
    return o.reshape(B, L, GDN_V).astype(h.dtype) @ w_out


def cross_attn(h, m, w_q, w_kv, w_o):
    B, L, _ = h.shape
    M = m.shape[1]
    q = (h @ w_q).reshape(B, L, XA_HEADS, XA_DH)
    k, v = jnp.split(m @ w_kv, 2, axis=-1)
    k = k.reshape(B, M, XA_HEADS, XA_DH)
    v = v.reshape(B, M, XA_HEADS, XA_DH)
    s = jnp.einsum('blhd,bmhd->bhlm', q, k).astype(F32) * (XA_DH ** -0.5)
    pr = jax.nn.softmax(s, axis=-1).astype(v.dtype)
    o = jnp.einsum('bhlm,bmhd->blhd', pr, v).reshape(B, L, D_MODEL)
    return o @ w_o


def swiglu(h, w_gu, w_down):
    gt, up = jnp.split(h @ w_gu, 2, axis=-1)
    return (jax.nn.silu(gt) * up) @ w_down


def trunk(x, mem, norm_mix, norm_xq, norm_mem, norm_ffn, norm_final,
          even_w_in, even_w_out, hgrn_lb_logits, ret_norm, hgrn_norm,
          gdn_w_in, gdn_conv, gdn_a_log, gdn_dt_bias, gdn_norm, gdn_w_out,
          xa_w_q, xa_w_kv, xa_w_o, ffn_w_gu, ffn_w_down):
    L = x.shape[1]
    cos, sin = rope_tables(L, RET_DK)
    lb_all = jnp.cumsum(jax.nn.softmax(hgrn_lb_logits.astype(F32), axis=0), axis=0)
    lb_all = lb_all - lb_all[:1]
    for i in range(DEPTH):
        j = i // 2
        h = rmsnorm(x, norm_mix[i])
        if i % 2 == 0:
            x = x + even_mixer(h, even_w_in[j], even_w_out[j], lb_all[j], ret_norm[j],
                               hgrn_norm[j], cos, sin)
        else:
            x = x + odd_mixer(h, gdn_w_in[j], gdn_conv[j], gdn_a_log[j], gdn_dt_bias[j],
                              gdn_norm[j], gdn_w_out[j])
        x = x + cross_attn(rmsnorm(x, norm_xq[i]), rmsnorm(mem, norm_mem[i]),
                           xa_w_q[i], xa_w_kv[i], xa_w_o[i])
        x = x + swiglu(rmsnorm(x, norm_ffn[i]), ffn_w_gu[i], ffn_w_down[i])
    return rmsnorm(x, norm_final)


def setup_inputs(seed: int = 0) -> dict:
    key = jax.random.key(seed)
    ks = jax.random.split(key, 32)
    nrm = lambda k, shape, scale: jax.random.normal(k, shape, F32) * scale
    gain = lambda k, shape: 1.0 + 0.02 * jax.random.normal(k, shape, F32)
    u_dt = jax.random.uniform(ks[13], (N_ODD, 2, GDN_HEADS), F32)
    dt = jnp.exp(u_dt * (math.log(0.1) - math.log(0.001)) + math.log(0.001))
    return {
        "x_prompt": nrm(ks[0], (BATCH, SEQ, D_MODEL), 1.0),
        "x_sample": nrm(ks[1], (DEC_BATCH, DEC_SEQ, D_MODEL), 1.0),
        "mem_prompt": nrm(ks[2], (BATCH, N_MEM, D_MODEL), 1.0),
        "mem_sample": nrm(ks[3], (DEC_BATCH, N_MEM, D_MODEL), 1.0),
        "norm_mix": gain(ks[4], (DEPTH, D_MODEL)),
        "norm_xq": gain(ks[5], (DEPTH, D_MODEL)),
        "norm_mem": gain(ks[6], (DEPTH, D_MODEL)),
        "norm_ffn": gain(ks[7], (DEPTH, D_MODEL)),
        "norm_final": gain(ks[8], (D_MODEL,)),
        "even_w_in": nrm(ks[9], (N_EVEN, D_MODEL, EVEN_IN), D_MODEL ** -0.5),
        "even_w_out": nrm(ks[10], (N_EVEN, EVEN_MIX, D_MODEL), EVEN_MIX ** -0.5),
        "hgrn_lb_logits": nrm(ks[11], (N_EVEN, HG_K), 0.5),
        "ret_norm": gain(ks[12], (N_EVEN, RET_HEADS, RET_DV)),
        "hgrn_norm": gain(ks[14], (N_EVEN, HG_HEADS, HG_DV)),
        "gdn_w_in": nrm(ks[15], (N_ODD, D_MODEL, ODD_IN), D_MODEL ** -0.5),
        "gdn_conv": nrm(ks[16], (N_ODD, CONV_W, GDN_CONV_CH), CONV_W ** -0.5),
        "gdn_a_log": jnp.log(jax.random.uniform(ks[17], (N_ODD, 2, GDN_HEADS), F32, 1.0, 16.0)),
        "gdn_dt_bias": dt + jnp.log(-jnp.expm1(-dt)),
        "gdn_norm": gain(ks[18], (N_ODD, GDN_DV)),
        "gdn_w_out": nrm(ks[19], (N_ODD, GDN_V, D_MODEL), GDN_V ** -0.5),
        "xa_w_q": nrm(ks[20], (DEPTH, D_MODEL, D_MODEL), D_MODEL ** -0.5),
        "xa_w_kv": nrm(ks[21], (DEPTH, D_MODEL, 2 * D_MODEL), D_MODEL ** -0.5),
        "xa_w_o": nrm(ks[22], (DEPTH, D_MODEL, D_MODEL), D_MODEL ** -0.5),
        "ffn_w_gu": nrm(ks[23], (DEPTH, D_MODEL, 2 * D_FF), D_MODEL ** -0.5),
        "ffn_w_down": nrm(ks[24], (DEPTH, D_FF, D_MODEL), D_FF ** -0.5),
    }


def reference(x_prompt, x_sample, mem_prompt, mem_sample, norm_mix, norm_xq, norm_mem,
              norm_ffn, norm_final, even_w_in, even_w_out, hgrn_lb_logits, ret_norm,
              hgrn_norm, gdn_w_in, gdn_conv, gdn_a_log, gdn_dt_bias, gdn_norm, gdn_w_out,
              xa_w_q, xa_w_kv, xa_w_o, ffn_w_gu, ffn_w_down):
    y_prompt = trunk(x_prompt, mem_prompt, norm_mix, norm_xq, norm_mem, norm_ffn, norm_final,
                     even_w_in, even_w_out, hgrn_lb_logits, ret_norm, hgrn_norm,
                     gdn_w_in, gdn_conv, gdn_a_log, gdn_dt_bias, gdn_norm, gdn_w_out,
                     xa_w_q, xa_w_kv, xa_w_o, ffn_w_gu, ffn_w_down)
    y_sample = trunk(x_sample, mem_sample, norm_mix, norm_xq, norm_mem, norm_ffn, norm_final,
                     even_w_in, even_w_out, hgrn_lb_logits, ret_norm, hgrn_norm,
                     gdn_w_in, gdn_conv, gdn_a_log, gdn_dt_bias, gdn_norm, gdn_w_out,
                     xa_w_q, xa_w_kv, xa_w_o, ffn_w_gu, ffn_w_down)
    return (y_prompt, y_sample)
```

```python
import contextlib
import math
import numpy as np
import ml_dtypes
import concourse.bass as bass
import concourse.mybir as mybir
from concourse.bass_utils import run_bass_kernel_spmd

F32 = mybir.dt.float32
BF16 = mybir.dt.bfloat16
AF = mybir.ActivationFunctionType
ALU = mybir.AluOpType
AX = mybir.AxisListType

D = 1024
DEPTH = 4
N_MEM = 256
RMS_EPS = 1e-6
D_FF = 2816
EVEN_IN = 4608
ODD_IN = 4128
XA_SCALE = 256 ** -0.5
NCORES = 8


class TB:
    __slots__ = ("t", "w", "r", "name")

    def __init__(self, t, name=""):
        self.t = t
        self.w = None
        self.r = {}
        self.name = name

    def __getitem__(self, k):
        return self.t[k]


class Ctx:
    def __init__(self, nc, es, n_dma_sems=20):
        self.nc = nc
        self.es = es
        self.engs = {"pe": nc.tensor, "act": nc.scalar, "dve": nc.vector, "pool": nc.gpsimd, "sp": nc.sync}
        self.sem = {}
        self.cnt = {}
        self.seen = {k: {} for k in self.engs}
        for k in self.engs:
            self.sem[k] = es.enter_context(nc.semaphore("s_" + k))
            self.cnt[k] = 0
        self.dsem = {}
        self.dval = {}
        self.drot = {}
        for q in ("sp", "pool", "act"):
            n = n_dma_sems if q != "act" else 8
            self.dsem[q] = [es.enter_context(nc.semaphore("d_%s_%d" % (q, i))) for i in range(n)]
            self.dval[q] = [0] * n
            self.drot[q] = 0
        self.n_inst = 0
        self.n_wait = 0

    def sb(self, name, shape, dt, es=None):
        es = es or self.es
        self.uid = getattr(self, "uid", 0) + 1
        t = es.enter_context(self.nc.sbuf_tensor("%s_%d" % (name, self.uid), list(shape), dt))
        return TB(t, name)

    def ps(self, name, shape, dt, es=None):
        es = es or self.es
        self.uid = getattr(self, "uid", 0) + 1
        t = es.enter_context(self.nc.psum_tensor("%s_%d" % (name, self.uid), list(shape), dt))
        return TB(t, name)

    def _deps(self, engname, reads, writes):
        need = {}

        def add(ev, raw):
            if ev is None:
                return
            s, v, e = ev
            if e == engname and not raw:
                return
            if e == engname and engname == "pe":
                return
            key = id(s)
            if key not in need or need[key][1] < v:
                need[key] = (s, v)

        for b in reads:
            add(b.w, True)
        for b in writes:
            add(b.w, False)
            for ev in b.r.values():
                add(ev, False)
        return need

    def _emit_waits(self, engname, need):
        eng = self.engs[engname]
        seen = self.seen[engname]
        for key, (s, v) in need.items():
            if seen.get(key, 0) >= v:
                continue
            eng.wait_ge(s, v)
            seen[key] = v
            self.n_wait += 1

    def _commit(self, ev, reads, writes):
        for b in writes:
            b.w = ev
            b.r = {}
        for b in reads:
            if b in writes:
                continue
            b.r[ev[2]] = ev

    def op(self, engname, fn, reads=(), writes=()):
        need = self._deps(engname, reads, writes)
        self._emit_waits(engname, need)
        ins = fn(self.engs[engname])
        self.cnt[engname] += 1
        ins.then_inc(self.sem[engname], 1)
        ev = (self.sem[engname], self.cnt[engname], engname)
        self._commit(ev, reads, writes)
        self.n_inst += 1
        return ins

    def dma(self, q, out, in_, reads=(), writes=(), slow=False):
        need = self._deps("dma_" + q, reads, writes)
        i = self.drot[q]
        self.drot[q] = (i + 1) % len(self.dsem[q])
        s = self.dsem[q][i]
        if self.dval[q][i] > 0:
            need[id(s)] = (s, self.dval[q][i])
        self._emit_waits(q, need)
        if slow:
            ins = self.engs[q].dma_start(out=out, in_=in_, allow_slow_non_contiguous=True)
        else:
            ins = self.engs[q].dma_start(out=out, in_=in_)
        self.dval[q][i] += 16
        ins.then_inc(s, 16)
        ev = (s, self.dval[q][i], "dma_" + q + str(i))
        self._commit(ev, reads, writes)
        self.n_inst += 1
        return ins

    def barrier(self):
        for e in self.engs:
            need = {}
            for k in self.engs:
                if k != e and self.cnt[k] > 0:
                    need[id(self.sem[k])] = (self.sem[k], self.cnt[k])
            for q in self.dsem:
                for s, v in zip(self.dsem[q], self.dval[q]):
                    if v > 0:
                        need[id(s)] = (s, v)
            self._emit_waits(e, need)


def rstd_inplace(c, ssq, inv_n, eps=RMS_EPS):
    c.op("dve", lambda e: e.tensor_scalar(ssq[:], ssq[:], inv_n, eps, ALU.mult, ALU.add), reads=[ssq], writes=[ssq])
    c.op("act", lambda e: e.activation(out=ssq[:], in_=ssq[:], func=AF.Sqrt), reads=[ssq], writes=[ssq])
    c.op("dve", lambda e: e.reciprocal(out=ssq[:], in_=ssq[:]), reads=[ssq], writes=[ssq])


class _View:
    def __init__(self, tb, key):
        self.tb = tb
        self.key = key

    def __getitem__(self, k):
        v = self.tb.t[self.key]
        return v[k]

    @property
    def w(self):
        return self.tb.w

    @w.setter
    def w(self, v):
        self.tb.w = v

    @property
    def r(self):
        return self.tb.r

    @r.setter
    def r(self, v):
        self.tb.r = v


class Rot:
    def __init__(self, items):
        self.items = items
        self.i = 0

    def next(self):
        it = self.items[self.i]
        self.i = (self.i + 1) % len(self.items)
        return it


class Builder:
    def __init__(self, seqs, depth=DEPTH, do_mixer=True, do_xattn=True, do_ffn=True, kinds=None):
        self.seqs = list(seqs)
        self.depth = depth
        self.do_mixer = do_mixer
        self.do_xattn = do_xattn
        self.do_ffn = do_ffn
        self.kinds = kinds or [("even", l // 2) if l % 2 == 0 else ("odd", l // 2) for l in range(depth)]
        self.T = sum(self.seqs)
        self.seq_off = [sum(self.seqs[:i]) for i in range(len(self.seqs))]
        self.Lmax = max(self.seqs)

    def declare(self, nc):
        d = {}

        def inp(name, shape, dt=F32):
            d[name] = nc.dram_tensor(name, list(shape), dt, kind="ExternalInput").ap()

        ns = len(self.seqs)
        inp("x", [self.T, D])
        inp("mem", [ns * N_MEM, D])
        for n in ("norm_mix", "norm_xq", "norm_mem", "norm_ffn"):
            inp(n, [DEPTH, D])
        inp("norm_final", [1, D])
        inp("even_w_in", [2, D, EVEN_IN])
        inp("even_w_out", [2, D, D])
        inp("hgrn_lb_logits", [2, 512])
        inp("ret_norm", [2, 512])
        inp("hgrn_norm", [2, 512])
        inp("gdn_w_in", [2, D, ODD_IN])
        inp("gdn_conv", [2, 5, 3072])
        inp("gdn_a_log", [2, 16])
        inp("gdn_dt_bias", [2, 16])
        inp("gdn_norm", [2, 128])
        inp("gdn_w_out", [2, D, D])
        inp("xa_w_q", [DEPTH, D, D])
        inp("xa_w_kv", [DEPTH, D, 2 * D])
        inp("xa_w_o", [DEPTH, D, D])
        inp("ffn_w_gu", [DEPTH, D, 2 * D_FF])
        inp("ffn_w_down", [DEPTH, D_FF, D])
        for name, arr in host_constants(self.Lmax).items():
            inp(name, arr.shape, F32 if arr.dtype == np.float32 else BF16)
        d["y"] = nc.dram_tensor("y", [self.T, D], F32, kind="ExternalOutput").ap()
        d["X"] = nc.dram_tensor("X_scr", [self.T, D], F32).ap()
        d["P"] = nc.dram_tensor("P_scr", [self.Lmax, EVEN_IN], F32).ap()
        d["OF"] = nc.dram_tensor("OF_scr", [self.Lmax, D], F32).ap()
        d["HT"] = nc.dram_tensor("HT_scr", [128, 8 * (self.Lmax + 4)], BF16).ap()
        d["QT"] = nc.dram_tensor("QT_scr", [128, 8 * self.Lmax], BF16).ap()
        d["KT"] = nc.dram_tensor("KT_scr", [128, 8 * self.Lmax], BF16).ap()
        d["VM"] = nc.dram_tensor("VM_scr", [self.Lmax, D], BF16).ap()
        self.d = d

    def rmsnorm_to_hT(self, c, xt, grow, hT_dst, junk, ssq, hn, pst, ident):
        c.op("act", lambda e: e.activation(out=junk[:], in_=xt[:], func=AF.Square, accum_out=ssq[:]),
             reads=[xt], writes=[junk, ssq])
        rstd_inplace(c, ssq, 1.0 / D)
        c.op("dve", lambda e: e.scalar_tensor_tensor(hn[:], xt[:], ssq[:], grow[:], ALU.mult, ALU.mult),
             reads=[xt, ssq, grow], writes=[hn])
        for k in range(8):
            c.op("pe", lambda e, k=k: e.transpose(out=pst[:, k * 128:(k + 1) * 128],
                                                   in_=hn[:, k * 128:(k + 1) * 128], identity=ident[:]),
                 reads=[hn, ident], writes=[pst])
        tb, ap = hT_dst
        c.op("act", lambda e: e.copy(out=ap, in_=pst[:].rearrange("p (k t) -> p k t", k=8)),
             reads=[pst], writes=[tb])

    def x_src(self, layer_first):
        return self.d["x"] if layer_first else self.d["X"]

    def phase_ffn(self, c, layer, x_in, last):
        nc, d = c.nc, self.d
        TBK = 256
        with contextlib.ExitStack() as es:
            wgu = c.sb("wgu", [128, 8, 2 * D_FF], BF16, es)
            wdn = c.sb("wdn", [128, 22, D], BF16, es)
            grow = c.sb("grow", [128, D], F32, es)
            gfin = c.sb("gfin", [128, D], F32, es)
            ident = c.sb("identb", [128, 128], BF16, es)
            xts = Rot([[c.sb("xt%d_%d" % (i, j), [128, D], F32, es) for j in range(2)] for i in range(2)])
            hTs = Rot([c.sb("hT%d" % i, [128, 8, TBK], BF16, es) for i in range(2)])
            aT = c.sb("aT", [128, 22, TBK], BF16, es)
            sg = Rot([c.sb("sg%d" % i, [128, TBK], F32, es) for i in range(2)])
            junk = c.sb("junk", [128, D], BF16, es)
            ssq = c.sb("ssq", [128, 1], F32, es)
            hn = c.sb("hn", [128, D], BF16, es)
            yts = Rot([c.sb("yt%d" % i, [128, D], F32, es) for i in range(2)])
            pst = c.ps("pst", [128, D], BF16, es)
            psg = Rot([c.ps("psg%d" % i, [128, 512], F32, es) for i in range(2)])
            psu = Rot([c.ps("psu%d" % i, [128, 512], F32, es) for i in range(2)])
            psy = Rot([c.ps("psy%d" % i, [128, 512], F32, es) for i in range(2)])

            c.dma("sp", ident[:], d["c_ident_bf"], writes=[ident])
            c.dma("sp", grow[:], d["norm_ffn"][layer:layer + 1, :].partition_broadcast(128), writes=[grow])
            c.dma("sp", gfin[:], d["norm_final"][0:1, :].partition_broadcast(128), writes=[gfin])
            for k in range(8):
                c.dma("pool", wgu[:, k, :], d["ffn_w_gu"][layer, k * 128:(k + 1) * 128, :], writes=[wgu])
            for f in range(22):
                c.dma("pool", wdn[:, f, :], d["ffn_w_down"][layer, f * 128:(f + 1) * 128, :], writes=[wdn])

            for b0 in range(0, self.T, TBK):
                xt = xts.next()
                hT = hTs.next()
                for j in range(2):
                    r0 = b0 + j * 128
                    c.dma("sp", xt[j][:], x_in[r0:r0 + 128, :], reads=[self.xtb(r0)], writes=[xt[j]])
                    self.rmsnorm_to_hT(c, xt[j], grow, (hT, hT[:, :, j * 128:(j + 1) * 128]), junk, ssq, hn, pst, ident)
                for fb in range(22):
                    pg, pu, s = psg.next(), psu.next(), sg.next()
                    for k in range(8):
                        c.op("pe", lambda e, k=k: e.matmul(pg[:, 0:TBK], wgu[:, k, fb * 128:(fb + 1) * 128], hT[:, k, :],
                                                          start=(k == 0), stop=(k == 7)),
                             reads=[wgu, hT], writes=[pg])
                    for k in range(8):
                        c.op("pe", lambda e, k=k: e.matmul(pu[:, 0:TBK], wgu[:, k, D_FF + fb * 128:D_FF + (fb + 1) * 128],
                                                          hT[:, k, :], start=(k == 0), stop=(k == 7)),
                             reads=[wgu, hT], writes=[pu])
                    c.op("act", lambda e: e.activation(out=s[:], in_=pg[:, 0:TBK], func=AF.Silu), reads=[pg], writes=[s])
                    c.op("dve", lambda e: e.tensor_tensor(aT[:, fb, :], pu[:, 0:TBK], s[:], ALU.mult),
                         reads=[pu, s], writes=[aT])
                for j in range(2):
                    r0 = b0 + j * 128
                    yt = yts.next()
                    for n in range(2):
                        py = psy.next()
                        for fb in range(22):
                            c.op("pe", lambda e, fb=fb: e.matmul(py[:], aT[:, fb, j * 128:(j + 1) * 128],
                                                                wdn[:, fb, n * 512:(n + 1) * 512],
                                                                start=(fb == 0), stop=(fb == 21)),
                                 reads=[aT, wdn], writes=[py])
                        c.op("dve", lambda e: e.tensor_tensor(yt[:, n * 512:(n + 1) * 512], py[:],
                                                              xt[j][:, n * 512:(n + 1) * 512], ALU.add),
                             reads=[py, xt[j]], writes=[yt])
                    if last:
                        self.final_norm_store(c, yt, gfin, junk, ssq, r0)
                    else:
                        c.dma("sp", d["X"][r0:r0 + 128, :], yt[:], reads=[yt], writes=[self.xtb(r0)])
            c.barrier()

    def final_norm_store(self, c, yt, gfin, junk, ssq, r0):
        c.op("act", lambda e: e.activation(out=junk[:], in_=yt[:], func=AF.Square, accum_out=ssq[:]),
             reads=[yt], writes=[junk, ssq])
        rstd_inplace(c, ssq, 1.0 / D)
        c.op("dve", lambda e: e.scalar_tensor_tensor(yt[:], yt[:], ssq[:], gfin[:], ALU.mult, ALU.mult),
             reads=[yt, ssq, gfin], writes=[yt])
        c.dma("sp", self.d["y"][r0:r0 + 128, :], yt[:], reads=[yt], writes=[self.ytb(r0)])

    def xtb(self, r0):
        return self._xtb[r0 // 128]

    def ytb(self, r0):
        return self._ytb[r0 // 128]

    def phase_xattn(self, c, layer, x_in):
        nc, d = c.nc, self.d
        TBK = 512
        with contextlib.ExitStack() as es:
            wq = c.sb("wq", [128, 8, D], BF16, es)
            wkv = c.sb("wkv", [128, 8, 2 * D], BF16, es)
            wo = c.sb("wo", [128, 8, D], BF16, es)
            gq = c.sb("gq", [128, D], F32, es)
            gm = c.sb("gm", [128, D], F32, es)
            ident = c.sb("identb", [128, 128], BF16, es)
            ones = c.sb("onesb", [128, 128], BF16, es)
            xts = [c.sb("xt%d" % j, [128, D], F32, es) for j in range(4)]
            hT = c.sb("hT", [128, 8, TBK], BF16, es)
            qT = c.sb("qT", [128, 8, TBK], BF16, es)
            oT = c.sb("oT", [128, 8, TBK], BF16, es)
            memT = c.sb("memT", [128, 8, N_MEM], BF16, es)
            KT = c.sb("KT", [128, 8, N_MEM], BF16, es)
            Vt = c.sb("Vt", [128, 2, D], BF16, es)
            PT = [Rot([c.sb("PT%d_%d" % (m, i), [128, TBK], BF16, es) for i in range(2)]) for m in range(2)]
            rden = c.sb("rden", [128, TBK], F32, es)
            junk = c.sb("junk", [128, D], BF16, es)
            ssq = c.sb("ssq", [128, 1], F32, es)
            hn = c.sb("hn", [128, D], BF16, es)
            yts = Rot([c.sb("yt%d" % i, [128, D], F32, es) for i in range(2)])
            pst = c.ps("pst", [128, D], BF16, es)
            psA = Rot([c.ps("psA%d" % i, [128, 512], F32, es) for i in range(4)])
            psD = c.ps("psD", [128, 512], F32, es)
            psy = Rot([c.ps("psy%d" % i, [128, 512], F32, es) for i in range(2)])

            c.dma("sp", ident[:], d["c_ident_bf"], writes=[ident])
            c.dma("sp", ones[:], d["c_ones_bf"], writes=[ones])
            c.dma("sp", gq[:], d["norm_xq"][layer:layer + 1, :].partition_broadcast(128), writes=[gq])
            c.dma("sp", gm[:], d["norm_mem"][layer:layer + 1, :].partition_broadcast(128), writes=[gm])
            for k in range(8):
                c.dma("pool", wq[:, k, :], d["xa_w_q"][layer, k * 128:(k + 1) * 128, :], writes=[wq])
                c.dma("pool", wkv[:, k, :], d["xa_w_kv"][layer, k * 128:(k + 1) * 128, :], writes=[wkv])
                c.dma("pool", wo[:, k, :], d["xa_w_o"][layer, k * 128:(k + 1) * 128, :], writes=[wo])

            for si, L in enumerate(self.seqs):
                for m in range(2):
                    mt = xts[m]
                    c.dma("sp", mt[:], d["mem"][si * N_MEM + m * 128: si * N_MEM + (m + 1) * 128, :], writes=[mt])
                    self.rmsnorm_to_hT(c, mt, gm, (memT, memT[:, :, m * 128:(m + 1) * 128]), junk, ssq, hn, pst, ident)
                for fb in range(8):
                    pa = psA.next()
                    for k in range(8):
                        c.op("pe", lambda e, k=k: e.matmul(pa[:, 0:N_MEM], wkv[:, k, fb * 128:(fb + 1) * 128], memT[:, k, :],
                                                          start=(k == 0), stop=(k == 7)), reads=[wkv, memT], writes=[pa])
                    c.op("act", lambda e: e.copy(out=KT[:, fb, :], in_=pa[:, 0:N_MEM]), reads=[pa], writes=[KT])
                for m in range(2):
                    for n in range(2):
                        pa = psA.next()
                        for k in range(8):
                            c.op("pe", lambda e, k=k: e.matmul(pa[:], memT[:, k, m * 128:(m + 1) * 128],
                                                              wkv[:, k, D + n * 512:D + (n + 1) * 512],
                                                              start=(k == 0), stop=(k == 7)), reads=[wkv, memT], writes=[pa])
                        c.op("act", lambda e: e.copy(out=Vt[:, m, n * 512:(n + 1) * 512], in_=pa[:]), reads=[pa], writes=[Vt])
                for b0 in range(self.seq_off[si], self.seq_off[si] + L, TBK):
                    ntile = min(4, (self.seq_off[si] + L - b0) // 128)
                    W = ntile * 128
                    for j in range(ntile):
                        r0 = b0 + j * 128
                        c.dma("sp", xts[j][:], x_in[r0:r0 + 128, :], reads=[self.xtb(r0)], writes=[xts[j]])
                        self.rmsnorm_to_hT(c, xts[j], gq, (hT, hT[:, :, j * 128:(j + 1) * 128]), junk, ssq, hn, pst, ident)
                    for fb in range(8):
                        pa = psA.next()
                        for k in range(8):
                            c.op("pe", lambda e, k=k: e.matmul(pa[:, 0:W], wq[:, k, fb * 128:(fb + 1) * 128], hT[:, k, 0:W],
                                                              start=(k == 0), stop=(k == 7)), reads=[wq, hT], writes=[pa])
                        c.op("act", lambda e: e.copy(out=qT[:, fb, 0:W], in_=pa[:, 0:W]), reads=[pa], writes=[qT])
                    for h in range(4):
                        pts = []
                        for mb in range(2):
                            pa = psA.next()
                            for dd in range(2):
                                c.op("pe", lambda e, dd=dd: e.matmul(pa[:, 0:W], KT[:, 2 * h + dd, mb * 128:(mb + 1) * 128],
                                                                    qT[:, 2 * h + dd, 0:W], start=(dd == 0), stop=(dd == 1)),
                                     reads=[KT, qT], writes=[pa])
                            pt = PT[mb].next()
                            c.op("act", lambda e: e.activation(out=pt[:, 0:W], in_=pa[:, 0:W], func=AF.Exp, scale=XA_SCALE),
                                 reads=[pa], writes=[pt])
                            pts.append(pt)
                        for mb in range(2):
                            c.op("pe", lambda e, mb=mb: e.matmul(psD[:, 0:W], ones[:], pts[mb][:, 0:W],
                                                                start=(mb == 0), stop=(mb == 1)), reads=[ones, pts[mb]], writes=[psD])
                        c.op("dve", lambda e: e.reciprocal(out=rden[:, 0:W], in_=psD[:, 0:W]), reads=[psD], writes=[rden])
                        for dd in range(2):
                            pa = psA.next()
                            for mb in range(2):
                                c.op("pe", lambda e, mb=mb: e.matmul(pa[:, 0:W], Vt[:, mb, (2 * h + dd) * 128:(2 * h + dd + 1) * 128],
                                                                    pts[mb][:, 0:W], start=(mb == 0), stop=(mb == 1)),
                                     reads=[Vt, pts[mb]], writes=[pa])
                            c.op("dve", lambda e: e.tensor_tensor(oT[:, 2 * h + dd, 0:W], pa[:, 0:W], rden[:, 0:W], ALU.mult),
                                 reads=[pa, rden], writes=[oT])
                    for j in range(ntile):
                        r0 = b0 + j * 128
                        yt = yts.next()
                        for n in range(2):
                            py = psy.next()
                            for fb in range(8):
                                c.op("pe", lambda e, fb=fb: e.matmul(py[:], oT[:, fb, j * 128:(j + 1) * 128],
                                                                    wo[:, fb, n * 512:(n + 1) * 512],
                                                                    start=(fb == 0), stop=(fb == 7)), reads=[oT, wo], writes=[py])
                            c.op("dve", lambda e: e.tensor_tensor(yt[:, n * 512:(n + 1) * 512], py[:],
                                                                  xts[j][:, n * 512:(n + 1) * 512], ALU.add),
                                 reads=[py, xts[j]], writes=[yt])
                        c.dma("sp", d["X"][r0:r0 + 128, :], yt[:], reads=[yt], writes=[self.xtb(r0)])
            c.barrier()

    def gla_tile(self, c, R, q_ap, q_tb, k_tb, k_ap, v_ap, v_tb, lf_tb, lf_ap, dr, S, Sb, o_tb, o_col0, add_tb=None):
        M1, M2, M4, MT = R["gm"][dr]
        cs1, cs2, cs4 = R["psA"].next(), R["psA"].next(), R["psA"].next()
        for ps_, M in ((cs1, M1), (cs2, M2), (cs4, M4)):
            c.op("pe", lambda e: e.matmul(ps_[:], M[:], lf_ap, start=True, stop=True), reads=[M, lf_tb], writes=[ps_])
        E = R["E"]
        c.op("act", lambda e: e.activation(out=E[0][:], in_=cs1[:], func=AF.Exp), reads=[cs1], writes=[E[0]])
        c.op("act", lambda e: e.activation(out=E[1][:], in_=cs2[:], func=AF.Exp), reads=[cs2], writes=[E[1]])
        c.op("act", lambda e: e.activation(out=E[2][:], in_=cs2[:], func=AF.Exp, scale=-1.0), reads=[cs2], writes=[E[2]])
        c.op("act", lambda e: e.activation(out=E[3][:], in_=cs4[:], func=AF.Exp), reads=[cs4], writes=[E[3]])
        q1, q2, k3, k4, vb = R["q1"], R["q2"], R["k3"], R["k4"], R["vb"]
        BIG = 4.0e18
        c.op("pool", lambda e: e.tensor_tensor(q1[:], E[0][:], q_ap, ALU.mult), reads=[E[0], q_tb], writes=[q1])
        c.op("dve", lambda e: e.scalar_tensor_tensor(q2[:], E[1][:], BIG, q_ap, ALU.min, ALU.mult), reads=[E[1], q_tb], writes=[q2])
        c.op("dve", lambda e: e.scalar_tensor_tensor(k3[:], E[2][:], BIG, k_ap, ALU.min, ALU.mult), reads=[E[2], k_tb], writes=[k3])
        c.op("pool", lambda e: e.tensor_tensor(k4[:], E[3][:], k_ap, ALU.mult), reads=[E[3], k_tb], writes=[k4])
        c.op("act", lambda e: e.copy(out=vb[:], in_=v_ap), reads=[v_tb], writes=[vb])
        pe_l = R["psA"].next()
        for h in range(4):
            c.op("pe", lambda e: e.matmul(pe_l[:, 2 * h:2 * h + 2], lf_ap[:, h * 128:(h + 1) * 128], R["ind"][:],
                                          start=True, stop=True), reads=[lf_tb, R["ind"]], writes=[pe_l])
        eL = R["eL"]
        c.op("act", lambda e: e.activation(out=eL[:], in_=pe_l[:, 0:8], func=AF.Exp), reads=[pe_l], writes=[eL])
        Ts = []
        for src, nm in ((q1, "q1T"), (q2, "q2T"), (k3, "k3T")):
            pt = R["pstB"].next()
            for h in range(4):
                c.op("pe", lambda e: e.transpose(out=pt[:, h * 128:(h + 1) * 128], in_=src[:, h * 128:(h + 1) * 128],
                                                 identity=R["ident"][:]), reads=[src, R["ident"]], writes=[pt])
            dst = R[nm]
            c.op("act" if nm != "q2T" else "dve", lambda e: e.tensor_copy(out=dst[:], in_=pt[:, 0:512]) if nm == "q2T"
                 else e.copy(out=dst[:], in_=pt[:, 0:512]), reads=[pt], writes=[dst])
            Ts.append(dst)
        q1T, q2T, k3T = Ts
        order = (0, 1) if dr == 0 else (1, 0)
        for h in range(4):
            hs = slice(h * 128, (h + 1) * 128)
            pa = R["psH"].next()
            c.op("pe", lambda e: e.matmul(pa[:, 0:128], k3T[:, hs], q2T[:, hs], start=True, stop=True),
                 reads=[k3T, q2T], writes=[pa])
            atm = R["ATm"].next()
            c.op("dve", lambda e: e.tensor_tensor(atm[:], pa[:, 0:128], MT[:], ALU.mult), reads=[pa, MT], writes=[atm])
            for ci in order:
                rs = slice(ci * 64, (ci + 1) * 64)
                po = R["psH"].next()
                c.op("pe", lambda e: e.matmul(po[:, 0:128], q1T[:, hs], Sb[:, hs], start=True, stop=False),
                     reads=[q1T, Sb], writes=[po])
                c.op("pe", lambda e: e.matmul(po[:, 0:128], atm[:], vb[:, hs], start=False, stop=True),
                     reads=[atm, vb], writes=[po])
                ocs = slice(o_col0 + h * 128, o_col0 + (h + 1) * 128)
                if add_tb is None:
                    c.op("act", lambda e: e.copy(out=o_tb[rs, ocs], in_=po[rs, 0:128]), reads=[po], writes=[o_tb])
                else:
                    c.op("dve", lambda e: e.tensor_tensor(o_tb[rs, ocs], po[rs, 0:128], add_tb[rs, ocs], ALU.add),
                         reads=[po, add_tb], writes=[o_tb])
                pu = R["psH"].next()
                c.op("pe", lambda e: e.matmul(pu[:, 0:128], k4[rs, hs], vb[rs, hs], start=True, stop=True),
                     reads=[k4, vb], writes=[pu])
                c.op("dve", lambda e: e.scalar_tensor_tensor(S[:, hs], S[:, hs], eL[:, 2 * h + ci:2 * h + ci + 1], pu[:, 0:128],
                                                             ALU.mult, ALU.add), reads=[S, eL, pu], writes=[S])
                c.op("act", lambda e: e.copy(out=Sb[:, hs], in_=S[:, hs]), reads=[S], writes=[Sb])

    def phase_even(self, c, j, layer, x_in):
        nc, d = c.nc, self.d
        with contextlib.ExitStack() as es:
            win = c.sb("win", [128, 8, EVEN_IN], BF16, es)
            wout = c.sb("wout", [128, 8, D], BF16, es)
            gmix = c.sb("gmix", [128, D], F32, es)
            gh = c.sb("gh", [128, D], F32, es)
            lbr = c.sb("lbr", [128, 512], F32, es)
            omlb = c.sb("omlb", [128, 512], F32, es)
            lgam = c.sb("lgam", [128, 512], F32, es)
            hmask = c.sb("hmask", [128, 8], F32, es)
            ident = c.sb("identb", [128, 128], BF16, es)
            gmt = [[c.sb("gm%d%d" % (a, b_), [128, 128], F32, es) for b_ in range(4)] for a in range(2)]
            ind = c.sb("ind", [128, 2], F32, es)
            p = c.sb("p", [128, EVEN_IN], F32, es)
            xts = Rot([c.sb("xt%d" % i, [128, D], F32, es) for i in range(2)])
            oft = c.sb("oft", [128, D], F32, es)
            ot = c.sb("ot", [128, D], F32, es)
            rope = c.sb("rope", [128, 4, 64], F32, es)
            rt = [c.sb("rt%d" % i, [128, 4, 64], F32, es) for i in range(4)]
            fg = c.sb("fg", [128, 512], F32, es)
            lf = c.sb("lf", [128, 512], F32, es)
            kk = c.sb("kk", [128, 512], F32, es)
            junk = c.sb("junk", [128, D], BF16, es)
            sq = c.sb("sq", [128, D], F32, es)
            ssq = c.sb("ssq", [128, 1], F32, es)
            s1 = c.sb("s1", [128, 8], F32, es)
            s2 = c.sb("s2", [128, 8], F32, es)
            hn = c.sb("hn", [128, D], BF16, es)
            hT = c.sb("hT", [128, 8, 128], BF16, es)
            omix = c.sb("omix", [128, D], BF16, es)
            oT = c.sb("oT", [128, 8, 128], BF16, es)
            R = {
                "gm": gmt, "ind": ind, "ident": ident,
                "E": [c.sb("E%d" % i, [128, 512], F32, es) for i in range(4)],
                "q1": c.sb("q1", [128, 512], BF16, es), "q2": c.sb("q2", [128, 512], BF16, es),
                "k3": c.sb("k3", [128, 512], BF16, es), "k4": c.sb("k4", [128, 512], BF16, es),
                "vb": c.sb("vb", [128, 512], BF16, es),
                "q1T": c.sb("q1T", [128, 512], BF16, es), "q2T": c.sb("q2T", [128, 512], BF16, es),
                "k3T": c.sb("k3T", [128, 512], BF16, es),
                "eL": c.sb("eL", [128, 8], F32, es),
                "ATm": Rot([c.sb("ATm%d" % i, [128, 128], BF16, es) for i in range(2)]),
                "psA": Rot([c.ps("psA%d" % i, [128, 512], F32, es) for i in range(3)]),
                "psH": Rot([c.ps("psH%d" % i, [128, 512], F32, es) for i in range(3)]),
                "pstB": Rot([c.ps("pstB%d" % i, [128, D], BF16, es) for i in range(2)]),
            }
            S = [c.sb("S%d" % g, [128, 512], F32, es) for g in range(2)]
            Sb = [c.sb("Sb%d" % g, [128, 512], BF16, es) for g in range(2)]

            c.dma("sp", ident[:], d["c_ident_bf"], writes=[ident])
            c.dma("sp", ind[:], d["c_ind"], writes=[ind])
            for a in range(2):
                for b_ in range(4):
                    c.dma("sp", gmt[a][b_][:], d["c_gla"][a * 4 + b_], writes=[gmt[a][b_]])
            c.dma("sp", gmix[:], d["norm_mix"][layer:layer + 1, :].partition_broadcast(128), writes=[gmix])
            c.dma("sp", gh[:, 0:512], d["ret_norm"][j:j + 1, :].partition_broadcast(128), writes=[gh])
            c.dma("sp", gh[:, 512:1024], d["hgrn_norm"][j:j + 1, :].partition_broadcast(128), writes=[gh])
            c.dma("sp", lgam[:], d["c_lgam"][0:1, :].partition_broadcast(128), writes=[lgam])
            c.dma("sp", hmask[:], d["c_hmask"][0:1, :].partition_broadcast(128), writes=[hmask])
            if j == 0:
                c.op("dve", lambda e: e.memset(lbr[:], 0.0), writes=[lbr])
            else:
                c.dma("sp", lbr[:], d["hgrn_lb_logits"][1:2, :].partition_broadcast(128), writes=[lbr])
                c.dma("sp", omlb[:], d["hgrn_lb_logits"][0:1, :].partition_broadcast(128), writes=[omlb])
                c.op("dve", lambda e: e.tensor_tensor(lbr[:], lbr[:], omlb[:], ALU.subtract), reads=[lbr, omlb], writes=[lbr])
                c.op("act", lambda e: e.activation(out=lbr[:], in_=lbr[:], func=AF.Sigmoid), reads=[lbr], writes=[lbr])
            c.op("dve", lambda e: e.tensor_scalar(omlb[:], lbr[:], -1.0, 1.0, ALU.mult, ALU.add), reads=[lbr], writes=[omlb])
            for k in range(8):
                c.dma("pool", win[:, k, :], d["even_w_in"][j, k * 128:(k + 1) * 128, :], writes=[win])
                c.dma("pool", wout[:, k, :], d["even_w_out"][j, k * 128:(k + 1) * 128, :], writes=[wout])

            def gates(zcol):
                c.op("act", lambda e: e.activation(out=fg[:], in_=p[:, zcol:zcol + 512], func=AF.Sigmoid), reads=[p], writes=[fg])
                c.op("pool", lambda e: e.tensor_tensor(fg[:], fg[:], omlb[:], ALU.mult), reads=[fg, omlb], writes=[fg])
                c.op("pool", lambda e: e.tensor_tensor(fg[:], fg[:], lbr[:], ALU.add), reads=[fg, lbr], writes=[fg])
                c.op("act", lambda e: e.activation(out=lf[:], in_=fg[:], func=AF.Ln), reads=[fg], writes=[lf])
                c.op("pool", lambda e: e.tensor_scalar(kk[:], fg[:], -1.0, 1.0, ALU.mult, ALU.add), reads=[fg], writes=[kk])

            for si, L in enumerate(self.seqs):
                nt = L // 128
                base = self.seq_off[si]
                ptb = [TB(None, "P%d" % t) for t in range(nt)]
                oftb = [TB(None, "OF%d" % t) for t in range(nt)]
                for g in range(2):
                    c.op("dve", lambda e: e.memset(S[g][:], 0.0), writes=[S[g]])
                    c.op("pool", lambda e: e.memset(Sb[g][:], 0.0), writes=[Sb[g]])
                for t in range(nt):
                    r0 = base + t * 128
                    xt = xts.next()
                    c.dma("sp", xt[:], x_in[r0:r0 + 128, :], reads=[self.xtb(r0)], writes=[xt])
                    c.dma("sp", rope[:], d["c_rope"][t * 128:(t + 1) * 128, :].rearrange("p (a b) -> p a b", a=4), writes=[rope])
                    self.rmsnorm_to_hT(c, xt, gmix, (hT, hT[:]), junk, ssq, hn, R["pstB"].next(), ident)
                    for n in range(9):
                        pa = R["psA"].next()
                        for k in range(8):
                            c.op("pe", lambda e: e.matmul(pa[:], hT[:, k, :], win[:, k, n * 512:(n + 1) * 512],
                                                          start=(k == 0), stop=(k == 7)), reads=[hT, win], writes=[pa])
                        c.op("act" if n % 2 == 0 else "dve",
                             lambda e: (e.copy(out=p[:, n * 512:(n + 1) * 512], in_=pa[:]) if n % 2 == 0
                                        else e.tensor_copy(out=p[:, n * 512:(n + 1) * 512], in_=pa[:])),
                             reads=[pa], writes=[p])
                    for col0, ci_, si_ in ((0, 0, 1), (512, 2, 3)):
                        v4 = p[:, col0:col0 + 512].rearrange("p (h two x) -> p h two x", h=4, two=2)
                        x1, x2 = v4[:, :, 0, :], v4[:, :, 1, :]
                        cb = rope[:, ci_:ci_ + 1, :].to_broadcast([128, 4, 64])
                        sb_ = rope[:, si_:si_ + 1, :].to_broadcast([128, 4, 64])
                        c.op("pool", lambda e: e.tensor_tensor(rt[0][:], x1, cb, ALU.mult), reads=[p, rope], writes=[rt[0]])
                        c.op("pool", lambda e: e.tensor_tensor(rt[1][:], x2, sb_, ALU.mult), reads=[p, rope], writes=[rt[1]])
                        c.op("pool", lambda e: e.tensor_tensor(rt[2][:], x1, sb_, ALU.mult), reads=[p, rope], writes=[rt[2]])
                        c.op("pool", lambda e: e.tensor_tensor(rt[3][:], x2, cb, ALU.mult), reads=[p, rope], writes=[rt[3]])
                        c.op("pool", lambda e: e.tensor_tensor(x1, rt[0][:], rt[1][:], ALU.subtract), reads=[rt[0], rt[1]], writes=[p])
                        c.op("pool", lambda e: e.tensor_tensor(x2, rt[2][:], rt[3][:], ALU.add), reads=[rt[2], rt[3]], writes=[p])
                    c.dma("sp", d["P"][t * 128:(t + 1) * 128, :], p[:], reads=[p], writes=[ptb[t]])
                    self.gla_tile(c, R, p[:, 0:512], p, p, p[:, 512:1024], p[:, 1024:1536], p, lgam, lgam[:], 0,
                                  S[0], Sb[0], oft, 0)
                    gates(2560)
                    self.gla_tile(c, R, p[:, 2048:2560], p, kk, kk[:], p[:, 3584:4096], p, lf, lf[:], 0,
                                  S[1], Sb[1], oft, 512)
                    c.dma("sp", d["OF"][t * 128:(t + 1) * 128, :], oft[:], reads=[oft], writes=[oftb[t]])
                for g in range(2):
                    c.op("dve", lambda e: e.memset(S[g][:], 0.0), writes=[S[g]])
                    c.op("pool", lambda e: e.memset(Sb[g][:], 0.0), writes=[Sb[g]])
                for t in range(nt - 1, -1, -1):
                    r0 = base + t * 128
                    xt = xts.next()
                    c.dma("sp", xt[:], x_in[r0:r0 + 128, :], reads=[self.xtb(r0)], writes=[xt])
                    c.dma("sp", p[:], d["P"][t * 128:(t + 1) * 128, :], reads=[ptb[t]], writes=[p])
                    c.dma("sp", oft[:], d["OF"][t * 128:(t + 1) * 128, :], reads=[oftb[t]], writes=[oft])
                    self.gla_tile(c, R, p[:, 0:512], p, p, p[:, 512:1024], p[:, 1024:1536], p, lgam, lgam[:], 1,
                                  S[0], Sb[0], ot, 0, add_tb=oft)
                    gates(3072)
                    self.gla_tile(c, R, p[:, 2048:2560], p, kk, kk[:], p[:, 3584:4096], p, lf, lf[:], 1,
                                  S[1], Sb[1], ot, 512, add_tb=oft)
                    o3 = ot[:].rearrange("p (h x) -> p h x", h=8)
                    c.op("dve", lambda e: e.reduce_sum(out=s1[:], in_=o3, axis=AX.X), reads=[ot], writes=[s1])
                    c.op("act", lambda e: e.activation(out=sq[:], in_=ot[:], func=AF.Square), reads=[ot], writes=[sq])
                    c.op("dve", lambda e: e.reduce_sum(out=s2[:], in_=sq[:].rearrange("p (h x) -> p h x", h=8), axis=AX.X),
                         reads=[sq], writes=[s2])
                    c.op("dve", lambda e: e.scalar_tensor_tensor(s1[:], s1[:], 1.0 / 128, hmask[:], ALU.mult, ALU.mult),
                         reads=[s1, hmask], writes=[s1])
                    c.op("dve", lambda e: e.tensor_scalar(s2[:], s2[:], 1.0 / 128, RMS_EPS, ALU.mult, ALU.add), reads=[s2], writes=[s2])
                    c.op("dve", lambda e: e.tensor_tensor(sq[:, 0:8], s1[:], s1[:], ALU.mult), reads=[s1], writes=[sq])
                    c.op("dve", lambda e: e.tensor_tensor(s2[:], s2[:], sq[:, 0:8], ALU.subtract), reads=[s2, sq], writes=[s2])
                    c.op("act", lambda e: e.activation(out=s2[:], in_=s2[:], func=AF.Sqrt), reads=[s2], writes=[s2])
                    c.op("dve", lambda e: e.reciprocal(out=s2[:], in_=s2[:]), reads=[s2], writes=[s2])
                    for h in range(8):
                        hs = slice(h * 128, (h + 1) * 128)
                        c.op("dve" if h % 2 == 0 else "pool",
                             lambda e: e.tensor_scalar(ot[:, hs], ot[:, hs], s1[:, h:h + 1], s2[:, h:h + 1], ALU.subtract, ALU.mult),
                             reads=[ot, s1, s2], writes=[ot])
                    c.op("pool", lambda e: e.tensor_tensor(ot[:], ot[:], gh[:], ALU.mult), reads=[ot, gh], writes=[ot])
                    c.op("act", lambda e: e.activation(out=sq[:, 0:512], in_=p[:, 1536:2048], func=AF.Silu), reads=[p], writes=[sq])
                    c.op("act", lambda e: e.activation(out=sq[:, 512:1024], in_=p[:, 4096:4608], func=AF.Silu), reads=[p], writes=[sq])
                    c.op("dve", lambda e: e.tensor_tensor(omix[:], ot[:], sq[:], ALU.mult), reads=[ot, sq], writes=[omix])
                    pt = R["pstB"].next()
                    for k in range(8):
                        c.op("pe", lambda e: e.transpose(out=pt[:, k * 128:(k + 1) * 128], in_=omix[:, k * 128:(k + 1) * 128],
                                                         identity=ident[:]), reads=[omix, ident], writes=[pt])
                    c.op("act", lambda e: e.copy(out=oT[:], in_=pt[:].rearrange("p (k t) -> p k t", k=8)), reads=[pt], writes=[oT])
                    for n in range(2):
                        py = R["psA"].next()
                        for k in range(8):
                            c.op("pe", lambda e: e.matmul(py[:], oT[:, k, :], wout[:, k, n * 512:(n + 1) * 512],
                                                          start=(k == 0), stop=(k == 7)), reads=[oT, wout], writes=[py])
                        c.op("dve", lambda e: e.tensor_tensor(xt[:, n * 512:(n + 1) * 512], py[:], xt[:, n * 512:(n + 1) * 512], ALU.add),
                             reads=[py, xt], writes=[xt])
                    c.dma("sp", d["X"][r0:r0 + 128, :], xt[:], reads=[xt], writes=[self.xtb(r0)])
            c.barrier()

    def gdn_tile(self, c, R, h, dr, qT, kT, ktm, vtm, sc, S, Sb, o_tb, add_tb=None):
        M1, M2, M4, MT = R["gm"][dr]
        hs = slice(h * 128, (h + 1) * 128)
        col = slice(h, h + 1)
        identf = R["identf"]
        W = R["w"]
        ula = W.next()
        c.op("pool", lambda e: e.tensor_scalar(ula[:], M1[:], sc["la"][:, col], None, ALU.mult), reads=[M1, sc["la"]], writes=[ula])
        pg, pgt = R["psH"].next(), R["psH"].next()
        c.op("pe", lambda e: e.matmul(pg[:, 0:128], ula[:], M4[:], start=True, stop=True), reads=[ula, M4], writes=[pg])
        c.op("pe", lambda e: e.matmul(pgt[:, 0:128], M4[:], ula[:], start=True, stop=True), reads=[ula, M4], writes=[pgt])
        dm, dtm = W.next(), R["dtm"].next()
        c.op("act", lambda e: e.activation(out=dm[:], in_=pg[:, 0:128], func=AF.Exp), reads=[pg], writes=[dm])
        c.op("act", lambda e: e.activation(out=dtm[:], in_=pgt[:, 0:128], func=AF.Exp), reads=[pgt], writes=[dtm])
        c.op("pool", lambda e: e.tensor_tensor(dm[:], dm[:], R["mstrict"][dr][:], ALU.mult), reads=[dm, R["mstrict"][dr]], writes=[dm])
        c.op("pool", lambda e: e.tensor_tensor(dtm[:], dtm[:], MT[:], ALU.mult), reads=[dtm, MT], writes=[dtm])
        if getattr(self, "gsub", 99) < 2:
            return
        pk = R["psH"].next()
        c.op("pe", lambda e: e.matmul(pk[:, 0:128], kT[:, h, :], kT[:, h, :], start=True, stop=True), reads=[kT], writes=[pk])
        if getattr(self, "gsub", 99) < 2.25:
            return
        X = W.next()
        c.op("dve", lambda e: e.scalar_tensor_tensor(X[:], pk[:, 0:128], sc["nbeta"][:, col], dm[:], ALU.mult, ALU.mult),
             reads=[pk, sc["nbeta"], dm], writes=[X])
        if getattr(self, "gsub", 99) < 2.5:
            return
        py = R["psH"].next()
        c.op("pe", lambda e: e.matmul(py[:, 0:128], X[:], identf[:], start=True, stop=True), reads=[X, identf], writes=[py])
        if getattr(self, "gsub", 99) < 2.75:
            return
        Y = W.next()
        RT = W.next()
        c.op("act", lambda e: e.copy(out=Y[:], in_=py[:, 0:128]), reads=[py], writes=[Y])
        c.op("pool", lambda e: e.tensor_tensor(RT[:], Y[:], identf[:], ALU.add), reads=[Y, identf], writes=[RT])
        if getattr(self, "gsub", 99) < 3:
            return
        ttb = R["ttb"].next()
        for m in range(5):
            px = R["psH"].next()
            c.op("pe", lambda e: e.matmul(px[:, 0:128], Y[:], X[:], start=True, stop=True), reads=[X, Y], writes=[px])
            if m < 4:
                c.op("pe", lambda e: e.matmul(px[:, 128:256], X[:], Y[:], start=True, stop=True), reads=[X, Y], writes=[px])
            X2 = W.next()
            c.op("act", lambda e: e.copy(out=X2[:], in_=px[:, 0:128]), reads=[px], writes=[X2])
            if m < 4:
                Y2 = W.next()
                c.op("act", lambda e: e.copy(out=Y2[:], in_=px[:, 128:256]), reads=[px], writes=[Y2])
            pr = R["psH"].next()
            c.op("pe", lambda e: e.matmul(pr[:, 0:128], X2[:], RT[:], start=True, stop=True), reads=[X2, RT], writes=[pr])
            if m < 4:
                RT2 = W.next()
                c.op("dve", lambda e: e.tensor_tensor(RT2[:], pr[:, 0:128], RT[:], ALU.add), reads=[pr, RT], writes=[RT2])
                RT = RT2
                X, Y = X2, Y2
            else:
                c.op("dve", lambda e: e.tensor_tensor(ttb[:], pr[:, 0:128], RT[:], ALU.add), reads=[pr, RT], writes=[ttb])
        if getattr(self, "gsub", 99) < 4:
            return
        vb_, kbg, kdec = R["vbeta"].next(), R["kbg"].next(), R["kdec"].next()
        c.op("pool", lambda e: e.tensor_scalar(vb_[:], vtm[:, hs], sc["beta"][:, col], None, ALU.mult), reads=[vtm, sc["beta"]], writes=[vb_])
        c.op("pool", lambda e: e.tensor_scalar(kbg[:], ktm[:, hs], sc["beg"][:, col], None, ALU.mult), reads=[ktm, sc["beg"]], writes=[kbg])
        c.op("pool", lambda e: e.tensor_scalar(kdec[:], ktm[:, hs], sc["egd"][:, col], None, ALU.mult), reads=[ktm, sc["egd"]], writes=[kdec])
        pu_ = R["psH"].next()
        c.op("pe", lambda e: e.matmul(pu_[:, 0:128], ttb[:], vb_[:], start=True, stop=True), reads=[ttb, vb_], writes=[pu_])
        c.op("pe", lambda e: e.matmul(pu_[:, 128:256], kbg[:], ttb[:], start=True, stop=True), reads=[ttb, kbg], writes=[pu_])
        u = R["u"].next()
        wT = R["wT"].next()
        c.op("act", lambda e: e.copy(out=u[:], in_=pu_[:, 0:128]), reads=[pu_], writes=[u])
        c.op("act", lambda e: e.copy(out=wT[:], in_=pu_[:, 128:256]), reads=[pu_], writes=[wT])
        pq = R["psH"].next()
        c.op("pe", lambda e: e.matmul(pq[:, 0:128], kT[:, h, :], qT[:, h, :], start=True, stop=True), reads=[kT, qT], writes=[pq])
        qkm = R["qkm"].next()
        c.op("dve", lambda e: e.tensor_tensor(qkm[:], pq[:, 0:128], dtm[:], ALU.mult), reads=[pq, dtm], writes=[qkm])
        if getattr(self, "gsub", 99) < 5:
            return
        vn = R["vn"].next()
        tmp = R["tmp"].next()
        order = (0, 1) if dr == 0 else (1, 0)
        for ci in order:
            rs = slice(ci * 64, (ci + 1) * 64)
            pa = R["psH"].next()
            c.op("pe", lambda e: e.matmul(pa[:, 0:128], wT[:], Sb[:, hs], start=True, stop=True), reads=[wT, Sb], writes=[pa])
            c.op("pe", lambda e: e.matmul(pa[:, 128:256], qT[:, h, :], Sb[:, hs], start=True, stop=True), reads=[qT, Sb], writes=[pa])
            c.op("dve", lambda e: e.tensor_tensor(vn[rs, :], u[rs, :], pa[rs, 0:128], ALU.subtract), reads=[u, pa], writes=[vn])
            c.op("dve", lambda e: e.tensor_scalar(tmp[rs, :], pa[rs, 128:256], sc["eg"][rs, col], None, ALU.mult),
                 reads=[pa, sc["eg"]], writes=[tmp])
            ps_ = R["psH"].next()
            c.op("pe", lambda e: e.matmul(ps_[:, 0:128], kdec[rs, :], vn[rs, :], start=True, stop=True), reads=[kdec, vn], writes=[ps_])
            c.op("dve", lambda e: e.scalar_tensor_tensor(S[:, hs], S[:, hs], sc["egl"][ci][:, col], ps_[:, 0:128], ALU.mult, ALU.add),
                 reads=[S, sc["egl"][ci], ps_], writes=[S])
            c.op("act", lambda e: e.copy(out=Sb[:, hs], in_=S[:, hs]), reads=[S], writes=[Sb])
        if getattr(self, "gsub", 99) < 6:
            return
        p2 = R["psH"].next()
        c.op("pe", lambda e: e.matmul(p2[:, 0:128], qkm[:], vn[:], start=True, stop=True), reads=[qkm, vn], writes=[p2])
        if add_tb is None:
            c.op("dve", lambda e: e.tensor_tensor(o_tb[:, hs], p2[:, 0:128], tmp[:], ALU.add), reads=[p2, tmp], writes=[o_tb])
        else:
            c.op("pool", lambda e: e.tensor_tensor(tmp[:], tmp[:], add_tb[:, hs], ALU.add), reads=[tmp, add_tb], writes=[tmp])
            c.op("dve", lambda e: e.tensor_tensor(o_tb[:, hs], p2[:, 0:128], tmp[:], ALU.add), reads=[p2, tmp], writes=[o_tb])

    def gdn_scalars(self, c, R, dr, ab, sc):
        M1, M2, M4, MT = R["gm"][dr]
        la = ab[:, dr * 8:(dr + 1) * 8]
        be = ab[:, 16 + dr * 8:16 + (dr + 1) * 8]
        ps = R["psA"].next()
        c.op("pe", lambda e: e.matmul(ps[:, 0:8], M1[:], la, start=True, stop=True), reads=[M1, ab], writes=[ps])
        c.op("pe", lambda e: e.matmul(ps[:, 8:16], M4[:], la, start=True, stop=True), reads=[M4, ab], writes=[ps])
        c.op("pe", lambda e: e.matmul(ps[:, 16:24], R["sel"][0][:], la, start=True, stop=True), reads=[R["sel"][0], ab], writes=[ps])
        c.op("pe", lambda e: e.matmul(ps[:, 24:32], R["sel"][1][:], la, start=True, stop=True), reads=[R["sel"][1], ab], writes=[ps])
        ex = sc["ex"]
        c.op("act", lambda e: e.activation(out=ex[:], in_=ps[:, 0:32], func=AF.Exp), reads=[ps], writes=[ex])
        c.op("dve", lambda e: e.tensor_copy(out=sc["la"][:], in_=la), reads=[ab], writes=[sc["la"]])
        c.op("dve", lambda e: e.tensor_copy(out=sc["beta"][:], in_=be), reads=[ab], writes=[sc["beta"]])
        c.op("dve", lambda e: e.tensor_scalar(sc["nbeta"][:], be, -1.0, None, ALU.mult), reads=[ab], writes=[sc["nbeta"]])
        c.op("dve", lambda e: e.tensor_tensor(sc["beg"][:], be, ex[:, 0:8], ALU.mult), reads=[ab, ex], writes=[sc["beg"]])
        sc["eg"] = _View(ex, (slice(None), slice(0, 8)))
        sc["egd"] = _View(ex, (slice(None), slice(8, 16)))
        sc["egl"] = [_View(ex, (slice(None), slice(16, 24))), _View(ex, (slice(None), slice(24, 32)))]

    def phase_odd(self, c, j, layer, x_in):
        nc, d = c.nc, self.d
        Lm = self.Lmax
        with contextlib.ExitStack() as es:
            win = c.sb("win", [128, 8, ODD_IN], BF16, es)
            wout = c.sb("wout", [128, 8, D], BF16, es)
            gmix = c.sb("gmix", [128, D], F32, es)
            gh = c.sb("gh", [128, D], F32, es)
            ident = c.sb("identb", [128, 128], BF16, es)
            identf = c.sb("identf", [128, 128], F32, es)
            ones = c.sb("onesb", [128, 128], BF16, es)
            gmt = [[c.sb("gm%d%d" % (a, b_), [128, 128], F32, es) for b_ in range(4)] for a in range(2)]
            mstrict = [c.sb("mstr%d" % a, [128, 128], F32, es) for a in range(2)]
            sel = [c.sb("sel%d" % a, [128, 128], F32, es) for a in range(2)]
            cw = c.sb("cw", [128, 24, 5], F32, es)
            dtb = c.sb("dtb", [128, 16], F32, es)
            nea = c.sb("nea", [128, 16], F32, es)
            zer = c.sb("zer", [128, 16], BF16, es)
            ab_all = c.sb("ab_all", [128, Lm // 128, 32], F32, es)
            xts = Rot([c.sb("xt%d" % i, [128, D], F32, es) for i in range(2)])
            junk = c.sb("junk", [128, D], BF16, es)
            sq = c.sb("sq", [128, D], F32, es)
            ssq = c.sb("ssq", [128, 1], F32, es)
            s2 = c.sb("s2", [128, 8], F32, es)
            hn = c.sb("hn", [128, D], BF16, es)
            hT = c.sb("hT", [128, 8, 128], BF16, es)
            hTb = c.sb("hTb", [128, 8, 260], BF16, es)
            gt = c.sb("gt", [128, D], F32, es)
            raw = Rot([c.sb("raw%d" % i, [128, 260], F32, es) for i in range(2)])
            acc = Rot([c.sb("acc%d" % i, [128, 256], F32, es) for i in range(2)])
            sqb = c.sb("sqb", [128, 256], BF16, es)
            rst = c.sb("rst", [128, 256], F32, es)
            qTb = c.sb("qTb", [128, 8, 256], BF16, es)
            kTb = c.sb("kTb", [128, 8, 256], BF16, es)
            vTb = c.sb("vTb", [128, 8, 256], BF16, es)
            qTt = c.sb("qTt", [128, 8, 128], BF16, es)
            kTt = c.sb("kTt", [128, 8, 128], BF16, es)
            vtm = c.sb("vtm", [128, D], BF16, es)
            ktm = c.sb("ktm", [128, D], BF16, es)
            oft = c.sb("oft", [128, D], F32, es)
            ot = c.sb("ot", [128, D], F32, es)
            omix = c.sb("omix", [128, D], BF16, es)
            oT = c.sb("oT", [128, 8, 128], BF16, es)
            abt = c.sb("abt", [128, 32], F32, es)
            sc = {k: c.sb("sc_" + k, [128, 8], F32, es) for k in ("la", "beta", "nbeta", "beg")}
            sc["ex"] = c.sb("sc_ex", [128, 32], F32, es)
            R = {
                "gm": gmt, "ident": ident, "identf": identf, "mstrict": mstrict, "sel": sel,
                "w": Rot([c.sb("w%d" % i, [128, 128], F32, es) for i in range(14)]),
                "ttb": Rot([c.sb("ttb%d" % i, [128, 128], BF16, es) for i in range(2)]),
                "dtm": Rot([c.sb("dtm%d" % i, [128, 128], F32, es) for i in range(2)]),
                "u": Rot([c.sb("u%d" % i, [128, 128], F32, es) for i in range(2)]),
                "tmp": Rot([c.sb("tmp%d" % i, [128, 128], F32, es) for i in range(2)]),
                "vbeta": Rot([c.sb("vbe%d" % i, [128, 128], BF16, es) for i in range(2)]),
                "kbg": Rot([c.sb("kbg%d" % i, [128, 128], BF16, es) for i in range(2)]),
                "kdec": Rot([c.sb("kdec%d" % i, [128, 128], BF16, es) for i in range(2)]),
                "wT": Rot([c.sb("wT%d" % i, [128, 128], BF16, es) for i in range(2)]),
                "qkm": Rot([c.sb("qkm%d" % i, [128, 128], BF16, es) for i in range(2)]),
                "vn": Rot([c.sb("vn%d" % i, [128, 128], BF16, es) for i in range(2)]),
                "psA": Rot([c.ps("psA%d" % i, [128, 512], F32, es) for i in range(2)]),
                "psH": Rot([c.ps("psH%d" % i, [128, 512], F32, es) for i in range(4)]),
                "pstB": Rot([c.ps("pstB%d" % i, [128, D], BF16, es) for i in range(2)]),
            }
            S = c.sb("S", [128, D], F32, es)
            Sb = c.sb("Sb", [128, D], BF16, es)

            c.dma("sp", ident[:], d["c_ident_bf"], writes=[ident])
            c.dma("sp", identf[:], d["c_ident_f"], writes=[identf])
            c.dma("sp", ones[:], d["c_ones_bf"], writes=[ones])
            for a in range(2):
                for b_ in range(4):
                    c.dma("sp", gmt[a][b_][:], d["c_gla"][a * 4 + b_], writes=[gmt[a][b_]])
                c.dma("sp", mstrict[a][:], d["c_mstrict"][a], writes=[mstrict[a]])
                c.dma("sp", sel[a][:], d["c_sel"][a], writes=[sel[a]])
            c.dma("sp", gmix[:], d["norm_mix"][layer:layer + 1, :].partition_broadcast(128), writes=[gmix])
            for h in range(8):
                c.dma("sp", gh[:, h * 128:(h + 1) * 128], d["gdn_norm"][j:j + 1, :].partition_broadcast(128), writes=[gh])
            c.dma("sp", dtb[:], d["gdn_dt_bias"][j:j + 1, :].partition_broadcast(128), writes=[dtb])
            c.dma("sp", nea[:], d["gdn_a_log"][j:j + 1, :].partition_broadcast(128), writes=[nea])
            c.op("act", lambda e: e.activation(out=nea[:], in_=nea[:], func=AF.Exp), reads=[nea], writes=[nea])
            c.op("dve", lambda e: e.tensor_scalar(nea[:], nea[:], -1.0, None, ALU.mult), reads=[nea], writes=[nea])
            c.op("dve", lambda e: e.memset(zer[:], 0.0), writes=[zer])
            for w in range(5):
                c.dma("sp", cw[:, :, w], d["gdn_conv"][j, w, :].rearrange("(f p) -> p f", p=128), writes=[cw], slow=True)
            for k in range(8):
                c.dma("pool", win[:, k, :], d["gdn_w_in"][j, k * 128:(k + 1) * 128, :], writes=[win])
                c.dma("pool", wout[:, k, :], d["gdn_w_out"][j, k * 128:(k + 1) * 128, :], writes=[wout])

            HT = d["HT"].rearrange("p (k t) -> p k t", k=8)
            QT = d["QT"].rearrange("p (k t) -> p k t", k=8)
            KT = d["KT"].rearrange("p (k t) -> p k t", k=8)

            for si, L in enumerate(self.seqs):
                nt = L // 128
                base = self.seq_off[si]
                httb = TB(None, "HT")
                gtb = [TB(None, "G%d" % t) for t in range(nt)]
                qktb = [TB(None, "QK%d" % t) for t in range(nt // 2)]
                vmtb = [TB(None, "VM%d" % t) for t in range(nt)]
                oftb = [TB(None, "OF%d" % t) for t in range(nt)]
                c.dma("sp", HT[:, :, 0:2], zer[:, 0:16].rearrange("p (k t) -> p k t", k=8), reads=[zer], writes=[httb])
                c.dma("sp", HT[:, :, L + 2:L + 4], zer[:, 0:16].rearrange("p (k t) -> p k t", k=8), reads=[zer], writes=[httb])
                dbg = getattr(self, "dbg", 99)
                for t in range(nt if dbg >= 1 else 0):
                    r0 = base + t * 128
                    xt = xts.next()
                    c.dma("sp", xt[:], x_in[r0:r0 + 128, :], reads=[self.xtb(r0)], writes=[xt])
                    self.rmsnorm_to_hT(c, xt, gmix, (hT, hT[:]), junk, ssq, hn, R["pstB"].next(), ident)
                    c.dma("sp", HT[:, :, 2 + t * 128:2 + (t + 1) * 128], hT[:], reads=[hT], writes=[httb])
                    for n in range(2):
                        pa = R["psA"].next()
                        for k in range(8):
                            c.op("pe", lambda e: e.matmul(pa[:], hT[:, k, :], win[:, k, 3072 + n * 512:3072 + (n + 1) * 512],
                                                          start=(k == 0), stop=(k == 7)), reads=[hT, win], writes=[pa])
                        c.op("act", lambda e: e.copy(out=gt[:, n * 512:(n + 1) * 512], in_=pa[:]), reads=[pa], writes=[gt])
                    c.dma("sp", d["P"][t * 128:(t + 1) * 128, 0:D], gt[:], reads=[gt], writes=[gtb[t]])
                    pa = R["psA"].next()
                    for k in range(8):
                        c.op("pe", lambda e: e.matmul(pa[:, 0:32], hT[:, k, :], win[:, k, 4096:4128],
                                                      start=(k == 0), stop=(k == 7)), reads=[hT, win], writes=[pa])
                    c.op("dve", lambda e: e.tensor_tensor(abt[:, 0:16], pa[:, 0:16], dtb[:], ALU.add), reads=[pa, dtb], writes=[abt])
                    c.op("act", lambda e: e.activation(out=abt[:, 0:16], in_=abt[:, 0:16], func=AF.Exp), reads=[abt], writes=[abt])
                    c.op("act", lambda e: e.activation(out=abt[:, 0:16], in_=abt[:, 0:16], func=AF.Ln, bias=1.0), reads=[abt], writes=[abt])
                    c.op("dve", lambda e: e.tensor_tensor(ab_all[:, t, 0:16], abt[:, 0:16], nea[:], ALU.mult), reads=[abt, nea], writes=[ab_all])
                    c.op("dve", lambda e: e.tensor_copy(out=abt[:, 16:32], in_=pa[:, 16:32]), reads=[pa], writes=[abt])
                    c.op("act", lambda e: e.activation(out=ab_all[:, t, 16:32], in_=abt[:, 16:32], func=AF.Sigmoid), reads=[abt], writes=[ab_all])
                c.op("dve", lambda e: e.memset(S[:], 0.0), writes=[S])
                c.op("pool", lambda e: e.memset(Sb[:], 0.0), writes=[Sb])
                for blk in range(nt // 2 if dbg >= 2 else 0):
                    b0 = blk * 256
                    c.dma("sp", hTb[:], HT[:, :, b0:b0 + 260], reads=[httb], writes=[hTb])
                    for fb in range(24):
                        pa = R["psA"].next()
                        for k in range(8):
                            c.op("pe", lambda e: e.matmul(pa[:, 0:260], win[:, k, fb * 128:(fb + 1) * 128], hTb[:, k, :],
                                                          start=(k == 0), stop=(k == 7)), reads=[hTb, win], writes=[pa])
                        rw, ac = raw.next(), acc.next()
                        c.op("act", lambda e: e.copy(out=rw[:], in_=pa[:, 0:260]), reads=[pa], writes=[rw])
                        c.op("dve", lambda e: e.tensor_scalar(ac[:], rw[:, 0:256], cw[:, fb, 0:1], None, ALU.mult), reads=[rw, cw], writes=[ac])
                        for w in range(1, 5):
                            c.op("dve", lambda e: e.scalar_tensor_tensor(ac[:], rw[:, w:w + 256], cw[:, fb, w:w + 1], ac[:], ALU.mult, ALU.add),
                                 reads=[rw, cw, ac], writes=[ac])
                        hh = fb % 8
                        if fb >= 16:
                            c.op("act", lambda e: e.activation(out=vTb[:, hh, :], in_=ac[:], func=AF.Silu), reads=[ac], writes=[vTb])
                            continue
                        c.op("act", lambda e: e.activation(out=ac[:], in_=ac[:], func=AF.Silu), reads=[ac], writes=[ac])
                        c.op("act", lambda e: e.activation(out=sqb[:], in_=ac[:], func=AF.Square), reads=[ac], writes=[sqb])
                        pn = R["psA"].next()
                        c.op("pe", lambda e: e.matmul(pn[:, 0:256], ones[:], sqb[:], start=True, stop=True), reads=[ones, sqb], writes=[pn])
                        c.op("act", lambda e: e.activation(out=rst[:], in_=pn[:, 0:256], func=AF.Ln, bias=RMS_EPS), reads=[pn], writes=[rst])
                        c.op("act", lambda e: e.activation(out=rst[:], in_=rst[:], func=AF.Exp, scale=-0.5), reads=[rst], writes=[rst])
                        dst = qTb if fb < 8 else kTb
                        if fb < 8:
                            c.op("dve", lambda e: e.scalar_tensor_tensor(dst[:, hh, :], ac[:], 128 ** -0.5, rst[:], ALU.mult, ALU.mult),
                                 reads=[ac, rst], writes=[dst])
                        else:
                            c.op("pool", lambda e: e.tensor_tensor(dst[:, hh, :], ac[:], rst[:], ALU.mult), reads=[ac, rst], writes=[dst])
                    c.dma("sp", QT[:, :, b0:b0 + 256], qTb[:], reads=[qTb], writes=[qktb[blk]])
                    c.dma("sp", KT[:, :, b0:b0 + 256], kTb[:], reads=[kTb], writes=[qktb[blk]])
                    for tt in range(2):
                        t = blk * 2 + tt
                        cs_ = slice(tt * 128, (tt + 1) * 128)
                        for src, dstm in ((vTb, vtm), (kTb, ktm)):
                            pt = R["pstB"].next()
                            for h in range(8):
                                c.op("pe", lambda e: e.transpose(out=pt[:, h * 128:(h + 1) * 128], in_=src[:, h, cs_], identity=ident[:]),
                                     reads=[src, ident], writes=[pt])
                            c.op("act", lambda e: e.copy(out=dstm[:], in_=pt[:]), reads=[pt], writes=[dstm])
                        c.dma("sp", d["VM"][t * 128:(t + 1) * 128, :], vtm[:], reads=[vtm], writes=[vmtb[t]])
                        c.op("pool", lambda e: e.tensor_copy(out=qTt[:], in_=qTb[:, :, cs_]), reads=[qTb], writes=[qTt])
                        c.op("pool", lambda e: e.tensor_copy(out=kTt[:], in_=kTb[:, :, cs_]), reads=[kTb], writes=[kTt])
                        abv = _View(ab_all, (slice(None), t, slice(None)))
                        self.gdn_scalars(c, R, 0, abv, sc)
                        for h in range(8 if dbg >= 3 else 0):
                            self.gdn_tile(c, R, h, 0, qTt, kTt, ktm, vtm, sc, S, Sb, oft)
                        c.dma("sp", d["OF"][t * 128:(t + 1) * 128, :], oft[:], reads=[oft], writes=[oftb[t]])
                c.op("dve", lambda e: e.memset(S[:], 0.0), writes=[S])
                c.op("pool", lambda e: e.memset(Sb[:], 0.0), writes=[Sb])
                for t in (range(nt - 1, -1, -1) if dbg >= 4 else []):
                    r0 = base + t * 128
                    xt = xts.next()
                    c.dma("sp", xt[:], x_in[r0:r0 + 128, :], reads=[self.xtb(r0)], writes=[xt])
                    c.dma("sp", qTt[:], QT[:, :, t * 128:(t + 1) * 128], reads=[qktb[t // 2]], writes=[qTt])
                    c.dma("sp", kTt[:], KT[:, :, t * 128:(t + 1) * 128], reads=[qktb[t // 2]], writes=[kTt])
                    c.dma("sp", vtm[:], d["VM"][t * 128:(t + 1) * 128, :], reads=[vmtb[t]], writes=[vtm])
                    c.dma("sp", oft[:], d["OF"][t * 128:(t + 1) * 128, :], reads=[oftb[t]], writes=[oft])
                    c.dma("sp", gt[:], d["P"][t * 128:(t + 1) * 128, 0:D], reads=[gtb[t]], writes=[gt])
                    pt = R["pstB"].next()
                    for h in range(8):
                        c.op("pe", lambda e: e.transpose(out=pt[:, h * 128:(h + 1) * 128], in_=kTt[:, h, :], identity=ident[:]),
                             reads=[kTt, ident], writes=[pt])
                    c.op("act", lambda e: e.copy(out=ktm[:], in_=pt[:]), reads=[pt], writes=[ktm])
                    abv = _View(ab_all, (slice(None), t, slice(None)))
                    self.gdn_scalars(c, R, 1, abv, sc)
                    for h in range(8):
                        self.gdn_tile(c, R, h, 1, qTt, kTt, ktm, vtm, sc, S, Sb, ot, add_tb=oft)
                    c.op("act", lambda e: e.activation(out=sq[:], in_=ot[:], func=AF.Square), reads=[ot], writes=[sq])
                    c.op("dve", lambda e: e.reduce_sum(out=s2[:], in_=sq[:].rearrange("p (h x) -> p h x", h=8), axis=AX.X),
                         reads=[sq], writes=[s2])
                    c.op("dve", lambda e: e.tensor_scalar(s2[:], s2[:], 1.0 / 128, RMS_EPS, ALU.mult, ALU.add), reads=[s2], writes=[s2])
                    c.op("act", lambda e: e.activation(out=s2[:], in_=s2[:], func=AF.Sqrt), reads=[s2], writes=[s2])
                    c.op("dve", lambda e: e.reciprocal(out=s2[:], in_=s2[:]), reads=[s2], writes=[s2])
                    for h in range(8):
                        hs = slice(h * 128, (h + 1) * 128)
                        c.op("dve" if h % 2 == 0 else "pool",
                             lambda e: e.tensor_scalar(ot[:, hs], ot[:, hs], s2[:, h:h + 1], None, ALU.mult), reads=[ot, s2], writes=[ot])
                    c.op("pool", lambda e: e.tensor_tensor(ot[:], ot[:], gh[:], ALU.mult), reads=[ot, gh], writes=[ot])
                    c.op("act", lambda e: e.activation(out=sq[:], in_=gt[:], func=AF.Silu), reads=[gt], writes=[sq])
                    c.op("dve", lambda e: e.tensor_tensor(omix[:], ot[:], sq[:], ALU.mult), reads=[ot, sq], writes=[omix])
                    pt = R["pstB"].next()
                    for k in range(8):
                        c.op("pe", lambda e: e.transpose(out=pt[:, k * 128:(k + 1) * 128], in_=omix[:, k * 128:(k + 1) * 128],
                                                         identity=ident[:]), reads=[omix, ident], writes=[pt])
                    c.op("act", lambda e: e.copy(out=oT[:], in_=pt[:].rearrange("p (k t) -> p k t", k=8)), reads=[pt], writes=[oT])
                    for n in range(2):
                        py = R["psA"].next()
                        for k in range(8):
                            c.op("pe", lambda e: e.matmul(py[:], oT[:, k, :], wout[:, k, n * 512:(n + 1) * 512],
                                                          start=(k == 0), stop=(k == 7)), reads=[oT, wout], writes=[py])
                        c.op("dve", lambda e: e.tensor_tensor(xt[:, n * 512:(n + 1) * 512], py[:], xt[:, n * 512:(n + 1) * 512], ALU.add),
                             reads=[py, xt], writes=[xt])
                    c.dma("sp", d["X"][r0:r0 + 128, :], xt[:], reads=[xt], writes=[self.xtb(r0)])
            c.barrier()

    def phase_copy(self, c, x_in, to_y=False):
        d = self.d
        with contextlib.ExitStack() as es:
            xts = Rot([c.sb("cx%d" % i, [128, D], F32, es) for i in range(4)])
            for r0 in range(0, self.T, 128):
                xt = xts.next()
                c.dma("sp", xt[:], x_in[r0:r0 + 128, :], reads=[self.xtb(r0)], writes=[xt])
                if to_y:
                    c.dma("sp", d["y"][r0:r0 + 128, :], xt[:], reads=[xt], writes=[self.ytb(r0)])
                else:
                    c.dma("sp", d["X"][r0:r0 + 128, :], xt[:], reads=[xt], writes=[self.xtb(r0)])
            c.barrier()

    def build(self):
        nc = bass.Bass("TRN2", target_bir_lowering=False)
        self.declare(nc)
        with contextlib.ExitStack() as es:
            c = Ctx(nc, es)
            self.c = c
            self._xtb = [TB(None, "X%d" % i) for i in range(self.T // 128)]
            self._ytb = [TB(None, "Y%d" % i) for i in range(self.T // 128)]
            first = True
            for layer in range(self.depth):
                last = layer == self.depth - 1
                if self.do_mixer:
                    from_x = self.d["x"] if first else self.d["X"]
                    kind, jj = self.kinds[layer]
                    if kind == "even":
                        self.phase_even(c, jj, layer, from_x)
                    else:
                        self.phase_odd(c, jj, layer, from_x)
                    first = False
                if self.do_xattn:
                    self.phase_xattn(c, layer, self.d["x"] if first else self.d["X"])
                    first = False
                if self.do_ffn:
                    self.phase_ffn(c, layer, self.d["x"] if first else self.d["X"], last)
                    first = False
            c.barrier()
            self.stats = (c.n_inst, c.n_wait)
        return nc


def gla_masks(direction):
    t = np.arange(128)
    same = (t[:, None] // 64) == (t[None, :] // 64)
    if direction == 0:
        le = t[:, None] <= t[None, :]
        mid = 64 * (t // 64) + 31
        lemid = t[:, None] <= mid[None, :]
        gt = t[:, None] > t[None, :]
    else:
        le = t[:, None] >= t[None, :]
        mid = 64 * (t // 64) + 32
        lemid = t[:, None] >= mid[None, :]
        gt = t[:, None] < t[None, :]
    M1 = (same & le).astype(np.float32)
    M2 = (same * (le.astype(np.float32) - lemid.astype(np.float32))).astype(np.float32)
    M4 = (same & gt).astype(np.float32)
    MT = (same & le).astype(np.float32)
    return M1, M2, M4, MT


def host_constants(Lmax):
    cst = {}
    cst["c_ident_bf"] = np.eye(128, dtype=np.float32).astype(ml_dtypes.bfloat16)
    cst["c_ones_bf"] = np.ones((128, 128), np.float32).astype(ml_dtypes.bfloat16)
    cst["c_ident_f"] = np.eye(128, dtype=np.float32)
    gm = np.zeros((2, 4, 128, 128), np.float32)
    for dr in range(2):
        for i, m in enumerate(gla_masks(dr)):
            gm[dr, i] = m
    cst["c_gla"] = gm.reshape(8, 128, 128)
    ind = np.zeros((128, 2), np.float32)
    ind[:64, 0] = 1.0
    ind[64:, 1] = 1.0
    cst["c_ind"] = ind
    inv = (1.0 / (np.float32(10000.0) ** (np.arange(64, dtype=np.float32) / np.float32(64)))).astype(np.float32)
    ang = (np.arange(Lmax, dtype=np.float32)[:, None] * inv[None, :]).astype(np.float32)
    cs = np.zeros((Lmax, 4, 64), np.float32)
    cs[:, 0] = np.cos(ang)
    cs[:, 1] = np.sin(ang)
    cs[:, 2] = np.cos(ang) * np.float32(128 ** -0.5)
    cs[:, 3] = np.sin(ang) * np.float32(128 ** -0.5)
    cst["c_rope"] = cs.reshape(Lmax, 256)
    lg = np.log1p(-np.exp2(-5.0 - np.arange(4, dtype=np.float32))).astype(np.float32)
    cst["c_lgam"] = np.repeat(lg, 128)[None, :].astype(np.float32)
    t = np.arange(128)
    same = (t[:, None] // 64) == (t[None, :] // 64)
    ms = np.zeros((2, 128, 128), np.float32)
    ms[0] = same & (t[None, :] < t[:, None])
    ms[1] = same & (t[None, :] > t[:, None])
    cst["c_mstrict"] = ms
    sl = np.zeros((2, 128, 128), np.float32)
    sl[0, :64, :] = 1.0
    sl[1, 64:, :] = 1.0
    cst["c_sel"] = sl
    hm = np.zeros((1, 8), np.float32)
    hm[0, :4] = 1.0
    cst["c_hmask"] = hm
    return cst


_CACHE = {}


def _get_program(seqs):
    key = tuple(seqs)
    if key not in _CACHE:
        b = Builder(seqs)
        nc = b.build()
        _CACHE[key] = (b, nc)
    return _CACHE[key]


def kernel(**inputs):
    inp = {k: np.asarray(v) for k, v in inputs.items()}
    seqs = [2048, 2048, 4096, 4096]
    b, nc = _get_program(seqs)
    cst = host_constants(max(seqs))
    shared = {}
    for k in ("norm_mix", "norm_xq", "norm_mem", "norm_ffn", "even_w_in", "even_w_out", "hgrn_lb_logits",
              "gdn_w_in", "gdn_conv", "gdn_norm", "gdn_w_out", "xa_w_q", "xa_w_kv", "xa_w_o", "ffn_w_gu", "ffn_w_down"):
        shared[k] = np.ascontiguousarray(inp[k], dtype=np.float32)
    shared["norm_final"] = np.ascontiguousarray(inp["norm_final"].reshape(1, D), dtype=np.float32)
    shared["ret_norm"] = np.ascontiguousarray(inp["ret_norm"].reshape(2, 512), dtype=np.float32)
    shared["hgrn_norm"] = np.ascontiguousarray(inp["hgrn_norm"].reshape(2, 512), dtype=np.float32)
    shared["gdn_a_log"] = np.ascontiguousarray(inp["gdn_a_log"].reshape(2, 16), dtype=np.float32)
    shared["gdn_dt_bias"] = np.ascontiguousarray(inp["gdn_dt_bias"].reshape(2, 16), dtype=np.float32)
    shared.update(cst)
    in_maps = []
    for ci in range(NCORES):
        m = dict(shared)
        xp = inp["x_prompt"][2 * ci:2 * ci + 2].reshape(-1, D)
        xs = inp["x_sample"][2 * ci:2 * ci + 2].reshape(-1, D)
        m["x"] = np.ascontiguousarray(np.concatenate([xp, xs], axis=0), dtype=np.float32)
        mp = inp["mem_prompt"][2 * ci:2 * ci + 2].reshape(-1, D)
        ms = inp["mem_sample"][2 * ci:2 * ci + 2].reshape(-1, D)
        m["mem"] = np.ascontiguousarray(np.concatenate([mp, ms], axis=0), dtype=np.float32)
        in_maps.append(m)
    res = run_bass_kernel_spmd(nc, in_maps, core_ids=list(range(NCORES)))
    yp = np.empty((16, 2048, D), np.float32)
    ys = np.empty((16, 4096, D), np.float32)
    for ci in range(NCORES):
        y = np.asarray(res.results[ci]["y"])
        yp[2 * ci:2 * ci + 2] = y[:4096].reshape(2, 2048, D)
        ys[2 * ci:2 * ci + 2] = y[4096:].reshape(2, 4096, D)
    return (yp, ys)
```

```python
import contextlib
import math
import numpy as np
import ml_dtypes
import concourse.bass as bass
import concourse.mybir as mybir
from concourse.bass_utils import run_bass_kernel_spmd

F32 = mybir.dt.float32
BF16 = mybir.dt.bfloat16
AF = mybir.ActivationFunctionType
ALU = mybir.AluOpType
AX = mybir.AxisListType

D = 1024
DEPTH = 4
N_MEM = 256
RMS_EPS = 1e-6
D_FF = 2816
EVEN_IN = 4608
ODD_IN = 4128
XA_SCALE = 256 ** -0.5
NCORES = 8


class TB:
    __slots__ = ("t", "w", "r", "name")

    def __init__(self, t, name=""):
        self.t = t
        self.w = None
        self.r = {}
        self.name = name

    def __getitem__(self, k):
        return self.t[k]


class Ctx:
    def __init__(self, nc, es, n_dma_sems=20):
        self.nc = nc
        self.es = es
        self.engs = {"pe": nc.tensor, "act": nc.scalar, "dve": nc.vector, "pool": nc.gpsimd, "sp": nc.sync}
        self.sem = {}
        self.cnt = {}
        self.seen = {k: {} for k in self.engs}
        for k in self.engs:
            self.sem[k] = es.enter_context(nc.semaphore("s_" + k))
            self.cnt[k] = 0
        self.dsem = {}
        self.dval = {}
        self.drot = {}
        for q in ("sp", "pool", "act"):
            n = n_dma_sems if q != "act" else 8
            self.dsem[q] = [es.enter_context(nc.semaphore("d_%s_%d" % (q, i))) for i in range(n)]
            self.dval[q] = [0] * n
            self.drot[q] = 0
        self.n_inst = 0
        self.n_wait = 0

    def sb(self, name, shape, dt, es=None):
        es = es or self.es
        self.uid = getattr(self, "uid", 0) + 1
        t = es.enter_context(self.nc.sbuf_tensor("%s_%d" % (name, self.uid), list(shape), dt))
        return TB(t, name)

    def ps(self, name, shape, dt, es=None):
        es = es or self.es
        self.uid = getattr(self, "uid", 0) + 1
        t = es.enter_context(self.nc.psum_tensor("%s_%d" % (name, self.uid), list(shape), dt))
        return TB(t, name)

    def _deps(self, engname, reads, writes):
        need = {}

        def add(ev, raw):
            if ev is None:
                return
            s, v, e = ev
            if e == engname and not raw:
                return
            if e == engname and engname == "pe":
                return
            key = id(s)
            if key not in need or need[key][1] < v:
                need[key] = (s, v)

        for b in reads:
            add(b.w, True)
        for b in writes:
            add(b.w, False)
            for ev in b.r.values():
                add(ev, False)
        return need

    def _emit_waits(self, engname, need):
        eng = self.engs[engname]
        seen = self.seen[engname]
        for key, (s, v) in need.items():
            if seen.get(key, 0) >= v:
                continue
            eng.wait_ge(s, v)
            seen[key] = v
            self.n_wait += 1

    def _commit(self, ev, reads, writes):
        for b in writes:
            b.w = ev
            b.r = {}
        for b in reads:
            if b in writes:
                continue
            b.r[ev[2]] = ev

    def op(self, engname, fn, reads=(), writes=()):
        need = self._deps(engname, reads, writes)
        self._emit_waits(engname, need)
        ins = fn(self.engs[engname])
        self.cnt[engname] += 1
        ins.then_inc(self.sem[engname], 1)
        ev = (self.sem[engname], self.cnt[engname], engname)
        self._commit(ev, reads, writes)
        self.n_inst += 1
        return ins

    def dma(self, q, out, in_, reads=(), writes=(), slow=False):
        need = self._deps("dma_" + q, reads, writes)
        i = self.drot[q]
        self.drot[q] = (i + 1) % len(self.dsem[q])
        s = self.dsem[q][i]
        if self.dval[q][i] > 0:
            need[id(s)] = (s, self.dval[q][i])
        self._emit_waits(q, need)
        if slow:
            ins = self.engs[q].dma_start(out=out, in_=in_, allow_slow_non_contiguous=True)
        else:
            ins = self.engs[q].dma_start(out=out, in_=in_)
        self.dval[q][i] += 16
        ins.then_inc(s, 16)
        ev = (s, self.dval[q][i], "dma_" + q + str(i))
        self._commit(ev, reads, writes)
        self.n_inst += 1
        return ins

    def barrier(self):
        for e in self.engs:
            need = {}
            for k in self.engs:
                if k != e and self.cnt[k] > 0:
                    need[id(self.sem[k])] = (self.sem[k], self.cnt[k])
            for q in self.dsem:
                for s, v in zip(self.dsem[q], self.dval[q]):
                    if v > 0:
                        need[id(s)] = (s, v)
            self._emit_waits(e, need)


def rstd_inplace(c, ssq, inv_n, eps=RMS_EPS):
    c.op("dve", lambda e: e.tensor_scalar(ssq[:], ssq[:], inv_n, eps, ALU.mult, ALU.add), reads=[ssq], writes=[ssq])
    c.op("act", lambda e: e.activation(out=ssq[:], in_=ssq[:], func=AF.Sqrt), reads=[ssq], writes=[ssq])
    c.op("dve", lambda e: e.reciprocal(out=ssq[:], in_=ssq[:]), reads=[ssq], writes=[ssq])


class _View:
    def __init__(self, tb, key):
        self.tb = tb
        self.key = key

    def __getitem__(self, k):
        v = self.tb.t[self.key]
        return v[k]

    @property
    def w(self):
        return self.tb.w

    @w.setter
    def w(self, v):
        self.tb.w = v

    @property
    def r(self):
        return self.tb.r

    @r.setter
    def r(self, v):
        self.tb.r = v


def run_interleaved(gens):
    gens = list(gens)
    while gens:
        for g in list(gens):
            try:
                next(g)
            except StopIteration:
                gens.remove(g)


class Rot:
    def __init__(self, items):
        self.items = items
        self.i = 0

    def next(self):
        it = self.items[self.i]
        self.i = (self.i + 1) % len(self.items)
        return it


class Builder:
    def __init__(self, seqs, depth=DEPTH, do_mixer=True, do_xattn=True, do_ffn=True, kinds=None):
        self.seqs = list(seqs)
        self.depth = depth
        self.do_mixer = do_mixer
        self.do_xattn = do_xattn
        self.do_ffn = do_ffn
        self.kinds = kinds or [("even", l // 2) if l % 2 == 0 else ("odd", l // 2) for l in range(depth)]
        self.T = sum(self.seqs)
        self.seq_off = [sum(self.seqs[:i]) for i in range(len(self.seqs))]
        self.Lmax = max(self.seqs)

    def declare(self, nc):
        d = {}

        def inp(name, shape, dt=F32):
            d[name] = nc.dram_tensor(name, list(shape), dt, kind="ExternalInput").ap()

        ns = len(self.seqs)
        inp("x", [self.T, D])
        inp("mem", [ns * N_MEM, D])
        for n in ("norm_mix", "norm_xq", "norm_mem", "norm_ffn"):
            inp(n, [DEPTH, D])
        inp("norm_final", [1, D])
        inp("even_w_in", [2, D, EVEN_IN])
        inp("even_w_out", [2, D, D])
        inp("hgrn_lb_logits", [2, 512])
        inp("ret_norm", [2, 512])
        inp("hgrn_norm", [2, 512])
        inp("gdn_w_in", [2, D, ODD_IN])
        inp("gdn_conv", [2, 5, 3072])
        inp("gdn_a_log", [2, 16])
        inp("gdn_dt_bias", [2, 16])
        inp("gdn_norm", [2, 128])
        inp("gdn_w_out", [2, D, D])
        inp("xa_w_q", [DEPTH, D, D])
        inp("xa_w_kv", [DEPTH, D, 2 * D])
        inp("xa_w_o", [DEPTH, D, D])
        inp("ffn_w_gu", [DEPTH, D, 2 * D_FF])
        inp("ffn_w_down", [DEPTH, D_FF, D])
        for name, arr in host_constants(self.Lmax).items():
            inp(name, arr.shape, F32 if arr.dtype == np.float32 else BF16)
        d["y"] = nc.dram_tensor("y", [self.T, D], F32, kind="ExternalOutput").ap()
        d["X"] = nc.dram_tensor("X_scr", [self.T, D], F32).ap()
        d["P"] = nc.dram_tensor("P_scr", [self.Lmax, EVEN_IN], F32).ap()
        d["OF"] = nc.dram_tensor("OF_scr", [self.Lmax, D], F32).ap()
        d["HT"] = nc.dram_tensor("HT_scr", [128, 8 * (self.Lmax + 4)], BF16).ap()
        d["QT"] = nc.dram_tensor("QT_scr", [128, 8 * self.Lmax], BF16).ap()
        d["KT"] = nc.dram_tensor("KT_scr", [128, 8 * self.Lmax], BF16).ap()
        d["VM"] = nc.dram_tensor("VM_scr", [self.Lmax, D], BF16).ap()
        self.d = d

    def rmsnorm_to_hT(self, c, xt, grow, hT_dst, junk, ssq, hn, pst, ident):
        c.op("act", lambda e: e.activation(out=junk[:], in_=xt[:], func=AF.Square, accum_out=ssq[:]),
             reads=[xt], writes=[junk, ssq])
        rstd_inplace(c, ssq, 1.0 / D)
        c.op("dve", lambda e: e.scalar_tensor_tensor(hn[:], xt[:], ssq[:], grow[:], ALU.mult, ALU.mult),
             reads=[xt, ssq, grow], writes=[hn])
        for k in range(8):
            c.op("pe", lambda e, k=k: e.transpose(out=pst[:, k * 128:(k + 1) * 128],
                                                   in_=hn[:, k * 128:(k + 1) * 128], identity=ident[:]),
                 reads=[hn, ident], writes=[pst])
        tb, ap = hT_dst
        c.op("act", lambda e: e.copy(out=ap, in_=pst[:].rearrange("p (k t) -> p k t", k=8)),
             reads=[pst], writes=[tb])

    def x_src(self, layer_first):
        return self.d["x"] if layer_first else self.d["X"]

    def phase_ffn(self, c, layer, x_in, last):
        nc, d = c.nc, self.d
        TBK = 256
        with contextlib.ExitStack() as es:
            wgu = c.sb("wgu", [128, 8, 2 * D_FF], BF16, es)
            wdn = c.sb("wdn", [128, 22, D], BF16, es)
            grow = c.sb("grow", [128, D], F32, es)
            gfin = c.sb("gfin", [128, D], F32, es)
            ident = c.sb("identb", [128, 128], BF16, es)
            xts = Rot([[c.sb("xt%d_%d" % (i, j), [128, D], F32, es) for j in range(2)] for i in range(2)])
            hTs = Rot([c.sb("hT%d" % i, [128, 8, TBK], BF16, es) for i in range(2)])
            aT = c.sb("aT", [128, 22, TBK], BF16, es)
            sg = Rot([c.sb("sg%d" % i, [128, TBK], F32, es) for i in range(2)])
            junk = c.sb("junk", [128, D], BF16, es)
            ssq = c.sb("ssq", [128, 1], F32, es)
            hn = c.sb("hn", [128, D], BF16, es)
            yts = Rot([c.sb("yt%d" % i, [128, D], F32, es) for i in range(2)])
            pst = c.ps("pst", [128, D], BF16, es)
            psg = Rot([c.ps("psg%d" % i, [128, 512], F32, es) for i in range(2)])
            psu = Rot([c.ps("psu%d" % i, [128, 512], F32, es) for i in range(2)])
            psy = Rot([c.ps("psy%d" % i, [128, 512], F32, es) for i in range(2)])

            c.dma("sp", ident[:], d["c_ident_bf"], writes=[ident])
            c.dma("sp", grow[:], d["norm_ffn"][layer:layer + 1, :].partition_broadcast(128), writes=[grow])
            c.dma("sp", gfin[:], d["norm_final"][0:1, :].partition_broadcast(128), writes=[gfin])
            for k in range(8):
                c.dma("pool", wgu[:, k, :], d["ffn_w_gu"][layer, k * 128:(k + 1) * 128, :], writes=[wgu])
            for f in range(22):
                c.dma("pool", wdn[:, f, :], d["ffn_w_down"][layer, f * 128:(f + 1) * 128, :], writes=[wdn])

            for b0 in range(0, self.T, TBK):
                xt = xts.next()
                hT = hTs.next()
                for j in range(2):
                    r0 = b0 + j * 128
                    c.dma("sp", xt[j][:], x_in[r0:r0 + 128, :], reads=[self.xtb(r0)], writes=[xt[j]])
                    self.rmsnorm_to_hT(c, xt[j], grow, (hT, hT[:, :, j * 128:(j + 1) * 128]), junk, ssq, hn, pst, ident)
                for fb in range(22):
                    pg, pu, s = psg.next(), psu.next(), sg.next()
                    for k in range(8):
                        c.op("pe", lambda e, k=k: e.matmul(pg[:, 0:TBK], wgu[:, k, fb * 128:(fb + 1) * 128], hT[:, k, :],
                                                          start=(k == 0), stop=(k == 7)),
                             reads=[wgu, hT], writes=[pg])
                    for k in range(8):
                        c.op("pe", lambda e, k=k: e.matmul(pu[:, 0:TBK], wgu[:, k, D_FF + fb * 128:D_FF + (fb + 1) * 128],
                                                          hT[:, k, :], start=(k == 0), stop=(k == 7)),
                             reads=[wgu, hT], writes=[pu])
                    c.op("act", lambda e: e.activation(out=s[:], in_=pg[:, 0:TBK], func=AF.Silu), reads=[pg], writes=[s])
                    c.op("dve", lambda e: e.tensor_tensor(aT[:, fb, :], pu[:, 0:TBK], s[:], ALU.mult),
                         reads=[pu, s], writes=[aT])
                for j in range(2):
                    r0 = b0 + j * 128
                    yt = yts.next()
                    for n in range(2):
                        py = psy.next()
                        for fb in range(22):
                            c.op("pe", lambda e, fb=fb: e.matmul(py[:], aT[:, fb, j * 128:(j + 1) * 128],
                                                                wdn[:, fb, n * 512:(n + 1) * 512],
                                                                start=(fb == 0), stop=(fb == 21)),
                                 reads=[aT, wdn], writes=[py])
                        c.op("dve", lambda e: e.tensor_tensor(yt[:, n * 512:(n + 1) * 512], py[:],
                                                              xt[j][:, n * 512:(n + 1) * 512], ALU.add),
                             reads=[py, xt[j]], writes=[yt])
                    if last:
                        self.final_norm_store(c, yt, gfin, junk, ssq, r0)
                    else:
                        c.dma("sp", d["X"][r0:r0 + 128, :], yt[:], reads=[yt], writes=[self.xtb(r0)])
            c.barrier()

    def final_norm_store(self, c, yt, gfin, junk, ssq, r0):
        c.op("act", lambda e: e.activation(out=junk[:], in_=yt[:], func=AF.Square, accum_out=ssq[:]),
             reads=[yt], writes=[junk, ssq])
        rstd_inplace(c, ssq, 1.0 / D)
        c.op("dve", lambda e: e.scalar_tensor_tensor(yt[:], yt[:], ssq[:], gfin[:], ALU.mult, ALU.mult),
             reads=[yt, ssq, gfin], writes=[yt])
        c.dma("sp", self.d["y"][r0:r0 + 128, :], yt[:], reads=[yt], writes=[self.ytb(r0)])

    def xtb(self, r0):
        return self._xtb[r0 // 128]

    def ytb(self, r0):
        return self._ytb[r0 // 128]

    def phase_xattn(self, c, layer, x_in):
        nc, d = c.nc, self.d
        TBK = 512
        with contextlib.ExitStack() as es:
            wq = c.sb("wq", [128, 8, D], BF16, es)
            wkv = c.sb("wkv", [128, 8, 2 * D], BF16, es)
            wo = c.sb("wo", [128, 8, D], BF16, es)
            gq = c.sb("gq", [128, D], F32, es)
            gm = c.sb("gm", [128, D], F32, es)
            ident = c.sb("identb", [128, 128], BF16, es)
            ones = c.sb("onesb", [128, 128], BF16, es)
            xts = [c.sb("xt%d" % j, [128, D], F32, es) for j in range(4)]
            hT = c.sb("hT", [128, 8, TBK], BF16, es)
            qT = c.sb("qT", [128, 8, TBK], BF16, es)
            oT = c.sb("oT", [128, 8, TBK], BF16, es)
            memT = c.sb("memT", [128, 8, N_MEM], BF16, es)
            KT = c.sb("KT", [128, 8, N_MEM], BF16, es)
            Vt = c.sb("Vt", [128, 2, D], BF16, es)
            PT = [Rot([c.sb("PT%d_%d" % (m, i), [128, TBK], BF16, es) for i in range(2)]) for m in range(2)]
            rden = c.sb("rden", [128, TBK], F32, es)
            junk = c.sb("junk", [128, D], BF16, es)
            ssq = c.sb("ssq", [128, 1], F32, es)
            hn = c.sb("hn", [128, D], BF16, es)
            yts = Rot([c.sb("yt%d" % i, [128, D], F32, es) for i in range(2)])
            pst = c.ps("pst", [128, D], BF16, es)
            psA = Rot([c.ps("psA%d" % i, [128, 512], F32, es) for i in range(4)])
            psD = c.ps("psD", [128, 512], F32, es)
            psy = Rot([c.ps("psy%d" % i, [128, 512], F32, es) for i in range(2)])

            c.dma("sp", ident[:], d["c_ident_bf"], writes=[ident])
            c.dma("sp", ones[:], d["c_ones_bf"], writes=[ones])
            c.dma("sp", gq[:], d["norm_xq"][layer:layer + 1, :].partition_broadcast(128), writes=[gq])
            c.dma("sp", gm[:], d["norm_mem"][layer:layer + 1, :].partition_broadcast(128), writes=[gm])
            for k in range(8):
                c.dma("pool", wq[:, k, :], d["xa_w_q"][layer, k * 128:(k + 1) * 128, :], writes=[wq])
                c.dma("pool", wkv[:, k, :], d["xa_w_kv"][layer, k * 128:(k + 1) * 128, :], writes=[wkv])
                c.dma("pool", wo[:, k, :], d["xa_w_o"][layer, k * 128:(k + 1) * 128, :], writes=[wo])

            for si, L in enumerate(self.seqs):
                for m in range(2):
                    mt = xts[m]
                    c.dma("sp", mt[:], d["mem"][si * N_MEM + m * 128: si * N_MEM + (m + 1) * 128, :], writes=[mt])
                    self.rmsnorm_to_hT(c, mt, gm, (memT, memT[:, :, m * 128:(m + 1) * 128]), junk, ssq, hn, pst, ident)
                for fb in range(8):
                    pa = psA.next()
                    for k in range(8):
                        c.op("pe", lambda e, k=k: e.matmul(pa[:, 0:N_MEM], wkv[:, k, fb * 128:(fb + 1) * 128], memT[:, k, :],
                                                          start=(k == 0), stop=(k == 7)), reads=[wkv, memT], writes=[pa])
                    c.op("act", lambda e: e.copy(out=KT[:, fb, :], in_=pa[:, 0:N_MEM]), reads=[pa], writes=[KT])
                for m in range(2):
                    for n in range(2):
                        pa = psA.next()
                        for k in range(8):
                            c.op("pe", lambda e, k=k: e.matmul(pa[:], memT[:, k, m * 128:(m + 1) * 128],
                                                              wkv[:, k, D + n * 512:D + (n + 1) * 512],
                                                              start=(k == 0), stop=(k == 7)), reads=[wkv, memT], writes=[pa])
                        c.op("act", lambda e: e.copy(out=Vt[:, m, n * 512:(n + 1) * 512], in_=pa[:]), reads=[pa], writes=[Vt])
                for b0 in range(self.seq_off[si], self.seq_off[si] + L, TBK):
                    ntile = min(4, (self.seq_off[si] + L - b0) // 128)
                    W = ntile * 128
                    for j in range(ntile):
                        r0 = b0 + j * 128
                        c.dma("sp", xts[j][:], x_in[r0:r0 + 128, :], reads=[self.xtb(r0)], writes=[xts[j]])
                        self.rmsnorm_to_hT(c, xts[j], gq, (hT, hT[:, :, j * 128:(j + 1) * 128]), junk, ssq, hn, pst, ident)
                    for fb in range(8):
                        pa = psA.next()
                        for k in range(8):
                            c.op("pe", lambda e, k=k: e.matmul(pa[:, 0:W], wq[:, k, fb * 128:(fb + 1) * 128], hT[:, k, 0:W],
                                                              start=(k == 0), stop=(k == 7)), reads=[wq, hT], writes=[pa])
                        c.op("act", lambda e: e.copy(out=qT[:, fb, 0:W], in_=pa[:, 0:W]), reads=[pa], writes=[qT])
                    for h in range(4):
                        pts = []
                        for mb in range(2):
                            pa = psA.next()
                            for dd in range(2):
                                c.op("pe", lambda e, dd=dd: e.matmul(pa[:, 0:W], KT[:, 2 * h + dd, mb * 128:(mb + 1) * 128],
                                                                    qT[:, 2 * h + dd, 0:W], start=(dd == 0), stop=(dd == 1)),
                                     reads=[KT, qT], writes=[pa])
                            pt = PT[mb].next()
                            c.op("act", lambda e: e.activation(out=pt[:, 0:W], in_=pa[:, 0:W], func=AF.Exp, scale=XA_SCALE),
                                 reads=[pa], writes=[pt])
                            pts.append(pt)
                        for mb in range(2):
                            c.op("pe", lambda e, mb=mb: e.matmul(psD[:, 0:W], ones[:], pts[mb][:, 0:W],
                                                                start=(mb == 0), stop=(mb == 1)), reads=[ones, pts[mb]], writes=[psD])
                        c.op("dve", lambda e: e.reciprocal(out=rden[:, 0:W], in_=psD[:, 0:W]), reads=[psD], writes=[rden])
                        for dd in range(2):
                            pa = psA.next()
                            for mb in range(2):
                                c.op("pe", lambda e, mb=mb: e.matmul(pa[:, 0:W], Vt[:, mb, (2 * h + dd) * 128:(2 * h + dd + 1) * 128],
                                                                    pts[mb][:, 0:W], start=(mb == 0), stop=(mb == 1)),
                                     reads=[Vt, pts[mb]], writes=[pa])
                            c.op("dve", lambda e: e.tensor_tensor(oT[:, 2 * h + dd, 0:W], pa[:, 0:W], rden[:, 0:W], ALU.mult),
                                 reads=[pa, rden], writes=[oT])
                    for j in range(ntile):
                        r0 = b0 + j * 128
                        yt = yts.next()
                        for n in range(2):
                            py = psy.next()
                            for fb in range(8):
                                c.op("pe", lambda e, fb=fb: e.matmul(py[:], oT[:, fb, j * 128:(j + 1) * 128],
                                                                    wo[:, fb, n * 512:(n + 1) * 512],
                                                                    start=(fb == 0), stop=(fb == 7)), reads=[oT, wo], writes=[py])
                            c.op("dve", lambda e: e.tensor_tensor(yt[:, n * 512:(n + 1) * 512], py[:],
                                                                  xts[j][:, n * 512:(n + 1) * 512], ALU.add),
                                 reads=[py, xts[j]], writes=[yt])
                        c.dma("sp", d["X"][r0:r0 + 128, :], yt[:], reads=[yt], writes=[self.xtb(r0)])
            c.barrier()

    def gla_tile(self, c, R, G, q_ap, q_tb, k_tb, k_ap, v_ap, v_tb, lf_tb, lf_ap, dr, S, Sb, o_tb, o_col0, add_tb=None,
                 pre=None):
        if pre is not None:
            for _ in pre():
                yield
        M1, M2, M4, MT = R["gm"][dr]
        cs1, cs2, cs4 = R["psA"].next(), R["psA"].next(), R["psA"].next()
        for ps_, M in ((cs1, M1), (cs2, M2), (cs4, M4)):
            c.op("pe", lambda e: e.matmul(ps_[:], M[:], lf_ap, start=True, stop=True), reads=[M, lf_tb], writes=[ps_])
        E = G["E"]
        c.op("act", lambda e: e.activation(out=E[0][:], in_=cs1[:], func=AF.Exp), reads=[cs1], writes=[E[0]])
        c.op("act", lambda e: e.activation(out=E[1][:], in_=cs2[:], func=AF.Exp), reads=[cs2], writes=[E[1]])
        c.op("act", lambda e: e.activation(out=E[2][:], in_=cs2[:], func=AF.Exp, scale=-1.0), reads=[cs2], writes=[E[2]])
        c.op("act", lambda e: e.activation(out=E[3][:], in_=cs4[:], func=AF.Exp), reads=[cs4], writes=[E[3]])
        yield
        q1, q2, k3, k4, vb = G["q1"], G["q2"], G["k3"], G["k4"], G["vb"]
        BIG = 4.0e18
        c.op("pool", lambda e: e.tensor_tensor(q1[:], E[0][:], q_ap, ALU.mult), reads=[E[0], q_tb], writes=[q1])
        c.op("dve", lambda e: e.scalar_tensor_tensor(q2[:], E[1][:], BIG, q_ap, ALU.min, ALU.mult), reads=[E[1], q_tb], writes=[q2])
        c.op("dve", lambda e: e.scalar_tensor_tensor(k3[:], E[2][:], BIG, k_ap, ALU.min, ALU.mult), reads=[E[2], k_tb], writes=[k3])
        c.op("pool", lambda e: e.tensor_tensor(k4[:], E[3][:], k_ap, ALU.mult), reads=[E[3], k_tb], writes=[k4])
        c.op("act", lambda e: e.copy(out=vb[:], in_=v_ap), reads=[v_tb], writes=[vb])
        pe_l = R["psA"].next()
        for h in range(4):
            c.op("pe", lambda e: e.matmul(pe_l[:, 2 * h:2 * h + 2], lf_ap[:, h * 128:(h + 1) * 128], R["ind"][:],
                                          start=True, stop=True), reads=[lf_tb, R["ind"]], writes=[pe_l])
        eL = G["eL"]
        c.op("act", lambda e: e.activation(out=eL[:], in_=pe_l[:, 0:8], func=AF.Exp), reads=[pe_l], writes=[eL])
        yield
        Ts = []
        for src, nm in ((q1, "q1T"), (q2, "q2T"), (k3, "k3T")):
            pt = R["pstB"].next()
            for h in range(4):
                c.op("pe", lambda e: e.transpose(out=pt[:, h * 128:(h + 1) * 128], in_=src[:, h * 128:(h + 1) * 128],
                                                 identity=R["ident"][:]), reads=[src, R["ident"]], writes=[pt])
            dst = G[nm]
            c.op("act" if nm != "q2T" else "dve", lambda e: e.tensor_copy(out=dst[:], in_=pt[:, 0:512]) if nm == "q2T"
                 else e.copy(out=dst[:], in_=pt[:, 0:512]), reads=[pt], writes=[dst])
            Ts.append(dst)
            yield
        q1T, q2T, k3T = Ts
        order = (0, 1) if dr == 0 else (1, 0)

        def head_gen(h):
            hs = slice(h * 128, (h + 1) * 128)
            pa = R["psA"].next()
            c.op("pe", lambda e: e.matmul(pa[:, 0:128], k3T[:, hs], q2T[:, hs], start=True, stop=True),
                 reads=[k3T, q2T], writes=[pa])
            atm = G["ATm"][h]
            c.op("dve", lambda e: e.tensor_tensor(atm[:], pa[:, 0:128], MT[:], ALU.mult), reads=[pa, MT], writes=[atm])
            yield
            for ci in order:
                rs = slice(ci * 64, (ci + 1) * 64)
                po = R["psA"].next()
                c.op("pe", lambda e: e.matmul(po[:, 0:128], q1T[:, hs], Sb[h][:], start=True, stop=False),
                     reads=[q1T, Sb[h]], writes=[po])
                c.op("pe", lambda e: e.matmul(po[:, 0:128], atm[:], vb[:, hs], start=False, stop=True),
                     reads=[atm, vb], writes=[po])
                ocs = slice(o_col0 + h * 128, o_col0 + (h + 1) * 128)
                if add_tb is None:
                    c.op("act", lambda e: e.copy(out=o_tb[rs, ocs], in_=po[rs, 0:128]), reads=[po], writes=[o_tb])
                else:
                    c.op("dve", lambda e: e.tensor_tensor(o_tb[rs, ocs], po[rs, 0:128], add_tb[rs, ocs], ALU.add),
                         reads=[po, add_tb], writes=[o_tb])
                pu = R["psA"].next()
                c.op("pe", lambda e: e.matmul(pu[:, 0:128], k4[rs, hs], vb[rs, hs], start=True, stop=True),
                     reads=[k4, vb], writes=[pu])
                c.op("dve", lambda e: e.scalar_tensor_tensor(S[h][:], S[h][:], eL[:, 2 * h + ci:2 * h + ci + 1], pu[:, 0:128],
                                                             ALU.mult, ALU.add), reads=[S[h], eL, pu], writes=[S[h]])
                yield
                c.op("act", lambda e: e.copy(out=Sb[h][:], in_=S[h][:]), reads=[S[h]], writes=[Sb[h]])
                yield

        gens = [head_gen(h) for h in range(4)]
        while gens:
            for g in list(gens):
                try:
                    next(g)
                except StopIteration:
                    gens.remove(g)
            yield

    def phase_even(self, c, j, layer, x_in):
        nc, d = c.nc, self.d
        with contextlib.ExitStack() as es:
            win = c.sb("win", [128, 8, EVEN_IN], BF16, es)
            wout = c.sb("wout", [128, 8, D], BF16, es)
            gmix = c.sb("gmix", [128, D], F32, es)
            gh = c.sb("gh", [128, D], F32, es)
            lbr = c.sb("lbr", [128, 512], F32, es)
            omlb = c.sb("omlb", [128, 512], F32, es)
            lgam = c.sb("lgam", [128, 512], F32, es)
            hmask = c.sb("hmask", [128, 8], F32, es)
            ident = c.sb("identb", [128, 128], BF16, es)
            gmt = [[c.sb("gm%d%d" % (a, b_), [128, 128], F32, es) for b_ in range(4)] for a in range(2)]
            ind = c.sb("ind", [128, 2], F32, es)
            p = c.sb("p", [128, EVEN_IN], F32, es)
            xts = Rot([c.sb("xt%d" % i, [128, D], F32, es) for i in range(2)])
            oft = c.sb("oft", [128, D], F32, es)
            ot = c.sb("ot", [128, D], F32, es)
            rope = c.sb("rope", [128, 4, 64], F32, es)
            rt = [c.sb("rt%d" % i, [128, 4, 64], F32, es) for i in range(4)]
            fg = c.sb("fg", [128, 512], F32, es)
            lf = c.sb("lf", [128, 512], F32, es)
            kk = c.sb("kk", [128, 512], F32, es)
            sq = c.sb("sq", [128, D], F32, es)
            ssq = c.sb("ssq", [128, 1], F32, es)
            s1 = c.sb("s1", [128, 8], F32, es)
            s2 = c.sb("s2", [128, 8], F32, es)
            hn = c.sb("hn", [128, D], BF16, es)
            hT = c.sb("hT", [128, 8, 128], BF16, es)
            omix = c.sb("omix", [128, D], BF16, es)
            junk = omix
            oT = hT
            R = {
                "gm": gmt, "ind": ind, "ident": ident,
                "psA": Rot([c.ps("psA%d" % i, [128, 512], F32, es) for i in range(6)]),
                "pstB": Rot([c.ps("pstB%d" % i, [128, D], BF16, es) for i in range(2)]),
            }
            Gs = []
            for g in range(2):
                G = {"E": [c.sb("E%d_%d" % (g, i), [128, 512], F32, es) for i in range(4)]}
                for nm in ("q1", "q2", "k3", "k4", "vb", "q1T", "q2T", "k3T"):
                    G[nm] = c.sb("%s_%d" % (nm, g), [128, 512], BF16, es)
                G["eL"] = c.sb("eL_%d" % g, [128, 8], F32, es)
                G["ATm"] = [c.sb("ATm%d_%d" % (g, i), [128, 128], BF16, es) for i in range(4)]
                Gs.append(G)
            S = [[c.sb("S%d_%d" % (g, h), [128, 128], F32, es) for h in range(4)] for g in range(2)]
            Sb = [[c.sb("Sb%d_%d" % (g, h), [128, 128], BF16, es) for h in range(4)] for g in range(2)]

            c.dma("sp", ident[:], d["c_ident_bf"], writes=[ident])
            c.dma("sp", ind[:], d["c_ind"], writes=[ind])
            for a in range(2):
                for b_ in range(4):
                    c.dma("sp", gmt[a][b_][:], d["c_gla"][a * 4 + b_], writes=[gmt[a][b_]])
            c.dma("sp", gmix[:], d["norm_mix"][layer:layer + 1, :].partition_broadcast(128), writes=[gmix])
            c.dma("sp", gh[:, 0:512], d["ret_norm"][j:j + 1, :].partition_broadcast(128), writes=[gh])
            c.dma("sp", gh[:, 512:1024], d["hgrn_norm"][j:j + 1, :].partition_broadcast(128), writes=[gh])
            c.dma("sp", lgam[:], d["c_lgam"][0:1, :].partition_broadcast(128), writes=[lgam])
            c.dma("sp", hmask[:], d["c_hmask"][0:1, :].partition_broadcast(128), writes=[hmask])
            if j == 0:
                c.op("dve", lambda e: e.memset(lbr[:], 0.0), writes=[lbr])
            else:
                c.dma("sp", lbr[:], d["hgrn_lb_logits"][1:2, :].partition_broadcast(128), writes=[lbr])
                c.dma("sp", omlb[:], d["hgrn_lb_logits"][0:1, :].partition_broadcast(128), writes=[omlb])
                c.op("dve", lambda e: e.tensor_tensor(lbr[:], lbr[:], omlb[:], ALU.subtract), reads=[lbr, omlb], writes=[lbr])
                c.op("act", lambda e: e.activation(out=lbr[:], in_=lbr[:], func=AF.Sigmoid), reads=[lbr], writes=[lbr])
            c.op("dve", lambda e: e.tensor_scalar(omlb[:], lbr[:], -1.0, 1.0, ALU.mult, ALU.add), reads=[lbr], writes=[omlb])
            for k in range(8):
                c.dma("pool", win[:, k, :], d["even_w_in"][j, k * 128:(k + 1) * 128, :], writes=[win])
                c.dma("pool", wout[:, k, :], d["even_w_out"][j, k * 128:(k + 1) * 128, :], writes=[wout])

            def gates(zcol):
                c.op("act", lambda e: e.activation(out=fg[:], in_=p[:, zcol:zcol + 512], func=AF.Sigmoid), reads=[p], writes=[fg])
                yield
                c.op("pool", lambda e: e.tensor_tensor(fg[:], fg[:], omlb[:], ALU.mult), reads=[fg, omlb], writes=[fg])
                yield
                c.op("pool", lambda e: e.tensor_tensor(fg[:], fg[:], lbr[:], ALU.add), reads=[fg, lbr], writes=[fg])
                yield
                c.op("act", lambda e: e.activation(out=lf[:], in_=fg[:], func=AF.Ln), reads=[fg], writes=[lf])
                c.op("pool", lambda e: e.tensor_scalar(kk[:], fg[:], -1.0, 1.0, ALU.mult, ALU.add), reads=[fg], writes=[kk])
                yield

            for si, L in enumerate(self.seqs):
                nt = L // 128
                base = self.seq_off[si]
                ptb = [TB(None, "P%d" % t) for t in range(nt)]
                oftb = [TB(None, "OF%d" % t) for t in range(nt)]
                for g in range(2):
                    for h in range(4):
                        c.op("dve", lambda e: e.memset(S[g][h][:], 0.0), writes=[S[g][h]])
                        c.op("pool", lambda e: e.memset(Sb[g][h][:], 0.0), writes=[Sb[g][h]])
                for t in range(nt):
                    r0 = base + t * 128
                    xt = xts.next()
                    c.dma("sp", xt[:], x_in[r0:r0 + 128, :], reads=[self.xtb(r0)], writes=[xt])
                    c.dma("sp", rope[:], d["c_rope"][t * 128:(t + 1) * 128, :].rearrange("p (a b) -> p a b", a=4), writes=[rope])
                    self.rmsnorm_to_hT(c, xt, gmix, (hT, hT[:]), junk, ssq, hn, R["pstB"].next(), ident)
                    for n in range(9):
                        pa = R["psA"].next()
                        for k in range(8):
                            c.op("pe", lambda e: e.matmul(pa[:], hT[:, k, :], win[:, k, n * 512:(n + 1) * 512],
                                                          start=(k == 0), stop=(k == 7)), reads=[hT, win], writes=[pa])
                        c.op("act" if n % 2 == 0 else "dve",
                             lambda e: (e.copy(out=p[:, n * 512:(n + 1) * 512], in_=pa[:]) if n % 2 == 0
                                        else e.tensor_copy(out=p[:, n * 512:(n + 1) * 512], in_=pa[:])),
                             reads=[pa], writes=[p])
                    for col0, ci_, si_ in ((0, 0, 1), (512, 2, 3)):
                        v4 = p[:, col0:col0 + 512].rearrange("p (h two x) -> p h two x", h=4, two=2)
                        x1, x2 = v4[:, :, 0, :], v4[:, :, 1, :]
                        cb = rope[:, ci_:ci_ + 1, :].to_broadcast([128, 4, 64])
                        sb_ = rope[:, si_:si_ + 1, :].to_broadcast([128, 4, 64])
                        c.op("pool", lambda e: e.tensor_tensor(rt[0][:], x1, cb, ALU.mult), reads=[p, rope], writes=[rt[0]])
                        c.op("pool", lambda e: e.tensor_tensor(rt[1][:], x2, sb_, ALU.mult), reads=[p, rope], writes=[rt[1]])
                        c.op("pool", lambda e: e.tensor_tensor(rt[2][:], x1, sb_, ALU.mult), reads=[p, rope], writes=[rt[2]])
                        c.op("pool", lambda e: e.tensor_tensor(rt[3][:], x2, cb, ALU.mult), reads=[p, rope], writes=[rt[3]])
                        c.op("pool", lambda e: e.tensor_tensor(x1, rt[0][:], rt[1][:], ALU.subtract), reads=[rt[0], rt[1]], writes=[p])
                        c.op("pool", lambda e: e.tensor_tensor(x2, rt[2][:], rt[3][:], ALU.add), reads=[rt[2], rt[3]], writes=[p])
                    c.dma("sp", d["P"][t * 128:(t + 1) * 128, :], p[:], reads=[p], writes=[ptb[t]])
                    run_interleaved([
                        self.gla_tile(c, R, Gs[0], p[:, 0:512], p, p, p[:, 512:1024], p[:, 1024:1536], p, lgam, lgam[:], 0,
                                      S[0], Sb[0], oft, 0),
                        self.gla_tile(c, R, Gs[1], p[:, 2048:2560], p, kk, kk[:], p[:, 3584:4096], p, lf, lf[:], 0,
                                      S[1], Sb[1], oft, 512, pre=lambda: gates(2560))])
                    c.dma("sp", d["OF"][t * 128:(t + 1) * 128, :], oft[:], reads=[oft], writes=[oftb[t]])
                for g in range(2):
                    for h in range(4):
                        c.op("dve", lambda e: e.memset(S[g][h][:], 0.0), writes=[S[g][h]])
                        c.op("pool", lambda e: e.memset(Sb[g][h][:], 0.0), writes=[Sb[g][h]])
                for t in range(nt - 1, -1, -1):
                    r0 = base + t * 128
                    xt = xts.next()
                    c.dma("sp", xt[:], x_in[r0:r0 + 128, :], reads=[self.xtb(r0)], writes=[xt])
                    c.dma("sp", p[:], d["P"][t * 128:(t + 1) * 128, :], reads=[ptb[t]], writes=[p])
                    c.dma("sp", oft[:], d["OF"][t * 128:(t + 1) * 128, :], reads=[oftb[t]], writes=[oft])
                    run_interleaved([
                        self.gla_tile(c, R, Gs[0], p[:, 0:512], p, p, p[:, 512:1024], p[:, 1024:1536], p, lgam, lgam[:], 1,
                                      S[0], Sb[0], ot, 0, add_tb=oft),
                        self.gla_tile(c, R, Gs[1], p[:, 2048:2560], p, kk, kk[:], p[:, 3584:4096], p, lf, lf[:], 1,
                                      S[1], Sb[1], ot, 512, add_tb=oft, pre=lambda: gates(3072))])
                    o3 = ot[:].rearrange("p (h x) -> p h x", h=8)
                    c.op("dve", lambda e: e.reduce_sum(out=s1[:], in_=o3, axis=AX.X), reads=[ot], writes=[s1])
                    c.op("act", lambda e: e.activation(out=sq[:], in_=ot[:], func=AF.Square), reads=[ot], writes=[sq])
                    c.op("dve", lambda e: e.reduce_sum(out=s2[:], in_=sq[:].rearrange("p (h x) -> p h x", h=8), axis=AX.X),
                         reads=[sq], writes=[s2])
                    c.op("dve", lambda e: e.scalar_tensor_tensor(s1[:], s1[:], 1.0 / 128, hmask[:], ALU.mult, ALU.mult),
                         reads=[s1, hmask], writes=[s1])
                    c.op("dve", lambda e: e.tensor_scalar(s2[:], s2[:], 1.0 / 128, RMS_EPS, ALU.mult, ALU.add), reads=[s2], writes=[s2])
                    c.op("dve", lambda e: e.tensor_tensor(sq[:, 0:8], s1[:], s1[:], ALU.mult), reads=[s1], writes=[sq])
                    c.op("dve", lambda e: e.tensor_tensor(s2[:], s2[:], sq[:, 0:8], ALU.subtract), reads=[s2, sq], writes=[s2])
                    c.op("act", lambda e: e.activation(out=s2[:], in_=s2[:], func=AF.Sqrt), reads=[s2], writes=[s2])
                    c.op("dve", lambda e: e.reciprocal(out=s2[:], in_=s2[:]), reads=[s2], writes=[s2])
                    for h in range(8):
                        hs = slice(h * 128, (h + 1) * 128)
                        c.op("dve" if h % 2 == 0 else "pool",
                             lambda e: e.tensor_scalar(ot[:, hs], ot[:, hs], s1[:, h:h + 1], s2[:, h:h + 1], ALU.subtract, ALU.mult),
                             reads=[ot, s1, s2], writes=[ot])
                    c.op("pool", lambda e: e.tensor_tensor(ot[:], ot[:], gh[:], ALU.mult), reads=[ot, gh], writes=[ot])
                    c.op("act", lambda e: e.activation(out=sq[:, 0:512], in_=p[:, 1536:2048], func=AF.Silu), reads=[p], writes=[sq])
                    c.op("act", lambda e: e.activation(out=sq[:, 512:1024], in_=p[:, 4096:4608], func=AF.Silu), reads=[p], writes=[sq])
                    c.op("dve", lambda e: e.tensor_tensor(omix[:], ot[:], sq[:], ALU.mult), reads=[ot, sq], writes=[omix])
                    pt = R["pstB"].next()
                    for k in range(8):
                        c.op("pe", lambda e: e.transpose(out=pt[:, k * 128:(k + 1) * 128], in_=omix[:, k * 128:(k + 1) * 128],
                                                         identity=ident[:]), reads=[omix, ident], writes=[pt])
                    c.op("act", lambda e: e.copy(out=oT[:], in_=pt[:].rearrange("p (k t) -> p k t", k=8)), reads=[pt], writes=[oT])
                    for n in range(2):
                        py = R["psA"].next()
                        for k in range(8):
                            c.op("pe", lambda e: e.matmul(py[:], oT[:, k, :], wout[:, k, n * 512:(n + 1) * 512],
                                                          start=(k == 0), stop=(k == 7)), reads=[oT, wout], writes=[py])
                        c.op("dve", lambda e: e.tensor_tensor(xt[:, n * 512:(n + 1) * 512], py[:], xt[:, n * 512:(n + 1) * 512], ALU.add),
                             reads=[py, xt], writes=[xt])
                    c.dma("sp", d["X"][r0:r0 + 128, :], xt[:], reads=[xt], writes=[self.xtb(r0)])
            c.barrier()

    def gdn_tile(self, c, R, RS, h, dr, qT, kT, ktm, vtm, sc, S, Sb, o_tb, add_tb=None):
        M1, M2, M4, MT = R["gm"][dr]
        hs = slice(h * 128, (h + 1) * 128)
        col = slice(h, h + 1)
        identf = R["identf"]
        W = RS["w"]
        ula = RS["ula"]
        c.op("act", lambda e: e.mul(out=ula[:], in_=M1[:], mul=sc["la"][:, col]), reads=[M1, sc["la"]], writes=[ula])
        pg, pgt = R["psH"].next(), R["psH"].next()
        c.op("pe", lambda e: e.matmul(pg[:, 0:128], ula[:], M4[:], start=True, stop=True), reads=[ula, M4], writes=[pg])
        c.op("pe", lambda e: e.matmul(pgt[:, 0:128], M4[:], ula[:], start=True, stop=True), reads=[ula, M4], writes=[pgt])
        dm, dtm = RS["dm"], RS["dtm"]
        c.op("act", lambda e: e.activation(out=dm[:], in_=pg[:, 0:128], func=AF.Exp), reads=[pg], writes=[dm])
        c.op("act", lambda e: e.activation(out=dtm[:], in_=pgt[:, 0:128], func=AF.Exp), reads=[pgt], writes=[dtm])
        c.op("pool", lambda e: e.tensor_tensor(dm[:], dm[:], R["mstrict"][dr][:], ALU.mult), reads=[dm, R["mstrict"][dr]], writes=[dm])
        c.op("pool", lambda e: e.tensor_tensor(dtm[:], dtm[:], MT[:], ALU.mult), reads=[dtm, MT], writes=[dtm])
        yield
        pk = R["psH"].next()
        c.op("pe", lambda e: e.matmul(pk[:, 0:128], kT[:, h, :], kT[:, h, :], start=True, stop=True), reads=[kT], writes=[pk])
        X = W.next()
        c.op("dve", lambda e: e.scalar_tensor_tensor(X[:], pk[:, 0:128], sc["nbeta"][:, col], dm[:], ALU.mult, ALU.mult),
             reads=[pk, sc["nbeta"], dm], writes=[X])
        yield
        py = R["pstB"].next()
        c.op("pe", lambda e: e.transpose(out=py[:, 0:128], in_=X[:], identity=R["ident"][:]), reads=[X, R["ident"]], writes=[py])
        Y = W.next()
        RT = W.next()
        c.op("act", lambda e: e.copy(out=Y[:], in_=py[:, 0:128]), reads=[py], writes=[Y])
        c.op("dve", lambda e: e.tensor_tensor(RT[:], Y[:], R["ident"][:], ALU.add), reads=[Y, R["ident"]], writes=[RT])
        yield
        ttb = RS["ttb"]
        for m in range(5):
            px = R["psH"].next()
            c.op("pe", lambda e: e.matmul(px[:, 0:128], Y[:], X[:], start=True, stop=True), reads=[X, Y], writes=[px])
            if m < 4:
                c.op("pe", lambda e: e.matmul(px[:, 128:256], X[:], Y[:], start=True, stop=True), reads=[X, Y], writes=[px])
            X2 = W.next()
            c.op("act", lambda e: e.copy(out=X2[:], in_=px[:, 0:128]), reads=[px], writes=[X2])
            if m < 4:
                Y2 = W.next()
                c.op("act", lambda e: e.copy(out=Y2[:], in_=px[:, 128:256]), reads=[px], writes=[Y2])
            yield
            pr = R["psH"].next()
            c.op("pe", lambda e: e.matmul(pr[:, 0:128], X2[:], RT[:], start=True, stop=True), reads=[X2, RT], writes=[pr])
            if m < 4:
                RT2 = W.next()
                c.op("dve", lambda e: e.tensor_tensor(RT2[:], pr[:, 0:128], RT[:], ALU.add), reads=[pr, RT], writes=[RT2])
                RT = RT2
                X, Y = X2, Y2
            else:
                c.op("dve", lambda e: e.tensor_tensor(ttb[:], pr[:, 0:128], RT[:], ALU.add), reads=[pr, RT], writes=[ttb])
            yield
        vb_, kbg, kdec = RS["vbeta"], RS["kbg"], RS["kdec"]
        c.op("act", lambda e: e.mul(out=vb_[:], in_=vtm[:, hs], mul=sc["beta"][:, col]), reads=[vtm, sc["beta"]], writes=[vb_])
        c.op("dve", lambda e: e.tensor_scalar(kbg[:], ktm[:, hs], sc["beg"][:, col], None, ALU.mult), reads=[ktm, sc["beg"]], writes=[kbg])
        c.op("act", lambda e: e.mul(out=kdec[:], in_=ktm[:, hs], mul=sc["egd"][:, col]), reads=[ktm, sc["egd"]], writes=[kdec])
        pu_ = R["psH"].next()
        c.op("pe", lambda e: e.matmul(pu_[:, 0:128], ttb[:], vb_[:], start=True, stop=True), reads=[ttb, vb_], writes=[pu_])
        c.op("pe", lambda e: e.matmul(pu_[:, 128:256], kbg[:], ttb[:], start=True, stop=True), reads=[ttb, kbg], writes=[pu_])
        u = RS["u"]
        wT = RS["wT"]
        c.op("act", lambda e: e.copy(out=u[:], in_=pu_[:, 0:128]), reads=[pu_], writes=[u])
        c.op("act", lambda e: e.copy(out=wT[:], in_=pu_[:, 128:256]), reads=[pu_], writes=[wT])
        yield
        pq = R["psH"].next()
        c.op("pe", lambda e: e.matmul(pq[:, 0:128], kT[:, h, :], qT[:, h, :], start=True, stop=True), reads=[kT, qT], writes=[pq])
        qkm = RS["qkm"]
        c.op("dve", lambda e: e.tensor_tensor(qkm[:], pq[:, 0:128], dtm[:], ALU.mult), reads=[pq, dtm], writes=[qkm])
        yield
        vn = RS["vn"]
        tmp = RS["tmp"]
        order = (0, 1) if dr == 0 else (1, 0)
        for ci in order:
            rs = slice(ci * 64, (ci + 1) * 64)
            pa = R["psH"].next()
            c.op("pe", lambda e: e.matmul(pa[:, 0:128], wT[:], Sb[h][:], start=True, stop=True), reads=[wT, Sb[h]], writes=[pa])
            c.op("pe", lambda e: e.matmul(pa[:, 128:256], qT[:, h, :], Sb[h][:], start=True, stop=True), reads=[qT, Sb[h]], writes=[pa])
            c.op("dve", lambda e: e.tensor_tensor(vn[rs, :], u[rs, :], pa[rs, 0:128], ALU.subtract), reads=[u, pa], writes=[vn])
            c.op("dve", lambda e: e.tensor_scalar(tmp[rs, :], pa[rs, 128:256], sc["eg"][rs, col], None, ALU.mult),
                 reads=[pa, sc["eg"]], writes=[tmp])
            yield
            ps_ = R["psH"].next()
            c.op("pe", lambda e: e.matmul(ps_[:, 0:128], kdec[rs, :], vn[rs, :], start=True, stop=True), reads=[kdec, vn], writes=[ps_])
            c.op("dve", lambda e: e.scalar_tensor_tensor(S[h][:], S[h][:], sc["egl"][ci][:, col], ps_[:, 0:128], ALU.mult, ALU.add),
                 reads=[S[h], sc["egl"][ci], ps_], writes=[S[h]])
            c.op("act", lambda e: e.copy(out=Sb[h][:], in_=S[h][:]), reads=[S[h]], writes=[Sb[h]])
            yield
        p2 = R["psH"].next()
        c.op("pe", lambda e: e.matmul(p2[:, 0:128], qkm[:], vn[:], start=True, stop=True), reads=[qkm, vn], writes=[p2])
        if add_tb is None:
            c.op("dve", lambda e: e.tensor_tensor(o_tb[:, hs], p2[:, 0:128], tmp[:], ALU.add), reads=[p2, tmp], writes=[o_tb])
        else:
            c.op("pool", lambda e: e.tensor_tensor(tmp[:], tmp[:], add_tb[:, hs], ALU.add), reads=[tmp, add_tb], writes=[tmp])
            c.op("dve", lambda e: e.tensor_tensor(o_tb[:, hs], p2[:, 0:128], tmp[:], ALU.add), reads=[p2, tmp], writes=[o_tb])

    def gdn_scalars(self, c, R, dr, ab, sc):
        M1, M2, M4, MT = R["gm"][dr]
        la = ab[:, dr * 8:(dr + 1) * 8]
        be = ab[:, 16 + dr * 8:16 + (dr + 1) * 8]
        ps = R["psA"].next()
        c.op("pe", lambda e: e.matmul(ps[:, 0:8], M1[:], la, start=True, stop=True), reads=[M1, ab], writes=[ps])
        c.op("pe", lambda e: e.matmul(ps[:, 8:16], M4[:], la, start=True, stop=True), reads=[M4, ab], writes=[ps])
        c.op("pe", lambda e: e.matmul(ps[:, 16:24], R["sel"][0][:], la, start=True, stop=True), reads=[R["sel"][0], ab], writes=[ps])
        c.op("pe", lambda e: e.matmul(ps[:, 24:32], R["sel"][1][:], la, start=True, stop=True), reads=[R["sel"][1], ab], writes=[ps])
        ex = sc["ex"]
        c.op("act", lambda e: e.activation(out=ex[:], in_=ps[:, 0:32], func=AF.Exp), reads=[ps], writes=[ex])
        c.op("dve", lambda e: e.tensor_copy(out=sc["la"][:], in_=la), reads=[ab], writes=[sc["la"]])
        c.op("dve", lambda e: e.tensor_copy(out=sc["beta"][:], in_=be), reads=[ab], writes=[sc["beta"]])
        c.op("dve", lambda e: e.tensor_scalar(sc["nbeta"][:], be, -1.0, None, ALU.mult), reads=[ab], writes=[sc["nbeta"]])
        c.op("dve", lambda e: e.tensor_tensor(sc["beg"][:], be, ex[:, 0:8], ALU.mult), reads=[ab, ex], writes=[sc["beg"]])
        sc["eg"] = _View(ex, (slice(None), slice(0, 8)))
        sc["egd"] = _View(ex, (slice(None), slice(8, 16)))
        sc["egl"] = [_View(ex, (slice(None), slice(16, 24))), _View(ex, (slice(None), slice(24, 32)))]

    def phase_odd(self, c, j, layer, x_in):
        nc, d = c.nc, self.d
        Lm = self.Lmax
        with contextlib.ExitStack() as es:
            win = c.sb("win", [128, 8, ODD_IN], BF16, es)
            wout = c.sb("wout", [128, 8, D], BF16, es)
            gmix = c.sb("gmix", [128, D], F32, es)
            gh = c.sb("gh", [128, 128], F32, es)
            ident = c.sb("identb", [128, 128], BF16, es)
            identf = c.sb("identf", [128, 128], F32, es)
            ones = c.sb("onesb", [128, 128], BF16, es)
            gmt = [[c.sb("gm%d%d" % (a, b_), [128, 128], F32, es) for b_ in range(4)] for a in range(2)]
            mstrict = [c.sb("mstr%d" % a, [128, 128], F32, es) for a in range(2)]
            sel = [c.sb("sel%d" % a, [128, 128], F32, es) for a in range(2)]
            cw = c.sb("cw", [128, 24, 5], F32, es)
            dtb = c.sb("dtb", [128, 16], F32, es)
            nea = c.sb("nea", [128, 16], F32, es)
            zer = c.sb("zer", [128, 16], BF16, es)
            ab_all = c.sb("ab_all", [128, Lm // 128, 32], F32, es)
            xts = Rot([c.sb("xt%d" % i, [128, D], F32, es) for i in range(1)])
            ssq = c.sb("ssq", [128, 1], F32, es)
            s2 = c.sb("s2", [128, 8], F32, es)
            hn = c.sb("hn", [128, D], BF16, es)
            hT = c.sb("hT", [128, 8, 128], BF16, es)
            hTb = c.sb("hTb", [128, 8, 260], BF16, es)
            gt = c.sb("gt", [128, D], F32, es)
            NCS = 3
            CSs = [{"raw": c.sb("raw%d" % i, [128, 260], F32, es), "acc": c.sb("acc%d" % i, [128, 256], F32, es),
                    "sqb": c.sb("sqb%d" % i, [128, 256], BF16, es)} for i in range(NCS)]
            for cs_d in CSs:
                cs_d["rst"] = _View(cs_d["raw"], (slice(None), slice(0, 256)))
            qTb = c.sb("qTb", [128, 8, 256], BF16, es)
            kTb = c.sb("kTb", [128, 8, 256], BF16, es)
            vTb = c.sb("vTb", [128, 8, 256], BF16, es)
            qTt = c.sb("qTt", [128, 8, 128], BF16, es)
            kTt = c.sb("kTt", [128, 8, 128], BF16, es)
            vtm = c.sb("vtm", [128, D], BF16, es)
            ktm = c.sb("ktm", [128, D], BF16, es)
            oft = c.sb("oft", [128, D], F32, es)
            sq = oft
            ot = c.sb("ot", [128, D], F32, es)
            omix = c.sb("omix", [128, D], BF16, es)
            junk = omix
            oT = hT
            abt = c.sb("abt", [128, 32], F32, es)
            sc = {k: c.sb("sc_" + k, [128, 8], F32, es) for k in ("la", "beta", "nbeta", "beg")}
            sc["ex"] = c.sb("sc_ex", [128, 32], F32, es)
            R = {
                "gm": gmt, "ident": ident, "identf": identf, "mstrict": mstrict, "sel": sel,
                "psH": Rot([c.ps("psH%d" % i, [128, 512], F32, es) for i in range(6)]),
                "pstB": Rot([c.ps("pstB%d" % i, [128, D], BF16, es) for i in range(2)]),
            }
            R["psA"] = R["psH"]
            NS = 8
            RSs = []
            for sl in range(NS):
                RS = {"w": Rot([c.sb("w%d_%d" % (sl, i), [128, 128], BF16, es) for i in range(7)])}
                for nm in ("dtm", "u", "tmp", "ula", "dm"):
                    RS[nm] = c.sb("%s_%d" % (nm, sl), [128, 128], F32, es)
                for nm in ("ttb", "vbeta", "kbg", "kdec", "wT", "qkm", "vn"):
                    RS[nm] = c.sb("%s_%d" % (nm, sl), [128, 128], BF16, es)
                RSs.append(RS)
            S = [c.sb("S%d" % h, [128, 128], F32, es) for h in range(8)]
            Sb = [c.sb("Sb%d" % h, [128, 128], BF16, es) for h in range(8)]

            c.dma("sp", ident[:], d["c_ident_bf"], writes=[ident])
            c.dma("sp", identf[:], d["c_ident_f"], writes=[identf])
            c.dma("sp", ones[:], d["c_ones_bf"], writes=[ones])
            for a in range(2):
                for b_ in range(4):
                    c.dma("sp", gmt[a][b_][:], d["c_gla"][a * 4 + b_], writes=[gmt[a][b_]])
                c.dma("sp", mstrict[a][:], d["c_mstrict"][a], writes=[mstrict[a]])
                c.dma("sp", sel[a][:], d["c_sel"][a], writes=[sel[a]])
            c.dma("sp", gmix[:], d["norm_mix"][layer:layer + 1, :].partition_broadcast(128), writes=[gmix])
            c.dma("sp", gh[:], d["gdn_norm"][j:j + 1, :].partition_broadcast(128), writes=[gh])
            c.dma("sp", dtb[:], d["gdn_dt_bias"][j:j + 1, :].partition_broadcast(128), writes=[dtb])
            c.dma("sp", nea[:], d["gdn_a_log"][j:j + 1, :].partition_broadcast(128), writes=[nea])
            c.op("act", lambda e: e.activation(out=nea[:], in_=nea[:], func=AF.Exp), reads=[nea], writes=[nea])
            c.op("dve", lambda e: e.tensor_scalar(nea[:], nea[:], -1.0, None, ALU.mult), reads=[nea], writes=[nea])
            c.op("dve", lambda e: e.memset(zer[:], 0.0), writes=[zer])
            for w in range(5):
                c.dma("sp", cw[:, :, w], d["gdn_conv"][j, w, :].rearrange("(f p) -> p f", p=128), writes=[cw], slow=True)
            for k in range(8):
                c.dma("pool", win[:, k, :], d["gdn_w_in"][j, k * 128:(k + 1) * 128, :], writes=[win])
                c.dma("pool", wout[:, k, :], d["gdn_w_out"][j, k * 128:(k + 1) * 128, :], writes=[wout])

            HT = d["HT"].rearrange("p (k t) -> p k t", k=8)
            QT = d["QT"].rearrange("p (k t) -> p k t", k=8)
            KT = d["KT"].rearrange("p (k t) -> p k t", k=8)

            for si, L in enumerate(self.seqs):
                nt = L // 128
                base = self.seq_off[si]
                httb = TB(None, "HT")
                gtb = [TB(None, "G%d" % t) for t in range(nt)]
                qktb = [TB(None, "QK%d" % t) for t in range(nt // 2)]
                vmtb = [TB(None, "VM%d" % t) for t in range(nt)]
                oftb = [TB(None, "OF%d" % t) for t in range(nt)]
                c.dma("sp", HT[:, :, 0:2], zer[:, 0:16].rearrange("p (k t) -> p k t", k=8), reads=[zer], writes=[httb])
                c.dma("sp", HT[:, :, L + 2:L + 4], zer[:, 0:16].rearrange("p (k t) -> p k t", k=8), reads=[zer], writes=[httb])
                dbg = getattr(self, "dbg", 99)
                for t in range(nt if dbg >= 1 else 0):
                    r0 = base + t * 128
                    xt = xts.next()
                    c.dma("sp", xt[:], x_in[r0:r0 + 128, :], reads=[self.xtb(r0)], writes=[xt])
                    self.rmsnorm_to_hT(c, xt, gmix, (hT, hT[:]), junk, ssq, hn, R["pstB"].next(), ident)
                    c.dma("sp", HT[:, :, 2 + t * 128:2 + (t + 1) * 128], hT[:], reads=[hT], writes=[httb])
                    for n in range(2):
                        pa = R["psA"].next()
                        for k in range(8):
                            c.op("pe", lambda e: e.matmul(pa[:], hT[:, k, :], win[:, k, 3072 + n * 512:3072 + (n + 1) * 512],
                                                          start=(k == 0), stop=(k == 7)), reads=[hT, win], writes=[pa])
                        c.op("act", lambda e: e.copy(out=gt[:, n * 512:(n + 1) * 512], in_=pa[:]), reads=[pa], writes=[gt])
                    c.dma("sp", d["P"][t * 128:(t + 1) * 128, 0:D], gt[:], reads=[gt], writes=[gtb[t]])
                    pa = R["psA"].next()
                    for k in range(8):
                        c.op("pe", lambda e: e.matmul(pa[:, 0:32], hT[:, k, :], win[:, k, 4096:4128],
                                                      start=(k == 0), stop=(k == 7)), reads=[hT, win], writes=[pa])
                    c.op("dve", lambda e: e.tensor_tensor(abt[:, 0:16], pa[:, 0:16], dtb[:], ALU.add), reads=[pa, dtb], writes=[abt])
                    c.op("act", lambda e: e.activation(out=abt[:, 0:16], in_=abt[:, 0:16], func=AF.Exp), reads=[abt], writes=[abt])
                    c.op("act", lambda e: e.activation(out=abt[:, 0:16], in_=abt[:, 0:16], func=AF.Ln, bias=1.0), reads=[abt], writes=[abt])
                    c.op("dve", lambda e: e.tensor_tensor(ab_all[:, t, 0:16], abt[:, 0:16], nea[:], ALU.mult), reads=[abt, nea], writes=[ab_all])
                    c.op("dve", lambda e: e.tensor_copy(out=abt[:, 16:32], in_=pa[:, 16:32]), reads=[pa], writes=[abt])
                    c.op("act", lambda e: e.activation(out=ab_all[:, t, 16:32], in_=abt[:, 16:32], func=AF.Sigmoid), reads=[abt], writes=[ab_all])
                for h in range(8):
                    c.op("dve", lambda e: e.memset(S[h][:], 0.0), writes=[S[h]])
                    c.op("pool", lambda e: e.memset(Sb[h][:], 0.0), writes=[Sb[h]])
                for blk in range(nt // 2 if dbg >= 2 else 0):
                    b0 = blk * 256
                    c.dma("sp", hTb[:], HT[:, :, b0:b0 + 260], reads=[httb], writes=[hTb])
                    def fb_gen(fb, CS):
                        rw, ac, sqb_, rst_ = CS["raw"], CS["acc"], CS["sqb"], CS["rst"]
                        pa = R["psA"].next()
                        for k in range(8):
                            c.op("pe", lambda e: e.matmul(pa[:, 0:260], win[:, k, fb * 128:(fb + 1) * 128], hTb[:, k, :],
                                                          start=(k == 0), stop=(k == 7)), reads=[hTb, win], writes=[pa])
                        c.op("act", lambda e: e.copy(out=rw[:], in_=pa[:, 0:260]), reads=[pa], writes=[rw])
                        yield
                        c.op("dve", lambda e: e.tensor_scalar(ac[:], rw[:, 0:256], cw[:, fb, 0:1], None, ALU.mult), reads=[rw, cw], writes=[ac])
                        yield
                        for w in range(1, 5):
                            c.op("dve", lambda e: e.scalar_tensor_tensor(ac[:], rw[:, w:w + 256], cw[:, fb, w:w + 1], ac[:], ALU.mult, ALU.add),
                                 reads=[rw, cw, ac], writes=[ac])
                            yield
                        hh = fb % 8
                        if fb >= 16:
                            c.op("act", lambda e: e.activation(out=vTb[:, hh, :], in_=ac[:], func=AF.Silu), reads=[ac], writes=[vTb])
                            return
                        c.op("act", lambda e: e.activation(out=ac[:], in_=ac[:], func=AF.Silu), reads=[ac], writes=[ac])
                        yield
                        c.op("act", lambda e: e.activation(out=sqb_[:], in_=ac[:], func=AF.Square), reads=[ac], writes=[sqb_])
                        yield
                        pn = R["psA"].next()
                        c.op("pe", lambda e: e.matmul(pn[:, 0:256], ones[:], sqb_[:], start=True, stop=True), reads=[ones, sqb_], writes=[pn])
                        c.op("act", lambda e: e.activation(out=rst_[:], in_=pn[:, 0:256], func=AF.Ln, bias=RMS_EPS), reads=[pn], writes=[rst_])
                        yield
                        c.op("act", lambda e: e.activation(out=rst_[:], in_=rst_[:], func=AF.Exp, scale=-0.5), reads=[rst_], writes=[rst_])
                        yield
                        dst = qTb if fb < 8 else kTb
                        if fb < 8:
                            c.op("dve", lambda e: e.scalar_tensor_tensor(dst[:, hh, :], ac[:], 128 ** -0.5, rst_[:], ALU.mult, ALU.mult),
                                 reads=[ac, rst_], writes=[dst])
                        else:
                            c.op("pool", lambda e: e.tensor_tensor(dst[:, hh, :], ac[:], rst_[:], ALU.mult), reads=[ac, rst_], writes=[dst])

                    for fb0 in range(0, 24, NCS):
                        run_interleaved([fb_gen(fb0 + i, CSs[i]) for i in range(NCS)])
                    c.dma("sp", QT[:, :, b0:b0 + 256], qTb[:], reads=[qTb], writes=[qktb[blk]])
                    c.dma("sp", KT[:, :, b0:b0 + 256], kTb[:], reads=[kTb], writes=[qktb[blk]])
                    for tt in range(2):
                        t = blk * 2 + tt
                        cs_ = slice(tt * 128, (tt + 1) * 128)
                        for src, dstm in ((vTb, vtm), (kTb, ktm)):
                            pt = R["pstB"].next()
                            for h in range(8):
                                c.op("pe", lambda e: e.transpose(out=pt[:, h * 128:(h + 1) * 128], in_=src[:, h, cs_], identity=ident[:]),
                                     reads=[src, ident], writes=[pt])
                            c.op("act", lambda e: e.copy(out=dstm[:], in_=pt[:]), reads=[pt], writes=[dstm])
                        c.dma("sp", d["VM"][t * 128:(t + 1) * 128, :], vtm[:], reads=[vtm], writes=[vmtb[t]])
                        c.op("pool", lambda e: e.tensor_copy(out=qTt[:], in_=qTb[:, :, cs_]), reads=[qTb], writes=[qTt])
                        c.op("pool", lambda e: e.tensor_copy(out=kTt[:], in_=kTb[:, :, cs_]), reads=[kTb], writes=[kTt])
                        abv = _View(ab_all, (slice(None), t, slice(None)))
                        self.gdn_scalars(c, R, 0, abv, sc)
                        for h0 in range(0, 8, NS):
                            run_interleaved([self.gdn_tile(c, R, RSs[i], h0 + i, 0, qTt, kTt, ktm, vtm, sc, S, Sb, oft)
                                             for i in range(NS)])
                        c.dma("sp", d["OF"][t * 128:(t + 1) * 128, :], oft[:], reads=[oft], writes=[oftb[t]])
                for h in range(8):
                    c.op("dve", lambda e: e.memset(S[h][:], 0.0), writes=[S[h]])
                    c.op("pool", lambda e: e.memset(Sb[h][:], 0.0), writes=[Sb[h]])
                for t in (range(nt - 1, -1, -1) if dbg >= 4 else []):
                    r0 = base + t * 128
                    xt = xts.next()
                    c.dma("sp", xt[:], x_in[r0:r0 + 128, :], reads=[self.xtb(r0)], writes=[xt])
                    c.dma("sp", qTt[:], QT[:, :, t * 128:(t + 1) * 128], reads=[qktb[t // 2]], writes=[qTt])
                    c.dma("sp", kTt[:], KT[:, :, t * 128:(t + 1) * 128], reads=[qktb[t // 2]], writes=[kTt])
                    c.dma("sp", vtm[:], d["VM"][t * 128:(t + 1) * 128, :], reads=[vmtb[t]], writes=[vtm])
                    c.dma("sp", oft[:], d["OF"][t * 128:(t + 1) * 128, :], reads=[oftb[t]], writes=[oft])
                    c.dma("sp", gt[:], d["P"][t * 128:(t + 1) * 128, 0:D], reads=[gtb[t]], writes=[gt])
                    pt = R["pstB"].next()
                    for h in range(8):
                        c.op("pe", lambda e: e.transpose(out=pt[:, h * 128:(h + 1) * 128], in_=kTt[:, h, :], identity=ident[:]),
                             reads=[kTt, ident], writes=[pt])
                    c.op("act", lambda e: e.copy(out=ktm[:], in_=pt[:]), reads=[pt], writes=[ktm])
                    abv = _View(ab_all, (slice(None), t, slice(None)))
                    self.gdn_scalars(c, R, 1, abv, sc)
                    for h0 in range(0, 8, NS):
                        run_interleaved([self.gdn_tile(c, R, RSs[i], h0 + i, 1, qTt, kTt, ktm, vtm, sc, S, Sb, ot, add_tb=oft)
                                         for i in range(NS)])
                    c.op("act", lambda e: e.activation(out=sq[:], in_=ot[:], func=AF.Square), reads=[ot], writes=[sq])
                    c.op("dve", lambda e: e.reduce_sum(out=s2[:], in_=sq[:].rearrange("p (h x) -> p h x", h=8), axis=AX.X),
                         reads=[sq], writes=[s2])
                    c.op("dve", lambda e: e.tensor_scalar(s2[:], s2[:], 1.0 / 128, RMS_EPS, ALU.mult, ALU.add), reads=[s2], writes=[s2])
                    c.op("act", lambda e: e.activation(out=s2[:], in_=s2[:], func=AF.Sqrt), reads=[s2], writes=[s2])
                    c.op("dve", lambda e: e.reciprocal(out=s2[:], in_=s2[:]), reads=[s2], writes=[s2])
                    for h in range(8):
                        hs = slice(h * 128, (h + 1) * 128)
                        c.op("dve", lambda e: e.scalar_tensor_tensor(ot[:, hs], ot[:, hs], s2[:, h:h + 1], gh[:], ALU.mult, ALU.mult),
                             reads=[ot, s2, gh], writes=[ot])
                    c.op("act", lambda e: e.activation(out=gt[:], in_=gt[:], func=AF.Silu), reads=[gt], writes=[gt])
                    c.op("dve", lambda e: e.tensor_tensor(omix[:], ot[:], gt[:], ALU.mult), reads=[ot, gt], writes=[omix])
                    pt = R["pstB"].next()
                    for k in range(8):
                        c.op("pe", lambda e: e.transpose(out=pt[:, k * 128:(k + 1) * 128], in_=omix[:, k * 128:(k + 1) * 128],
                                                         identity=ident[:]), reads=[omix, ident], writes=[pt])
                    c.op("act", lambda e: e.copy(out=oT[:], in_=pt[:].rearrange("p (k t) -> p k t", k=8)), reads=[pt], writes=[oT])
                    for n in range(2):
                        py = R["psA"].next()
                        for k in range(8):
                            c.op("pe", lambda e: e.matmul(py[:], oT[:, k, :], wout[:, k, n * 512:(n + 1) * 512],
                                                          start=(k == 0), stop=(k == 7)), reads=[oT, wout], writes=[py])
                        c.op("dve", lambda e: e.tensor_tensor(xt[:, n * 512:(n + 1) * 512], py[:], xt[:, n * 512:(n + 1) * 512], ALU.add),
                             reads=[py, xt], writes=[xt])
                    c.dma("sp", d["X"][r0:r0 + 128, :], xt[:], reads=[xt], writes=[self.xtb(r0)])
            c.barrier()

    def phase_copy(self, c, x_in, to_y=False):
        d = self.d
        with contextlib.ExitStack() as es:
            xts = Rot([c.sb("cx%d" % i, [128, D], F32, es) for i in range(4)])
            for r0 in range(0, self.T, 128):
                xt = xts.next()
                c.dma("sp", xt[:], x_in[r0:r0 + 128, :], reads=[self.xtb(r0)], writes=[xt])
                if to_y:
                    c.dma("sp", d["y"][r0:r0 + 128, :], xt[:], reads=[xt], writes=[self.ytb(r0)])
                else:
                    c.dma("sp", d["X"][r0:r0 + 128, :], xt[:], reads=[xt], writes=[self.xtb(r0)])
            c.barrier()

    def build(self):
        nc = bass.Bass("TRN2", target_bir_lowering=False)
        self.declare(nc)
        with contextlib.ExitStack() as es:
            c = Ctx(nc, es)
            self.c = c
            self._xtb = [TB(None, "X%d" % i) for i in range(self.T // 128)]
            self._ytb = [TB(None, "Y%d" % i) for i in range(self.T // 128)]
            first = True
            for layer in range(self.depth):
                last = layer == self.depth - 1
                if self.do_mixer:
                    from_x = self.d["x"] if first else self.d["X"]
                    kind, jj = self.kinds[layer]
                    if kind == "even":
                        self.phase_even(c, jj, layer, from_x)
                    else:
                        self.phase_odd(c, jj, layer, from_x)
                    first = False
                if self.do_xattn:
                    self.phase_xattn(c, layer, self.d["x"] if first else self.d["X"])
                    first = False
                if self.do_ffn:
                    self.phase_ffn(c, layer, self.d["x"] if first else self.d["X"], last)
                    first = False
            c.barrier()
            self.stats = (c.n_inst, c.n_wait)
        return nc


def gla_masks(direction):
    t = np.arange(128)
    same = (t[:, None] // 64) == (t[None, :] // 64)
    if direction == 0:
        le = t[:, None] <= t[None, :]
        mid = 64 * (t // 64) + 31
        lemid = t[:, None] <= mid[None, :]
        gt = t[:, None] > t[None, :]
    else:
        le = t[:, None] >= t[None, :]
        mid = 64 * (t // 64) + 32
        lemid = t[:, None] >= mid[None, :]
        gt = t[:, None] < t[None, :]
    M1 = (same & le).astype(np.float32)
    M2 = (same * (le.astype(np.float32) - lemid.astype(np.float32))).astype(np.float32)
    M4 = (same & gt).astype(np.float32)
    MT = (same & le).astype(np.float32)
    return M1, M2, M4, MT


def host_constants(Lmax):
    cst = {}
    cst["c_ident_bf"] = np.eye(128, dtype=np.float32).astype(ml_dtypes.bfloat16)
    cst["c_ones_bf"] = np.ones((128, 128), np.float32).astype(ml_dtypes.bfloat16)
    cst["c_ident_f"] = np.eye(128, dtype=np.float32)
    gm = np.zeros((2, 4, 128, 128), np.float32)
    for dr in range(2):
        for i, m in enumerate(gla_masks(dr)):
            gm[dr, i] = m
    cst["c_gla"] = gm.reshape(8, 128, 128)
    ind = np.zeros((128, 2), np.float32)
    ind[:64, 0] = 1.0
    ind[64:, 1] = 1.0
    cst["c_ind"] = ind
    inv = (1.0 / (np.float32(10000.0) ** (np.arange(64, dtype=np.float32) / np.float32(64)))).astype(np.float32)
    ang = (np.arange(Lmax, dtype=np.float32)[:, None] * inv[None, :]).astype(np.float32)
    cs = np.zeros((Lmax, 4, 64), np.float32)
    cs[:, 0] = np.cos(ang)
    cs[:, 1] = np.sin(ang)
    cs[:, 2] = np.cos(ang) * np.float32(128 ** -0.5)
    cs[:, 3] = np.sin(ang) * np.float32(128 ** -0.5)
    cst["c_rope"] = cs.reshape(Lmax, 256)
    lg = np.log1p(-np.exp2(-5.0 - np.arange(4, dtype=np.float32))).astype(np.float32)
    cst["c_lgam"] = np.repeat(lg, 128)[None, :].astype(np.float32)
    t = np.arange(128)
    same = (t[:, None] // 64) == (t[None, :] // 64)
    ms = np.zeros((2, 128, 128), np.float32)
    ms[0] = same & (t[None, :] < t[:, None])
    ms[1] = same & (t[None, :] > t[:, None])
    cst["c_mstrict"] = ms
    sl = np.zeros((2, 128, 128), np.float32)
    sl[0, :64, :] = 1.0
    sl[1, 64:, :] = 1.0
    cst["c_sel"] = sl
    hm = np.zeros((1, 8), np.float32)
    hm[0, :4] = 1.0
    cst["c_hmask"] = hm
    return cst


_CACHE = {}


def _get_program(seqs):
    key = tuple(seqs)
    if key not in _CACHE:
        b = Builder(seqs)
        nc = b.build()
        _CACHE[key] = (b, nc)
    return _CACHE[key]


def kernel(**inputs):
    inp = {k: np.asarray(v) for k, v in inputs.items()}
    seqs = [2048, 2048, 4096, 4096]
    b, nc = _get_program(seqs)
    cst = host_constants(max(seqs))
    shared = {}
    for k in ("norm_mix", "norm_xq", "norm_mem", "norm_ffn", "even_w_in", "even_w_out", "hgrn_lb_logits",
              "gdn_w_in", "gdn_conv", "gdn_norm", "gdn_w_out", "xa_w_q", "xa_w_kv", "xa_w_o", "ffn_w_gu", "ffn_w_down"):
        shared[k] = np.ascontiguousarray(inp[k], dtype=np.float32)
    shared["norm_final"] = np.ascontiguousarray(inp["norm_final"].reshape(1, D), dtype=np.float32)
    shared["ret_norm"] = np.ascontiguousarray(inp["ret_norm"].reshape(2, 512), dtype=np.float32)
    shared["hgrn_norm"] = np.ascontiguousarray(inp["hgrn_norm"].reshape(2, 512), dtype=np.float32)
    shared["gdn_a_log"] = np.ascontiguousarray(inp["gdn_a_log"].reshape(2, 16), dtype=np.float32)
    shared["gdn_dt_bias"] = np.ascontiguousarray(inp["gdn_dt_bias"].reshape(2, 16), dtype=np.float32)
    shared.update(cst)
    in_maps = []
    for ci in range(NCORES):
        m = dict(shared)
        xp = inp["x_prompt"][2 * ci:2 * ci + 2].reshape(-1, D)
        xs = inp["x_sample"][2 * ci:2 * ci + 2].reshape(-1, D)
        m["x"] = np.ascontiguousarray(np.concatenate([xp, xs], axis=0), dtype=np.float32)
        mp = inp["mem_prompt"][2 * ci:2 * ci + 2].reshape(-1, D)
        ms = inp["mem_sample"][2 * ci:2 * ci + 2].reshape(-1, D)
        m["mem"] = np.ascontiguousarray(np.concatenate([mp, ms], axis=0), dtype=np.float32)
        in_maps.append(m)
    res = run_bass_kernel_spmd(nc, in_maps, core_ids=list(range(NCORES)))
    yp = np.empty((16, 2048, D), np.float32)
    ys = np.empty((16, 4096, D), np.float32)
    for ci in range(NCORES):
        y = np.asarray(res.results[ci]["y"])
        yp[2 * ci:2 * ci + 2] = y[:4096].reshape(2, 2048, D)
        ys[2 * ci:2 * ci + 2] = y[4096:].reshape(2, 4096, D)
    return (yp, ys)
```

```python
import contextlib
import math
import numpy as np
import ml_dtypes
import concourse.bass as bass
import concourse.mybir as mybir
from concourse.bass_utils import run_bass_kernel_spmd

F32 = mybir.dt.float32
BF16 = mybir.dt.bfloat16
AF = mybir.ActivationFunctionType
ALU = mybir.AluOpType
AX = mybir.AxisListType

D = 1024
DEPTH = 4
N_MEM = 256
RMS_EPS = 1e-6
D_FF = 2816
EVEN_IN = 4608
ODD_IN = 4128
XA_SCALE = 256 ** -0.5
NCORES = 8


class TB:
    __slots__ = ("t", "w", "r", "name")

    def __init__(self, t, name=""):
        self.t = t
        self.w = None
        self.r = {}
        self.name = name

    def __getitem__(self, k):
        return self.t[k]


class Ctx:
    def __init__(self, nc, es, n_dma_sems=20):
        self.nc = nc
        self.es = es
        self.engs = {"pe": nc.tensor, "act": nc.scalar, "dve": nc.vector, "pool": nc.gpsimd, "sp": nc.sync}
        self.sem = {}
        self.cnt = {}
        self.seen = {k: {} for k in self.engs}
        for k in self.engs:
            self.sem[k] = es.enter_context(nc.semaphore("s_" + k))
            self.cnt[k] = 0
        self.dsem = {}
        self.dval = {}
        self.drot = {}
        for q in ("sp", "pool", "act"):
            n = n_dma_sems if q != "act" else 8
            self.dsem[q] = [es.enter_context(nc.semaphore("d_%s_%d" % (q, i))) for i in range(n)]
            self.dval[q] = [0] * n
            self.drot[q] = 0
        self.n_inst = 0
        self.n_wait = 0

    def sb(self, name, shape, dt, es=None):
        es = es or self.es
        self.uid = getattr(self, "uid", 0) + 1
        t = es.enter_context(self.nc.sbuf_tensor("%s_%d" % (name, self.uid), list(shape), dt))
        return TB(t, name)

    def ps(self, name, shape, dt, es=None):
        es = es or self.es
        self.uid = getattr(self, "uid", 0) + 1
        t = es.enter_context(self.nc.psum_tensor("%s_%d" % (name, self.uid), list(shape), dt))
        return TB(t, name)

    def _deps(self, engname, reads, writes):
        need = {}

        def add(ev, raw):
            if ev is None:
                return
            s, v, e = ev
            if e == engname and not raw:
                return
            if e == engname and engname == "pe":
                return
            key = id(s)
            if key not in need or need[key][1] < v:
                need[key] = (s, v)

        for b in reads:
            add(b.w, True)
        for b in writes:
            add(b.w, False)
            for ev in b.r.values():
                add(ev, False)
        return need

    def _emit_waits(self, engname, need):
        eng = self.engs[engname]
        seen = self.seen[engname]
        for key, (s, v) in need.items():
            if seen.get(key, 0) >= v:
                continue
            eng.wait_ge(s, v)
            seen[key] = v
            self.n_wait += 1

    def _commit(self, ev, reads, writes):
        for b in writes:
            b.w = ev
            b.r = {}
        for b in reads:
            if b in writes:
                continue
            b.r[ev[2]] = ev

    def op(self, engname, fn, reads=(), writes=()):
        need = self._deps(engname, reads, writes)
        self._emit_waits(engname, need)
        ins = fn(self.engs[engname])
        self.cnt[engname] += 1
        ins.then_inc(self.sem[engname], 1)
        ev = (self.sem[engname], self.cnt[engname], engname)
        self._commit(ev, reads, writes)
        self.n_inst += 1
        return ins

    def dma(self, q, out, in_, reads=(), writes=(), slow=False):
        need = self._deps("dma_" + q, reads, writes)
        i = self.drot[q]
        self.drot[q] = (i + 1) % len(self.dsem[q])
        s = self.dsem[q][i]
        if self.dval[q][i] > 0:
            need[id(s)] = (s, self.dval[q][i])
        self._emit_waits(q, need)
        if slow:
            ins = self.engs[q].dma_start(out=out, in_=in_, allow_slow_non_contiguous=True)
        else:
            ins = self.engs[q].dma_start(out=out, in_=in_)
        self.dval[q][i] += 16
        ins.then_inc(s, 16)
        ev = (s, self.dval[q][i], "dma_" + q + str(i))
        self._commit(ev, reads, writes)
        self.n_inst += 1
        return ins

    def barrier(self):
        for e in self.engs:
            need = {}
            for k in self.engs:
                if k != e and self.cnt[k] > 0:
                    need[id(self.sem[k])] = (self.sem[k], self.cnt[k])
            for q in self.dsem:
                for s, v in zip(self.dsem[q], self.dval[q]):
                    if v > 0:
                        need[id(s)] = (s, v)
            self._emit_waits(e, need)


def rstd_inplace(c, ssq, inv_n, eps=RMS_EPS):
    c.op("dve", lambda e: e.tensor_scalar(ssq[:], ssq[:], inv_n, eps, ALU.mult, ALU.add), reads=[ssq], writes=[ssq])
    c.op("act", lambda e: e.activation(out=ssq[:], in_=ssq[:], func=AF.Sqrt), reads=[ssq], writes=[ssq])
    c.op("dve", lambda e: e.reciprocal(out=ssq[:], in_=ssq[:]), reads=[ssq], writes=[ssq])


class _View:
    def __init__(self, tb, key):
        self.tb = tb
        self.key = key

    def __getitem__(self, k):
        v = self.tb.t[self.key]
        return v[k]

    @property
    def w(self):
        return self.tb.w

    @w.setter
    def w(self, v):
        self.tb.w = v

    @property
    def r(self):
        return self.tb.r

    @r.setter
    def r(self, v):
        self.tb.r = v


def run_interleaved(gens):
    gens = list(gens)
    while gens:
        for g in list(gens):
            try:
                next(g)
            except StopIteration:
                gens.remove(g)


class Rot:
    def __init__(self, items):
        self.items = items
        self.i = 0

    def next(self):
        it = self.items[self.i]
        self.i = (self.i + 1) % len(self.items)
        return it


class Builder:
    def __init__(self, seqs, depth=DEPTH, do_mixer=True, do_xattn=True, do_ffn=True, kinds=None):
        self.seqs = list(seqs)
        self.depth = depth
        self.do_mixer = do_mixer
        self.do_xattn = do_xattn
        self.do_ffn = do_ffn
        self.kinds = kinds or [("even", l // 2) if l % 2 == 0 else ("odd", l // 2) for l in range(depth)]
        self.T = sum(self.seqs)
        self.seq_off = [sum(self.seqs[:i]) for i in range(len(self.seqs))]
        self.Lmax = max(self.seqs)

    def declare(self, nc):
        d = {}

        def inp(name, shape, dt=F32):
            d[name] = nc.dram_tensor(name, list(shape), dt, kind="ExternalInput").ap()

        ns = len(self.seqs)
        inp("x", [self.T, D])
        inp("mem", [ns * N_MEM, D])
        for n in ("norm_mix", "norm_xq", "norm_mem", "norm_ffn"):
            inp(n, [DEPTH, D])
        inp("norm_final", [1, D])
        inp("even_w_in", [2, D, EVEN_IN])
        inp("even_w_out", [2, D, D])
        inp("hgrn_lb_logits", [2, 512])
        inp("ret_norm", [2, 512])
        inp("hgrn_norm", [2, 512])
        inp("gdn_w_in", [2, D, ODD_IN])
        inp("gdn_conv", [2, 5, 3072])
        inp("gdn_a_log", [2, 16])
        inp("gdn_dt_bias", [2, 16])
        inp("gdn_norm", [2, 128])
        inp("gdn_w_out", [2, D, D])
        inp("xa_w_q", [DEPTH, D, D])
        inp("xa_w_kv", [DEPTH, D, 2 * D])
        inp("xa_w_o", [DEPTH, D, D])
        inp("ffn_w_gu", [DEPTH, D, 2 * D_FF])
        inp("ffn_w_down", [DEPTH, D_FF, D])
        for name, arr in host_constants(self.Lmax).items():
            inp(name, arr.shape, F32 if arr.dtype == np.float32 else BF16)
        d["y"] = nc.dram_tensor("y", [self.T, D], F32, kind="ExternalOutput").ap()
        d["X"] = nc.dram_tensor("X_scr", [self.T, D], F32).ap()
        d["P"] = nc.dram_tensor("P_scr", [self.Lmax, EVEN_IN], F32).ap()
        d["OF"] = nc.dram_tensor("OF_scr", [self.Lmax, D], F32).ap()
        d["HT"] = nc.dram_tensor("HT_scr", [128, 8 * (self.Lmax + 4)], BF16).ap()
        d["QT"] = nc.dram_tensor("QT_scr", [128, 8 * self.Lmax], BF16).ap()
        d["KT"] = nc.dram_tensor("KT_scr", [128, 8 * self.Lmax], BF16).ap()
        d["VM"] = nc.dram_tensor("VM_scr", [self.Lmax, D], BF16).ap()
        self.d = d

    def rmsnorm_to_hT(self, c, xt, grow, hT_dst, junk, ssq, hn, pst, ident):
        c.op("act", lambda e: e.activation(out=junk[:], in_=xt[:], func=AF.Square, accum_out=ssq[:]),
             reads=[xt], writes=[junk, ssq])
        rstd_inplace(c, ssq, 1.0 / D)
        c.op("dve", lambda e: e.scalar_tensor_tensor(hn[:], xt[:], ssq[:], grow[:], ALU.mult, ALU.mult),
             reads=[xt, ssq, grow], writes=[hn])
        for k in range(8):
            c.op("pe", lambda e, k=k: e.transpose(out=pst[:, k * 128:(k + 1) * 128],
                                                   in_=hn[:, k * 128:(k + 1) * 128], identity=ident[:]),
                 reads=[hn, ident], writes=[pst])
        tb, ap = hT_dst
        c.op("act", lambda e: e.copy(out=ap, in_=pst[:].rearrange("p (k t) -> p k t", k=8)),
             reads=[pst], writes=[tb])

    def x_src(self, layer_first):
        return self.d["x"] if layer_first else self.d["X"]

    def phase_ffn(self, c, layer, x_in, last):
        nc, d = c.nc, self.d
        TBK = 256
        with contextlib.ExitStack() as es:
            wgu = c.sb("wgu", [128, 8, 2 * D_FF], BF16, es)
            wdn = c.sb("wdn", [128, 22, D], BF16, es)
            grow = c.sb("grow", [128, D], F32, es)
            gfin = c.sb("gfin", [128, D], F32, es)
            ident = c.sb("identb", [128, 128], BF16, es)
            xts = Rot([[c.sb("xt%d_%d" % (i, j), [128, D], F32, es) for j in range(2)] for i in range(2)])
            hTs = Rot([c.sb("hT%d" % i, [128, 8, TBK], BF16, es) for i in range(2)])
            aT = c.sb("aT", [128, 22, TBK], BF16, es)
            sg = Rot([c.sb("sg%d" % i, [128, TBK], F32, es) for i in range(2)])
            junk = c.sb("junk", [128, D], BF16, es)
            ssq = c.sb("ssq", [128, 1], F32, es)
            hn = c.sb("hn", [128, D], BF16, es)
            yts = Rot([c.sb("yt%d" % i, [128, D], F32, es) for i in range(2)])
            pst = c.ps("pst", [128, D], BF16, es)
            psg = Rot([c.ps("psg%d" % i, [128, 512], F32, es) for i in range(2)])
            psu = Rot([c.ps("psu%d" % i, [128, 512], F32, es) for i in range(2)])
            psy = Rot([c.ps("psy%d" % i, [128, 512], F32, es) for i in range(2)])

            c.dma("sp", ident[:], d["c_ident_bf"], writes=[ident])
            c.dma("sp", grow[:], d["norm_ffn"][layer:layer + 1, :].partition_broadcast(128), writes=[grow])
            c.dma("sp", gfin[:], d["norm_final"][0:1, :].partition_broadcast(128), writes=[gfin])
            for k in range(8):
                c.dma("pool", wgu[:, k, :], d["ffn_w_gu"][layer, k * 128:(k + 1) * 128, :], writes=[wgu])
            for f in range(22):
                c.dma("pool", wdn[:, f, :], d["ffn_w_down"][layer, f * 128:(f + 1) * 128, :], writes=[wdn])

            for b0 in range(0, self.T, TBK):
                xt = xts.next()
                hT = hTs.next()
                for j in range(2):
                    r0 = b0 + j * 128
                    c.dma("sp", xt[j][:], x_in[r0:r0 + 128, :], reads=[self.xtb(r0)], writes=[xt[j]])
                    self.rmsnorm_to_hT(c, xt[j], grow, (hT, hT[:, :, j * 128:(j + 1) * 128]), junk, ssq, hn, pst, ident)
                for fb in range(22):
                    pg, pu, s = psg.next(), psu.next(), sg.next()
                    for k in range(8):
                        c.op("pe", lambda e, k=k: e.matmul(pg[:, 0:TBK], wgu[:, k, fb * 128:(fb + 1) * 128], hT[:, k, :],
                                                          start=(k == 0), stop=(k == 7)),
                             reads=[wgu, hT], writes=[pg])
                    for k in range(8):
                        c.op("pe", lambda e, k=k: e.matmul(pu[:, 0:TBK], wgu[:, k, D_FF + fb * 128:D_FF + (fb + 1) * 128],
                                                          hT[:, k, :], start=(k == 0), stop=(k == 7)),
                             reads=[wgu, hT], writes=[pu])
                    c.op("act", lambda e: e.activation(out=s[:], in_=pg[:, 0:TBK], func=AF.Silu), reads=[pg], writes=[s])
                    c.op("dve", lambda e: e.tensor_tensor(aT[:, fb, :], pu[:, 0:TBK], s[:], ALU.mult),
                         reads=[pu, s], writes=[aT])
                for j in range(2):
                    r0 = b0 + j * 128
                    yt = yts.next()
                    for n in range(2):
                        py = psy.next()
                        for fb in range(22):
                            c.op("pe", lambda e, fb=fb: e.matmul(py[:], aT[:, fb, j * 128:(j + 1) * 128],
                                                                wdn[:, fb, n * 512:(n + 1) * 512],
                                                                start=(fb == 0), stop=(fb == 21)),
                                 reads=[aT, wdn], writes=[py])
                        c.op("dve", lambda e: e.tensor_tensor(yt[:, n * 512:(n + 1) * 512], py[:],
                                                              xt[j][:, n * 512:(n + 1) * 512], ALU.add),
                             reads=[py, xt[j]], writes=[yt])
                    if last:
                        self.final_norm_store(c, yt, gfin, junk, ssq, r0)
                    else:
                        c.dma("sp", d["X"][r0:r0 + 128, :], yt[:], reads=[yt], writes=[self.xtb(r0)])
            c.barrier()

    def final_norm_store(self, c, yt, gfin, junk, ssq, r0):
        c.op("act", lambda e: e.activation(out=junk[:], in_=yt[:], func=AF.Square, accum_out=ssq[:]),
             reads=[yt], writes=[junk, ssq])
        rstd_inplace(c, ssq, 1.0 / D)
        c.op("dve", lambda e: e.scalar_tensor_tensor(yt[:], yt[:], ssq[:], gfin[:], ALU.mult, ALU.mult),
             reads=[yt, ssq, gfin], writes=[yt])
        c.dma("sp", self.d["y"][r0:r0 + 128, :], yt[:], reads=[yt], writes=[self.ytb(r0)])

    def xtb(self, r0):
        return self._xtb[r0 // 128]

    def ytb(self, r0):
        return self._ytb[r0 // 128]

    def phase_xattn(self, c, layer, x_in):
        nc, d = c.nc, self.d
        TBK = 512
        with contextlib.ExitStack() as es:
            wq = c.sb("wq", [128, 8, D], BF16, es)
            wkv = c.sb("wkv", [128, 8, 2 * D], BF16, es)
            wo = c.sb("wo", [128, 8, D], BF16, es)
            gq = c.sb("gq", [128, D], F32, es)
            gm = c.sb("gm", [128, D], F32, es)
            ident = c.sb("identb", [128, 128], BF16, es)
            ones = c.sb("onesb", [128, 128], BF16, es)
            xts_r = Rot([[c.sb("xt%d_%d" % (i, j), [128, D], F32, es) for j in range(4)] for i in range(2)])
            hT_r = Rot([c.sb("hT%d" % i, [128, 8, TBK], BF16, es) for i in range(2)])
            qT_r = Rot([c.sb("qT%d" % i, [128, 8, TBK], BF16, es) for i in range(2)])
            oT_r = Rot([c.sb("oT%d" % i, [128, 8, TBK], BF16, es) for i in range(2)])
            xts = xts_r.next()
            memT = c.sb("memT", [128, 8, N_MEM], BF16, es)
            KT = c.sb("KT", [128, 8, N_MEM], BF16, es)
            Vt = c.sb("Vt", [128, 2, D], BF16, es)
            PT = [Rot([c.sb("PT%d_%d" % (m, i), [128, TBK], BF16, es) for i in range(2)]) for m in range(2)]
            rden = c.sb("rden", [128, TBK], F32, es)
            junk = c.sb("junk", [128, D], BF16, es)
            ssq = c.sb("ssq", [128, 1], F32, es)
            hn = c.sb("hn", [128, D], BF16, es)
            yts = Rot([c.sb("yt%d" % i, [128, D], F32, es) for i in range(2)])
            pst = c.ps("pst", [128, D], BF16, es)
            psA = Rot([c.ps("psA%d" % i, [128, 512], F32, es) for i in range(4)])
            psD = c.ps("psD", [128, 512], F32, es)
            psy = Rot([c.ps("psy%d" % i, [128, 512], F32, es) for i in range(2)])

            c.dma("sp", ident[:], d["c_ident_bf"], writes=[ident])
            c.dma("sp", ones[:], d["c_ones_bf"], writes=[ones])
            c.dma("sp", gq[:], d["norm_xq"][layer:layer + 1, :].partition_broadcast(128), writes=[gq])
            c.dma("sp", gm[:], d["norm_mem"][layer:layer + 1, :].partition_broadcast(128), writes=[gm])
            for k in range(8):
                c.dma("pool", wq[:, k, :], d["xa_w_q"][layer, k * 128:(k + 1) * 128, :], writes=[wq])
                c.dma("pool", wkv[:, k, :], d["xa_w_kv"][layer, k * 128:(k + 1) * 128, :], writes=[wkv])
                c.dma("pool", wo[:, k, :], d["xa_w_o"][layer, k * 128:(k + 1) * 128, :], writes=[wo])

            for si, L in enumerate(self.seqs):
                for m in range(2):
                    mt = xts[m]
                    c.dma("sp", mt[:], d["mem"][si * N_MEM + m * 128: si * N_MEM + (m + 1) * 128, :], writes=[mt])
                    self.rmsnorm_to_hT(c, mt, gm, (memT, memT[:, :, m * 128:(m + 1) * 128]), junk, ssq, hn, pst, ident)
                for fb in range(8):
                    pa = psA.next()
                    for k in range(8):
                        c.op("pe", lambda e, k=k: e.matmul(pa[:, 0:N_MEM], wkv[:, k, fb * 128:(fb + 1) * 128], memT[:, k, :],
                                                          start=(k == 0), stop=(k == 7)), reads=[wkv, memT], writes=[pa])
                    c.op("act", lambda e: e.copy(out=KT[:, fb, :], in_=pa[:, 0:N_MEM]), reads=[pa], writes=[KT])
                for m in range(2):
                    for n in range(2):
                        pa = psA.next()
                        for k in range(8):
                            c.op("pe", lambda e, k=k: e.matmul(pa[:], memT[:, k, m * 128:(m + 1) * 128],
                                                              wkv[:, k, D + n * 512:D + (n + 1) * 512],
                                                              start=(k == 0), stop=(k == 7)), reads=[wkv, memT], writes=[pa])
                        c.op("act", lambda e: e.copy(out=Vt[:, m, n * 512:(n + 1) * 512], in_=pa[:]), reads=[pa], writes=[Vt])
                for b0 in range(self.seq_off[si], self.seq_off[si] + L, TBK):
                    ntile = min(4, (self.seq_off[si] + L - b0) // 128)
                    W = ntile * 128
                    xts, hT, qT, oT = xts_r.next(), hT_r.next(), qT_r.next(), oT_r.next()
                    for j in range(ntile):
                        r0 = b0 + j * 128
                        c.dma("sp", xts[j][:], x_in[r0:r0 + 128, :], reads=[self.xtb(r0)], writes=[xts[j]])
                        self.rmsnorm_to_hT(c, xts[j], gq, (hT, hT[:, :, j * 128:(j + 1) * 128]), junk, ssq, hn, pst, ident)
                    for fb in range(8):
                        pa = psA.next()
                        for k in range(8):
                            c.op("pe", lambda e, k=k: e.matmul(pa[:, 0:W], wq[:, k, fb * 128:(fb + 1) * 128], hT[:, k, 0:W],
                                                              start=(k == 0), stop=(k == 7)), reads=[wq, hT], writes=[pa])
                        c.op("act", lambda e: e.copy(out=qT[:, fb, 0:W], in_=pa[:, 0:W]), reads=[pa], writes=[qT])
                    for h in range(4):
                        pts = []
                        for mb in range(2):
                            pa = psA.next()
                            for dd in range(2):
                                c.op("pe", lambda e, dd=dd: e.matmul(pa[:, 0:W], KT[:, 2 * h + dd, mb * 128:(mb + 1) * 128],
                                                                    qT[:, 2 * h + dd, 0:W], start=(dd == 0), stop=(dd == 1)),
                                     reads=[KT, qT], writes=[pa])
                            pt = PT[mb].next()
                            c.op("act", lambda e: e.activation(out=pt[:, 0:W], in_=pa[:, 0:W], func=AF.Exp, scale=XA_SCALE),
                                 reads=[pa], writes=[pt])
                            pts.append(pt)
                        for mb in range(2):
                            c.op("pe", lambda e, mb=mb: e.matmul(psD[:, 0:W], ones[:], pts[mb][:, 0:W],
                                                                start=(mb == 0), stop=(mb == 1)), reads=[ones, pts[mb]], writes=[psD])
                        c.op("dve", lambda e: e.reciprocal(out=rden[:, 0:W], in_=psD[:, 0:W]), reads=[psD], writes=[rden])
                        for dd in range(2):
                            pa = psA.next()
                            for mb in range(2):
                                c.op("pe", lambda e, mb=mb: e.matmul(pa[:, 0:W], Vt[:, mb, (2 * h + dd) * 128:(2 * h + dd + 1) * 128],
                                                                    pts[mb][:, 0:W], start=(mb == 0), stop=(mb == 1)),
                                     reads=[Vt, pts[mb]], writes=[pa])
                            c.op("dve", lambda e: e.tensor_tensor(oT[:, 2 * h + dd, 0:W], pa[:, 0:W], rden[:, 0:W], ALU.mult),
                                 reads=[pa, rden], writes=[oT])
                    for j in range(ntile):
                        r0 = b0 + j * 128
                        yt = yts.next()
                        for n in range(2):
                            py = psy.next()
                            for fb in range(8):
                                c.op("pe", lambda e, fb=fb: e.matmul(py[:], oT[:, fb, j * 128:(j + 1) * 128],
                                                                    wo[:, fb, n * 512:(n + 1) * 512],
                                                                    start=(fb == 0), stop=(fb == 7)), reads=[oT, wo], writes=[py])
                            c.op("dve", lambda e: e.tensor_tensor(yt[:, n * 512:(n + 1) * 512], py[:],
                                                                  xts[j][:, n * 512:(n + 1) * 512], ALU.add),
                                 reads=[py, xts[j]], writes=[yt])
                        c.dma("sp", d["X"][r0:r0 + 128, :], yt[:], reads=[yt], writes=[self.xtb(r0)])
            c.barrier()

    def gla_tile(self, c, R, G, q_ap, q_tb, k_tb, k_ap, v_ap, v_tb, lf_tb, lf_ap, dr, S, Sb, o_tb, o_col0, add_tb=None,
                 pre=None):
        if pre is not None:
            for _ in pre():
                yield
        M1, M2, M4, MT = R["gm"][dr]
        cs1, cs2, cs4 = R["psA"].next(), R["psA"].next(), R["psA"].next()
        for ps_, M in ((cs1, M1), (cs2, M2), (cs4, M4)):
            c.op("pe", lambda e: e.matmul(ps_[:], M[:], lf_ap, start=True, stop=True), reads=[M, lf_tb], writes=[ps_])
        E = G["E"]
        c.op("act", lambda e: e.activation(out=E[0][:], in_=cs1[:], func=AF.Exp), reads=[cs1], writes=[E[0]])
        c.op("act", lambda e: e.activation(out=E[1][:], in_=cs2[:], func=AF.Exp), reads=[cs2], writes=[E[1]])
        c.op("act", lambda e: e.activation(out=E[2][:], in_=cs2[:], func=AF.Exp, scale=-1.0), reads=[cs2], writes=[E[2]])
        c.op("act", lambda e: e.activation(out=E[3][:], in_=cs4[:], func=AF.Exp), reads=[cs4], writes=[E[3]])
        yield
        q1, q2, k3, k4, vb = G["q1"], G["q2"], G["k3"], G["k4"], G["vb"]
        BIG = 4.0e18
        c.op("pool", lambda e: e.tensor_tensor(q1[:], E[0][:], q_ap, ALU.mult), reads=[E[0], q_tb], writes=[q1])
        c.op("dve", lambda e: e.scalar_tensor_tensor(q2[:], E[1][:], BIG, q_ap, ALU.min, ALU.mult), reads=[E[1], q_tb], writes=[q2])
        c.op("dve", lambda e: e.scalar_tensor_tensor(k3[:], E[2][:], BIG, k_ap, ALU.min, ALU.mult), reads=[E[2], k_tb], writes=[k3])
        c.op("pool", lambda e: e.tensor_tensor(k4[:], E[3][:], k_ap, ALU.mult), reads=[E[3], k_tb], writes=[k4])
        c.op("act", lambda e: e.copy(out=vb[:], in_=v_ap), reads=[v_tb], writes=[vb])
        pe_l = R["psA"].next()
        for h in range(4):
            c.op("pe", lambda e: e.matmul(pe_l[:, 2 * h:2 * h + 2], lf_ap[:, h * 128:(h + 1) * 128], R["ind"][:],
                                          start=True, stop=True), reads=[lf_tb, R["ind"]], writes=[pe_l])
        eL = G["eL"]
        c.op("act", lambda e: e.activation(out=eL[:], in_=pe_l[:, 0:8], func=AF.Exp), reads=[pe_l], writes=[eL])
        yield
        Ts = []
        for src, nm in ((q1, "q1T"), (q2, "q2T"), (k3, "k3T")):
            pt = R["pstB"].next()
            for h in range(4):
                c.op("pe", lambda e: e.transpose(out=pt[:, h * 128:(h + 1) * 128], in_=src[:, h * 128:(h + 1) * 128],
                                                 identity=R["ident"][:]), reads=[src, R["ident"]], writes=[pt])
            dst = G[nm]
            c.op("act" if nm != "q2T" else "dve", lambda e: e.tensor_copy(out=dst[:], in_=pt[:, 0:512]) if nm == "q2T"
                 else e.copy(out=dst[:], in_=pt[:, 0:512]), reads=[pt], writes=[dst])
            Ts.append(dst)
            yield
        q1T, q2T, k3T = Ts
        order = (0, 1) if dr == 0 else (1, 0)

        def head_gen(h):
            hs = slice(h * 128, (h + 1) * 128)
            pa = R["psA"].next()
            c.op("pe", lambda e: e.matmul(pa[:, 0:128], k3T[:, hs], q2T[:, hs], start=True, stop=True),
                 reads=[k3T, q2T], writes=[pa])
            atm = G["ATm"][h]
            c.op("dve", lambda e: e.tensor_tensor(atm[:], pa[:, 0:128], MT[:], ALU.mult), reads=[pa, MT], writes=[atm])
            yield
            for ci in order:
                rs = slice(ci * 64, (ci + 1) * 64)
                po = R["psA"].next()
                c.op("pe", lambda e: e.matmul(po[:, 0:128], q1T[:, hs], Sb[h][:], start=True, stop=False),
                     reads=[q1T, Sb[h]], writes=[po])
                c.op("pe", lambda e: e.matmul(po[:, 0:128], atm[:], vb[:, hs], start=False, stop=True),
                     reads=[atm, vb], writes=[po])
                ocs = slice(o_col0 + h * 128, o_col0 + (h + 1) * 128)
                if add_tb is None:
                    c.op("act", lambda e: e.copy(out=o_tb[rs, ocs], in_=po[rs, 0:128]), reads=[po], writes=[o_tb])
                else:
                    c.op("dve", lambda e: e.tensor_tensor(o_tb[rs, ocs], po[rs, 0:128], add_tb[rs, ocs], ALU.add),
                         reads=[po, add_tb], writes=[o_tb])
                pu = R["psA"].next()
                c.op("pe", lambda e: e.matmul(pu[:, 0:128], k4[rs, hs], vb[rs, hs], start=True, stop=True),
                     reads=[k4, vb], writes=[pu])
                c.op("dve", lambda e: e.scalar_tensor_tensor(S[h][:], S[h][:], eL[:, 2 * h + ci:2 * h + ci + 1], pu[:, 0:128],
                                                             ALU.mult, ALU.add), reads=[S[h], eL, pu], writes=[S[h]])
                yield
                c.op("act", lambda e: e.copy(out=Sb[h][:], in_=S[h][:]), reads=[S[h]], writes=[Sb[h]])
                yield

        gens = [head_gen(h) for h in range(4)]
        while gens:
            for g in list(gens):
                try:
                    next(g)
                except StopIteration:
                    gens.remove(g)
            yield

    def phase_even(self, c, j, layer, x_in):
        nc, d = c.nc, self.d
        with contextlib.ExitStack() as es:
            win = c.sb("win", [128, 8, EVEN_IN], BF16, es)
            wout = c.sb("wout", [128, 8, D], BF16, es)
            gmix = c.sb("gmix", [128, D], F32, es)
            gh = c.sb("gh", [128, D], F32, es)
            lbr = c.sb("lbr", [128, 512], F32, es)
            omlb = c.sb("omlb", [128, 512], F32, es)
            lgam = c.sb("lgam", [128, 512], F32, es)
            hmask = c.sb("hmask", [128, 8], F32, es)
            ident = c.sb("identb", [128, 128], BF16, es)
            gmt = [[c.sb("gm%d%d" % (a, b_), [128, 128], F32, es) for b_ in range(4)] for a in range(2)]
            ind = c.sb("ind", [128, 2], F32, es)
            p = c.sb("p", [128, EVEN_IN], F32, es)
            xts = Rot([c.sb("xt%d" % i, [128, D], F32, es) for i in range(2)])
            oft = c.sb("oft", [128, D], F32, es)
            ot = c.sb("ot", [128, D], F32, es)
            rope = c.sb("rope", [128, 4, 64], F32, es)
            rt = [c.sb("rt%d" % i, [128, 4, 64], F32, es) for i in range(4)]
            fg = c.sb("fg", [128, 512], F32, es)
            lf = c.sb("lf", [128, 512], F32, es)
            kk = c.sb("kk", [128, 512], F32, es)
            sq = c.sb("sq", [128, D], F32, es)
            ssq = c.sb("ssq", [128, 1], F32, es)
            s1 = c.sb("s1", [128, 8], F32, es)
            s2 = c.sb("s2", [128, 8], F32, es)
            hn = c.sb("hn", [128, D], BF16, es)
            hT = c.sb("hT", [128, 8, 128], BF16, es)
            omix = c.sb("omix", [128, D], BF16, es)
            junk = omix
            oT = hT
            R = {
                "gm": gmt, "ind": ind, "ident": ident,
                "psA": Rot([c.ps("psA%d" % i, [128, 512], F32, es) for i in range(6)]),
                "pstB": Rot([c.ps("pstB%d" % i, [128, D], BF16, es) for i in range(2)]),
            }
            Gs = []
            for g in range(2):
                G = {"E": [c.sb("E%d_%d" % (g, i), [128, 512], F32, es) for i in range(4)]}
                for nm in ("q1", "q2", "k3", "k4", "vb", "q1T", "q2T", "k3T"):
                    G[nm] = c.sb("%s_%d" % (nm, g), [128, 512], BF16, es)
                G["eL"] = c.sb("eL_%d" % g, [128, 8], F32, es)
                G["ATm"] = [c.sb("ATm%d_%d" % (g, i), [128, 128], BF16, es) for i in range(4)]
                Gs.append(G)
            S = [[c.sb("S%d_%d" % (g, h), [128, 128], F32, es) for h in range(4)] for g in range(2)]
            Sb = [[c.sb("Sb%d_%d" % (g, h), [128, 128], BF16, es) for h in range(4)] for g in range(2)]

            c.dma("sp", ident[:], d["c_ident_bf"], writes=[ident])
            c.dma("sp", ind[:], d["c_ind"], writes=[ind])
            for a in range(2):
                for b_ in range(4):
                    c.dma("sp", gmt[a][b_][:], d["c_gla"][a * 4 + b_], writes=[gmt[a][b_]])
            c.dma("sp", gmix[:], d["norm_mix"][layer:layer + 1, :].partition_broadcast(128), writes=[gmix])
            c.dma("sp", gh[:, 0:512], d["ret_norm"][j:j + 1, :].partition_broadcast(128), writes=[gh])
            c.dma("sp", gh[:, 512:1024], d["hgrn_norm"][j:j + 1, :].partition_broadcast(128), writes=[gh])
            c.dma("sp", lgam[:], d["c_lgam"][0:1, :].partition_broadcast(128), writes=[lgam])
            c.dma("sp", hmask[:], d["c_hmask"][0:1, :].partition_broadcast(128), writes=[hmask])
            if j == 0:
                c.op("dve", lambda e: e.memset(lbr[:], 0.0), writes=[lbr])
            else:
                c.dma("sp", lbr[:], d["hgrn_lb_logits"][1:2, :].partition_broadcast(128), writes=[lbr])
                c.dma("sp", omlb[:], d["hgrn_lb_logits"][0:1, :].partition_broadcast(128), writes=[omlb])
                c.op("dve", lambda e: e.tensor_tensor(lbr[:], lbr[:], omlb[:], ALU.subtract), reads=[lbr, omlb], writes=[lbr])
                c.op("act", lambda e: e.activation(out=lbr[:], in_=lbr[:], func=AF.Sigmoid), reads=[lbr], writes=[lbr])
            c.op("dve", lambda e: e.tensor_scalar(omlb[:], lbr[:], -1.0, 1.0, ALU.mult, ALU.add), reads=[lbr], writes=[omlb])
            for k in range(8):
                c.dma("pool", win[:, k, :], d["even_w_in"][j, k * 128:(k + 1) * 128, :], writes=[win])
                c.dma("pool", wout[:, k, :], d["even_w_out"][j, k * 128:(k + 1) * 128, :], writes=[wout])

            def gates(zcol):
                c.op("act", lambda e: e.activation(out=fg[:], in_=p[:, zcol:zcol + 512], func=AF.Sigmoid), reads=[p], writes=[fg])
                yield
                c.op("pool", lambda e: e.tensor_tensor(fg[:], fg[:], omlb[:], ALU.mult), reads=[fg, omlb], writes=[fg])
                yield
                c.op("pool", lambda e: e.tensor_tensor(fg[:], fg[:], lbr[:], ALU.add), reads=[fg, lbr], writes=[fg])
                yield
                c.op("act", lambda e: e.activation(out=lf[:], in_=fg[:], func=AF.Ln), reads=[fg], writes=[lf])
                c.op("pool", lambda e: e.tensor_scalar(kk[:], fg[:], -1.0, 1.0, ALU.mult, ALU.add), reads=[fg], writes=[kk])
                yield

            for si, L in enumerate(self.seqs):
                nt = L // 128
                base = self.seq_off[si]
                ptb = [TB(None, "P%d" % t) for t in range(nt)]
                oftb = [TB(None, "OF%d" % t) for t in range(nt)]
                for g in range(2):
                    for h in range(4):
                        c.op("dve", lambda e: e.memset(S[g][h][:], 0.0), writes=[S[g][h]])
                        c.op("pool", lambda e: e.memset(Sb[g][h][:], 0.0), writes=[Sb[g][h]])
                for t in range(nt):
                    r0 = base + t * 128
                    xt = xts.next()
                    c.dma("sp", xt[:], x_in[r0:r0 + 128, :], reads=[self.xtb(r0)], writes=[xt])
                    c.dma("sp", rope[:], d["c_rope"][t * 128:(t + 1) * 128, :].rearrange("p (a b) -> p a b", a=4), writes=[rope])
                    self.rmsnorm_to_hT(c, xt, gmix, (hT, hT[:]), junk, ssq, hn, R["pstB"].next(), ident)
                    for n in range(9):
                        pa = R["psA"].next()
                        for k in range(8):
                            c.op("pe", lambda e: e.matmul(pa[:], hT[:, k, :], win[:, k, n * 512:(n + 1) * 512],
                                                          start=(k == 0), stop=(k == 7)), reads=[hT, win], writes=[pa])
                        c.op("act" if n % 2 == 0 else "dve",
                             lambda e: (e.copy(out=p[:, n * 512:(n + 1) * 512], in_=pa[:]) if n % 2 == 0
                                        else e.tensor_copy(out=p[:, n * 512:(n + 1) * 512], in_=pa[:])),
                             reads=[pa], writes=[p])
                    for col0, ci_, si_ in ((0, 0, 1), (512, 2, 3)):
                        v4 = p[:, col0:col0 + 512].rearrange("p (h two x) -> p h two x", h=4, two=2)
                        x1, x2 = v4[:, :, 0, :], v4[:, :, 1, :]
                        cb = rope[:, ci_:ci_ + 1, :].to_broadcast([128, 4, 64])
                        sb_ = rope[:, si_:si_ + 1, :].to_broadcast([128, 4, 64])
                        c.op("pool", lambda e: e.tensor_tensor(rt[0][:], x1, cb, ALU.mult), reads=[p, rope], writes=[rt[0]])
                        c.op("pool", lambda e: e.tensor_tensor(rt[1][:], x2, sb_, ALU.mult), reads=[p, rope], writes=[rt[1]])
                        c.op("pool", lambda e: e.tensor_tensor(rt[2][:], x1, sb_, ALU.mult), reads=[p, rope], writes=[rt[2]])
                        c.op("pool", lambda e: e.tensor_tensor(rt[3][:], x2, cb, ALU.mult), reads=[p, rope], writes=[rt[3]])
                        c.op("pool", lambda e: e.tensor_tensor(x1, rt[0][:], rt[1][:], ALU.subtract), reads=[rt[0], rt[1]], writes=[p])
                        c.op("pool", lambda e: e.tensor_tensor(x2, rt[2][:], rt[3][:], ALU.add), reads=[rt[2], rt[3]], writes=[p])
                    c.dma("sp", d["P"][t * 128:(t + 1) * 128, :], p[:], reads=[p], writes=[ptb[t]])
                    run_interleaved([
                        self.gla_tile(c, R, Gs[0], p[:, 0:512], p, p, p[:, 512:1024], p[:, 1024:1536], p, lgam, lgam[:], 0,
                                      S[0], Sb[0], oft, 0),
                        self.gla_tile(c, R, Gs[1], p[:, 2048:2560], p, kk, kk[:], p[:, 3584:4096], p, lf, lf[:], 0,
                                      S[1], Sb[1], oft, 512, pre=lambda: gates(2560))])
                    c.dma("sp", d["OF"][t * 128:(t + 1) * 128, :], oft[:], reads=[oft], writes=[oftb[t]])
                for g in range(2):
                    for h in range(4):
                        c.op("dve", lambda e: e.memset(S[g][h][:], 0.0), writes=[S[g][h]])
                        c.op("pool", lambda e: e.memset(Sb[g][h][:], 0.0), writes=[Sb[g][h]])
                for t in range(nt - 1, -1, -1):
                    r0 = base + t * 128
                    xt = xts.next()
                    c.dma("sp", xt[:], x_in[r0:r0 + 128, :], reads=[self.xtb(r0)], writes=[xt])
                    c.dma("sp", p[:], d["P"][t * 128:(t + 1) * 128, :], reads=[ptb[t]], writes=[p])
                    c.dma("sp", oft[:], d["OF"][t * 128:(t + 1) * 128, :], reads=[oftb[t]], writes=[oft])
                    run_interleaved([
                        self.gla_tile(c, R, Gs[0], p[:, 0:512], p, p, p[:, 512:1024], p[:, 1024:1536], p, lgam, lgam[:], 1,
                                      S[0], Sb[0], ot, 0, add_tb=oft),
                        self.gla_tile(c, R, Gs[1], p[:, 2048:2560], p, kk, kk[:], p[:, 3584:4096], p, lf, lf[:], 1,
                                      S[1], Sb[1], ot, 512, add_tb=oft, pre=lambda: gates(3072))])
                    o3 = ot[:].rearrange("p (h x) -> p h x", h=8)
                    c.op("dve", lambda e: e.reduce_sum(out=s1[:], in_=o3, axis=AX.X), reads=[ot], writes=[s1])
                    c.op("act", lambda e: e.activation(out=sq[:], in_=ot[:], func=AF.Square), reads=[ot], writes=[sq])
                    c.op("dve", lambda e: e.reduce_sum(out=s2[:], in_=sq[:].rearrange("p (h x) -> p h x", h=8), axis=AX.X),
                         reads=[sq], writes=[s2])
                    c.op("dve", lambda e: e.scalar_tensor_tensor(s1[:], s1[:], 1.0 / 128, hmask[:], ALU.mult, ALU.mult),
                         reads=[s1, hmask], writes=[s1])
                    c.op("dve", lambda e: e.tensor_scalar(s2[:], s2[:], 1.0 / 128, RMS_EPS, ALU.mult, ALU.add), reads=[s2], writes=[s2])
                    c.op("dve", lambda e: e.tensor_tensor(sq[:, 0:8], s1[:], s1[:], ALU.mult), reads=[s1], writes=[sq])
                    c.op("dve", lambda e: e.tensor_tensor(s2[:], s2[:], sq[:, 0:8], ALU.subtract), reads=[s2, sq], writes=[s2])
                    c.op("act", lambda e: e.activation(out=s2[:], in_=s2[:], func=AF.Sqrt), reads=[s2], writes=[s2])
                    c.op("dve", lambda e: e.reciprocal(out=s2[:], in_=s2[:]), reads=[s2], writes=[s2])
                    for h in range(8):
                        hs = slice(h * 128, (h + 1) * 128)
                        c.op("dve" if h % 2 == 0 else "pool",
                             lambda e: e.tensor_scalar(ot[:, hs], ot[:, hs], s1[:, h:h + 1], s2[:, h:h + 1], ALU.subtract, ALU.mult),
                             reads=[ot, s1, s2], writes=[ot])
                    c.op("pool", lambda e: e.tensor_tensor(ot[:], ot[:], gh[:], ALU.mult), reads=[ot, gh], writes=[ot])
                    c.op("act", lambda e: e.activation(out=sq[:, 0:512], in_=p[:, 1536:2048], func=AF.Silu), reads=[p], writes=[sq])
                    c.op("act", lambda e: e.activation(out=sq[:, 512:1024], in_=p[:, 4096:4608], func=AF.Silu), reads=[p], writes=[sq])
                    c.op("dve", lambda e: e.tensor_tensor(omix[:], ot[:], sq[:], ALU.mult), reads=[ot, sq], writes=[omix])
                    pt = R["pstB"].next()
                    for k in range(8):
                        c.op("pe", lambda e: e.transpose(out=pt[:, k * 128:(k + 1) * 128], in_=omix[:, k * 128:(k + 1) * 128],
                                                         identity=ident[:]), reads=[omix, ident], writes=[pt])
                    c.op("act", lambda e: e.copy(out=oT[:], in_=pt[:].rearrange("p (k t) -> p k t", k=8)), reads=[pt], writes=[oT])
                    for n in range(2):
                        py = R["psA"].next()
                        for k in range(8):
                            c.op("pe", lambda e: e.matmul(py[:], oT[:, k, :], wout[:, k, n * 512:(n + 1) * 512],
                                                          start=(k == 0), stop=(k == 7)), reads=[oT, wout], writes=[py])
                        c.op("dve", lambda e: e.tensor_tensor(xt[:, n * 512:(n + 1) * 512], py[:], xt[:, n * 512:(n + 1) * 512], ALU.add),
                             reads=[py, xt], writes=[xt])
                    c.dma("sp", d["X"][r0:r0 + 128, :], xt[:], reads=[xt], writes=[self.xtb(r0)])
            c.barrier()

    def gdn_tile(self, c, R, RS, h, dr, qT, kT, ktm, vtm, sc, S, Sb, o_tb, add_tb=None):
        M1, M2, M4, MT = R["gm"][dr]
        hs = slice(h * 128, (h + 1) * 128)
        col = slice(h, h + 1)
        identf = R["identf"]
        W = RS["w"]
        ula = RS["ula"]
        c.op("act", lambda e: e.mul(out=ula[:], in_=M1[:], mul=sc["la"][:, col]), reads=[M1, sc["la"]], writes=[ula])
        pg, pgt = R["psH"].next(), R["psH"].next()
        c.op("pe", lambda e: e.matmul(pg[:, 0:128], ula[:], M4[:], start=True, stop=True), reads=[ula, M4], writes=[pg])
        c.op("pe", lambda e: e.matmul(pgt[:, 0:128], M4[:], ula[:], start=True, stop=True), reads=[ula, M4], writes=[pgt])
        dm, dtm = RS["dm"], RS["dtm"]
        c.op("act", lambda e: e.activation(out=dm[:], in_=pg[:, 0:128], func=AF.Exp), reads=[pg], writes=[dm])
        c.op("act", lambda e: e.activation(out=dtm[:], in_=pgt[:, 0:128], func=AF.Exp), reads=[pgt], writes=[dtm])
        c.op("pool", lambda e: e.tensor_tensor(dm[:], dm[:], R["mstrict"][dr][:], ALU.mult), reads=[dm, R["mstrict"][dr]], writes=[dm])
        c.op("pool", lambda e: e.tensor_tensor(dtm[:], dtm[:], MT[:], ALU.mult), reads=[dtm, MT], writes=[dtm])
        yield
        pk = R["psH"].next()
        c.op("pe", lambda e: e.matmul(pk[:, 0:128], kT[:, h, :], kT[:, h, :], start=True, stop=True), reads=[kT], writes=[pk])
        X = W.next()
        c.op("dve", lambda e: e.scalar_tensor_tensor(X[:], pk[:, 0:128], sc["nbeta"][:, col], dm[:], ALU.mult, ALU.mult),
             reads=[pk, sc["nbeta"], dm], writes=[X])
        yield
        py = R["pstB"].next()
        c.op("pe", lambda e: e.transpose(out=py[:, 0:128], in_=X[:], identity=R["ident"][:]), reads=[X, R["ident"]], writes=[py])
        Y = W.next()
        RT = W.next()
        c.op("act", lambda e: e.copy(out=Y[:], in_=py[:, 0:128]), reads=[py], writes=[Y])
        c.op("dve", lambda e: e.tensor_tensor(RT[:], Y[:], R["ident"][:], ALU.add), reads=[Y, R["ident"]], writes=[RT])
        yield
        ttb = RS["ttb"]
        for m in range(5):
            px = R["psH"].next()
            c.op("pe", lambda e: e.matmul(px[:, 0:128], Y[:], X[:], start=True, stop=True), reads=[X, Y], writes=[px])
            if m < 4:
                c.op("pe", lambda e: e.matmul(px[:, 128:256], X[:], Y[:], start=True, stop=True), reads=[X, Y], writes=[px])
            X2 = W.next()
            c.op("act", lambda e: e.copy(out=X2[:], in_=px[:, 0:128]), reads=[px], writes=[X2])
            if m < 4:
                Y2 = W.next()
                c.op("act", lambda e: e.copy(out=Y2[:], in_=px[:, 128:256]), reads=[px], writes=[Y2])
            yield
            pr = R["psH"].next()
            c.op("pe", lambda e: e.matmul(pr[:, 0:128], X2[:], RT[:], start=True, stop=True), reads=[X2, RT], writes=[pr])
            if m < 4:
                RT2 = W.next()
                c.op("dve", lambda e: e.tensor_tensor(RT2[:], pr[:, 0:128], RT[:], ALU.add), reads=[pr, RT], writes=[RT2])
                RT = RT2
                X, Y = X2, Y2
            else:
                c.op("dve", lambda e: e.tensor_tensor(ttb[:], pr[:, 0:128], RT[:], ALU.add), reads=[pr, RT], writes=[ttb])
            yield
        vb_, kbg, kdec = RS["vbeta"], RS["kbg"], RS["kdec"]
        c.op("act", lambda e: e.mul(out=vb_[:], in_=vtm[:, hs], mul=sc["beta"][:, col]), reads=[vtm, sc["beta"]], writes=[vb_])
        c.op("dve", lambda e: e.tensor_scalar(kbg[:], ktm[:, hs], sc["beg"][:, col], None, ALU.mult), reads=[ktm, sc["beg"]], writes=[kbg])
        c.op("act", lambda e: e.mul(out=kdec[:], in_=ktm[:, hs], mul=sc["egd"][:, col]), reads=[ktm, sc["egd"]], writes=[kdec])
        pu_ = R["psH"].next()
        c.op("pe", lambda e: e.matmul(pu_[:, 0:128], ttb[:], vb_[:], start=True, stop=True), reads=[ttb, vb_], writes=[pu_])
        c.op("pe", lambda e: e.matmul(pu_[:, 128:256], kbg[:], ttb[:], start=True, stop=True), reads=[ttb, kbg], writes=[pu_])
        u = RS["u"]
        wT = RS["wT"]
        c.op("act", lambda e: e.copy(out=u[:], in_=pu_[:, 0:128]), reads=[pu_], writes=[u])
        c.op("act", lambda e: e.copy(out=wT[:], in_=pu_[:, 128:256]), reads=[pu_], writes=[wT])
        yield
        pq = R["psH"].next()
        c.op("pe", lambda e: e.matmul(pq[:, 0:128], kT[:, h, :], qT[:, h, :], start=True, stop=True), reads=[kT, qT], writes=[pq])
        qkm = RS["qkm"]
        c.op("dve", lambda e: e.tensor_tensor(qkm[:], pq[:, 0:128], dtm[:], ALU.mult), reads=[pq, dtm], writes=[qkm])
        yield
        vn = RS["vn"]
        tmp = RS["tmp"]
        order = (0, 1) if dr == 0 else (1, 0)
        for ci in order:
            rs = slice(ci * 64, (ci + 1) * 64)
            pa = R["psH"].next()
            c.op("pe", lambda e: e.matmul(pa[:, 0:128], wT[:], Sb[h][:], start=True, stop=True), reads=[wT, Sb[h]], writes=[pa])
            c.op("pe", lambda e: e.matmul(pa[:, 128:256], qT[:, h, :], Sb[h][:], start=True, stop=True), reads=[qT, Sb[h]], writes=[pa])
            c.op("dve", lambda e: e.tensor_tensor(vn[rs, :], u[rs, :], pa[rs, 0:128], ALU.subtract), reads=[u, pa], writes=[vn])
            c.op("dve", lambda e: e.tensor_scalar(tmp[rs, :], pa[rs, 128:256], sc["eg"][rs, col], None, ALU.mult),
                 reads=[pa, sc["eg"]], writes=[tmp])
            yield
            ps_ = R["psH"].next()
            c.op("pe", lambda e: e.matmul(ps_[:, 0:128], kdec[rs, :], vn[rs, :], start=True, stop=True), reads=[kdec, vn], writes=[ps_])
            c.op("dve", lambda e: e.scalar_tensor_tensor(S[h][:], S[h][:], sc["egl"][ci][:, col], ps_[:, 0:128], ALU.mult, ALU.add),
                 reads=[S[h], sc["egl"][ci], ps_], writes=[S[h]])
            c.op("act", lambda e: e.copy(out=Sb[h][:], in_=S[h][:]), reads=[S[h]], writes=[Sb[h]])
            yield
        p2 = R["psH"].next()
        c.op("pe", lambda e: e.matmul(p2[:, 0:128], qkm[:], vn[:], start=True, stop=True), reads=[qkm, vn], writes=[p2])
        if add_tb is None:
            c.op("dve", lambda e: e.tensor_tensor(o_tb[:, hs], p2[:, 0:128], tmp[:], ALU.add), reads=[p2, tmp], writes=[o_tb])
        else:
            c.op("pool", lambda e: e.tensor_tensor(tmp[:], tmp[:], add_tb[:, hs], ALU.add), reads=[tmp, add_tb], writes=[tmp])
            c.op("dve", lambda e: e.tensor_tensor(o_tb[:, hs], p2[:, 0:128], tmp[:], ALU.add), reads=[p2, tmp], writes=[o_tb])

    def gdn_pair(self, c, R, PS, h0, dr, qT, kT, ktm, vtm, sc, S, Sb, o_tb, add_tb=None):
        M1, M2, M4, MT = R["gm"][dr]
        heads = (h0, h0 + 1)
        ident = R["ident"]
        dd = PS["dd"]
        pgb = R["psH"].next()
        for i, h in enumerate(heads):
            ula = PS["ula"][i]
            c.op("act", lambda e: e.mul(out=ula[:], in_=M1[:], mul=sc["la"][:, h:h + 1]), reads=[M1, sc["la"]], writes=[ula])
            c.op("pe", lambda e: e.matmul(pgb[:, (2 * i) * 128:(2 * i + 1) * 128], ula[:], M4[:], start=True, stop=True),
                 reads=[ula, M4], writes=[pgb])
            c.op("pe", lambda e: e.matmul(pgb[:, (2 * i + 1) * 128:(2 * i + 2) * 128], M4[:], ula[:], start=True, stop=True),
                 reads=[ula, M4], writes=[pgb])
        c.op("act", lambda e: e.activation(out=dd[:], in_=pgb[:], func=AF.Exp), reads=[pgb], writes=[dd])
        yield
        c.op("pool", lambda e: e.tensor_tensor(dd[:], dd[:], R["mask4"][dr][:], ALU.mult), reads=[dd, R["mask4"][dr]], writes=[dd])
        yield
        xy = PS["xy"]
        cur = 0
        XY = xy[cur]
        for i, h in enumerate(heads):
            pk = R["psH"].next()
            c.op("pe", lambda e: e.matmul(pk[:, 0:128], kT[:, h, :], kT[:, h, :], start=True, stop=True), reads=[kT], writes=[pk])
            c.op("dve", lambda e: e.scalar_tensor_tensor(XY[:, (2 * i) * 128:(2 * i + 1) * 128], pk[:, 0:128], sc["nbeta"][:, h:h + 1],
                                                         dd[:, (2 * i) * 128:(2 * i + 1) * 128], ALU.mult, ALU.mult),
                 reads=[pk, sc["nbeta"], dd], writes=[XY])
        yield
        py = R["pstB"].next()
        for i in range(2):
            c.op("pe", lambda e: e.transpose(out=py[:, i * 128:(i + 1) * 128], in_=XY[:, (2 * i) * 128:(2 * i + 1) * 128],
                                             identity=ident[:]), reads=[XY, ident], writes=[py])
        XYv = XY[:].rearrange("p (i t x) -> p i t x", i=2, t=2)
        c.op("act", lambda e: e.copy(out=XYv[:, :, 1, :], in_=py[:, 0:256].rearrange("p (i x) -> p i x", i=2)),
             reads=[py], writes=[XY])
        yield
        rt = PS["rt"]
        RT = rt[0]
        for i in range(2):
            c.op("dve", lambda e: e.tensor_tensor(RT[:, i * 128:(i + 1) * 128], XY[:, (2 * i + 1) * 128:(2 * i + 2) * 128],
                                                  ident[:], ALU.add), reads=[XY, ident], writes=[RT])
        yield
        rcur = 0
        for m in range(5):
            px = R["psH"].next()
            for i in range(2):
                Xi = XY[:, (2 * i) * 128:(2 * i + 1) * 128]
                Yi = XY[:, (2 * i + 1) * 128:(2 * i + 2) * 128]
                c.op("pe", lambda e: e.matmul(px[:, (2 * i) * 128:(2 * i + 1) * 128], Yi, Xi, start=True, stop=True),
                     reads=[XY], writes=[px])
                if m < 4:
                    c.op("pe", lambda e: e.matmul(px[:, (2 * i + 1) * 128:(2 * i + 2) * 128], Xi, Yi, start=True, stop=True),
                         reads=[XY], writes=[px])
            XY2 = xy[1 - cur]
            if m < 4:
                c.op("act", lambda e: e.copy(out=XY2[:], in_=px[:]), reads=[px], writes=[XY2])
            else:
                c.op("act", lambda e: e.copy(out=XY2[:].rearrange("p (i t x) -> p i t x", i=2, t=2)[:, :, 0, :],
                                             in_=px[:].rearrange("p (i t x) -> p i t x", i=2, t=2)[:, :, 0, :]),
                     reads=[px], writes=[XY2])
            yield
            pr = R["psH"].next()
            for i in range(2):
                c.op("pe", lambda e: e.matmul(pr[:, i * 128:(i + 1) * 128], XY2[:, (2 * i) * 128:(2 * i + 1) * 128],
                                              RT[:, i * 128:(i + 1) * 128], start=True, stop=True), reads=[XY2, RT], writes=[pr])
            RT2 = rt[1 - rcur]
            c.op("dve", lambda e: e.tensor_tensor(RT2[:], pr[:, 0:256], RT[:], ALU.add), reads=[pr, RT], writes=[RT2])
            RT = RT2
            rcur = 1 - rcur
            XY = XY2
            cur = 1 - cur
            yield
        H = []
        for i, h in enumerate(heads):
            hs = slice(h * 128, (h + 1) * 128)
            col = slice(h, h + 1)
            B = PS["hd"][i]
            vb_, kbg, kdec = B["vbeta"], B["kbg"], B["kdec"]
            c.op("act", lambda e: e.mul(out=vb_[:], in_=vtm[:, hs], mul=sc["beta"][:, col]), reads=[vtm, sc["beta"]], writes=[vb_])
            c.op("dve", lambda e: e.tensor_scalar(kbg[:], ktm[:, hs], sc["beg"][:, col], None, ALU.mult), reads=[ktm, sc["beg"]], writes=[kbg])
            c.op("act", lambda e: e.mul(out=kdec[:], in_=ktm[:, hs], mul=sc["egd"][:, col]), reads=[ktm, sc["egd"]], writes=[kdec])
            ttb = RT[:, i * 128:(i + 1) * 128]
            pu_ = R["psH"].next()
            c.op("pe", lambda e: e.matmul(pu_[:, 0:128], ttb, vb_[:], start=True, stop=True), reads=[RT, vb_], writes=[pu_])
            c.op("pe", lambda e: e.matmul(pu_[:, 128:256], kbg[:], ttb, start=True, stop=True), reads=[RT, kbg], writes=[pu_])
            u, wT = B["u"], B["wT"]
            c.op("act", lambda e: e.copy(out=u[:], in_=pu_[:, 0:128]), reads=[pu_], writes=[u])
            c.op("act", lambda e: e.copy(out=wT[:], in_=pu_[:, 128:256]), reads=[pu_], writes=[wT])
            H.append((h, hs, col, B))
        yield
        for i, (h, hs, col, B) in enumerate(H):
            pq = R["psH"].next()
            c.op("pe", lambda e: e.matmul(pq[:, 0:128], kT[:, h, :], qT[:, h, :], start=True, stop=True), reads=[kT, qT], writes=[pq])
            c.op("dve", lambda e: e.tensor_tensor(B["qkm"][:], pq[:, 0:128], dd[:, (2 * i + 1) * 128:(2 * i + 2) * 128], ALU.mult),
                 reads=[pq, dd], writes=[B["qkm"]])
        yield
        order = (0, 1) if dr == 0 else (1, 0)
        for ci in order:
            rs = slice(ci * 64, (ci + 1) * 64)
            for i, (h, hs, col, B) in enumerate(H):
                pa = R["psH"].next()
                c.op("pe", lambda e: e.matmul(pa[:, 0:128], B["wT"][:], Sb[h][:], start=True, stop=True), reads=[B["wT"], Sb[h]], writes=[pa])
                c.op("pe", lambda e: e.matmul(pa[:, 128:256], qT[:, h, :], Sb[h][:], start=True, stop=True), reads=[qT, Sb[h]], writes=[pa])
                c.op("dve", lambda e: e.tensor_tensor(B["vn"][rs, :], B["u"][rs, :], pa[rs, 0:128], ALU.subtract),
                     reads=[B["u"], pa], writes=[B["vn"]])
                c.op("dve", lambda e: e.tensor_scalar(B["tmp"][rs, :], pa[rs, 128:256], sc["eg"][rs, col], None, ALU.mult),
                     reads=[pa, sc["eg"]], writes=[B["tmp"]])
            yield
            for i, (h, hs, col, B) in enumerate(H):
                ps_ = R["psH"].next()
                c.op("pe", lambda e: e.matmul(ps_[:, 0:128], B["kdec"][rs, :], B["vn"][rs, :], start=True, stop=True),
                     reads=[B["kdec"], B["vn"]], writes=[ps_])
                c.op("dve", lambda e: e.scalar_tensor_tensor(S[h][:], S[h][:], sc["egl"][ci][:, col], ps_[:, 0:128], ALU.mult, ALU.add),
                     reads=[S[h], sc["egl"][ci], ps_], writes=[S[h]])
            yield
            for i, (h, hs, col, B) in enumerate(H):
                c.op("act", lambda e: e.copy(out=Sb[h][:], in_=S[h][:]), reads=[S[h]], writes=[Sb[h]])
            yield
        for i, (h, hs, col, B) in enumerate(H):
            p2 = R["psH"].next()
            c.op("pe", lambda e: e.matmul(p2[:, 0:128], B["qkm"][:], B["vn"][:], start=True, stop=True), reads=[B["qkm"], B["vn"]], writes=[p2])
            if add_tb is None:
                c.op("dve", lambda e: e.tensor_tensor(o_tb[:, hs], p2[:, 0:128], B["tmp"][:], ALU.add), reads=[p2, B["tmp"]], writes=[o_tb])
            else:
                c.op("pool", lambda e: e.tensor_tensor(B["tmp"][:], B["tmp"][:], add_tb[:, hs], ALU.add), reads=[B["tmp"], add_tb], writes=[B["tmp"]])
                c.op("dve", lambda e: e.tensor_tensor(o_tb[:, hs], p2[:, 0:128], B["tmp"][:], ALU.add), reads=[p2, B["tmp"]], writes=[o_tb])

    def gdn_scalars(self, c, R, dr, ab, sc):
        M1, M2, M4, MT = R["gm"][dr]
        la = ab[:, dr * 8:(dr + 1) * 8]
        be = ab[:, 16 + dr * 8:16 + (dr + 1) * 8]
        ps = R["psA"].next()
        c.op("pe", lambda e: e.matmul(ps[:, 0:8], M1[:], la, start=True, stop=True), reads=[M1, ab], writes=[ps])
        c.op("pe", lambda e: e.matmul(ps[:, 8:16], M4[:], la, start=True, stop=True), reads=[M4, ab], writes=[ps])
        c.op("pe", lambda e: e.matmul(ps[:, 16:24], R["sel"][0][:], la, start=True, stop=True), reads=[R["sel"][0], ab], writes=[ps])
        c.op("pe", lambda e: e.matmul(ps[:, 24:32], R["sel"][1][:], la, start=True, stop=True), reads=[R["sel"][1], ab], writes=[ps])
        ex = sc["ex"]
        c.op("act", lambda e: e.activation(out=ex[:], in_=ps[:, 0:32], func=AF.Exp), reads=[ps], writes=[ex])
        c.op("dve", lambda e: e.tensor_copy(out=sc["la"][:], in_=la), reads=[ab], writes=[sc["la"]])
        c.op("dve", lambda e: e.tensor_copy(out=sc["beta"][:], in_=be), reads=[ab], writes=[sc["beta"]])
        c.op("dve", lambda e: e.tensor_scalar(sc["nbeta"][:], be, -1.0, None, ALU.mult), reads=[ab], writes=[sc["nbeta"]])
        c.op("dve", lambda e: e.tensor_tensor(sc["beg"][:], be, ex[:, 0:8], ALU.mult), reads=[ab, ex], writes=[sc["beg"]])
        sc["eg"] = _View(ex, (slice(None), slice(0, 8)))
        sc["egd"] = _View(ex, (slice(None), slice(8, 16)))
        sc["egl"] = [_View(ex, (slice(None), slice(16, 24))), _View(ex, (slice(None), slice(24, 32)))]

    def phase_odd(self, c, j, layer, x_in):
        nc, d = c.nc, self.d
        Lm = self.Lmax
        with contextlib.ExitStack() as es:
            win = c.sb("win", [128, 8, ODD_IN], BF16, es)
            wout = c.sb("wout", [128, 8, D], BF16, es)
            gmix = c.sb("gmix", [128, D], F32, es)
            gh = c.sb("gh", [128, 128], F32, es)
            ident = c.sb("identb", [128, 128], BF16, es)
            identf = c.sb("identf", [128, 128], F32, es)
            ones = c.sb("onesb", [128, 128], BF16, es)
            gmt = [[c.sb("gm%d%d" % (a, b_), [128, 128], F32, es) for b_ in range(4)] for a in range(2)]
            mstrict = [c.sb("mstr%d" % a, [128, 128], F32, es) for a in range(2)]
            sel = [c.sb("sel%d" % a, [128, 128], F32, es) for a in range(2)]
            cw = c.sb("cw", [128, 24, 5], F32, es)
            dtb = c.sb("dtb", [128, 16], F32, es)
            nea = c.sb("nea", [128, 16], F32, es)
            zer = c.sb("zer", [128, 16], BF16, es)
            ab_all = c.sb("ab_all", [128, Lm // 128, 32], F32, es)
            xts = Rot([c.sb("xt%d" % i, [128, D], F32, es) for i in range(1)])
            ssq = c.sb("ssq", [128, 1], F32, es)
            s2 = c.sb("s2", [128, 8], F32, es)
            hn = c.sb("hn", [128, D], BF16, es)
            hT = c.sb("hT", [128, 8, 128], BF16, es)
            hTb = c.sb("hTb", [128, 8, 260], BF16, es)
            gt = c.sb("gt", [128, D], F32, es)
            NCS = 3
            CSs = [{"raw": c.sb("raw%d" % i, [128, 260], F32, es), "acc": c.sb("acc%d" % i, [128, 256], F32, es),
                    "sqb": c.sb("sqb%d" % i, [128, 256], BF16, es)} for i in range(NCS)]
            for cs_d in CSs:
                cs_d["rst"] = _View(cs_d["raw"], (slice(None), slice(0, 256)))
            qTb = c.sb("qTb", [128, 8, 256], BF16, es)
            kTb = c.sb("kTb", [128, 8, 256], BF16, es)
            vTb = c.sb("vTb", [128, 8, 256], BF16, es)
            qTt = c.sb("qTt", [128, 8, 128], BF16, es)
            kTt = c.sb("kTt", [128, 8, 128], BF16, es)
            vtm = c.sb("vtm", [128, D], BF16, es)
            ktm = c.sb("ktm", [128, D], BF16, es)
            oft = c.sb("oft", [128, D], F32, es)
            sq = oft
            ot = c.sb("ot", [128, D], F32, es)
            omix = c.sb("omix", [128, D], BF16, es)
            junk = omix
            oT = hT
            abt = c.sb("abt", [128, 32], F32, es)
            sc = {k: c.sb("sc_" + k, [128, 8], F32, es) for k in ("la", "beta", "nbeta", "beg")}
            sc["ex"] = c.sb("sc_ex", [128, 32], F32, es)
            R = {
                "gm": gmt, "ident": ident, "identf": identf, "mstrict": mstrict, "sel": sel,
                "psH": Rot([c.ps("psH%d" % i, [128, 512], F32, es) for i in range(6)]),
                "pstB": Rot([c.ps("pstB%d" % i, [128, D], BF16, es) for i in range(2)]),
            }
            R["psA"] = R["psH"]
            NP = 4
            PSs = []
            for sl in range(NP):
                PS = {"ula": [c.sb("ula%d_%d" % (sl, i), [128, 128], F32, es) for i in range(2)],
                      "dd": c.sb("dd%d" % sl, [128, 512], F32, es),
                      "xy": [c.sb("xy%d_%d" % (sl, i), [128, 512], BF16, es) for i in range(2)],
                      "rt": [c.sb("rt%d_%d" % (sl, i), [128, 256], BF16, es) for i in range(2)],
                      "hd": []}
                for i in range(2):
                    B = {}
                    for nm in ("u", "tmp"):
                        B[nm] = c.sb("%s%d_%d" % (nm, sl, i), [128, 128], F32, es)
                    for nm in ("vbeta", "kbg", "kdec", "wT", "qkm", "vn"):
                        B[nm] = c.sb("%s%d_%d" % (nm, sl, i), [128, 128], BF16, es)
                    PS["hd"].append(B)
                PSs.append(PS)
            mask4 = [c.sb("mask4_%d" % a, [128, 512], F32, es) for a in range(2)]
            R["mask4"] = mask4
            for a in range(2):
                for i in range(2):
                    c.dma("sp", mask4[a][:, (2 * i) * 128:(2 * i + 1) * 128], d["c_mstrict"][a], writes=[mask4[a]])
                    c.dma("sp", mask4[a][:, (2 * i + 1) * 128:(2 * i + 2) * 128], d["c_gla"][a * 4 + 3], writes=[mask4[a]])
            S = [c.sb("S%d" % h, [128, 128], F32, es) for h in range(8)]
            Sb = [c.sb("Sb%d" % h, [128, 128], BF16, es) for h in range(8)]

            c.dma("sp", ident[:], d["c_ident_bf"], writes=[ident])
            c.dma("sp", identf[:], d["c_ident_f"], writes=[identf])
            c.dma("sp", ones[:], d["c_ones_bf"], writes=[ones])
            for a in range(2):
                for b_ in range(4):
                    c.dma("sp", gmt[a][b_][:], d["c_gla"][a * 4 + b_], writes=[gmt[a][b_]])
                c.dma("sp", mstrict[a][:], d["c_mstrict"][a], writes=[mstrict[a]])
                c.dma("sp", sel[a][:], d["c_sel"][a], writes=[sel[a]])
            c.dma("sp", gmix[:], d["norm_mix"][layer:layer + 1, :].partition_broadcast(128), writes=[gmix])
            c.dma("sp", gh[:], d["gdn_norm"][j:j + 1, :].partition_broadcast(128), writes=[gh])
            c.dma("sp", dtb[:], d["gdn_dt_bias"][j:j + 1, :].partition_broadcast(128), writes=[dtb])
            c.dma("sp", nea[:], d["gdn_a_log"][j:j + 1, :].partition_broadcast(128), writes=[nea])
            c.op("act", lambda e: e.activation(out=nea[:], in_=nea[:], func=AF.Exp), reads=[nea], writes=[nea])
            c.op("dve", lambda e: e.tensor_scalar(nea[:], nea[:], -1.0, None, ALU.mult), reads=[nea], writes=[nea])
            c.op("dve", lambda e: e.memset(zer[:], 0.0), writes=[zer])
            for w in range(5):
                c.dma("sp", cw[:, :, w], d["gdn_conv"][j, w, :].rearrange("(f p) -> p f", p=128), writes=[cw], slow=True)
            for k in range(8):
                c.dma("pool", win[:, k, :], d["gdn_w_in"][j, k * 128:(k + 1) * 128, :], writes=[win])
                c.dma("pool", wout[:, k, :], d["gdn_w_out"][j, k * 128:(k + 1) * 128, :], writes=[wout])

            HT = d["HT"].rearrange("p (k t) -> p k t", k=8)
            QT = d["QT"].rearrange("p (k t) -> p k t", k=8)
            KT = d["KT"].rearrange("p (k t) -> p k t", k=8)

            for si, L in enumerate(self.seqs):
                nt = L // 128
                base = self.seq_off[si]
                httb = TB(None, "HT")
                gtb = [TB(None, "G%d" % t) for t in range(nt)]
                qktb = [TB(None, "QK%d" % t) for t in range(nt // 2)]
                vmtb = [TB(None, "VM%d" % t) for t in range(nt)]
                oftb = [TB(None, "OF%d" % t) for t in range(nt)]
                c.dma("sp", HT[:, :, 0:2], zer[:, 0:16].rearrange("p (k t) -> p k t", k=8), reads=[zer], writes=[httb])
                c.dma("sp", HT[:, :, L + 2:L + 4], zer[:, 0:16].rearrange("p (k t) -> p k t", k=8), reads=[zer], writes=[httb])
                dbg = getattr(self, "dbg", 99)
                for t in range(nt if dbg >= 1 else 0):
                    r0 = base + t * 128
                    xt = xts.next()
                    c.dma("sp", xt[:], x_in[r0:r0 + 128, :], reads=[self.xtb(r0)], writes=[xt])
                    self.rmsnorm_to_hT(c, xt, gmix, (hT, hT[:]), junk, ssq, hn, R["pstB"].next(), ident)
                    c.dma("sp", HT[:, :, 2 + t * 128:2 + (t + 1) * 128], hT[:], reads=[hT], writes=[httb])
                    for n in range(2):
                        pa = R["psA"].next()
                        for k in range(8):
                            c.op("pe", lambda e: e.matmul(pa[:], hT[:, k, :], win[:, k, 3072 + n * 512:3072 + (n + 1) * 512],
                                                          start=(k == 0), stop=(k == 7)), reads=[hT, win], writes=[pa])
                        c.op("act", lambda e: e.copy(out=gt[:, n * 512:(n + 1) * 512], in_=pa[:]), reads=[pa], writes=[gt])
                    c.dma("sp", d["P"][t * 128:(t + 1) * 128, 0:D], gt[:], reads=[gt], writes=[gtb[t]])
                    pa = R["psA"].next()
                    for k in range(8):
                        c.op("pe", lambda e: e.matmul(pa[:, 0:32], hT[:, k, :], win[:, k, 4096:4128],
                                                      start=(k == 0), stop=(k == 7)), reads=[hT, win], writes=[pa])
                    c.op("dve", lambda e: e.tensor_tensor(abt[:, 0:16], pa[:, 0:16], dtb[:], ALU.add), reads=[pa, dtb], writes=[abt])
                    c.op("act", lambda e: e.activation(out=abt[:, 0:16], in_=abt[:, 0:16], func=AF.Exp), reads=[abt], writes=[abt])
                    c.op("act", lambda e: e.activation(out=abt[:, 0:16], in_=abt[:, 0:16], func=AF.Ln, bias=1.0), reads=[abt], writes=[abt])
                    c.op("dve", lambda e: e.tensor_tensor(ab_all[:, t, 0:16], abt[:, 0:16], nea[:], ALU.mult), reads=[abt, nea], writes=[ab_all])
                    c.op("dve", lambda e: e.tensor_copy(out=abt[:, 16:32], in_=pa[:, 16:32]), reads=[pa], writes=[abt])
                    c.op("act", lambda e: e.activation(out=ab_all[:, t, 16:32], in_=abt[:, 16:32], func=AF.Sigmoid), reads=[abt], writes=[ab_all])
                for h in range(8):
                    c.op("dve", lambda e: e.memset(S[h][:], 0.0), writes=[S[h]])
                    c.op("pool", lambda e: e.memset(Sb[h][:], 0.0), writes=[Sb[h]])
                for blk in range(nt // 2 if dbg >= 2 else 0):
                    b0 = blk * 256
                    c.dma("sp", hTb[:], HT[:, :, b0:b0 + 260], reads=[httb], writes=[hTb])
                    def fb_gen(fb, CS):
                        rw, ac, sqb_, rst_ = CS["raw"], CS["acc"], CS["sqb"], CS["rst"]
                        pa = R["psA"].next()
                        for k in range(8):
                            c.op("pe", lambda e: e.matmul(pa[:, 0:260], win[:, k, fb * 128:(fb + 1) * 128], hTb[:, k, :],
                                                          start=(k == 0), stop=(k == 7)), reads=[hTb, win], writes=[pa])
                        c.op("act", lambda e: e.copy(out=rw[:], in_=pa[:, 0:260]), reads=[pa], writes=[rw])
                        yield
                        c.op("dve", lambda e: e.tensor_scalar(ac[:], rw[:, 0:256], cw[:, fb, 0:1], None, ALU.mult), reads=[rw, cw], writes=[ac])
                        yield
                        for w in range(1, 5):
                            c.op("dve", lambda e: e.scalar_tensor_tensor(ac[:], rw[:, w:w + 256], cw[:, fb, w:w + 1], ac[:], ALU.mult, ALU.add),
                                 reads=[rw, cw, ac], writes=[ac])
                            yield
                        hh = fb % 8
                        if fb >= 16:
                            c.op("act", lambda e: e.activation(out=vTb[:, hh, :], in_=ac[:], func=AF.Silu), reads=[ac], writes=[vTb])
                            return
                        c.op("act", lambda e: e.activation(out=ac[:], in_=ac[:], func=AF.Silu), reads=[ac], writes=[ac])
                        yield
                        c.op("act", lambda e: e.activation(out=sqb_[:], in_=ac[:], func=AF.Square), reads=[ac], writes=[sqb_])
                        yield
                        pn = R["psA"].next()
                        c.op("pe", lambda e: e.matmul(pn[:, 0:256], ones[:], sqb_[:], start=True, stop=True), reads=[ones, sqb_], writes=[pn])
                        c.op("act", lambda e: e.activation(out=rst_[:], in_=pn[:, 0:256], func=AF.Ln, bias=RMS_EPS), reads=[pn], writes=[rst_])
                        yield
                        c.op("act", lambda e: e.activation(out=rst_[:], in_=rst_[:], func=AF.Exp, scale=-0.5), reads=[rst_], writes=[rst_])
                        yield
                        dst = qTb if fb < 8 else kTb
                        if fb < 8:
                            c.op("dve", lambda e: e.scalar_tensor_tensor(dst[:, hh, :], ac[:], 128 ** -0.5, rst_[:], ALU.mult, ALU.mult),
                                 reads=[ac, rst_], writes=[dst])
                        else:
                            c.op("pool", lambda e: e.tensor_tensor(dst[:, hh, :], ac[:], rst_[:], ALU.mult), reads=[ac, rst_], writes=[dst])

                    for fb0 in range(0, 24, NCS):
                        run_interleaved([fb_gen(fb0 + i, CSs[i]) for i in range(NCS)])
                    c.dma("sp", QT[:, :, b0:b0 + 256], qTb[:], reads=[qTb], writes=[qktb[blk]])
                    c.dma("sp", KT[:, :, b0:b0 + 256], kTb[:], reads=[kTb], writes=[qktb[blk]])
                    for tt in range(2):
                        t = blk * 2 + tt
                        cs_ = slice(tt * 128, (tt + 1) * 128)
                        for src, dstm in ((vTb, vtm), (kTb, ktm)):
                            pt = R["pstB"].next()
                            for h in range(8):
                                c.op("pe", lambda e: e.transpose(out=pt[:, h * 128:(h + 1) * 128], in_=src[:, h, cs_], identity=ident[:]),
                                     reads=[src, ident], writes=[pt])
                            c.op("act", lambda e: e.copy(out=dstm[:], in_=pt[:]), reads=[pt], writes=[dstm])
                        c.dma("sp", d["VM"][t * 128:(t + 1) * 128, :], vtm[:], reads=[vtm], writes=[vmtb[t]])
                        c.op("pool", lambda e: e.tensor_copy(out=qTt[:], in_=qTb[:, :, cs_]), reads=[qTb], writes=[qTt])
                        c.op("pool", lambda e: e.tensor_copy(out=kTt[:], in_=kTb[:, :, cs_]), reads=[kTb], writes=[kTt])
                        abv = _View(ab_all, (slice(None), t, slice(None)))
                        self.gdn_scalars(c, R, 0, abv, sc)
                        run_interleaved([self.gdn_pair(c, R, PSs[i], 2 * i, 0, qTt, kTt, ktm, vtm, sc, S, Sb, oft)
                                         for i in range(NP)])
                        c.dma("sp", d["OF"][t * 128:(t + 1) * 128, :], oft[:], reads=[oft], writes=[oftb[t]])
                for h in range(8):
                    c.op("dve", lambda e: e.memset(S[h][:], 0.0), writes=[S[h]])
                    c.op("pool", lambda e: e.memset(Sb[h][:], 0.0), writes=[Sb[h]])
                for t in (range(nt - 1, -1, -1) if dbg >= 4 else []):
                    r0 = base + t * 128
                    xt = xts.next()
                    c.dma("sp", xt[:], x_in[r0:r0 + 128, :], reads=[self.xtb(r0)], writes=[xt])
                    c.dma("sp", qTt[:], QT[:, :, t * 128:(t + 1) * 128], reads=[qktb[t // 2]], writes=[qTt])
                    c.dma("sp", kTt[:], KT[:, :, t * 128:(t + 1) * 128], reads=[qktb[t // 2]], writes=[kTt])
                    c.dma("sp", vtm[:], d["VM"][t * 128:(t + 1) * 128, :], reads=[vmtb[t]], writes=[vtm])
                    c.dma("sp", oft[:], d["OF"][t * 128:(t + 1) * 128, :], reads=[oftb[t]], writes=[oft])
                    c.dma("sp", gt[:], d["P"][t * 128:(t + 1) * 128, 0:D], reads=[gtb[t]], writes=[gt])
                    pt = R["pstB"].next()
                    for h in range(8):
                        c.op("pe", lambda e: e.transpose(out=pt[:, h * 128:(h + 1) * 128], in_=kTt[:, h, :], identity=ident[:]),
                             reads=[kTt, ident], writes=[pt])
                    c.op("act", lambda e: e.copy(out=ktm[:], in_=pt[:]), reads=[pt], writes=[ktm])
                    abv = _View(ab_all, (slice(None), t, slice(None)))
                    self.gdn_scalars(c, R, 1, abv, sc)
                    run_interleaved([self.gdn_pair(c, R, PSs[i], 2 * i, 1, qTt, kTt, ktm, vtm, sc, S, Sb, ot, add_tb=oft)
                                     for i in range(NP)])
                    c.op("act", lambda e: e.activation(out=sq[:], in_=ot[:], func=AF.Square), reads=[ot], writes=[sq])
                    c.op("dve", lambda e: e.reduce_sum(out=s2[:], in_=sq[:].rearrange("p (h x) -> p h x", h=8), axis=AX.X),
                         reads=[sq], writes=[s2])
                    c.op("dve", lambda e: e.tensor_scalar(s2[:], s2[:], 1.0 / 128, RMS_EPS, ALU.mult, ALU.add), reads=[s2], writes=[s2])
                    c.op("act", lambda e: e.activation(out=s2[:], in_=s2[:], func=AF.Sqrt), reads=[s2], writes=[s2])
                    c.op("dve", lambda e: e.reciprocal(out=s2[:], in_=s2[:]), reads=[s2], writes=[s2])
                    for h in range(8):
                        hs = slice(h * 128, (h + 1) * 128)
                        c.op("dve", lambda e: e.scalar_tensor_tensor(ot[:, hs], ot[:, hs], s2[:, h:h + 1], gh[:], ALU.mult, ALU.mult),
                             reads=[ot, s2, gh], writes=[ot])
                    c.op("act", lambda e: e.activation(out=gt[:], in_=gt[:], func=AF.Silu), reads=[gt], writes=[gt])
                    c.op("dve", lambda e: e.tensor_tensor(omix[:], ot[:], gt[:], ALU.mult), reads=[ot, gt], writes=[omix])
                    pt = R["pstB"].next()
                    for k in range(8):
                        c.op("pe", lambda e: e.transpose(out=pt[:, k * 128:(k + 1) * 128], in_=omix[:, k * 128:(k + 1) * 128],
                                                         identity=ident[:]), reads=[omix, ident], writes=[pt])
                    c.op("act", lambda e: e.copy(out=oT[:], in_=pt[:].rearrange("p (k t) -> p k t", k=8)), reads=[pt], writes=[oT])
                    for n in range(2):
                        py = R["psA"].next()
                        for k in range(8):
                            c.op("pe", lambda e: e.matmul(py[:], oT[:, k, :], wout[:, k, n * 512:(n + 1) * 512],
                                                          start=(k == 0), stop=(k == 7)), reads=[oT, wout], writes=[py])
                        c.op("dve", lambda e: e.tensor_tensor(xt[:, n * 512:(n + 1) * 512], py[:], xt[:, n * 512:(n + 1) * 512], ALU.add),
                             reads=[py, xt], writes=[xt])
                    c.dma("sp", d["X"][r0:r0 + 128, :], xt[:], reads=[xt], writes=[self.xtb(r0)])
            c.barrier()

    def phase_copy(self, c, x_in, to_y=False):
        d = self.d
        with contextlib.ExitStack() as es:
            xts = Rot([c.sb("cx%d" % i, [128, D], F32, es) for i in range(4)])
            for r0 in range(0, self.T, 128):
                xt = xts.next()
                c.dma("sp", xt[:], x_in[r0:r0 + 128, :], reads=[self.xtb(r0)], writes=[xt])
                if to_y:
                    c.dma("sp", d["y"][r0:r0 + 128, :], xt[:], reads=[xt], writes=[self.ytb(r0)])
                else:
                    c.dma("sp", d["X"][r0:r0 + 128, :], xt[:], reads=[xt], writes=[self.xtb(r0)])
            c.barrier()

    def build(self):
        nc = bass.Bass("TRN2", target_bir_lowering=False)
        self.declare(nc)
        with contextlib.ExitStack() as es:
            c = Ctx(nc, es)
            self.c = c
            self._xtb = [TB(None, "X%d" % i) for i in range(self.T // 128)]
            self._ytb = [TB(None, "Y%d" % i) for i in range(self.T // 128)]
            first = True
            for layer in range(self.depth):
                last = layer == self.depth - 1
                if self.do_mixer:
                    from_x = self.d["x"] if first else self.d["X"]
                    kind, jj = self.kinds[layer]
                    if kind == "even":
                        self.phase_even(c, jj, layer, from_x)
                    else:
                        self.phase_odd(c, jj, layer, from_x)
                    first = False
                if self.do_xattn:
                    self.phase_xattn(c, layer, self.d["x"] if first else self.d["X"])
                    first = False
                if self.do_ffn:
                    self.phase_ffn(c, layer, self.d["x"] if first else self.d["X"], last)
                    first = False
            c.barrier()
            self.stats = (c.n_inst, c.n_wait)
        return nc


def gla_masks(direction):
    t = np.arange(128)
    same = (t[:, None] // 64) == (t[None, :] // 64)
    if direction == 0:
        le = t[:, None] <= t[None, :]
        mid = 64 * (t // 64) + 31
        lemid = t[:, None] <= mid[None, :]
        gt = t[:, None] > t[None, :]
    else:
        le = t[:, None] >= t[None, :]
        mid = 64 * (t // 64) + 32
        lemid = t[:, None] >= mid[None, :]
        gt = t[:, None] < t[None, :]
    M1 = (same & le).astype(np.float32)
    M2 = (same * (le.astype(np.float32) - lemid.astype(np.float32))).astype(np.float32)
    M4 = (same & gt).astype(np.float32)
    MT = (same & le).astype(np.float32)
    return M1, M2, M4, MT


def host_constants(Lmax):
    cst = {}
    cst["c_ident_bf"] = np.eye(128, dtype=np.float32).astype(ml_dtypes.bfloat16)
    cst["c_ones_bf"] = np.ones((128, 128), np.float32).astype(ml_dtypes.bfloat16)
    cst["c_ident_f"] = np.eye(128, dtype=np.float32)
    gm = np.zeros((2, 4, 128, 128), np.float32)
    for dr in range(2):
        for i, m in enumerate(gla_masks(dr)):
            gm[dr, i] = m
    cst["c_gla"] = gm.reshape(8, 128, 128)
    ind = np.zeros((128, 2), np.float32)
    ind[:64, 0] = 1.0
    ind[64:, 1] = 1.0
    cst["c_ind"] = ind
    inv = (1.0 / (np.float32(10000.0) ** (np.arange(64, dtype=np.float32) / np.float32(64)))).astype(np.float32)
    ang = (np.arange(Lmax, dtype=np.float32)[:, None] * inv[None, :]).astype(np.float32)
    cs = np.zeros((Lmax, 4, 64), np.float32)
    cs[:, 0] = np.cos(ang)
    cs[:, 1] = np.sin(ang)
    cs[:, 2] = np.cos(ang) * np.float32(128 ** -0.5)
    cs[:, 3] = np.sin(ang) * np.float32(128 ** -0.5)
    cst["c_rope"] = cs.reshape(Lmax, 256)
    lg = np.log1p(-np.exp2(-5.0 - np.arange(4, dtype=np.float32))).astype(np.float32)
    cst["c_lgam"] = np.repeat(lg, 128)[None, :].astype(np.float32)
    t = np.arange(128)
    same = (t[:, None] // 64) == (t[None, :] // 64)
    ms = np.zeros((2, 128, 128), np.float32)
    ms[0] = same & (t[None, :] < t[:, None])
    ms[1] = same & (t[None, :] > t[:, None])
    cst["c_mstrict"] = ms
    sl = np.zeros((2, 128, 128), np.float32)
    sl[0, :64, :] = 1.0
    sl[1, 64:, :] = 1.0
    cst["c_sel"] = sl
    hm = np.zeros((1, 8), np.float32)
    hm[0, :4] = 1.0
    cst["c_hmask"] = hm
    return cst


_CACHE = {}


def _get_program(seqs):
    key = tuple(seqs)
    if key not in _CACHE:
        b = Builder(seqs)
        nc = b.build()
        _CACHE[key] = (b, nc)
    return _CACHE[key]


def kernel(**inputs):
    inp = {k: np.asarray(v) for k, v in inputs.items()}
    seqs = [2048, 2048, 4096, 4096]
    b, nc = _get_program(seqs)
    cst = host_constants(max(seqs))
    shared = {}
    for k in ("norm_mix", "norm_xq", "norm_mem", "norm_ffn", "even_w_in", "even_w_out", "hgrn_lb_logits",
              "gdn_w_in", "gdn_conv", "gdn_norm", "gdn_w_out", "xa_w_q", "xa_w_kv", "xa_w_o", "ffn_w_gu", "ffn_w_down"):
        shared[k] = np.ascontiguousarray(inp[k], dtype=np.float32)
    shared["norm_final"] = np.ascontiguousarray(inp["norm_final"].reshape(1, D), dtype=np.float32)
    shared["ret_norm"] = np.ascontiguousarray(inp["ret_norm"].reshape(2, 512), dtype=np.float32)
    shared["hgrn_norm"] = np.ascontiguousarray(inp["hgrn_norm"].reshape(2, 512), dtype=np.float32)
    shared["gdn_a_log"] = np.ascontiguousarray(inp["gdn_a_log"].reshape(2, 16), dtype=np.float32)
    shared["gdn_dt_bias"] = np.ascontiguousarray(inp["gdn_dt_bias"].reshape(2, 16), dtype=np.float32)
    shared.update(cst)
    in_maps = []
    for ci in range(NCORES):
        m = dict(shared)
        xp = inp["x_prompt"][2 * ci:2 * ci + 2].reshape(-1, D)
        xs = inp["x_sample"][2 * ci:2 * ci + 2].reshape(-1, D)
        m["x"] = np.ascontiguousarray(np.concatenate([xp, xs], axis=0), dtype=np.float32)
        mp = inp["mem_prompt"][2 * ci:2 * ci + 2].reshape(-1, D)
        ms = inp["mem_sample"][2 * ci:2 * ci + 2].reshape(-1, D)
        m["mem"] = np.ascontiguousarray(np.concatenate([mp, ms], axis=0), dtype=np.float32)
        in_maps.append(m)
    res = run_bass_kernel_spmd(nc, in_maps, core_ids=list(range(NCORES)))
    yp = np.empty((16, 2048, D), np.float32)
    ys = np.empty((16, 4096, D), np.float32)
    for ci in range(NCORES):
        y = np.asarray(res.results[ci]["y"])
        yp[2 * ci:2 * ci + 2] = y[:4096].reshape(2, 2048, D)
        ys[2 * ci:2 * ci + 2] = y[4096:].reshape(2, 4096, D)
    return (yp, ys)
```

```python
import contextlib
import math
import numpy as np
import ml_dtypes
import concourse.bass as bass
import concourse.mybir as mybir
from concourse.bass_utils import run_bass_kernel_spmd

F32 = mybir.dt.float32
BF16 = mybir.dt.bfloat16
AF = mybir.ActivationFunctionType
ALU = mybir.AluOpType
AX = mybir.AxisListType

D = 1024
DEPTH = 4
N_MEM = 256
RMS_EPS = 1e-6
D_FF = 2816
EVEN_IN = 4608
ODD_IN = 4128
XA_SCALE = 256 ** -0.5
NCORES = 8


class TB:
    __slots__ = ("t", "w", "r", "name")

    def __init__(self, t, name=""):
        self.t = t
        self.w = None
        self.r = {}
        self.name = name

    def __getitem__(self, k):
        return self.t[k]


class Ctx:
    def __init__(self, nc, es, n_dma_sems=20):
        self.nc = nc
        self.es = es
        self.engs = {"pe": nc.tensor, "act": nc.scalar, "dve": nc.vector, "pool": nc.gpsimd, "sp": nc.sync}
        self.sem = {}
        self.cnt = {}
        self.seen = {k: {} for k in self.engs}
        for k in self.engs:
            self.sem[k] = es.enter_context(nc.semaphore("s_" + k))
            self.cnt[k] = 0
        self.dsem = {}
        self.dval = {}
        self.drot = {}
        for q in ("sp", "pool", "act"):
            n = n_dma_sems if q != "act" else 8
            self.dsem[q] = [es.enter_context(nc.semaphore("d_%s_%d" % (q, i))) for i in range(n)]
            self.dval[q] = [0] * n
            self.drot[q] = 0
        self.n_inst = 0
        self.n_wait = 0

    def sb(self, name, shape, dt, es=None):
        es = es or self.es
        self.uid = getattr(self, "uid", 0) + 1
        t = es.enter_context(self.nc.sbuf_tensor("%s_%d" % (name, self.uid), list(shape), dt))
        return TB(t, name)

    def ps(self, name, shape, dt, es=None):
        es = es or self.es
        self.uid = getattr(self, "uid", 0) + 1
        t = es.enter_context(self.nc.psum_tensor("%s_%d" % (name, self.uid), list(shape), dt))
        return TB(t, name)

    def _deps(self, engname, reads, writes):
        need = {}

        def add(ev, raw):
            if ev is None:
                return
            s, v, e = ev
            if e == engname and engname == "pe":
                return
            key = id(s)
            if key not in need or need[key][1] < v:
                need[key] = (s, v)

        for b in reads:
            add(b.w, True)
        for b in writes:
            add(b.w, False)
            for ev in b.r.values():
                add(ev, False)
        return need

    def _emit_waits(self, engname, need):
        eng = self.engs[engname]
        seen = self.seen[engname]
        for key, (s, v) in need.items():
            if seen.get(key, 0) >= v:
                continue
            eng.wait_ge(s, v)
            seen[key] = v
            self.n_wait += 1

    def _commit(self, ev, reads, writes):
        for b in writes:
            b.w = ev
            b.r = {}
        for b in reads:
            if b in writes:
                continue
            b.r[ev[2]] = ev

    def op(self, engname, fn, reads=(), writes=()):
        need = self._deps(engname, reads, writes)
        self._emit_waits(engname, need)
        ins = fn(self.engs[engname])
        self.cnt[engname] += 1
        ins.then_inc(self.sem[engname], 1)
        ev = (self.sem[engname], self.cnt[engname], engname)
        self._commit(ev, reads, writes)
        self.n_inst += 1
        return ins

    def dma(self, q, out, in_, reads=(), writes=(), slow=False):
        need = self._deps("dma_" + q, reads, writes)
        i = self.drot[q]
        self.drot[q] = (i + 1) % len(self.dsem[q])
        s = self.dsem[q][i]
        if self.dval[q][i] > 0:
            need[id(s)] = (s, self.dval[q][i])
        self._emit_waits(q, need)
        if slow:
            ins = self.engs[q].dma_start(out=out, in_=in_, allow_slow_non_contiguous=True)
        else:
            ins = self.engs[q].dma_start(out=out, in_=in_)
        self.dval[q][i] += 16
        ins.then_inc(s, 16)
        ev = (s, self.dval[q][i], "dma_" + q + str(i))
        self._commit(ev, reads, writes)
        self.n_inst += 1
        return ins

    def barrier(self):
        for e in self.engs:
            need = {}
            for k in self.engs:
                if k != e and self.cnt[k] > 0:
                    need[id(self.sem[k])] = (self.sem[k], self.cnt[k])
            for q in self.dsem:
                for s, v in zip(self.dsem[q], self.dval[q]):
                    if v > 0:
                        need[id(s)] = (s, v)
            self._emit_waits(e, need)


def rstd_inplace(c, ssq, inv_n, eps=RMS_EPS):
    c.op("dve", lambda e: e.tensor_scalar(ssq[:], ssq[:], inv_n, eps, ALU.mult, ALU.add), reads=[ssq], writes=[ssq])
    c.op("act", lambda e: e.activation(out=ssq[:], in_=ssq[:], func=AF.Sqrt), reads=[ssq], writes=[ssq])
    c.op("dve", lambda e: e.reciprocal(out=ssq[:], in_=ssq[:]), reads=[ssq], writes=[ssq])


class _View:
    def __init__(self, tb, key):
        self.tb = tb
        self.key = key

    def __getitem__(self, k):
        v = self.tb.t[self.key]
        return v[k]

    @property
    def w(self):
        return self.tb.w

    @w.setter
    def w(self, v):
        self.tb.w = v

    @property
    def r(self):
        return self.tb.r

    @r.setter
    def r(self, v):
        self.tb.r = v


def run_interleaved(gens):
    gens = list(gens)
    while gens:
        for g in list(gens):
            try:
                next(g)
            except StopIteration:
                gens.remove(g)


class Rot:
    def __init__(self, items):
        self.items = items
        self.i = 0

    def next(self):
        it = self.items[self.i]
        self.i = (self.i + 1) % len(self.items)
        return it


class Builder:
    def __init__(self, seqs, depth=DEPTH, do_mixer=True, do_xattn=True, do_ffn=True, kinds=None):
        self.seqs = list(seqs)
        self.depth = depth
        self.do_mixer = do_mixer
        self.do_xattn = do_xattn
        self.do_ffn = do_ffn
        self.kinds = kinds or [("even", l // 2) if l % 2 == 0 else ("odd", l // 2) for l in range(depth)]
        self.T = sum(self.seqs)
        self.seq_off = [sum(self.seqs[:i]) for i in range(len(self.seqs))]
        self.Lmax = max(self.seqs)

    def declare(self, nc):
        d = {}

        def inp(name, shape, dt=F32):
            d[name] = nc.dram_tensor(name, list(shape), dt, kind="ExternalInput").ap()

        ns = len(self.seqs)
        inp("x", [self.T, D])
        inp("mem", [ns * N_MEM, D])
        for n in ("norm_mix", "norm_xq", "norm_mem", "norm_ffn"):
            inp(n, [DEPTH, D])
        inp("norm_final", [1, D])
        inp("even_w_in", [2, D, EVEN_IN])
        inp("even_w_out", [2, D, D])
        inp("hgrn_lb_logits", [2, 512])
        inp("ret_norm", [2, 512])
        inp("hgrn_norm", [2, 512])
        inp("gdn_w_in", [2, D, ODD_IN])
        inp("gdn_conv", [2, 5, 3072])
        inp("gdn_a_log", [2, 16])
        inp("gdn_dt_bias", [2, 16])
        inp("gdn_norm", [2, 128])
        inp("gdn_w_out", [2, D, D])
        inp("xa_w_q", [DEPTH, D, D])
        inp("xa_w_kv", [DEPTH, D, 2 * D])
        inp("xa_w_o", [DEPTH, D, D])
        inp("ffn_w_gu", [DEPTH, D, 2 * D_FF])
        inp("ffn_w_down", [DEPTH, D_FF, D])
        for name, arr in host_constants(self.Lmax).items():
            inp(name, arr.shape, F32 if arr.dtype == np.float32 else BF16)
        d["y"] = nc.dram_tensor("y", [self.T, D], F32, kind="ExternalOutput").ap()
        d["X"] = nc.dram_tensor("X_scr", [self.T, D], F32).ap()
        d["P"] = nc.dram_tensor("P_scr", [self.Lmax, EVEN_IN], F32).ap()
        d["OF"] = nc.dram_tensor("OF_scr", [self.Lmax, D], F32).ap()
        d["HT"] = nc.dram_tensor("HT_scr", [128, 8 * (self.Lmax + 4)], BF16).ap()
        d["QT"] = nc.dram_tensor("QT_scr", [128, 8 * self.Lmax], BF16).ap()
        d["KT"] = nc.dram_tensor("KT_scr", [128, 8 * self.Lmax], BF16).ap()
        d["VM"] = nc.dram_tensor("VM_scr", [self.Lmax, D], BF16).ap()
        self.d = d

    def rmsnorm_to_hT(self, c, xt, grow, hT_dst, junk, ssq, hn, pst, ident):
        c.op("act", lambda e: e.activation(out=junk[:], in_=xt[:], func=AF.Square, accum_out=ssq[:]),
             reads=[xt], writes=[junk, ssq])
        rstd_inplace(c, ssq, 1.0 / D)
        c.op("dve", lambda e: e.scalar_tensor_tensor(hn[:], xt[:], ssq[:], grow[:], ALU.mult, ALU.mult),
             reads=[xt, ssq, grow], writes=[hn])
        for k in range(8):
            c.op("pe", lambda e, k=k: e.transpose(out=pst[:, k * 128:(k + 1) * 128],
                                                   in_=hn[:, k * 128:(k + 1) * 128], identity=ident[:]),
                 reads=[hn, ident], writes=[pst])
        tb, ap = hT_dst
        c.op("act", lambda e: e.copy(out=ap, in_=pst[:].rearrange("p (k t) -> p k t", k=8)),
             reads=[pst], writes=[tb])

    def x_src(self, layer_first):
        return self.d["x"] if layer_first else self.d["X"]

    def phase_ffn(self, c, layer, x_in, last):
        nc, d = c.nc, self.d
        TBK = 512
        NTB = TBK // 128
        with contextlib.ExitStack() as es:
            wgu = c.sb("wgu", [128, 8, 2 * D_FF], BF16, es)
            wdn = c.sb("wdn", [128, 22, D], BF16, es)
            grow = c.sb("grow", [128, D], F32, es)
            gfin = c.sb("gfin", [128, D], F32, es) if last else None
            ident = c.sb("identb", [128, 128], BF16, es)
            xts = Rot([[c.sb("xt%d_%d" % (i, j), [128, D], F32, es) for j in range(NTB)] for i in range(1)])
            hTs = Rot([c.sb("hT%d" % i, [128, 8, TBK], BF16, es) for i in range(1 if last else 2)])
            aT = c.sb("aT", [128, 22, TBK], BF16, es)
            sg = Rot([c.sb("sg%d" % i, [128, TBK], F32, es) for i in range(2)])
            ssq = c.sb("ssq", [128, 1], F32, es)
            hn = c.sb("hn", [128, D], BF16, es)
            junk = hn
            yts = Rot([c.sb("yt%d" % i, [128, D], F32, es) for i in range(2)])
            pst = c.ps("pst", [128, D], BF16, es)
            psg = Rot([c.ps("psg%d" % i, [128, 512], F32, es) for i in range(2)])
            psu = Rot([c.ps("psu%d" % i, [128, 512], F32, es) for i in range(2)])
            psy = Rot([c.ps("psy%d" % i, [128, 512], F32, es) for i in range(2)])

            c.dma("sp", ident[:], d["c_ident_bf"], writes=[ident])
            c.dma("sp", grow[:], d["norm_ffn"][layer:layer + 1, :].partition_broadcast(128), writes=[grow])
            if last:
                c.dma("sp", gfin[:], d["norm_final"][0:1, :].partition_broadcast(128), writes=[gfin])
            for k in range(8):
                c.dma("pool", wgu[:, k, :], d["ffn_w_gu"][layer, k * 128:(k + 1) * 128, :], writes=[wgu])
            for f in range(22):
                c.dma("pool", wdn[:, f, :], d["ffn_w_down"][layer, f * 128:(f + 1) * 128, :], writes=[wdn])

            for b0 in range(0, self.T, TBK):
                xt = xts.next()
                hT = hTs.next()
                for j in range(NTB):
                    r0 = b0 + j * 128
                    c.dma("sp", xt[j][:], x_in[r0:r0 + 128, :], reads=[self.xtb(r0)], writes=[xt[j]])
                    self.rmsnorm_to_hT(c, xt[j], grow, (hT, hT[:, :, j * 128:(j + 1) * 128]), junk, ssq, hn, pst, ident)
                for fb in range(22):
                    pg, pu, s = psg.next(), psu.next(), sg.next()
                    for k in range(8):
                        c.op("pe", lambda e, k=k: e.matmul(pg[:, 0:TBK], wgu[:, k, fb * 128:(fb + 1) * 128], hT[:, k, :],
                                                          start=(k == 0), stop=(k == 7)),
                             reads=[wgu, hT], writes=[pg])
                    for k in range(8):
                        c.op("pe", lambda e, k=k: e.matmul(pu[:, 0:TBK], wgu[:, k, D_FF + fb * 128:D_FF + (fb + 1) * 128],
                                                          hT[:, k, :], start=(k == 0), stop=(k == 7)),
                             reads=[wgu, hT], writes=[pu])
                    c.op("act", lambda e: e.activation(out=s[:], in_=pg[:, 0:TBK], func=AF.Silu), reads=[pg], writes=[s])
                    c.op("dve", lambda e: e.tensor_tensor(aT[:, fb, :], pu[:, 0:TBK], s[:], ALU.mult),
                         reads=[pu, s], writes=[aT])
                for j in range(NTB):
                    r0 = b0 + j * 128
                    yt = yts.next()
                    for n in range(2):
                        py = psy.next()
                        for fb in range(22):
                            c.op("pe", lambda e, fb=fb: e.matmul(py[:], aT[:, fb, j * 128:(j + 1) * 128],
                                                                wdn[:, fb, n * 512:(n + 1) * 512],
                                                                start=(fb == 0), stop=(fb == 21)),
                                 reads=[aT, wdn], writes=[py])
                        c.op("dve", lambda e: e.tensor_tensor(yt[:, n * 512:(n + 1) * 512], py[:],
                                                              xt[j][:, n * 512:(n + 1) * 512], ALU.add),
                             reads=[py, xt[j]], writes=[yt])
                    if last:
                        self.final_norm_store(c, yt, gfin, junk, ssq, r0)
                    else:
                        c.dma("sp", d["X"][r0:r0 + 128, :], yt[:], reads=[yt], writes=[self.xtb(r0)])
            c.barrier()

    def final_norm_store(self, c, yt, gfin, junk, ssq, r0):
        c.op("act", lambda e: e.activation(out=junk[:], in_=yt[:], func=AF.Square, accum_out=ssq[:]),
             reads=[yt], writes=[junk, ssq])
        rstd_inplace(c, ssq, 1.0 / D)
        c.op("dve", lambda e: e.scalar_tensor_tensor(yt[:], yt[:], ssq[:], gfin[:], ALU.mult, ALU.mult),
             reads=[yt, ssq, gfin], writes=[yt])
        c.dma("sp", self.d["y"][r0:r0 + 128, :], yt[:], reads=[yt], writes=[self.ytb(r0)])

    def xtb(self, r0):
        return self._xtb[r0 // 128]

    def ytb(self, r0):
        return self._ytb[r0 // 128]

    def phase_xattn(self, c, layer, x_in):
        nc, d = c.nc, self.d
        TBK = 512
        with contextlib.ExitStack() as es:
            wq = c.sb("wq", [128, 8, D], BF16, es)
            wkv = c.sb("wkv", [128, 8, 2 * D], BF16, es)
            wo = c.sb("wo", [128, 8, D], BF16, es)
            gq = c.sb("gq", [128, D], F32, es)
            gm = c.sb("gm", [128, D], F32, es)
            ident = c.sb("identb", [128, 128], BF16, es)
            ones = c.sb("onesb", [128, 128], BF16, es)
            xts_r = Rot([[c.sb("xt%d_%d" % (i, j), [128, D], F32, es) for j in range(4)] for i in range(2)])
            hT_r = Rot([c.sb("hT%d" % i, [128, 8, TBK], BF16, es) for i in range(2)])
            qT_r = Rot([c.sb("qT%d" % i, [128, 8, TBK], BF16, es) for i in range(2)])
            oT_r = Rot([c.sb("oT%d" % i, [128, 8, TBK], BF16, es) for i in range(2)])
            xts = xts_r.next()
            memT = c.sb("memT", [128, 8, N_MEM], BF16, es)
            KT = c.sb("KT", [128, 8, N_MEM], BF16, es)
            Vt = c.sb("Vt", [128, 2, D], BF16, es)
            PT = [Rot([c.sb("PT%d_%d" % (m, i), [128, TBK], BF16, es) for i in range(2)]) for m in range(2)]
            rden = c.sb("rden", [128, TBK], F32, es)
            junk = c.sb("junk", [128, D], BF16, es)
            ssq = c.sb("ssq", [128, 1], F32, es)
            hn = c.sb("hn", [128, D], BF16, es)
            yts = Rot([c.sb("yt%d" % i, [128, D], F32, es) for i in range(2)])
            pst = c.ps("pst", [128, D], BF16, es)
            psA = Rot([c.ps("psA%d" % i, [128, 512], F32, es) for i in range(4)])
            psD = c.ps("psD", [128, 512], F32, es)
            psy = Rot([c.ps("psy%d" % i, [128, 512], F32, es) for i in range(2)])

            c.dma("sp", ident[:], d["c_ident_bf"], writes=[ident])
            c.dma("sp", ones[:], d["c_ones_bf"], writes=[ones])
            c.dma("sp", gq[:], d["norm_xq"][layer:layer + 1, :].partition_broadcast(128), writes=[gq])
            c.dma("sp", gm[:], d["norm_mem"][layer:layer + 1, :].partition_broadcast(128), writes=[gm])
            for k in range(8):
                c.dma("pool", wq[:, k, :], d["xa_w_q"][layer, k * 128:(k + 1) * 128, :], writes=[wq])
                c.dma("pool", wkv[:, k, :], d["xa_w_kv"][layer, k * 128:(k + 1) * 128, :], writes=[wkv])
                c.dma("pool", wo[:, k, :], d["xa_w_o"][layer, k * 128:(k + 1) * 128, :], writes=[wo])

            for si, L in enumerate(self.seqs):
                for m in range(2):
                    mt = xts[m]
                    c.dma("sp", mt[:], d["mem"][si * N_MEM + m * 128: si * N_MEM + (m + 1) * 128, :], writes=[mt])
                    self.rmsnorm_to_hT(c, mt, gm, (memT, memT[:, :, m * 128:(m + 1) * 128]), junk, ssq, hn, pst, ident)
                for fb in range(8):
                    pa = psA.next()
                    for k in range(8):
                        c.op("pe", lambda e, k=k: e.matmul(pa[:, 0:N_MEM], wkv[:, k, fb * 128:(fb + 1) * 128], memT[:, k, :],
                                                          start=(k == 0), stop=(k == 7)), reads=[wkv, memT], writes=[pa])
                    c.op("act", lambda e: e.copy(out=KT[:, fb, :], in_=pa[:, 0:N_MEM]), reads=[pa], writes=[KT])
                for m in range(2):
                    for n in range(2):
                        pa = psA.next()
                        for k in range(8):
                            c.op("pe", lambda e, k=k: e.matmul(pa[:], memT[:, k, m * 128:(m + 1) * 128],
                                                              wkv[:, k, D + n * 512:D + (n + 1) * 512],
                                                              start=(k == 0), stop=(k == 7)), reads=[wkv, memT], writes=[pa])
                        c.op("act", lambda e: e.copy(out=Vt[:, m, n * 512:(n + 1) * 512], in_=pa[:]), reads=[pa], writes=[Vt])
                for b0 in range(self.seq_off[si], self.seq_off[si] + L, TBK):
                    ntile = min(4, (self.seq_off[si] + L - b0) // 128)
                    W = ntile * 128
                    xts, hT, qT, oT = xts_r.next(), hT_r.next(), qT_r.next(), oT_r.next()
                    for j in range(ntile):
                        r0 = b0 + j * 128
                        c.dma("sp", xts[j][:], x_in[r0:r0 + 128, :], reads=[self.xtb(r0)], writes=[xts[j]])
                        self.rmsnorm_to_hT(c, xts[j], gq, (hT, hT[:, :, j * 128:(j + 1) * 128]), junk, ssq, hn, pst, ident)
                    for fb in range(8):
                        pa = psA.next()
                        for k in range(8):
                            c.op("pe", lambda e, k=k: e.matmul(pa[:, 0:W], wq[:, k, fb * 128:(fb + 1) * 128], hT[:, k, 0:W],
                                                              start=(k == 0), stop=(k == 7)), reads=[wq, hT], writes=[pa])
                        c.op("act", lambda e: e.copy(out=qT[:, fb, 0:W], in_=pa[:, 0:W]), reads=[pa], writes=[qT])
                    for h in range(4):
                        pts = []
                        for mb in range(2):
                            pa = psA.next()
                            for dd in range(2):
                                c.op("pe", lambda e, dd=dd: e.matmul(pa[:, 0:W], KT[:, 2 * h + dd, mb * 128:(mb + 1) * 128],
                                                                    qT[:, 2 * h + dd, 0:W], start=(dd == 0), stop=(dd == 1)),
                                     reads=[KT, qT], writes=[pa])
                            pt = PT[mb].next()
                            c.op("act", lambda e: e.activation(out=pt[:, 0:W], in_=pa[:, 0:W], func=AF.Exp, scale=XA_SCALE),
                                 reads=[pa], writes=[pt])
                            pts.append(pt)
                        for mb in range(2):
                            c.op("pe", lambda e, mb=mb: e.matmul(psD[:, 0:W], ones[:], pts[mb][:, 0:W],
                                                                start=(mb == 0), stop=(mb == 1)), reads=[ones, pts[mb]], writes=[psD])
                        c.op("dve", lambda e: e.reciprocal(out=rden[:, 0:W], in_=psD[:, 0:W]), reads=[psD], writes=[rden])
                        for dd in range(2):
                            pa = psA.next()
                            for mb in range(2):
                                c.op("pe", lambda e, mb=mb: e.matmul(pa[:, 0:W], Vt[:, mb, (2 * h + dd) * 128:(2 * h + dd + 1) * 128],
                                                                    pts[mb][:, 0:W], start=(mb == 0), stop=(mb == 1)),
                                     reads=[Vt, pts[mb]], writes=[pa])
                            c.op("dve", lambda e: e.tensor_tensor(oT[:, 2 * h + dd, 0:W], pa[:, 0:W], rden[:, 0:W], ALU.mult),
                                 reads=[pa, rden], writes=[oT])
                    for j in range(ntile):
                        r0 = b0 + j * 128
                        yt = yts.next()
                        for n in range(2):
                            py = psy.next()
                            for fb in range(8):
                                c.op("pe", lambda e, fb=fb: e.matmul(py[:], oT[:, fb, j * 128:(j + 1) * 128],
                                                                    wo[:, fb, n * 512:(n + 1) * 512],
                                                                    start=(fb == 0), stop=(fb == 7)), reads=[oT, wo], writes=[py])
                            c.op("dve", lambda e: e.tensor_tensor(yt[:, n * 512:(n + 1) * 512], py[:],
                                                                  xts[j][:, n * 512:(n + 1) * 512], ALU.add),
                                 reads=[py, xts[j]], writes=[yt])
                        c.dma("sp", d["X"][r0:r0 + 128, :], yt[:], reads=[yt], writes=[self.xtb(r0)])
            c.barrier()

    def gla_tile(self, c, R, G, q_ap, q_tb, k_tb, k_ap, v_ap, v_tb, lf_tb, lf_ap, dr, S, Sb, o_tb, o_col0, add_tb=None,
                 pre=None):
        if pre is not None:
            for _ in pre():
                yield
        M1, M2, M4, MT = R["gm"][dr]
        cs1, cs2, cs4 = R["psA"].next(), R["psA"].next(), R["psA"].next()
        for ps_, M in ((cs1, M1), (cs2, M2), (cs4, M4)):
            c.op("pe", lambda e: e.matmul(ps_[:], M[:], lf_ap, start=True, stop=True), reads=[M, lf_tb], writes=[ps_])
        E = G["E"]
        c.op("act", lambda e: e.activation(out=E[0][:], in_=cs1[:], func=AF.Exp), reads=[cs1], writes=[E[0]])
        c.op("act", lambda e: e.activation(out=E[1][:], in_=cs2[:], func=AF.Exp), reads=[cs2], writes=[E[1]])
        c.op("act", lambda e: e.activation(out=E[2][:], in_=cs2[:], func=AF.Exp, scale=-1.0), reads=[cs2], writes=[E[2]])
        c.op("act", lambda e: e.activation(out=E[3][:], in_=cs4[:], func=AF.Exp), reads=[cs4], writes=[E[3]])
        yield
        q1, q2, k3, k4, vb = G["q1"], G["q2"], G["k3"], G["k4"], G["vb"]
        BIG = 4.0e18
        c.op("pool", lambda e: e.tensor_tensor(q1[:], E[0][:], q_ap, ALU.mult), reads=[E[0], q_tb], writes=[q1])
        c.op("dve", lambda e: e.scalar_tensor_tensor(q2[:], E[1][:], BIG, q_ap, ALU.min, ALU.mult), reads=[E[1], q_tb], writes=[q2])
        c.op("dve", lambda e: e.scalar_tensor_tensor(k3[:], E[2][:], BIG, k_ap, ALU.min, ALU.mult), reads=[E[2], k_tb], writes=[k3])
        c.op("pool", lambda e: e.tensor_tensor(k4[:], E[3][:], k_ap, ALU.mult), reads=[E[3], k_tb], writes=[k4])
        c.op("act", lambda e: e.copy(out=vb[:], in_=v_ap), reads=[v_tb], writes=[vb])
        pe_l = R["psA"].next()
        for h in range(4):
            c.op("pe", lambda e: e.matmul(pe_l[:, 2 * h:2 * h + 2], lf_ap[:, h * 128:(h + 1) * 128], R["ind"][:],
                                          start=True, stop=True), reads=[lf_tb, R["ind"]], writes=[pe_l])
        eL = G["eL"]
        c.op("act", lambda e: e.activation(out=eL[:], in_=pe_l[:, 0:8], func=AF.Exp), reads=[pe_l], writes=[eL])
        yield
        Ts = []
        for src, nm in ((q1, "q1T"), (q2, "q2T"), (k3, "k3T")):
            pt = R["pstB"].next()
            for h in range(4):
                c.op("pe", lambda e: e.transpose(out=pt[:, h * 128:(h + 1) * 128], in_=src[:, h * 128:(h + 1) * 128],
                                                 identity=R["ident"][:]), reads=[src, R["ident"]], writes=[pt])
            dst = G[nm]
            c.op("act" if nm != "q2T" else "dve", lambda e: e.tensor_copy(out=dst[:], in_=pt[:, 0:512]) if nm == "q2T"
                 else e.copy(out=dst[:], in_=pt[:, 0:512]), reads=[pt], writes=[dst])
            Ts.append(dst)
            yield
        q1T, q2T, k3T = Ts
        order = (0, 1) if dr == 0 else (1, 0)

        MT4 = R["MT4"][dr]
        pa = R["psA"].next()
        for h in range(4):
            hs = slice(h * 128, (h + 1) * 128)
            c.op("pe", lambda e: e.matmul(pa[:, hs], k3T[:, hs], q2T[:, hs], start=True, stop=True), reads=[k3T, q2T], writes=[pa])
        atm = G["ATm4"]
        c.op("dve", lambda e: e.tensor_tensor(atm[:], pa[:], MT4[:], ALU.mult), reads=[pa, MT4], writes=[atm])
        yield
        eLv = eL[:].rearrange("p (h c) -> p h c", c=2)
        S3 = S[:].rearrange("p (h x) -> p h x", h=4)
        for ci in order:
            rs = slice(ci * 64, (ci + 1) * 64)
            po = R["psA"].next()
            for h in range(4):
                hs = slice(h * 128, (h + 1) * 128)
                c.op("pe", lambda e: e.matmul(po[:, hs], q1T[:, hs], Sb[:, hs], start=True, stop=False), reads=[q1T, Sb], writes=[po])
                c.op("pe", lambda e: e.matmul(po[:, hs], atm[:, hs], vb[:, hs], start=False, stop=True), reads=[atm, vb], writes=[po])
            ocs = slice(o_col0, o_col0 + 512)
            if add_tb is None:
                c.op("act", lambda e: e.copy(out=o_tb[rs, ocs], in_=po[rs, :]), reads=[po], writes=[o_tb])
            else:
                c.op("dve", lambda e: e.tensor_tensor(o_tb[rs, ocs], po[rs, :], add_tb[rs, ocs], ALU.add), reads=[po, add_tb], writes=[o_tb])
            pu = R["psA"].next()
            for h in range(4):
                hs = slice(h * 128, (h + 1) * 128)
                c.op("pe", lambda e: e.matmul(pu[:, hs], k4[rs, hs], vb[rs, hs], start=True, stop=True), reads=[k4, vb], writes=[pu])
            yield
            c.op("dve", lambda e: e.tensor_tensor(S3, S3, eLv[:, :, ci:ci + 1].to_broadcast([128, 4, 128]), ALU.mult),
                 reads=[S, eL], writes=[S])
            c.op("dve", lambda e: e.tensor_tensor(S[:], S[:], pu[:], ALU.add), reads=[S, pu], writes=[S])
            yield
            c.op("act", lambda e: e.copy(out=Sb[:], in_=S[:]), reads=[S], writes=[Sb])
            yield

    def phase_even(self, c, j, layer, x_in):
        nc, d = c.nc, self.d
        with contextlib.ExitStack() as es:
            win = c.sb("win", [128, 8, EVEN_IN], BF16, es)
            wout = c.sb("wout", [128, 8, D], BF16, es)
            gmix = c.sb("gmix", [128, D], F32, es)
            gh = c.sb("gh", [128, D], F32, es)
            lbr = c.sb("lbr", [128, 512], F32, es)
            omlb = c.sb("omlb", [128, 512], F32, es)
            lgam = c.sb("lgam", [128, 512], F32, es)
            hmask = c.sb("hmask", [128, 8], F32, es)
            ident = c.sb("identb", [128, 128], BF16, es)
            gmt = [[c.sb("gm%d%d" % (a, b_), [128, 128], F32, es) for b_ in range(4)] for a in range(2)]
            ind = c.sb("ind", [128, 2], F32, es)
            p = c.sb("p", [128, EVEN_IN], F32, es)
            xts = Rot([c.sb("xt%d" % i, [128, D], F32, es) for i in range(2)])
            oft = c.sb("oft", [128, D], F32, es)
            ot = c.sb("ot", [128, D], F32, es)
            rope = c.sb("rope", [128, 4, 64], F32, es)
            rt = [c.sb("rt%d" % i, [128, 4, 64], F32, es) for i in range(4)]
            fg = c.sb("fg", [128, 512], F32, es)
            lf = c.sb("lf", [128, 512], F32, es)
            kk = c.sb("kk", [128, 512], F32, es)
            sq = c.sb("sq", [128, D], F32, es)
            ssq = c.sb("ssq", [128, 1], F32, es)
            s1 = c.sb("s1", [128, 8], F32, es)
            s2 = c.sb("s2", [128, 8], F32, es)
            hn = c.sb("hn", [128, D], BF16, es)
            hT = c.sb("hT", [128, 8, 128], BF16, es)
            omix = c.sb("omix", [128, D], BF16, es)
            junk = omix
            oT = hT
            R = {
                "gm": gmt, "ind": ind, "ident": ident,
                "psA": Rot([c.ps("psA%d" % i, [128, 512], F32, es) for i in range(6)]),
                "pstB": Rot([c.ps("pstB%d" % i, [128, D], BF16, es) for i in range(2)]),
            }
            Gs = []
            for g in range(2):
                G = {"E": [c.sb("E%d_%d" % (g, i), [128, 512], F32, es) for i in range(4)]}
                for nm in ("q1", "q2", "k3", "k4", "vb", "q1T", "q2T", "k3T"):
                    G[nm] = c.sb("%s_%d" % (nm, g), [128, 512], BF16, es)
                G["eL"] = c.sb("eL_%d" % g, [128, 8], F32, es)
                G["ATm4"] = c.sb("ATm4_%d" % g, [128, 512], BF16, es)
                Gs.append(G)
            S = [c.sb("S%d" % g, [128, 512], F32, es) for g in range(2)]
            Sb = [c.sb("Sb%d" % g, [128, 512], BF16, es) for g in range(2)]
            MT4 = [c.sb("MT4_%d" % a, [128, 512], F32, es) for a in range(2)]
            R["MT4"] = MT4
            for a in range(2):
                for h in range(4):
                    c.dma("sp", MT4[a][:, h * 128:(h + 1) * 128], d["c_gla"][a * 4 + 3], writes=[MT4[a]])

            c.dma("sp", ident[:], d["c_ident_bf"], writes=[ident])
            c.dma("sp", ind[:], d["c_ind"], writes=[ind])
            for a in range(2):
                for b_ in range(4):
                    c.dma("sp", gmt[a][b_][:], d["c_gla"][a * 4 + b_], writes=[gmt[a][b_]])
            c.dma("sp", gmix[:], d["norm_mix"][layer:layer + 1, :].partition_broadcast(128), writes=[gmix])
            c.dma("sp", gh[:, 0:512], d["ret_norm"][j:j + 1, :].partition_broadcast(128), writes=[gh])
            c.dma("sp", gh[:, 512:1024], d["hgrn_norm"][j:j + 1, :].partition_broadcast(128), writes=[gh])
            c.dma("sp", lgam[:], d["c_lgam"][0:1, :].partition_broadcast(128), writes=[lgam])
            c.dma("sp", hmask[:], d["c_hmask"][0:1, :].partition_broadcast(128), writes=[hmask])
            if j == 0:
                c.op("dve", lambda e: e.memset(lbr[:], 0.0), writes=[lbr])
            else:
                c.dma("sp", lbr[:], d["hgrn_lb_logits"][1:2, :].partition_broadcast(128), writes=[lbr])
                c.dma("sp", omlb[:], d["hgrn_lb_logits"][0:1, :].partition_broadcast(128), writes=[omlb])
                c.op("dve", lambda e: e.tensor_tensor(lbr[:], lbr[:], omlb[:], ALU.subtract), reads=[lbr, omlb], writes=[lbr])
                c.op("act", lambda e: e.activation(out=lbr[:], in_=lbr[:], func=AF.Sigmoid), reads=[lbr], writes=[lbr])
            c.op("dve", lambda e: e.tensor_scalar(omlb[:], lbr[:], -1.0, 1.0, ALU.mult, ALU.add), reads=[lbr], writes=[omlb])
            for k in range(8):
                c.dma("pool", win[:, k, :], d["even_w_in"][j, k * 128:(k + 1) * 128, :], writes=[win])
                c.dma("pool", wout[:, k, :], d["even_w_out"][j, k * 128:(k + 1) * 128, :], writes=[wout])

            def gates(zcol):
                c.op("act", lambda e: e.activation(out=fg[:], in_=p[:, zcol:zcol + 512], func=AF.Sigmoid), reads=[p], writes=[fg])
                yield
                c.op("pool", lambda e: e.tensor_tensor(fg[:], fg[:], omlb[:], ALU.mult), reads=[fg, omlb], writes=[fg])
                yield
                c.op("pool", lambda e: e.tensor_tensor(fg[:], fg[:], lbr[:], ALU.add), reads=[fg, lbr], writes=[fg])
                yield
                c.op("act", lambda e: e.activation(out=lf[:], in_=fg[:], func=AF.Ln), reads=[fg], writes=[lf])
                c.op("pool", lambda e: e.tensor_scalar(kk[:], fg[:], -1.0, 1.0, ALU.mult, ALU.add), reads=[fg], writes=[kk])
                yield

            for si, L in enumerate(self.seqs):
                nt = L // 128
                base = self.seq_off[si]
                ptb = [TB(None, "P%d" % t) for t in range(nt)]
                oftb = [TB(None, "OF%d" % t) for t in range(nt)]
                for g in range(2):
                    c.op("dve", lambda e: e.memset(S[g][:], 0.0), writes=[S[g]])
                    c.op("pool", lambda e: e.memset(Sb[g][:], 0.0), writes=[Sb[g]])
                def front_gen(t):
                    r0 = base + t * 128
                    xt = xts.next()
                    c.dma("sp", xt[:], x_in[r0:r0 + 128, :], reads=[self.xtb(r0)], writes=[xt])
                    c.dma("sp", rope[:], d["c_rope"][t * 128:(t + 1) * 128, :].rearrange("p (a b) -> p a b", a=4), writes=[rope])
                    yield
                    self.rmsnorm_to_hT(c, xt, gmix, (hT, hT[:]), junk, ssq, hn, R["pstB"].next(), ident)
                    for _ in range(5):
                        yield
                    for n in range(9):
                        pa = R["psA"].next()
                        for k in range(8):
                            c.op("pe", lambda e: e.matmul(pa[:], hT[:, k, :], win[:, k, n * 512:(n + 1) * 512],
                                                          start=(k == 0), stop=(k == 7)), reads=[hT, win], writes=[pa])
                        c.op("act" if n % 2 == 0 else "dve",
                             lambda e: (e.copy(out=p[:, n * 512:(n + 1) * 512], in_=pa[:]) if n % 2 == 0
                                        else e.tensor_copy(out=p[:, n * 512:(n + 1) * 512], in_=pa[:])),
                             reads=[pa], writes=[p])
                        yield
                    for col0, ci_, si_ in ((0, 0, 1), (512, 2, 3)):
                        v4 = p[:, col0:col0 + 512].rearrange("p (h two x) -> p h two x", h=4, two=2)
                        x1, x2 = v4[:, :, 0, :], v4[:, :, 1, :]
                        cb = rope[:, ci_:ci_ + 1, :].to_broadcast([128, 4, 64])
                        sb_ = rope[:, si_:si_ + 1, :].to_broadcast([128, 4, 64])
                        c.op("pool", lambda e: e.tensor_tensor(rt[0][:], x1, cb, ALU.mult), reads=[p, rope], writes=[rt[0]])
                        c.op("pool", lambda e: e.tensor_tensor(rt[1][:], x2, sb_, ALU.mult), reads=[p, rope], writes=[rt[1]])
                        c.op("pool", lambda e: e.tensor_tensor(rt[2][:], x1, sb_, ALU.mult), reads=[p, rope], writes=[rt[2]])
                        c.op("pool", lambda e: e.tensor_tensor(rt[3][:], x2, cb, ALU.mult), reads=[p, rope], writes=[rt[3]])
                        yield
                        c.op("pool", lambda e: e.tensor_tensor(x1, rt[0][:], rt[1][:], ALU.subtract), reads=[rt[0], rt[1]], writes=[p])
                        c.op("pool", lambda e: e.tensor_tensor(x2, rt[2][:], rt[3][:], ALU.add), reads=[rt[2], rt[3]], writes=[p])
                        yield
                    c.dma("sp", d["P"][t * 128:(t + 1) * 128, :], p[:], reads=[p], writes=[ptb[t]])

                def scan_gens_f():
                    return [
                        self.gla_tile(c, R, Gs[0], p[:, 0:512], p, p, p[:, 512:1024], p[:, 1024:1536], p, lgam, lgam[:], 0,
                                      S[0], Sb[0], oft, 0),
                        self.gla_tile(c, R, Gs[1], p[:, 2048:2560], p, kk, kk[:], p[:, 3584:4096], p, lf, lf[:], 0,
                                      S[1], Sb[1], oft, 512, pre=lambda: gates(2560))]

                run_interleaved([front_gen(0)])
                for t in range(nt):
                    gens = scan_gens_f()
                    if t + 1 < nt:
                        gens.append(front_gen(t + 1))
                    run_interleaved(gens)
                    c.dma("sp", d["OF"][t * 128:(t + 1) * 128, :], oft[:], reads=[oft], writes=[oftb[t]])
                for g in range(2):
                    c.op("dve", lambda e: e.memset(S[g][:], 0.0), writes=[S[g]])
                    c.op("pool", lambda e: e.memset(Sb[g][:], 0.0), writes=[Sb[g]])
                for t in range(nt - 1, -1, -1):
                    r0 = base + t * 128
                    xt = xts.next()
                    c.dma("sp", xt[:], x_in[r0:r0 + 128, :], reads=[self.xtb(r0)], writes=[xt])
                    c.dma("sp", p[:], d["P"][t * 128:(t + 1) * 128, :], reads=[ptb[t]], writes=[p])
                    c.dma("sp", oft[:], d["OF"][t * 128:(t + 1) * 128, :], reads=[oftb[t]], writes=[oft])
                    run_interleaved([
                        self.gla_tile(c, R, Gs[0], p[:, 0:512], p, p, p[:, 512:1024], p[:, 1024:1536], p, lgam, lgam[:], 1,
                                      S[0], Sb[0], ot, 0, add_tb=oft),
                        self.gla_tile(c, R, Gs[1], p[:, 2048:2560], p, kk, kk[:], p[:, 3584:4096], p, lf, lf[:], 1,
                                      S[1], Sb[1], ot, 512, add_tb=oft, pre=lambda: gates(3072))])
                    o3 = ot[:].rearrange("p (h x) -> p h x", h=8)
                    c.op("dve", lambda e: e.reduce_sum(out=s1[:], in_=o3, axis=AX.X), reads=[ot], writes=[s1])
                    c.op("act", lambda e: e.activation(out=sq[:], in_=ot[:], func=AF.Square), reads=[ot], writes=[sq])
                    c.op("dve", lambda e: e.reduce_sum(out=s2[:], in_=sq[:].rearrange("p (h x) -> p h x", h=8), axis=AX.X),
                         reads=[sq], writes=[s2])
                    c.op("dve", lambda e: e.scalar_tensor_tensor(s1[:], s1[:], 1.0 / 128, hmask[:], ALU.mult, ALU.mult),
                         reads=[s1, hmask], writes=[s1])
                    c.op("dve", lambda e: e.tensor_scalar(s2[:], s2[:], 1.0 / 128, RMS_EPS, ALU.mult, ALU.add), reads=[s2], writes=[s2])
                    c.op("dve", lambda e: e.tensor_tensor(sq[:, 0:8], s1[:], s1[:], ALU.mult), reads=[s1], writes=[sq])
                    c.op("dve", lambda e: e.tensor_tensor(s2[:], s2[:], sq[:, 0:8], ALU.subtract), reads=[s2, sq], writes=[s2])
                    c.op("act", lambda e: e.activation(out=s2[:], in_=s2[:], func=AF.Sqrt), reads=[s2], writes=[s2])
                    c.op("dve", lambda e: e.reciprocal(out=s2[:], in_=s2[:]), reads=[s2], writes=[s2])
                    for h in range(8):
                        hs = slice(h * 128, (h + 1) * 128)
                        c.op("dve" if h % 2 == 0 else "pool",
                             lambda e: e.tensor_scalar(ot[:, hs], ot[:, hs], s1[:, h:h + 1], s2[:, h:h + 1], ALU.subtract, ALU.mult),
                             reads=[ot, s1, s2], writes=[ot])
                    c.op("pool", lambda e: e.tensor_tensor(ot[:], ot[:], gh[:], ALU.mult), reads=[ot, gh], writes=[ot])
                    c.op("act", lambda e: e.activation(out=sq[:, 0:512], in_=p[:, 1536:2048], func=AF.Silu), reads=[p], writes=[sq])
                    c.op("act", lambda e: e.activation(out=sq[:, 512:1024], in_=p[:, 4096:4608], func=AF.Silu), reads=[p], writes=[sq])
                    c.op("dve", lambda e: e.tensor_tensor(omix[:], ot[:], sq[:], ALU.mult), reads=[ot, sq], writes=[omix])
                    pt = R["pstB"].next()
                    for k in range(8):
                        c.op("pe", lambda e: e.transpose(out=pt[:, k * 128:(k + 1) * 128], in_=omix[:, k * 128:(k + 1) * 128],
                                                         identity=ident[:]), reads=[omix, ident], writes=[pt])
                    c.op("act", lambda e: e.copy(out=oT[:], in_=pt[:].rearrange("p (k t) -> p k t", k=8)), reads=[pt], writes=[oT])
                    for n in range(2):
                        py = R["psA"].next()
                        for k in range(8):
                            c.op("pe", lambda e: e.matmul(py[:], oT[:, k, :], wout[:, k, n * 512:(n + 1) * 512],
                                                          start=(k == 0), stop=(k == 7)), reads=[oT, wout], writes=[py])
                        c.op("dve", lambda e: e.tensor_tensor(xt[:, n * 512:(n + 1) * 512], py[:], xt[:, n * 512:(n + 1) * 512], ALU.add),
                             reads=[py, xt], writes=[xt])
                    c.dma("sp", d["X"][r0:r0 + 128, :], xt[:], reads=[xt], writes=[self.xtb(r0)])
            c.barrier()

    def gdn_tile(self, c, R, RS, h, dr, qT, kT, ktm, vtm, sc, S, Sb, o_tb, add_tb=None):
        M1, M2, M4, MT = R["gm"][dr]
        hs = slice(h * 128, (h + 1) * 128)
        col = slice(h, h + 1)
        identf = R["identf"]
        W = RS["w"]
        ula = RS["ula"]
        c.op("act", lambda e: e.mul(out=ula[:], in_=M1[:], mul=sc["la"][:, col]), reads=[M1, sc["la"]], writes=[ula])
        pg, pgt = R["psH"].next(), R["psH"].next()
        c.op("pe", lambda e: e.matmul(pg[:, 0:128], ula[:], M4[:], start=True, stop=True), reads=[ula, M4], writes=[pg])
        c.op("pe", lambda e: e.matmul(pgt[:, 0:128], M4[:], ula[:], start=True, stop=True), reads=[ula, M4], writes=[pgt])
        dm, dtm = RS["dm"], RS["dtm"]
        c.op("act", lambda e: e.activation(out=dm[:], in_=pg[:, 0:128], func=AF.Exp), reads=[pg], writes=[dm])
        c.op("act", lambda e: e.activation(out=dtm[:], in_=pgt[:, 0:128], func=AF.Exp), reads=[pgt], writes=[dtm])
        c.op("pool", lambda e: e.tensor_tensor(dm[:], dm[:], R["mstrict"][dr][:], ALU.mult), reads=[dm, R["mstrict"][dr]], writes=[dm])
        c.op("pool", lambda e: e.tensor_tensor(dtm[:], dtm[:], MT[:], ALU.mult), reads=[dtm, MT], writes=[dtm])
        yield
        pk = R["psH"].next()
        c.op("pe", lambda e: e.matmul(pk[:, 0:128], kT[:, h, :], kT[:, h, :], start=True, stop=True), reads=[kT], writes=[pk])
        X = W.next()
        c.op("dve", lambda e: e.scalar_tensor_tensor(X[:], pk[:, 0:128], sc["nbeta"][:, col], dm[:], ALU.mult, ALU.mult),
             reads=[pk, sc["nbeta"], dm], writes=[X])
        yield
        py = R["pstB"].next()
        c.op("pe", lambda e: e.transpose(out=py[:, 0:128], in_=X[:], identity=R["ident"][:]), reads=[X, R["ident"]], writes=[py])
        Y = W.next()
        RT = W.next()
        c.op("act", lambda e: e.copy(out=Y[:], in_=py[:, 0:128]), reads=[py], writes=[Y])
        c.op("dve", lambda e: e.tensor_tensor(RT[:], Y[:], R["ident"][:], ALU.add), reads=[Y, R["ident"]], writes=[RT])
        yield
        ttb = RS["ttb"]
        for m in range(5):
            px = R["psH"].next()
            c.op("pe", lambda e: e.matmul(px[:, 0:128], Y[:], X[:], start=True, stop=True), reads=[X, Y], writes=[px])
            if m < 4:
                c.op("pe", lambda e: e.matmul(px[:, 128:256], X[:], Y[:], start=True, stop=True), reads=[X, Y], writes=[px])
            X2 = W.next()
            c.op("act", lambda e: e.copy(out=X2[:], in_=px[:, 0:128]), reads=[px], writes=[X2])
            if m < 4:
                Y2 = W.next()
                c.op("act", lambda e: e.copy(out=Y2[:], in_=px[:, 128:256]), reads=[px], writes=[Y2])
            yield
            pr = R["psH"].next()
            c.op("pe", lambda e: e.matmul(pr[:, 0:128], X2[:], RT[:], start=True, stop=True), reads=[X2, RT], writes=[pr])
            if m < 4:
                RT2 = W.next()
                c.op("dve", lambda e: e.tensor_tensor(RT2[:], pr[:, 0:128], RT[:], ALU.add), reads=[pr, RT], writes=[RT2])
                RT = RT2
                X, Y = X2, Y2
            else:
                c.op("dve", lambda e: e.tensor_tensor(ttb[:], pr[:, 0:128], RT[:], ALU.add), reads=[pr, RT], writes=[ttb])
            yield
        vb_, kbg, kdec = RS["vbeta"], RS["kbg"], RS["kdec"]
        c.op("act", lambda e: e.mul(out=vb_[:], in_=vtm[:, hs], mul=sc["beta"][:, col]), reads=[vtm, sc["beta"]], writes=[vb_])
        c.op("dve", lambda e: e.tensor_scalar(kbg[:], ktm[:, hs], sc["beg"][:, col], None, ALU.mult), reads=[ktm, sc["beg"]], writes=[kbg])
        c.op("act", lambda e: e.mul(out=kdec[:], in_=ktm[:, hs], mul=sc["egd"][:, col]), reads=[ktm, sc["egd"]], writes=[kdec])
        pu_ = R["psH"].next()
        c.op("pe", lambda e: e.matmul(pu_[:, 0:128], ttb[:], vb_[:], start=True, stop=True), reads=[ttb, vb_], writes=[pu_])
        c.op("pe", lambda e: e.matmul(pu_[:, 128:256], kbg[:], ttb[:], start=True, stop=True), reads=[ttb, kbg], writes=[pu_])
        u = RS["u"]
        wT = RS["wT"]
        c.op("act", lambda e: e.copy(out=u[:], in_=pu_[:, 0:128]), reads=[pu_], writes=[u])
        c.op("act", lambda e: e.copy(out=wT[:], in_=pu_[:, 128:256]), reads=[pu_], writes=[wT])
        yield
        pq = R["psH"].next()
        c.op("pe", lambda e: e.matmul(pq[:, 0:128], kT[:, h, :], qT[:, h, :], start=True, stop=True), reads=[kT, qT], writes=[pq])
        qkm = RS["qkm"]
        c.op("dve", lambda e: e.tensor_tensor(qkm[:], pq[:, 0:128], dtm[:], ALU.mult), reads=[pq, dtm], writes=[qkm])
        yield
        vn = RS["vn"]
        tmp = RS["tmp"]
        order = (0, 1) if dr == 0 else (1, 0)
        for ci in order:
            rs = slice(ci * 64, (ci + 1) * 64)
            pa = R["psH"].next()
            c.op("pe", lambda e: e.matmul(pa[:, 0:128], wT[:], Sb[h][:], start=True, stop=True), reads=[wT, Sb[h]], writes=[pa])
            c.op("pe", lambda e: e.matmul(pa[:, 128:256], qT[:, h, :], Sb[h][:], start=True, stop=True), reads=[qT, Sb[h]], writes=[pa])
            c.op("dve", lambda e: e.tensor_tensor(vn[rs, :], u[rs, :], pa[rs, 0:128], ALU.subtract), reads=[u, pa], writes=[vn])
            c.op("dve", lambda e: e.tensor_scalar(tmp[rs, :], pa[rs, 128:256], sc["eg"][rs, col], None, ALU.mult),
                 reads=[pa, sc["eg"]], writes=[tmp])
            yield
            ps_ = R["psH"].next()
            c.op("pe", lambda e: e.matmul(ps_[:, 0:128], kdec[rs, :], vn[rs, :], start=True, stop=True), reads=[kdec, vn], writes=[ps_])
            c.op("dve", lambda e: e.scalar_tensor_tensor(S[h][:], S[h][:], sc["egl"][ci][:, col], ps_[:, 0:128], ALU.mult, ALU.add),
                 reads=[S[h], sc["egl"][ci], ps_], writes=[S[h]])
            c.op("act", lambda e: e.copy(out=Sb[h][:], in_=S[h][:]), reads=[S[h]], writes=[Sb[h]])
            yield
        p2 = R["psH"].next()
        c.op("pe", lambda e: e.matmul(p2[:, 0:128], qkm[:], vn[:], start=True, stop=True), reads=[qkm, vn], writes=[p2])
        if add_tb is None:
            c.op("dve", lambda e: e.tensor_tensor(o_tb[:, hs], p2[:, 0:128], tmp[:], ALU.add), reads=[p2, tmp], writes=[o_tb])
        else:
            c.op("pool", lambda e: e.tensor_tensor(tmp[:], tmp[:], add_tb[:, hs], ALU.add), reads=[tmp, add_tb], writes=[tmp])
            c.op("dve", lambda e: e.tensor_tensor(o_tb[:, hs], p2[:, 0:128], tmp[:], ALU.add), reads=[p2, tmp], writes=[o_tb])

    def gdn_pair(self, c, R, PS, h0, dr, qT, kT, ktm, vtm, sc, S, Sb, o_tb, add_tb=None):
        M1, M2, M4, MT = R["gm"][dr]
        heads = (h0, h0 + 1)
        ident = R["ident"]
        dd = PS["dd"]
        pgb = R["psH"].next()
        for i, h in enumerate(heads):
            ula = PS["ula"][i]
            c.op("act", lambda e: e.mul(out=ula[:], in_=M1[:], mul=sc["la"][:, h:h + 1]), reads=[M1, sc["la"]], writes=[ula])
            c.op("pe", lambda e: e.matmul(pgb[:, (2 * i) * 128:(2 * i + 1) * 128], ula[:], M4[:], start=True, stop=True),
                 reads=[ula, M4], writes=[pgb])
            c.op("pe", lambda e: e.matmul(pgb[:, (2 * i + 1) * 128:(2 * i + 2) * 128], M4[:], ula[:], start=True, stop=True),
                 reads=[ula, M4], writes=[pgb])
        c.op("act", lambda e: e.activation(out=dd[:], in_=pgb[:], func=AF.Exp), reads=[pgb], writes=[dd])
        yield
        c.op("pool", lambda e: e.tensor_tensor(dd[:], dd[:], R["mask4"][dr][:], ALU.mult), reads=[dd, R["mask4"][dr]], writes=[dd])
        yield
        xy = PS["xy"]
        cur = 0
        XY = xy[cur]
        for i, h in enumerate(heads):
            pk = R["psH"].next()
            c.op("pe", lambda e: e.matmul(pk[:, 0:128], kT[:, h, :], kT[:, h, :], start=True, stop=True), reads=[kT], writes=[pk])
            c.op("dve", lambda e: e.scalar_tensor_tensor(XY[:, (2 * i) * 128:(2 * i + 1) * 128], pk[:, 0:128], sc["nbeta"][:, h:h + 1],
                                                         dd[:, (2 * i) * 128:(2 * i + 1) * 128], ALU.mult, ALU.mult),
                 reads=[pk, sc["nbeta"], dd], writes=[XY])
        yield
        py = R["pstB"].next()
        for i in range(2):
            c.op("pe", lambda e: e.transpose(out=py[:, i * 128:(i + 1) * 128], in_=XY[:, (2 * i) * 128:(2 * i + 1) * 128],
                                             identity=ident[:]), reads=[XY, ident], writes=[py])
        XYv = XY[:].rearrange("p (i t x) -> p i t x", i=2, t=2)
        c.op("act", lambda e: e.copy(out=XYv[:, :, 1, :], in_=py[:, 0:256].rearrange("p (i x) -> p i x", i=2)),
             reads=[py], writes=[XY])
        yield
        rt = PS["rt"]
        RT = rt[0]
        for i in range(2):
            c.op("dve", lambda e: e.tensor_tensor(RT[:, i * 128:(i + 1) * 128], XY[:, (2 * i + 1) * 128:(2 * i + 2) * 128],
                                                  ident[:], ALU.add), reads=[XY, ident], writes=[RT])
        yield
        rcur = 0
        for m in range(5):
            px = R["psH"].next()
            for i in range(2):
                Xi = XY[:, (2 * i) * 128:(2 * i + 1) * 128]
                Yi = XY[:, (2 * i + 1) * 128:(2 * i + 2) * 128]
                c.op("pe", lambda e: e.matmul(px[:, (2 * i) * 128:(2 * i + 1) * 128], Yi, Xi, start=True, stop=True),
                     reads=[XY], writes=[px])
                if m < 4:
                    c.op("pe", lambda e: e.matmul(px[:, (2 * i + 1) * 128:(2 * i + 2) * 128], Xi, Yi, start=True, stop=True),
                         reads=[XY], writes=[px])
            XY2 = xy[1 - cur]
            if m < 4:
                c.op("act", lambda e: e.copy(out=XY2[:], in_=px[:]), reads=[px], writes=[XY2])
            else:
                c.op("act", lambda e: e.copy(out=XY2[:].rearrange("p (i t x) -> p i t x", i=2, t=2)[:, :, 0, :],
                                             in_=px[:].rearrange("p (i t x) -> p i t x", i=2, t=2)[:, :, 0, :]),
                     reads=[px], writes=[XY2])
            yield
            pr = R["psH"].next()
            for i in range(2):
                c.op("pe", lambda e: e.matmul(pr[:, i * 128:(i + 1) * 128], XY2[:, (2 * i) * 128:(2 * i + 1) * 128],
                                              RT[:, i * 128:(i + 1) * 128], start=True, stop=True), reads=[XY2, RT], writes=[pr])
            RT2 = rt[1 - rcur]
            c.op("dve", lambda e: e.tensor_tensor(RT2[:], pr[:, 0:256], RT[:], ALU.add), reads=[pr, RT], writes=[RT2])
            RT = RT2
            rcur = 1 - rcur
            XY = XY2
            cur = 1 - cur
            yield
        H = []
        for i, h in enumerate(heads):
            hs = slice(h * 128, (h + 1) * 128)
            col = slice(h, h + 1)
            B = PS["hd"][i]
            vb_, kbg, kdec = B["vbeta"], B["kbg"], B["kdec"]
            c.op("act", lambda e: e.mul(out=vb_[:], in_=vtm[:, hs], mul=sc["beta"][:, col]), reads=[vtm, sc["beta"]], writes=[vb_])
            c.op("dve", lambda e: e.tensor_scalar(kbg[:], ktm[:, hs], sc["beg"][:, col], None, ALU.mult), reads=[ktm, sc["beg"]], writes=[kbg])
            c.op("act", lambda e: e.mul(out=kdec[:], in_=ktm[:, hs], mul=sc["egd"][:, col]), reads=[ktm, sc["egd"]], writes=[kdec])
            ttb = RT[:, i * 128:(i + 1) * 128]
            pu_ = R["psH"].next()
            c.op("pe", lambda e: e.matmul(pu_[:, 0:128], ttb, vb_[:], start=True, stop=True), reads=[RT, vb_], writes=[pu_])
            c.op("pe", lambda e: e.matmul(pu_[:, 128:256], kbg[:], ttb, start=True, stop=True), reads=[RT, kbg], writes=[pu_])
            u, wT = B["u"], B["wT"]
            c.op("act", lambda e: e.copy(out=u[:], in_=pu_[:, 0:128]), reads=[pu_], writes=[u])
            c.op("act", lambda e: e.copy(out=wT[:], in_=pu_[:, 128:256]), reads=[pu_], writes=[wT])
            H.append((h, hs, col, B))
        yield
        for i, (h, hs, col, B) in enumerate(H):
            pq = R["psH"].next()
            c.op("pe", lambda e: e.matmul(pq[:, 0:128], kT[:, h, :], qT[:, h, :], start=True, stop=True), reads=[kT, qT], writes=[pq])
            c.op("dve", lambda e: e.tensor_tensor(B["qkm"][:], pq[:, 0:128], dd[:, (2 * i + 1) * 128:(2 * i + 2) * 128], ALU.mult),
                 reads=[pq, dd], writes=[B["qkm"]])
        yield
        order = (0, 1) if dr == 0 else (1, 0)
        for ci in order:
            rs = slice(ci * 64, (ci + 1) * 64)
            for i, (h, hs, col, B) in enumerate(H):
                pa = R["psH"].next()
                c.op("pe", lambda e: e.matmul(pa[:, 0:128], B["wT"][:], Sb[h][:], start=True, stop=True), reads=[B["wT"], Sb[h]], writes=[pa])
                c.op("pe", lambda e: e.matmul(pa[:, 128:256], qT[:, h, :], Sb[h][:], start=True, stop=True), reads=[qT, Sb[h]], writes=[pa])
                c.op("dve", lambda e: e.tensor_tensor(B["vn"][rs, :], B["u"][rs, :], pa[rs, 0:128], ALU.subtract),
                     reads=[B["u"], pa], writes=[B["vn"]])
                c.op("dve", lambda e: e.tensor_scalar(B["tmp"][rs, :], pa[rs, 128:256], sc["eg"][rs, col], None, ALU.mult),
                     reads=[pa, sc["eg"]], writes=[B["tmp"]])
            yield
            for i, (h, hs, col, B) in enumerate(H):
                ps_ = R["psH"].next()
                c.op("pe", lambda e: e.matmul(ps_[:, 0:128], B["kdec"][rs, :], B["vn"][rs, :], start=True, stop=True),
                     reads=[B["kdec"], B["vn"]], writes=[ps_])
                c.op("dve", lambda e: e.scalar_tensor_tensor(S[h][:], S[h][:], sc["egl"][ci][:, col], ps_[:, 0:128], ALU.mult, ALU.add),
                     reads=[S[h], sc["egl"][ci], ps_], writes=[S[h]])
            yield
            for i, (h, hs, col, B) in enumerate(H):
                c.op("act", lambda e: e.copy(out=Sb[h][:], in_=S[h][:]), reads=[S[h]], writes=[Sb[h]])
            yield
        for i, (h, hs, col, B) in enumerate(H):
            p2 = R["psH"].next()
            c.op("pe", lambda e: e.matmul(p2[:, 0:128], B["qkm"][:], B["vn"][:], start=True, stop=True), reads=[B["qkm"], B["vn"]], writes=[p2])
            if add_tb is None:
                c.op("dve", lambda e: e.tensor_tensor(o_tb[:, hs], p2[:, 0:128], B["tmp"][:], ALU.add), reads=[p2, B["tmp"]], writes=[o_tb])
            else:
                c.op("pool", lambda e: e.tensor_tensor(B["tmp"][:], B["tmp"][:], add_tb[:, hs], ALU.add), reads=[B["tmp"], add_tb], writes=[B["tmp"]])
                c.op("dve", lambda e: e.tensor_tensor(o_tb[:, hs], p2[:, 0:128], B["tmp"][:], ALU.add), reads=[p2, B["tmp"]], writes=[o_tb])

    def gdn_scalars(self, c, R, dr, ab, sc):
        M1, M2, M4, MT = R["gm"][dr]
        la = ab[:, dr * 8:(dr + 1) * 8]
        be = ab[:, 16 + dr * 8:16 + (dr + 1) * 8]
        ps = R["psA"].next()
        c.op("pe", lambda e: e.matmul(ps[:, 0:8], M1[:], la, start=True, stop=True), reads=[M1, ab], writes=[ps])
        c.op("pe", lambda e: e.matmul(ps[:, 8:16], M4[:], la, start=True, stop=True), reads=[M4, ab], writes=[ps])
        c.op("pe", lambda e: e.matmul(ps[:, 16:24], R["sel"][0][:], la, start=True, stop=True), reads=[R["sel"][0], ab], writes=[ps])
        c.op("pe", lambda e: e.matmul(ps[:, 24:32], R["sel"][1][:], la, start=True, stop=True), reads=[R["sel"][1], ab], writes=[ps])
        ex = sc["ex"]
        c.op("act", lambda e: e.activation(out=ex[:], in_=ps[:, 0:32], func=AF.Exp), reads=[ps], writes=[ex])
        c.op("dve", lambda e: e.tensor_copy(out=sc["la"][:], in_=la), reads=[ab], writes=[sc["la"]])
        c.op("dve", lambda e: e.tensor_copy(out=sc["beta"][:], in_=be), reads=[ab], writes=[sc["beta"]])
        c.op("dve", lambda e: e.tensor_scalar(sc["nbeta"][:], be, -1.0, None, ALU.mult), reads=[ab], writes=[sc["nbeta"]])
        c.op("dve", lambda e: e.tensor_tensor(sc["beg"][:], be, ex[:, 0:8], ALU.mult), reads=[ab, ex], writes=[sc["beg"]])
        sc["eg"] = _View(ex, (slice(None), slice(0, 8)))
        sc["egd"] = _View(ex, (slice(None), slice(8, 16)))
        sc["egl"] = [_View(ex, (slice(None), slice(16, 24))), _View(ex, (slice(None), slice(24, 32)))]

    def phase_odd(self, c, j, layer, x_in):
        nc, d = c.nc, self.d
        Lm = self.Lmax
        with contextlib.ExitStack() as es:
            win = c.sb("win", [128, 8, ODD_IN], BF16, es)
            wout = c.sb("wout", [128, 8, D], BF16, es)
            gmix = c.sb("gmix", [128, D], F32, es)
            gh = c.sb("gh", [128, 128], F32, es)
            ident = c.sb("identb", [128, 128], BF16, es)
            identf = c.sb("identf", [128, 128], F32, es)
            ones = c.sb("onesb", [128, 128], BF16, es)
            gmt = [[c.sb("gm%d%d" % (a, b_), [128, 128], F32, es) for b_ in range(4)] for a in range(2)]
            mstrict = [c.sb("mstr%d" % a, [128, 128], F32, es) for a in range(2)]
            sel = [c.sb("sel%d" % a, [128, 128], F32, es) for a in range(2)]
            cw = c.sb("cw", [128, 24, 5], F32, es)
            dtb = c.sb("dtb", [128, 16], F32, es)
            nea = c.sb("nea", [128, 16], F32, es)
            zer = c.sb("zer", [128, 16], BF16, es)
            ab_all = c.sb("ab_all", [128, Lm // 128, 32], F32, es)
            xts = Rot([c.sb("xt%d" % i, [128, D], F32, es) for i in range(1)])
            ssq = c.sb("ssq", [128, 1], F32, es)
            s2 = c.sb("s2", [128, 8], F32, es)
            hn = c.sb("hn", [128, D], BF16, es)
            hT = c.sb("hT", [128, 8, 128], BF16, es)
            hTb = c.sb("hTb", [128, 8, 260], BF16, es)
            gt = c.sb("gt", [128, D], F32, es)
            NCS = 3
            CSs = [{"raw": c.sb("raw%d" % i, [128, 260], F32, es), "acc": c.sb("acc%d" % i, [128, 256], F32, es),
                    "sqb": c.sb("sqb%d" % i, [128, 256], BF16, es)} for i in range(NCS)]
            for cs_d in CSs:
                cs_d["rst"] = _View(cs_d["raw"], (slice(None), slice(0, 256)))
            qTb = c.sb("qTb", [128, 8, 256], BF16, es)
            kTb = c.sb("kTb", [128, 8, 256], BF16, es)
            vTb = c.sb("vTb", [128, 8, 256], BF16, es)
            qTt = c.sb("qTt", [128, 8, 128], BF16, es)
            kTt = c.sb("kTt", [128, 8, 128], BF16, es)
            vtm = c.sb("vtm", [128, D], BF16, es)
            ktm = c.sb("ktm", [128, D], BF16, es)
            oft = c.sb("oft", [128, D], F32, es)
            sq = oft
            ot = c.sb("ot", [128, D], F32, es)
            omix = c.sb("omix", [128, D], BF16, es)
            junk = omix
            oT = hT
            abt = c.sb("abt", [128, 32], F32, es)
            sc = {k: c.sb("sc_" + k, [128, 8], F32, es) for k in ("la", "beta", "nbeta", "beg")}
            sc["ex"] = c.sb("sc_ex", [128, 32], F32, es)
            R = {
                "gm": gmt, "ident": ident, "identf": identf, "mstrict": mstrict, "sel": sel,
                "psH": Rot([c.ps("psH%d" % i, [128, 512], F32, es) for i in range(6)]),
                "pstB": Rot([c.ps("pstB%d" % i, [128, D], BF16, es) for i in range(2)]),
            }
            R["psA"] = R["psH"]
            NP = 4
            PSs = []
            for sl in range(NP):
                PS = {"ula": [c.sb("ula%d_%d" % (sl, i), [128, 128], F32, es) for i in range(2)],
                      "dd": c.sb("dd%d" % sl, [128, 512], F32, es),
                      "xy": [c.sb("xy%d_%d" % (sl, i), [128, 512], BF16, es) for i in range(2)],
                      "rt": [c.sb("rt%d_%d" % (sl, i), [128, 256], BF16, es) for i in range(2)],
                      "hd": []}
                for i in range(2):
                    B = {}
                    for nm in ("u", "tmp"):
                        B[nm] = c.sb("%s%d_%d" % (nm, sl, i), [128, 128], F32, es)
                    for nm in ("vbeta", "kbg", "kdec", "wT", "qkm", "vn"):
                        B[nm] = c.sb("%s%d_%d" % (nm, sl, i), [128, 128], BF16, es)
                    PS["hd"].append(B)
                PSs.append(PS)
            mask4 = [c.sb("mask4_%d" % a, [128, 512], F32, es) for a in range(2)]
            R["mask4"] = mask4
            for a in range(2):
                for i in range(2):
                    c.dma("sp", mask4[a][:, (2 * i) * 128:(2 * i + 1) * 128], d["c_mstrict"][a], writes=[mask4[a]])
                    c.dma("sp", mask4[a][:, (2 * i + 1) * 128:(2 * i + 2) * 128], d["c_gla"][a * 4 + 3], writes=[mask4[a]])
            S = [c.sb("S%d" % h, [128, 128], F32, es) for h in range(8)]
            Sb = [c.sb("Sb%d" % h, [128, 128], BF16, es) for h in range(8)]

            c.dma("sp", ident[:], d["c_ident_bf"], writes=[ident])
            c.dma("sp", identf[:], d["c_ident_f"], writes=[identf])
            c.dma("sp", ones[:], d["c_ones_bf"], writes=[ones])
            for a in range(2):
                for b_ in range(4):
                    c.dma("sp", gmt[a][b_][:], d["c_gla"][a * 4 + b_], writes=[gmt[a][b_]])
                c.dma("sp", mstrict[a][:], d["c_mstrict"][a], writes=[mstrict[a]])
                c.dma("sp", sel[a][:], d["c_sel"][a], writes=[sel[a]])
            c.dma("sp", gmix[:], d["norm_mix"][layer:layer + 1, :].partition_broadcast(128), writes=[gmix])
            c.dma("sp", gh[:], d["gdn_norm"][j:j + 1, :].partition_broadcast(128), writes=[gh])
            c.dma("sp", dtb[:], d["gdn_dt_bias"][j:j + 1, :].partition_broadcast(128), writes=[dtb])
            c.dma("sp", nea[:], d["gdn_a_log"][j:j + 1, :].partition_broadcast(128), writes=[nea])
            c.op("act", lambda e: e.activation(out=nea[:], in_=nea[:], func=AF.Exp), reads=[nea], writes=[nea])
            c.op("dve", lambda e: e.tensor_scalar(nea[:], nea[:], -1.0, None, ALU.mult), reads=[nea], writes=[nea])
            c.op("dve", lambda e: e.memset(zer[:], 0.0), writes=[zer])
            for w in range(5):
                c.dma("sp", cw[:, :, w], d["gdn_conv"][j, w, :].rearrange("(f p) -> p f", p=128), writes=[cw], slow=True)
            for k in range(8):
                c.dma("pool", win[:, k, :], d["gdn_w_in"][j, k * 128:(k + 1) * 128, :], writes=[win])
                c.dma("pool", wout[:, k, :], d["gdn_w_out"][j, k * 128:(k + 1) * 128, :], writes=[wout])

            HT = d["HT"].rearrange("p (k t) -> p k t", k=8)
            QT = d["QT"].rearrange("p (k t) -> p k t", k=8)
            KT = d["KT"].rearrange("p (k t) -> p k t", k=8)

            for si, L in enumerate(self.seqs):
                nt = L // 128
                base = self.seq_off[si]
                httb = TB(None, "HT")
                gtb = [TB(None, "G%d" % t) for t in range(nt)]
                qktb = [TB(None, "QK%d" % t) for t in range(nt // 2)]
                vmtb = [TB(None, "VM%d" % t) for t in range(nt)]
                oftb = [TB(None, "OF%d" % t) for t in range(nt)]
                c.dma("sp", HT[:, :, 0:2], zer[:, 0:16].rearrange("p (k t) -> p k t", k=8), reads=[zer], writes=[httb])
                c.dma("sp", HT[:, :, L + 2:L + 4], zer[:, 0:16].rearrange("p (k t) -> p k t", k=8), reads=[zer], writes=[httb])
                dbg = getattr(self, "dbg", 99)
                for t in range(nt if dbg >= 1 else 0):
                    r0 = base + t * 128
                    xt = xts.next()
                    c.dma("sp", xt[:], x_in[r0:r0 + 128, :], reads=[self.xtb(r0)], writes=[xt])
                    self.rmsnorm_to_hT(c, xt, gmix, (hT, hT[:]), junk, ssq, hn, R["pstB"].next(), ident)
                    c.dma("sp", HT[:, :, 2 + t * 128:2 + (t + 1) * 128], hT[:], reads=[hT], writes=[httb])
                    for n in range(2):
                        pa = R["psA"].next()
                        for k in range(8):
                            c.op("pe", lambda e: e.matmul(pa[:], hT[:, k, :], win[:, k, 3072 + n * 512:3072 + (n + 1) * 512],
                                                          start=(k == 0), stop=(k == 7)), reads=[hT, win], writes=[pa])
                        c.op("act", lambda e: e.copy(out=gt[:, n * 512:(n + 1) * 512], in_=pa[:]), reads=[pa], writes=[gt])
                    c.dma("sp", d["P"][t * 128:(t + 1) * 128, 0:D], gt[:], reads=[gt], writes=[gtb[t]])
                    pa = R["psA"].next()
                    for k in range(8):
                        c.op("pe", lambda e: e.matmul(pa[:, 0:32], hT[:, k, :], win[:, k, 4096:4128],
                                                      start=(k == 0), stop=(k == 7)), reads=[hT, win], writes=[pa])
                    c.op("dve", lambda e: e.tensor_tensor(abt[:, 0:16], pa[:, 0:16], dtb[:], ALU.add), reads=[pa, dtb], writes=[abt])
                    c.op("act", lambda e: e.activation(out=abt[:, 0:16], in_=abt[:, 0:16], func=AF.Exp), reads=[abt], writes=[abt])
                    c.op("act", lambda e: e.activation(out=abt[:, 0:16], in_=abt[:, 0:16], func=AF.Ln, bias=1.0), reads=[abt], writes=[abt])
                    c.op("dve", lambda e: e.tensor_tensor(ab_all[:, t, 0:16], abt[:, 0:16], nea[:], ALU.mult), reads=[abt, nea], writes=[ab_all])
                    c.op("dve", lambda e: e.tensor_copy(out=abt[:, 16:32], in_=pa[:, 16:32]), reads=[pa], writes=[abt])
                    c.op("act", lambda e: e.activation(out=ab_all[:, t, 16:32], in_=abt[:, 16:32], func=AF.Sigmoid), reads=[abt], writes=[ab_all])
                for h in range(8):
                    c.op("dve", lambda e: e.memset(S[h][:], 0.0), writes=[S[h]])
                    c.op("pool", lambda e: e.memset(Sb[h][:], 0.0), writes=[Sb[h]])
                for blk in range(nt // 2 if dbg >= 2 else 0):
                    b0 = blk * 256
                    c.dma("sp", hTb[:], HT[:, :, b0:b0 + 260], reads=[httb], writes=[hTb])
                    def fb_gen(fb, CS):
                        rw, ac, sqb_, rst_ = CS["raw"], CS["acc"], CS["sqb"], CS["rst"]
                        pa = R["psA"].next()
                        for k in range(8):
                            c.op("pe", lambda e: e.matmul(pa[:, 0:260], win[:, k, fb * 128:(fb + 1) * 128], hTb[:, k, :],
                                                          start=(k == 0), stop=(k == 7)), reads=[hTb, win], writes=[pa])
                        c.op("act", lambda e: e.copy(out=rw[:], in_=pa[:, 0:260]), reads=[pa], writes=[rw])
                        yield
                        c.op("dve", lambda e: e.tensor_scalar(ac[:], rw[:, 0:256], cw[:, fb, 0:1], None, ALU.mult), reads=[rw, cw], writes=[ac])
                        yield
                        for w in range(1, 5):
                            c.op("dve", lambda e: e.scalar_tensor_tensor(ac[:], rw[:, w:w + 256], cw[:, fb, w:w + 1], ac[:], ALU.mult, ALU.add),
                                 reads=[rw, cw, ac], writes=[ac])
                            yield
                        hh = fb % 8
                        if fb >= 16:
                            c.op("act", lambda e: e.activation(out=vTb[:, hh, :], in_=ac[:], func=AF.Silu), reads=[ac], writes=[vTb])
                            return
                        c.op("act", lambda e: e.activation(out=ac[:], in_=ac[:], func=AF.Silu), reads=[ac], writes=[ac])
                        yield
                        c.op("act", lambda e: e.activation(out=sqb_[:], in_=ac[:], func=AF.Square), reads=[ac], writes=[sqb_])
                        yield
                        pn = R["psA"].next()
                        c.op("pe", lambda e: e.matmul(pn[:, 0:256], ones[:], sqb_[:], start=True, stop=True), reads=[ones, sqb_], writes=[pn])
                        c.op("act", lambda e: e.activation(out=rst_[:], in_=pn[:, 0:256], func=AF.Ln, bias=RMS_EPS), reads=[pn], writes=[rst_])
                        yield
                        c.op("act", lambda e: e.activation(out=rst_[:], in_=rst_[:], func=AF.Exp, scale=-0.5), reads=[rst_], writes=[rst_])
                        yield
                        dst = qTb if fb < 8 else kTb
                        if fb < 8:
                            c.op("dve", lambda e: e.scalar_tensor_tensor(dst[:, hh, :], ac[:], 128 ** -0.5, rst_[:], ALU.mult, ALU.mult),
                                 reads=[ac, rst_], writes=[dst])
                        else:
                            c.op("pool", lambda e: e.tensor_tensor(dst[:, hh, :], ac[:], rst_[:], ALU.mult), reads=[ac, rst_], writes=[dst])

                    for fb0 in range(0, 24, NCS):
                        run_interleaved([fb_gen(fb0 + i, CSs[i]) for i in range(NCS)])
                    c.dma("sp", QT[:, :, b0:b0 + 256], qTb[:], reads=[qTb], writes=[qktb[blk]])
                    c.dma("sp", KT[:, :, b0:b0 + 256], kTb[:], reads=[kTb], writes=[qktb[blk]])
                    for tt in range(2):
                        t = blk * 2 + tt
                        cs_ = slice(tt * 128, (tt + 1) * 128)
                        for src, dstm in ((vTb, vtm), (kTb, ktm)):
                            pt = R["pstB"].next()
                            for h in range(8):
                                c.op("pe", lambda e: e.transpose(out=pt[:, h * 128:(h + 1) * 128], in_=src[:, h, cs_], identity=ident[:]),
                                     reads=[src, ident], writes=[pt])
                            c.op("act", lambda e: e.copy(out=dstm[:], in_=pt[:]), reads=[pt], writes=[dstm])
                        c.dma("sp", d["VM"][t * 128:(t + 1) * 128, :], vtm[:], reads=[vtm], writes=[vmtb[t]])
                        c.op("pool", lambda e: e.tensor_copy(out=qTt[:], in_=qTb[:, :, cs_]), reads=[qTb], writes=[qTt])
                        c.op("pool", lambda e: e.tensor_copy(out=kTt[:], in_=kTb[:, :, cs_]), reads=[kTb], writes=[kTt])
                        abv = _View(ab_all, (slice(None), t, slice(None)))
                        self.gdn_scalars(c, R, 0, abv, sc)
                        run_interleaved([self.gdn_pair(c, R, PSs[i], 2 * i, 0, qTt, kTt, ktm, vtm, sc, S, Sb, oft)
                                         for i in range(NP)])
                        c.dma("sp", d["OF"][t * 128:(t + 1) * 128, :], oft[:], reads=[oft], writes=[oftb[t]])
                for h in range(8):
                    c.op("dve", lambda e: e.memset(S[h][:], 0.0), writes=[S[h]])
                    c.op("pool", lambda e: e.memset(Sb[h][:], 0.0), writes=[Sb[h]])
                for t in (range(nt - 1, -1, -1) if dbg >= 4 else []):
                    r0 = base + t * 128
                    xt = xts.next()
                    c.dma("sp", xt[:], x_in[r0:r0 + 128, :], reads=[self.xtb(r0)], writes=[xt])
                    c.dma("sp", qTt[:], QT[:, :, t * 128:(t + 1) * 128], reads=[qktb[t // 2]], writes=[qTt])
                    c.dma("sp", kTt[:], KT[:, :, t * 128:(t + 1) * 128], reads=[qktb[t // 2]], writes=[kTt])
                    c.dma("sp", vtm[:], d["VM"][t * 128:(t + 1) * 128, :], reads=[vmtb[t]], writes=[vtm])
                    c.dma("sp", oft[:], d["OF"][t * 128:(t + 1) * 128, :], reads=[oftb[t]], writes=[oft])
                    c.dma("sp", gt[:], d["P"][t * 128:(t + 1) * 128, 0:D], reads=[gtb[t]], writes=[gt])
                    pt = R["pstB"].next()
                    for h in range(8):
                        c.op("pe", lambda e: e.transpose(out=pt[:, h * 128:(h + 1) * 128], in_=kTt[:, h, :], identity=ident[:]),
                             reads=[kTt, ident], writes=[pt])
                    c.op("act", lambda e: e.copy(out=ktm[:], in_=pt[:]), reads=[pt], writes=[ktm])
                    abv = _View(ab_all, (slice(None), t, slice(None)))
                    self.gdn_scalars(c, R, 1, abv, sc)
                    run_interleaved([self.gdn_pair(c, R, PSs[i], 2 * i, 1, qTt, kTt, ktm, vtm, sc, S, Sb, ot, add_tb=oft)
                                     for i in range(NP)])
                    c.op("act", lambda e: e.activation(out=sq[:], in_=ot[:], func=AF.Square), reads=[ot], writes=[sq])
                    c.op("dve", lambda e: e.reduce_sum(out=s2[:], in_=sq[:].rearrange("p (h x) -> p h x", h=8), axis=AX.X),
                         reads=[sq], writes=[s2])
                    c.op("dve", lambda e: e.tensor_scalar(s2[:], s2[:], 1.0 / 128, RMS_EPS, ALU.mult, ALU.add), reads=[s2], writes=[s2])
                    c.op("act", lambda e: e.activation(out=s2[:], in_=s2[:], func=AF.Sqrt), reads=[s2], writes=[s2])
                    c.op("dve", lambda e: e.reciprocal(out=s2[:], in_=s2[:]), reads=[s2], writes=[s2])
                    for h in range(8):
                        hs = slice(h * 128, (h + 1) * 128)
                        c.op("dve", lambda e: e.scalar_tensor_tensor(ot[:, hs], ot[:, hs], s2[:, h:h + 1], gh[:], ALU.mult, ALU.mult),
                             reads=[ot, s2, gh], writes=[ot])
                    c.op("act", lambda e: e.activation(out=gt[:], in_=gt[:], func=AF.Silu), reads=[gt], writes=[gt])
                    c.op("dve", lambda e: e.tensor_tensor(omix[:], ot[:], gt[:], ALU.mult), reads=[ot, gt], writes=[omix])
                    pt = R["pstB"].next()
                    for k in range(8):
                        c.op("pe", lambda e: e.transpose(out=pt[:, k * 128:(k + 1) * 128], in_=omix[:, k * 128:(k + 1) * 128],
                                                         identity=ident[:]), reads=[omix, ident], writes=[pt])
                    c.op("act", lambda e: e.copy(out=oT[:], in_=pt[:].rearrange("p (k t) -> p k t", k=8)), reads=[pt], writes=[oT])
                    for n in range(2):
                        py = R["psA"].next()
                        for k in range(8):
                            c.op("pe", lambda e: e.matmul(py[:], oT[:, k, :], wout[:, k, n * 512:(n + 1) * 512],
                                                          start=(k == 0), stop=(k == 7)), reads=[oT, wout], writes=[py])
                        c.op("dve", lambda e: e.tensor_tensor(xt[:, n * 512:(n + 1) * 512], py[:], xt[:, n * 512:(n + 1) * 512], ALU.add),
                             reads=[py, xt], writes=[xt])
                    c.dma("sp", d["X"][r0:r0 + 128, :], xt[:], reads=[xt], writes=[self.xtb(r0)])
            c.barrier()

    def phase_copy(self, c, x_in, to_y=False):
        d = self.d
        with contextlib.ExitStack() as es:
            xts = Rot([c.sb("cx%d" % i, [128, D], F32, es) for i in range(4)])
            for r0 in range(0, self.T, 128):
                xt = xts.next()
                c.dma("sp", xt[:], x_in[r0:r0 + 128, :], reads=[self.xtb(r0)], writes=[xt])
                if to_y:
                    c.dma("sp", d["y"][r0:r0 + 128, :], xt[:], reads=[xt], writes=[self.ytb(r0)])
                else:
                    c.dma("sp", d["X"][r0:r0 + 128, :], xt[:], reads=[xt], writes=[self.xtb(r0)])
            c.barrier()

    def build(self):
        nc = bass.Bass("TRN2", target_bir_lowering=False)
        self.declare(nc)
        with contextlib.ExitStack() as es:
            c = Ctx(nc, es)
            self.c = c
            self._xtb = [TB(None, "X%d" % i) for i in range(self.T // 128)]
            self._ytb = [TB(None, "Y%d" % i) for i in range(self.T // 128)]
            first = True
            for layer in range(self.depth):
                last = layer == self.depth - 1
                if self.do_mixer:
                    from_x = self.d["x"] if first else self.d["X"]
                    kind, jj = self.kinds[layer]
                    if kind == "even":
                        self.phase_even(c, jj, layer, from_x)
                    else:
                        self.phase_odd(c, jj, layer, from_x)
                    first = False
                if self.do_xattn:
                    self.phase_xattn(c, layer, self.d["x"] if first else self.d["X"])
                    first = False
                if self.do_ffn:
                    self.phase_ffn(c, layer, self.d["x"] if first else self.d["X"], last)
                    first = False
            c.barrier()
            self.stats = (c.n_inst, c.n_wait)
        return nc


def gla_masks(direction):
    t = np.arange(128)
    same = (t[:, None] // 64) == (t[None, :] // 64)
    if direction == 0:
        le = t[:, None] <= t[None, :]
        mid = 64 * (t // 64) + 31
        lemid = t[:, None] <= mid[None, :]
        gt = t[:, None] > t[None, :]
    else:
        le = t[:, None] >= t[None, :]
        mid = 64 * (t // 64) + 32
        lemid = t[:, None] >= mid[None, :]
        gt = t[:, None] < t[None, :]
    M1 = (same & le).astype(np.float32)
    M2 = (same * (le.astype(np.float32) - lemid.astype(np.float32))).astype(np.float32)
    M4 = (same & gt).astype(np.float32)
    MT = (same & le).astype(np.float32)
    return M1, M2, M4, MT


def host_constants(Lmax):
    cst = {}
    cst["c_ident_bf"] = np.eye(128, dtype=np.float32).astype(ml_dtypes.bfloat16)
    cst["c_ones_bf"] = np.ones((128, 128), np.float32).astype(ml_dtypes.bfloat16)
    cst["c_ident_f"] = np.eye(128, dtype=np.float32)
    gm = np.zeros((2, 4, 128, 128), np.float32)
    for dr in range(2):
        for i, m in enumerate(gla_masks(dr)):
            gm[dr, i] = m
    cst["c_gla"] = gm.reshape(8, 128, 128)
    ind = np.zeros((128, 2), np.float32)
    ind[:64, 0] = 1.0
    ind[64:, 1] = 1.0
    cst["c_ind"] = ind
    inv = (1.0 / (np.float32(10000.0) ** (np.arange(64, dtype=np.float32) / np.float32(64)))).astype(np.float32)
    ang = (np.arange(Lmax, dtype=np.float32)[:, None] * inv[None, :]).astype(np.float32)
    cs = np.zeros((Lmax, 4, 64), np.float32)
    cs[:, 0] = np.cos(ang)
    cs[:, 1] = np.sin(ang)
    cs[:, 2] = np.cos(ang) * np.float32(128 ** -0.5)
    cs[:, 3] = np.sin(ang) * np.float32(128 ** -0.5)
    cst["c_rope"] = cs.reshape(Lmax, 256)
    lg = np.log1p(-np.exp2(-5.0 - np.arange(4, dtype=np.float32))).astype(np.float32)
    cst["c_lgam"] = np.repeat(lg, 128)[None, :].astype(np.float32)
    t = np.arange(128)
    same = (t[:, None] // 64) == (t[None, :] // 64)
    ms = np.zeros((2, 128, 128), np.float32)
    ms[0] = same & (t[None, :] < t[:, None])
    ms[1] = same & (t[None, :] > t[:, None])
    cst["c_mstrict"] = ms
    sl = np.zeros((2, 128, 128), np.float32)
    sl[0, :64, :] = 1.0
    sl[1, 64:, :] = 1.0
    cst["c_sel"] = sl
    hm = np.zeros((1, 8), np.float32)
    hm[0, :4] = 1.0
    cst["c_hmask"] = hm
    return cst


_CACHE = {}


def _get_program(seqs):
    key = tuple(seqs)
    if key not in _CACHE:
        b = Builder(seqs)
        nc = b.build()
        _CACHE[key] = (b, nc)
    return _CACHE[key]


def kernel(**inputs):
    inp = {k: np.asarray(v) for k, v in inputs.items()}
    seqs = [2048, 2048, 4096, 4096]
    b, nc = _get_program(seqs)
    cst = host_constants(max(seqs))
    shared = {}
    for k in ("norm_mix", "norm_xq", "norm_mem", "norm_ffn", "even_w_in", "even_w_out", "hgrn_lb_logits",
              "gdn_w_in", "gdn_conv", "gdn_norm", "gdn_w_out", "xa_w_q", "xa_w_kv", "xa_w_o", "ffn_w_gu", "ffn_w_down"):
        shared[k] = np.ascontiguousarray(inp[k], dtype=np.float32)
    shared["norm_final"] = np.ascontiguousarray(inp["norm_final"].reshape(1, D), dtype=np.float32)
    shared["ret_norm"] = np.ascontiguousarray(inp["ret_norm"].reshape(2, 512), dtype=np.float32)
    shared["hgrn_norm"] = np.ascontiguousarray(inp["hgrn_norm"].reshape(2, 512), dtype=np.float32)
    shared["gdn_a_log"] = np.ascontiguousarray(inp["gdn_a_log"].reshape(2, 16), dtype=np.float32)
    shared["gdn_dt_bias"] = np.ascontiguousarray(inp["gdn_dt_bias"].reshape(2, 16), dtype=np.float32)
    shared.update(cst)
    in_maps = []
    for ci in range(NCORES):
        m = dict(shared)
        xp = inp["x_prompt"][2 * ci:2 * ci + 2].reshape(-1, D)
        xs = inp["x_sample"][2 * ci:2 * ci + 2].reshape(-1, D)
        m["x"] = np.ascontiguousarray(np.concatenate([xp, xs], axis=0), dtype=np.float32)
        mp = inp["mem_prompt"][2 * ci:2 * ci + 2].reshape(-1, D)
        ms = inp["mem_sample"][2 * ci:2 * ci + 2].reshape(-1, D)
        m["mem"] = np.ascontiguousarray(np.concatenate([mp, ms], axis=0), dtype=np.float32)
        in_maps.append(m)
    res = run_bass_kernel_spmd(nc, in_maps, core_ids=list(range(NCORES)))
    yp = np.empty((16, 2048, D), np.float32)
    ys = np.empty((16, 4096, D), np.float32)
    for ci in range(NCORES):
        y = np.asarray(res.results[ci]["y"])
        yp[2 * ci:2 * ci + 2] = y[:4096].reshape(2, 2048, D)
        ys[2 * ci:2 * ci + 2] = y[4096:].reshape(2, 4096, D)
    return (yp, ys)
```

```python
import contextlib
import math
import numpy as np
import ml_dtypes
import concourse.bass as bass
import concourse.mybir as mybir
from concourse.bass_utils import run_bass_kernel_spmd

F32 = mybir.dt.float32
BF16 = mybir.dt.bfloat16
AF = mybir.ActivationFunctionType
ALU = mybir.AluOpType
AX = mybir.AxisListType

D = 1024
DEPTH = 4
N_MEM = 256
RMS_EPS = 1e-6
D_FF = 2816
EVEN_IN = 4608
ODD_IN = 4128
XA_SCALE = 256 ** -0.5
NCORES = 8


class TB:
    __slots__ = ("t", "w", "r", "name")

    def __init__(self, t, name=""):
        self.t = t
        self.w = None
        self.r = {}
        self.name = name

    def __getitem__(self, k):
        return self.t[k]


class Ctx:
    def __init__(self, nc, es, n_dma_sems=20):
        self.nc = nc
        self.es = es
        self.engs = {"pe": nc.tensor, "act": nc.scalar, "dve": nc.vector, "pool": nc.gpsimd, "sp": nc.sync}
        self.sem = {}
        self.cnt = {}
        self.seen = {k: {} for k in self.engs}
        for k in self.engs:
            self.sem[k] = es.enter_context(nc.semaphore("s_" + k))
            self.cnt[k] = 0
        self.dsem = {}
        self.dval = {}
        self.drot = {}
        for q in ("sp", "pool", "act"):
            n = n_dma_sems if q != "act" else 8
            self.dsem[q] = [es.enter_context(nc.semaphore("d_%s_%d" % (q, i))) for i in range(n)]
            self.dval[q] = [0] * n
            self.drot[q] = 0
        self.n_inst = 0
        self.n_wait = 0

    def sb(self, name, shape, dt, es=None):
        es = es or self.es
        self.uid = getattr(self, "uid", 0) + 1
        t = es.enter_context(self.nc.sbuf_tensor("%s_%d" % (name, self.uid), list(shape), dt))
        return TB(t, name)

    def ps(self, name, shape, dt, es=None):
        es = es or self.es
        self.uid = getattr(self, "uid", 0) + 1
        t = es.enter_context(self.nc.psum_tensor("%s_%d" % (name, self.uid), list(shape), dt))
        return TB(t, name)

    def _deps(self, engname, reads, writes):
        need = {}

        def add(ev, raw):
            if ev is None:
                return
            s, v, e = ev
            if e == engname and engname == "pe":
                return
            key = id(s)
            if key not in need or need[key][1] < v:
                need[key] = (s, v)

        for b in reads:
            add(b.w, True)
        for b in writes:
            add(b.w, False)
            for ev in b.r.values():
                add(ev, False)
        return need

    def _emit_waits(self, engname, need):
        eng = self.engs[engname]
        seen = self.seen[engname]
        for key, (s, v) in need.items():
            if seen.get(key, 0) >= v:
                continue
            eng.wait_ge(s, v)
            seen[key] = v
            self.n_wait += 1

    def _commit(self, ev, reads, writes):
        for b in writes:
            b.w = ev
            b.r = {}
        for b in reads:
            if b in writes:
                continue
            b.r[ev[2]] = ev

    def op(self, engname, fn, reads=(), writes=()):
        need = self._deps(engname, reads, writes)
        self._emit_waits(engname, need)
        ins = fn(self.engs[engname])
        self.cnt[engname] += 1
        ins.then_inc(self.sem[engname], 1)
        ev = (self.sem[engname], self.cnt[engname], engname)
        self._commit(ev, reads, writes)
        self.n_inst += 1
        return ins

    def dma(self, q, out, in_, reads=(), writes=(), slow=False):
        need = self._deps("dma_" + q, reads, writes)
        i = self.drot[q]
        self.drot[q] = (i + 1) % len(self.dsem[q])
        s = self.dsem[q][i]
        if self.dval[q][i] > 0:
            need[id(s)] = (s, self.dval[q][i])
        self._emit_waits(q, need)
        if slow:
            ins = self.engs[q].dma_start(out=out, in_=in_, allow_slow_non_contiguous=True)
        else:
            ins = self.engs[q].dma_start(out=out, in_=in_)
        self.dval[q][i] += 16
        ins.then_inc(s, 16)
        ev = (s, self.dval[q][i], "dma_" + q + str(i))
        self._commit(ev, reads, writes)
        self.n_inst += 1
        return ins

    def barrier(self):
        for e in self.engs:
            need = {}
            for k in self.engs:
                if k != e and self.cnt[k] > 0:
                    need[id(self.sem[k])] = (self.sem[k], self.cnt[k])
            for q in self.dsem:
                for s, v in zip(self.dsem[q], self.dval[q]):
                    if v > 0:
                        need[id(s)] = (s, v)
            self._emit_waits(e, need)


def rstd_inplace(c, ssq, inv_n, eps=RMS_EPS):
    c.op("dve", lambda e: e.tensor_scalar(ssq[:], ssq[:], inv_n, eps, ALU.mult, ALU.add), reads=[ssq], writes=[ssq])
    c.op("act", lambda e: e.activation(out=ssq[:], in_=ssq[:], func=AF.Sqrt), reads=[ssq], writes=[ssq])
    c.op("dve", lambda e: e.reciprocal(out=ssq[:], in_=ssq[:]), reads=[ssq], writes=[ssq])


class _View:
    def __init__(self, tb, key):
        self.tb = tb
        self.key = key

    def __getitem__(self, k):
        v = self.tb.t[self.key]
        return v[k]

    @property
    def w(self):
        return self.tb.w

    @w.setter
    def w(self, v):
        self.tb.w = v

    @property
    def r(self):
        return self.tb.r

    @r.setter
    def r(self, v):
        self.tb.r = v


def run_interleaved(gens):
    gens = list(gens)
    while gens:
        for g in list(gens):
            try:
                next(g)
            except StopIteration:
                gens.remove(g)


class Rot:
    def __init__(self, items):
        self.items = items
        self.i = 0

    def next(self):
        it = self.items[self.i]
        self.i = (self.i + 1) % len(self.items)
        return it


class Builder:
    def __init__(self, seqs, depth=DEPTH, do_mixer=True, do_xattn=True, do_ffn=True, kinds=None):
        self.seqs = list(seqs)
        self.depth = depth
        self.do_mixer = do_mixer
        self.do_xattn = do_xattn
        self.do_ffn = do_ffn
        self.kinds = kinds or [("even", l // 2) if l % 2 == 0 else ("odd", l // 2) for l in range(depth)]
        self.T = sum(self.seqs)
        self.seq_off = [sum(self.seqs[:i]) for i in range(len(self.seqs))]
        self.Lmax = max(self.seqs)

    def declare(self, nc):
        d = {}

        def inp(name, shape, dt=F32):
            d[name] = nc.dram_tensor(name, list(shape), dt, kind="ExternalInput").ap()

        ns = len(self.seqs)
        inp("x", [self.T, D])
        inp("mem", [ns * N_MEM, D])
        for n in ("norm_mix", "norm_xq", "norm_mem", "norm_ffn"):
            inp(n, [DEPTH, D])
        inp("norm_final", [1, D])
        inp("even_w_in", [2, D, EVEN_IN])
        inp("even_w_out", [2, D, D])
        inp("hgrn_lb_logits", [2, 512])
        inp("ret_norm", [2, 512])
        inp("hgrn_norm", [2, 512])
        inp("gdn_w_in", [2, D, ODD_IN])
        inp("gdn_conv", [2, 5, 3072])
        inp("gdn_a_log", [2, 16])
        inp("gdn_dt_bias", [2, 16])
        inp("gdn_norm", [2, 128])
        inp("gdn_w_out", [2, D, D])
        inp("xa_w_q", [DEPTH, D, D])
        inp("xa_w_kv", [DEPTH, D, 2 * D])
        inp("xa_w_o", [DEPTH, D, D])
        inp("ffn_w_gu", [DEPTH, D, 2 * D_FF])
        inp("ffn_w_down", [DEPTH, D_FF, D])
        for name, arr in host_constants(self.Lmax).items():
            inp(name, arr.shape, F32 if arr.dtype == np.float32 else BF16)
        d["y"] = nc.dram_tensor("y", [self.T, D], F32, kind="ExternalOutput").ap()
        d["X"] = nc.dram_tensor("X_scr", [self.T, D], F32).ap()
        d["P"] = nc.dram_tensor("P_scr", [self.Lmax, EVEN_IN], F32).ap()
        d["OF"] = nc.dram_tensor("OF_scr", [self.Lmax, D], F32).ap()
        d["HT"] = nc.dram_tensor("HT_scr", [128, 8 * (self.Lmax + 4)], BF16).ap()
        d["QT"] = nc.dram_tensor("QT_scr", [128, 8 * self.Lmax], BF16).ap()
        d["KT"] = nc.dram_tensor("KT_scr", [128, 8 * self.Lmax], BF16).ap()
        d["VM"] = nc.dram_tensor("VM_scr", [self.Lmax, D], BF16).ap()
        self.d = d

    def rmsnorm_to_hT(self, c, xt, grow, hT_dst, junk, ssq, hn, pst, ident):
        c.op("act", lambda e: e.activation(out=junk[:], in_=xt[:], func=AF.Square, accum_out=ssq[:]),
             reads=[xt], writes=[junk, ssq])
        rstd_inplace(c, ssq, 1.0 / D)
        c.op("dve", lambda e: e.scalar_tensor_tensor(hn[:], xt[:], ssq[:], grow[:], ALU.mult, ALU.mult),
             reads=[xt, ssq, grow], writes=[hn])
        for k in range(8):
            c.op("pe", lambda e, k=k: e.transpose(out=pst[:, k * 128:(k + 1) * 128],
                                                   in_=hn[:, k * 128:(k + 1) * 128], identity=ident[:]),
                 reads=[hn, ident], writes=[pst])
        tb, ap = hT_dst
        c.op("act", lambda e: e.copy(out=ap, in_=pst[:].rearrange("p (k t) -> p k t", k=8)),
             reads=[pst], writes=[tb])

    def x_src(self, layer_first):
        return self.d["x"] if layer_first else self.d["X"]

    def phase_ffn(self, c, layer, x_in, last):
        nc, d = c.nc, self.d
        TBK = 512
        NTB = TBK // 128
        with contextlib.ExitStack() as es:
            wgu = c.sb("wgu", [128, 8, 2 * D_FF], BF16, es)
            wdn = c.sb("wdn", [128, 22, D], BF16, es)
            grow = c.sb("grow", [128, D], F32, es)
            gfin = c.sb("gfin", [128, D], F32, es) if last else None
            ident = c.sb("identb", [128, 128], BF16, es)
            xts = Rot([[c.sb("xt%d_%d" % (i, j), [128, D], F32, es) for j in range(NTB)] for i in range(1)])
            hTs = Rot([c.sb("hT%d" % i, [128, 8, TBK], BF16, es) for i in range(1 if last else 2)])
            aT = c.sb("aT", [128, 22, TBK], BF16, es)
            sg = Rot([c.sb("sg%d" % i, [128, TBK], F32, es) for i in range(2)])
            ssq = c.sb("ssq", [128, 1], F32, es)
            hn = c.sb("hn", [128, D], BF16, es)
            junk = hn
            yts = Rot([c.sb("yt%d" % i, [128, D], F32, es) for i in range(2)])
            pst = c.ps("pst", [128, D], BF16, es)
            psg = Rot([c.ps("psg%d" % i, [128, 512], F32, es) for i in range(2)])
            psu = Rot([c.ps("psu%d" % i, [128, 512], F32, es) for i in range(2)])
            psy = Rot([c.ps("psy%d" % i, [128, 512], F32, es) for i in range(2)])

            c.dma("sp", ident[:], d["c_ident_bf"], writes=[ident])
            c.dma("sp", grow[:], d["norm_ffn"][layer:layer + 1, :].partition_broadcast(128), writes=[grow])
            if last:
                c.dma("sp", gfin[:], d["norm_final"][0:1, :].partition_broadcast(128), writes=[gfin])
            for k in range(8):
                c.dma("pool", wgu[:, k, :], d["ffn_w_gu"][layer, k * 128:(k + 1) * 128, :], writes=[wgu])
            for f in range(22):
                c.dma("pool", wdn[:, f, :], d["ffn_w_down"][layer, f * 128:(f + 1) * 128, :], writes=[wdn])

            for b0 in range(0, self.T, TBK):
                xt = xts.next()
                hT = hTs.next()
                for j in range(NTB):
                    r0 = b0 + j * 128
                    c.dma("sp", xt[j][:], x_in[r0:r0 + 128, :], reads=[self.xtb(r0)], writes=[xt[j]])
                    self.rmsnorm_to_hT(c, xt[j], grow, (hT, hT[:, :, j * 128:(j + 1) * 128]), junk, ssq, hn, pst, ident)
                for fb in range(22):
                    pg, pu, s = psg.next(), psu.next(), sg.next()
                    for k in range(8):
                        c.op("pe", lambda e, k=k: e.matmul(pg[:, 0:TBK], wgu[:, k, fb * 128:(fb + 1) * 128], hT[:, k, :],
                                                          start=(k == 0), stop=(k == 7)),
                             reads=[wgu, hT], writes=[pg])
                    for k in range(8):
                        c.op("pe", lambda e, k=k: e.matmul(pu[:, 0:TBK], wgu[:, k, D_FF + fb * 128:D_FF + (fb + 1) * 128],
                                                          hT[:, k, :], start=(k == 0), stop=(k == 7)),
                             reads=[wgu, hT], writes=[pu])
                    c.op("act", lambda e: e.activation(out=s[:], in_=pg[:, 0:TBK], func=AF.Silu), reads=[pg], writes=[s])
                    c.op("dve", lambda e: e.tensor_tensor(aT[:, fb, :], pu[:, 0:TBK], s[:], ALU.mult),
                         reads=[pu, s], writes=[aT])
                for j in range(NTB):
                    r0 = b0 + j * 128
                    yt = yts.next()
                    for n in range(2):
                        py = psy.next()
                        for fb in range(22):
                            c.op("pe", lambda e, fb=fb: e.matmul(py[:], aT[:, fb, j * 128:(j + 1) * 128],
                                                                wdn[:, fb, n * 512:(n + 1) * 512],
                                                                start=(fb == 0), stop=(fb == 21)),
                                 reads=[aT, wdn], writes=[py])
                        c.op("dve", lambda e: e.tensor_tensor(yt[:, n * 512:(n + 1) * 512], py[:],
                                                              xt[j][:, n * 512:(n + 1) * 512], ALU.add),
                             reads=[py, xt[j]], writes=[yt])
                    if last:
                        self.final_norm_store(c, yt, gfin, junk, ssq, r0)
                    else:
                        c.dma("sp", d["X"][r0:r0 + 128, :], yt[:], reads=[yt], writes=[self.xtb(r0)])
            c.barrier()

    def final_norm_store(self, c, yt, gfin, junk, ssq, r0):
        c.op("act", lambda e: e.activation(out=junk[:], in_=yt[:], func=AF.Square, accum_out=ssq[:]),
             reads=[yt], writes=[junk, ssq])
        rstd_inplace(c, ssq, 1.0 / D)
        c.op("dve", lambda e: e.scalar_tensor_tensor(yt[:], yt[:], ssq[:], gfin[:], ALU.mult, ALU.mult),
             reads=[yt, ssq, gfin], writes=[yt])
        c.dma("sp", self.d["y"][r0:r0 + 128, :], yt[:], reads=[yt], writes=[self.ytb(r0)])

    def xtb(self, r0):
        return self._xtb[r0 // 128]

    def ytb(self, r0):
        return self._ytb[r0 // 128]

    def phase_xattn(self, c, layer, x_in):
        nc, d = c.nc, self.d
        TBK = 512
        with contextlib.ExitStack() as es:
            wq = c.sb("wq", [128, 8, D], BF16, es)
            wkv = c.sb("wkv", [128, 8, 2 * D], BF16, es)
            wo = c.sb("wo", [128, 8, D], BF16, es)
            gq = c.sb("gq", [128, D], F32, es)
            gm = c.sb("gm", [128, D], F32, es)
            ident = c.sb("identb", [128, 128], BF16, es)
            ones = c.sb("onesb", [128, 128], BF16, es)
            xts_r = Rot([[c.sb("xt%d_%d" % (i, j), [128, D], F32, es) for j in range(4)] for i in range(2)])
            hT_r = Rot([c.sb("hT%d" % i, [128, 8, TBK], BF16, es) for i in range(2)])
            qT_r = Rot([c.sb("qT%d" % i, [128, 8, TBK], BF16, es) for i in range(2)])
            oT_r = Rot([c.sb("oT%d" % i, [128, 8, TBK], BF16, es) for i in range(2)])
            xts = xts_r.next()
            memT = c.sb("memT", [128, 8, N_MEM], BF16, es)
            KT = c.sb("KT", [128, 8, N_MEM], BF16, es)
            Vt = c.sb("Vt", [128, 2, D], BF16, es)
            PT = [Rot([c.sb("PT%d_%d" % (m, i), [128, TBK], BF16, es) for i in range(2)]) for m in range(2)]
            rden = c.sb("rden", [128, TBK], F32, es)
            junk = c.sb("junk", [128, D], BF16, es)
            ssq = c.sb("ssq", [128, 1], F32, es)
            hn = c.sb("hn", [128, D], BF16, es)
            yts = Rot([c.sb("yt%d" % i, [128, D], F32, es) for i in range(2)])
            pst = c.ps("pst", [128, D], BF16, es)
            psA = Rot([c.ps("psA%d" % i, [128, 512], F32, es) for i in range(4)])
            psD = c.ps("psD", [128, 512], F32, es)
            psy = Rot([c.ps("psy%d" % i, [128, 512], F32, es) for i in range(2)])

            c.dma("sp", ident[:], d["c_ident_bf"], writes=[ident])
            c.dma("sp", ones[:], d["c_ones_bf"], writes=[ones])
            c.dma("sp", gq[:], d["norm_xq"][layer:layer + 1, :].partition_broadcast(128), writes=[gq])
            c.dma("sp", gm[:], d["norm_mem"][layer:layer + 1, :].partition_broadcast(128), writes=[gm])
            for k in range(8):
                c.dma("pool", wq[:, k, :], d["xa_w_q"][layer, k * 128:(k + 1) * 128, :], writes=[wq])
                c.dma("pool", wkv[:, k, :], d["xa_w_kv"][layer, k * 128:(k + 1) * 128, :], writes=[wkv])
                c.dma("pool", wo[:, k, :], d["xa_w_o"][layer, k * 128:(k + 1) * 128, :], writes=[wo])

            for si, L in enumerate(self.seqs):
                for m in range(2):
                    mt = xts[m]
                    c.dma("sp", mt[:], d["mem"][si * N_MEM + m * 128: si * N_MEM + (m + 1) * 128, :], writes=[mt])
                    self.rmsnorm_to_hT(c, mt, gm, (memT, memT[:, :, m * 128:(m + 1) * 128]), junk, ssq, hn, pst, ident)
                for fb in range(8):
                    pa = psA.next()
                    for k in range(8):
                        c.op("pe", lambda e, k=k: e.matmul(pa[:, 0:N_MEM], wkv[:, k, fb * 128:(fb + 1) * 128], memT[:, k, :],
                                                          start=(k == 0), stop=(k == 7)), reads=[wkv, memT], writes=[pa])
                    c.op("act", lambda e: e.copy(out=KT[:, fb, :], in_=pa[:, 0:N_MEM]), reads=[pa], writes=[KT])
                for m in range(2):
                    for n in range(2):
                        pa = psA.next()
                        for k in range(8):
                            c.op("pe", lambda e, k=k: e.matmul(pa[:], memT[:, k, m * 128:(m + 1) * 128],
                                                              wkv[:, k, D + n * 512:D + (n + 1) * 512],
                                                              start=(k == 0), stop=(k == 7)), reads=[wkv, memT], writes=[pa])
                        c.op("act", lambda e: e.copy(out=Vt[:, m, n * 512:(n + 1) * 512], in_=pa[:]), reads=[pa], writes=[Vt])
                for b0 in range(self.seq_off[si], self.seq_off[si] + L, TBK):
                    ntile = min(4, (self.seq_off[si] + L - b0) // 128)
                    W = ntile * 128
                    xts, hT, qT, oT = xts_r.next(), hT_r.next(), qT_r.next(), oT_r.next()
                    for j in range(ntile):
                        r0 = b0 + j * 128
                        c.dma("sp", xts[j][:], x_in[r0:r0 + 128, :], reads=[self.xtb(r0)], writes=[xts[j]])
                        self.rmsnorm_to_hT(c, xts[j], gq, (hT, hT[:, :, j * 128:(j + 1) * 128]), junk, ssq, hn, pst, ident)
                    for fb in range(8):
                        pa = psA.next()
                        for k in range(8):
                            c.op("pe", lambda e, k=k: e.matmul(pa[:, 0:W], wq[:, k, fb * 128:(fb + 1) * 128], hT[:, k, 0:W],
                                                              start=(k == 0), stop=(k == 7)), reads=[wq, hT], writes=[pa])
                        c.op("act", lambda e: e.copy(out=qT[:, fb, 0:W], in_=pa[:, 0:W]), reads=[pa], writes=[qT])
                    for h in range(4):
                        pts = []
                        for mb in range(2):
                            pa = psA.next()
                            for dd in range(2):
                                c.op("pe", lambda e, dd=dd: e.matmul(pa[:, 0:W], KT[:, 2 * h + dd, mb * 128:(mb + 1) * 128],
                                                                    qT[:, 2 * h + dd, 0:W], start=(dd == 0), stop=(dd == 1)),
                                     reads=[KT, qT], writes=[pa])
                            pt = PT[mb].next()
                            c.op("act", lambda e: e.activation(out=pt[:, 0:W], in_=pa[:, 0:W], func=AF.Exp, scale=XA_SCALE),
                                 reads=[pa], writes=[pt])
                            pts.append(pt)
                        for mb in range(2):
                            c.op("pe", lambda e, mb=mb: e.matmul(psD[:, 0:W], ones[:], pts[mb][:, 0:W],
                                                                start=(mb == 0), stop=(mb == 1)), reads=[ones, pts[mb]], writes=[psD])
                        c.op("dve", lambda e: e.reciprocal(out=rden[:, 0:W], in_=psD[:, 0:W]), reads=[psD], writes=[rden])
                        for dd in range(2):
                            pa = psA.next()
                            for mb in range(2):
                                c.op("pe", lambda e, mb=mb: e.matmul(pa[:, 0:W], Vt[:, mb, (2 * h + dd) * 128:(2 * h + dd + 1) * 128],
                                                                    pts[mb][:, 0:W], start=(mb == 0), stop=(mb == 1)),
                                     reads=[Vt, pts[mb]], writes=[pa])
                            c.op("dve", lambda e: e.tensor_tensor(oT[:, 2 * h + dd, 0:W], pa[:, 0:W], rden[:, 0:W], ALU.mult),
                                 reads=[pa, rden], writes=[oT])
                    for j in range(ntile):
                        r0 = b0 + j * 128
                        yt = yts.next()
                        for n in range(2):
                            py = psy.next()
                            for fb in range(8):
                                c.op("pe", lambda e, fb=fb: e.matmul(py[:], oT[:, fb, j * 128:(j + 1) * 128],
                                                                    wo[:, fb, n * 512:(n + 1) * 512],
                                                                    start=(fb == 0), stop=(fb == 7)), reads=[oT, wo], writes=[py])
                            c.op("dve", lambda e: e.tensor_tensor(yt[:, n * 512:(n + 1) * 512], py[:],
                                                                  xts[j][:, n * 512:(n + 1) * 512], ALU.add),
                                 reads=[py, xts[j]], writes=[yt])
                        c.dma("pool", d["X"][r0:r0 + 128, :], yt[:], reads=[yt], writes=[self.xtb(r0)])
            c.barrier()

    def gla_tile(self, c, R, G, q_ap, q_tb, k_tb, k_ap, v_ap, v_tb, lf_tb, lf_ap, dr, S, Sb, o_tb, o_col0, add_tb=None,
                 pre=None):
        if pre is not None:
            for _ in pre():
                yield
        M1, M2, M4, MT = R["gm"][dr]
        cs1, cs2, cs4 = R["psA"].next(), R["psA"].next(), R["psA"].next()
        for ps_, M in ((cs1, M1), (cs2, M2), (cs4, M4)):
            c.op("pe", lambda e: e.matmul(ps_[:], M[:], lf_ap, start=True, stop=True), reads=[M, lf_tb], writes=[ps_])
        E = G["E"]
        c.op("act", lambda e: e.activation(out=E[0][:], in_=cs1[:], func=AF.Exp), reads=[cs1], writes=[E[0]])
        c.op("act", lambda e: e.activation(out=E[1][:], in_=cs2[:], func=AF.Exp), reads=[cs2], writes=[E[1]])
        c.op("act", lambda e: e.activation(out=E[2][:], in_=cs2[:], func=AF.Exp, scale=-1.0), reads=[cs2], writes=[E[2]])
        c.op("act", lambda e: e.activation(out=E[3][:], in_=cs4[:], func=AF.Exp), reads=[cs4], writes=[E[3]])
        yield
        q1, q2, k3, k4, vb = G["q1"], G["q2"], G["k3"], G["k4"], G["vb"]
        BIG = 4.0e18
        c.op("pool", lambda e: e.tensor_tensor(q1[:], E[0][:], q_ap, ALU.mult), reads=[E[0], q_tb], writes=[q1])
        c.op("dve", lambda e: e.scalar_tensor_tensor(q2[:], E[1][:], BIG, q_ap, ALU.min, ALU.mult), reads=[E[1], q_tb], writes=[q2])
        c.op("dve", lambda e: e.scalar_tensor_tensor(k3[:], E[2][:], BIG, k_ap, ALU.min, ALU.mult), reads=[E[2], k_tb], writes=[k3])
        c.op("pool", lambda e: e.tensor_tensor(k4[:], E[3][:], k_ap, ALU.mult), reads=[E[3], k_tb], writes=[k4])
        c.op("act", lambda e: e.copy(out=vb[:], in_=v_ap), reads=[v_tb], writes=[vb])
        pe_l = R["psA"].next()
        for h in range(4):
            c.op("pe", lambda e: e.matmul(pe_l[:, 2 * h:2 * h + 2], lf_ap[:, h * 128:(h + 1) * 128], R["ind"][:],
                                          start=True, stop=True), reads=[lf_tb, R["ind"]], writes=[pe_l])
        eL = G["eL"]
        c.op("act", lambda e: e.activation(out=eL[:], in_=pe_l[:, 0:8], func=AF.Exp), reads=[pe_l], writes=[eL])
        yield
        Ts = []
        for src, nm in ((q1, "q1T"), (q2, "q2T"), (k3, "k3T")):
            pt = R["pstB"].next()
            for h in range(4):
                c.op("pe", lambda e: e.transpose(out=pt[:, h * 128:(h + 1) * 128], in_=src[:, h * 128:(h + 1) * 128],
                                                 identity=R["ident"][:]), reads=[src, R["ident"]], writes=[pt])
            dst = G[nm]
            c.op("act" if nm != "q2T" else "dve", lambda e: e.tensor_copy(out=dst[:], in_=pt[:, 0:512]) if nm == "q2T"
                 else e.copy(out=dst[:], in_=pt[:, 0:512]), reads=[pt], writes=[dst])
            Ts.append(dst)
            yield
        q1T, q2T, k3T = Ts
        order = (0, 1) if dr == 0 else (1, 0)

        MT4 = R["MT4"][dr]
        pa = R["psA"].next()
        for h in range(4):
            hs = slice(h * 128, (h + 1) * 128)
            c.op("pe", lambda e: e.matmul(pa[:, hs], k3T[:, hs], q2T[:, hs], start=True, stop=True), reads=[k3T, q2T], writes=[pa])
        atm = G["ATm4"]
        c.op("dve", lambda e: e.tensor_tensor(atm[:], pa[:], MT4[:], ALU.mult), reads=[pa, MT4], writes=[atm])
        yield
        eLv = eL[:].rearrange("p (h c) -> p h c", c=2)
        S3 = S[:].rearrange("p (h x) -> p h x", h=4)
        for ci in order:
            rs = slice(ci * 64, (ci + 1) * 64)
            po = R["psA"].next()
            for h in range(4):
                hs = slice(h * 128, (h + 1) * 128)
                c.op("pe", lambda e: e.matmul(po[:, hs], q1T[:, hs], Sb[:, hs], start=True, stop=False), reads=[q1T, Sb], writes=[po])
                c.op("pe", lambda e: e.matmul(po[:, hs], atm[:, hs], vb[:, hs], start=False, stop=True), reads=[atm, vb], writes=[po])
            ocs = slice(o_col0, o_col0 + 512)
            if add_tb is None:
                c.op("act", lambda e: e.copy(out=o_tb[rs, ocs], in_=po[rs, :]), reads=[po], writes=[o_tb])
            else:
                c.op("dve", lambda e: e.tensor_tensor(o_tb[rs, ocs], po[rs, :], add_tb[rs, ocs], ALU.add), reads=[po, add_tb], writes=[o_tb])
            pu = R["psA"].next()
            for h in range(4):
                hs = slice(h * 128, (h + 1) * 128)
                c.op("pe", lambda e: e.matmul(pu[:, hs], k4[rs, hs], vb[rs, hs], start=True, stop=True), reads=[k4, vb], writes=[pu])
            yield
            c.op("dve", lambda e: e.tensor_tensor(S3, S3, eLv[:, :, ci:ci + 1].to_broadcast([128, 4, 128]), ALU.mult),
                 reads=[S, eL], writes=[S])
            c.op("dve", lambda e: e.tensor_tensor(S[:], S[:], pu[:], ALU.add), reads=[S, pu], writes=[S])
            yield
            c.op("act", lambda e: e.copy(out=Sb[:], in_=S[:]), reads=[S], writes=[Sb])
            yield

    def phase_even(self, c, j, layer, x_in):
        nc, d = c.nc, self.d
        with contextlib.ExitStack() as es:
            win = c.sb("win", [128, 8, EVEN_IN], BF16, es)
            wout = c.sb("wout", [128, 8, D], BF16, es)
            gmix = c.sb("gmix", [128, D], F32, es)
            gh = c.sb("gh", [128, D], F32, es)
            lbr = c.sb("lbr", [128, 512], F32, es)
            omlb = c.sb("omlb", [128, 512], F32, es)
            lgam = c.sb("lgam", [128, 512], F32, es)
            hmask = c.sb("hmask", [128, 8], F32, es)
            ident = c.sb("identb", [128, 128], BF16, es)
            gmt = [[c.sb("gm%d%d" % (a, b_), [128, 128], F32, es) for b_ in range(4)] for a in range(2)]
            ind = c.sb("ind", [128, 2], F32, es)
            p = c.sb("p", [128, EVEN_IN], F32, es)
            xts = Rot([c.sb("xt%d" % i, [128, D], F32, es) for i in range(2)])
            oft = c.sb("oft", [128, D], F32, es)
            ot = c.sb("ot", [128, D], F32, es)
            rope = c.sb("rope", [128, 4, 64], F32, es)
            rt = [c.sb("rt%d" % i, [128, 4, 64], F32, es) for i in range(4)]
            fg = c.sb("fg", [128, 512], F32, es)
            lf = c.sb("lf", [128, 512], F32, es)
            kk = c.sb("kk", [128, 512], F32, es)
            sq = c.sb("sq", [128, D], F32, es)
            ssq = c.sb("ssq", [128, 1], F32, es)
            s1 = c.sb("s1", [128, 8], F32, es)
            s2 = c.sb("s2", [128, 8], F32, es)
            hn = c.sb("hn", [128, D], BF16, es)
            hT = c.sb("hT", [128, 8, 128], BF16, es)
            omix = c.sb("omix", [128, D], BF16, es)
            junk = omix
            oT = hT
            R = {
                "gm": gmt, "ind": ind, "ident": ident,
                "psA": Rot([c.ps("psA%d" % i, [128, 512], F32, es) for i in range(6)]),
                "pstB": Rot([c.ps("pstB%d" % i, [128, D], BF16, es) for i in range(2)]),
            }
            Gs = []
            for g in range(2):
                G = {"E": [c.sb("E%d_%d" % (g, i), [128, 512], F32, es) for i in range(4)]}
                for nm in ("q1", "q2", "k3", "k4", "vb", "q1T", "q2T", "k3T"):
                    G[nm] = c.sb("%s_%d" % (nm, g), [128, 512], BF16, es)
                G["eL"] = c.sb("eL_%d" % g, [128, 8], F32, es)
                G["ATm4"] = c.sb("ATm4_%d" % g, [128, 512], BF16, es)
                Gs.append(G)
            S = [c.sb("S%d" % g, [128, 512], F32, es) for g in range(2)]
            Sb = [c.sb("Sb%d" % g, [128, 512], BF16, es) for g in range(2)]
            MT4 = [c.sb("MT4_%d" % a, [128, 512], F32, es) for a in range(2)]
            R["MT4"] = MT4
            for a in range(2):
                for h in range(4):
                    c.dma("sp", MT4[a][:, h * 128:(h + 1) * 128], d["c_gla"][a * 4 + 3], writes=[MT4[a]])

            c.dma("sp", ident[:], d["c_ident_bf"], writes=[ident])
            c.dma("sp", ind[:], d["c_ind"], writes=[ind])
            for a in range(2):
                for b_ in range(4):
                    c.dma("sp", gmt[a][b_][:], d["c_gla"][a * 4 + b_], writes=[gmt[a][b_]])
            c.dma("sp", gmix[:], d["norm_mix"][layer:layer + 1, :].partition_broadcast(128), writes=[gmix])
            c.dma("sp", gh[:, 0:512], d["ret_norm"][j:j + 1, :].partition_broadcast(128), writes=[gh])
            c.dma("sp", gh[:, 512:1024], d["hgrn_norm"][j:j + 1, :].partition_broadcast(128), writes=[gh])
            c.dma("sp", lgam[:], d["c_lgam"][0:1, :].partition_broadcast(128), writes=[lgam])
            c.dma("sp", hmask[:], d["c_hmask"][0:1, :].partition_broadcast(128), writes=[hmask])
            if j == 0:
                c.op("dve", lambda e: e.memset(lbr[:], 0.0), writes=[lbr])
            else:
                c.dma("sp", lbr[:], d["hgrn_lb_logits"][1:2, :].partition_broadcast(128), writes=[lbr])
                c.dma("sp", omlb[:], d["hgrn_lb_logits"][0:1, :].partition_broadcast(128), writes=[omlb])
                c.op("dve", lambda e: e.tensor_tensor(lbr[:], lbr[:], omlb[:], ALU.subtract), reads=[lbr, omlb], writes=[lbr])
                c.op("act", lambda e: e.activation(out=lbr[:], in_=lbr[:], func=AF.Sigmoid), reads=[lbr], writes=[lbr])
            c.op("dve", lambda e: e.tensor_scalar(omlb[:], lbr[:], -1.0, 1.0, ALU.mult, ALU.add), reads=[lbr], writes=[omlb])
            for k in range(8):
                c.dma("pool", win[:, k, :], d["even_w_in"][j, k * 128:(k + 1) * 128, :], writes=[win])
                c.dma("pool", wout[:, k, :], d["even_w_out"][j, k * 128:(k + 1) * 128, :], writes=[wout])

            def gates(zcol):
                c.op("act", lambda e: e.activation(out=fg[:], in_=p[:, zcol:zcol + 512], func=AF.Sigmoid), reads=[p], writes=[fg])
                yield
                c.op("pool", lambda e: e.tensor_tensor(fg[:], fg[:], omlb[:], ALU.mult), reads=[fg, omlb], writes=[fg])
                yield
                c.op("pool", lambda e: e.tensor_tensor(fg[:], fg[:], lbr[:], ALU.add), reads=[fg, lbr], writes=[fg])
                yield
                c.op("act", lambda e: e.activation(out=lf[:], in_=fg[:], func=AF.Ln), reads=[fg], writes=[lf])
                c.op("pool", lambda e: e.tensor_scalar(kk[:], fg[:], -1.0, 1.0, ALU.mult, ALU.add), reads=[fg], writes=[kk])
                yield

            for si, L in enumerate(self.seqs):
                nt = L // 128
                base = self.seq_off[si]
                ptb = [TB(None, "P%d" % t) for t in range(nt)]
                oftb = [TB(None, "OF%d" % t) for t in range(nt)]
                for g in range(2):
                    c.op("dve", lambda e: e.memset(S[g][:], 0.0), writes=[S[g]])
                    c.op("pool", lambda e: e.memset(Sb[g][:], 0.0), writes=[Sb[g]])
                def front_gen(t):
                    r0 = base + t * 128
                    xt = xts.next()
                    c.dma("sp", xt[:], x_in[r0:r0 + 128, :], reads=[self.xtb(r0)], writes=[xt])
                    c.dma("sp", rope[:], d["c_rope"][t * 128:(t + 1) * 128, :].rearrange("p (a b) -> p a b", a=4), writes=[rope])
                    yield
                    self.rmsnorm_to_hT(c, xt, gmix, (hT, hT[:]), junk, ssq, hn, R["pstB"].next(), ident)
                    for _ in range(5):
                        yield
                    for n in range(9):
                        pa = R["psA"].next()
                        for k in range(8):
                            c.op("pe", lambda e: e.matmul(pa[:], hT[:, k, :], win[:, k, n * 512:(n + 1) * 512],
                                                          start=(k == 0), stop=(k == 7)), reads=[hT, win], writes=[pa])
                        c.op("act" if n % 2 == 0 else "dve",
                             lambda e: (e.copy(out=p[:, n * 512:(n + 1) * 512], in_=pa[:]) if n % 2 == 0
                                        else e.tensor_copy(out=p[:, n * 512:(n + 1) * 512], in_=pa[:])),
                             reads=[pa], writes=[p])
                        yield
                    for col0, ci_, si_ in ((0, 0, 1), (512, 2, 3)):
                        v4 = p[:, col0:col0 + 512].rearrange("p (h two x) -> p h two x", h=4, two=2)
                        x1, x2 = v4[:, :, 0, :], v4[:, :, 1, :]
                        cb = rope[:, ci_:ci_ + 1, :].to_broadcast([128, 4, 64])
                        sb_ = rope[:, si_:si_ + 1, :].to_broadcast([128, 4, 64])
                        c.op("pool", lambda e: e.tensor_tensor(rt[0][:], x1, cb, ALU.mult), reads=[p, rope], writes=[rt[0]])
                        c.op("pool", lambda e: e.tensor_tensor(rt[1][:], x2, sb_, ALU.mult), reads=[p, rope], writes=[rt[1]])
                        c.op("pool", lambda e: e.tensor_tensor(rt[2][:], x1, sb_, ALU.mult), reads=[p, rope], writes=[rt[2]])
                        c.op("pool", lambda e: e.tensor_tensor(rt[3][:], x2, cb, ALU.mult), reads=[p, rope], writes=[rt[3]])
                        yield
                        c.op("pool", lambda e: e.tensor_tensor(x1, rt[0][:], rt[1][:], ALU.subtract), reads=[rt[0], rt[1]], writes=[p])
                        c.op("pool", lambda e: e.tensor_tensor(x2, rt[2][:], rt[3][:], ALU.add), reads=[rt[2], rt[3]], writes=[p])
                        yield
                    c.dma("sp", d["P"][t * 128:(t + 1) * 128, :], p[:], reads=[p], writes=[ptb[t]])

                def scan_gens_f():
                    return [
                        self.gla_tile(c, R, Gs[0], p[:, 0:512], p, p, p[:, 512:1024], p[:, 1024:1536], p, lgam, lgam[:], 0,
                                      S[0], Sb[0], oft, 0),
                        self.gla_tile(c, R, Gs[1], p[:, 2048:2560], p, kk, kk[:], p[:, 3584:4096], p, lf, lf[:], 0,
                                      S[1], Sb[1], oft, 512, pre=lambda: gates(2560))]

                run_interleaved([front_gen(0)])
                for t in range(nt):
                    gens = scan_gens_f()
                    if t + 1 < nt:
                        gens.append(front_gen(t + 1))
                    run_interleaved(gens)
                    c.dma("sp", d["OF"][t * 128:(t + 1) * 128, :], oft[:], reads=[oft], writes=[oftb[t]])
                for g in range(2):
                    c.op("dve", lambda e: e.memset(S[g][:], 0.0), writes=[S[g]])
                    c.op("pool", lambda e: e.memset(Sb[g][:], 0.0), writes=[Sb[g]])
                for t in range(nt - 1, -1, -1):
                    r0 = base + t * 128
                    xt = xts.next()
                    c.dma("sp", xt[:], x_in[r0:r0 + 128, :], reads=[self.xtb(r0)], writes=[xt])
                    c.dma("sp", p[:], d["P"][t * 128:(t + 1) * 128, :], reads=[ptb[t]], writes=[p])
                    c.dma("sp", oft[:], d["OF"][t * 128:(t + 1) * 128, :], reads=[oftb[t]], writes=[oft])
                    run_interleaved([
                        self.gla_tile(c, R, Gs[0], p[:, 0:512], p, p, p[:, 512:1024], p[:, 1024:1536], p, lgam, lgam[:], 1,
                                      S[0], Sb[0], ot, 0, add_tb=oft),
                        self.gla_tile(c, R, Gs[1], p[:, 2048:2560], p, kk, kk[:], p[:, 3584:4096], p, lf, lf[:], 1,
                                      S[1], Sb[1], ot, 512, add_tb=oft, pre=lambda: gates(3072))])
                    o3 = ot[:].rearrange("p (h x) -> p h x", h=8)
                    c.op("dve", lambda e: e.reduce_sum(out=s1[:], in_=o3, axis=AX.X), reads=[ot], writes=[s1])
                    c.op("act", lambda e: e.activation(out=sq[:], in_=ot[:], func=AF.Square), reads=[ot], writes=[sq])
                    c.op("dve", lambda e: e.reduce_sum(out=s2[:], in_=sq[:].rearrange("p (h x) -> p h x", h=8), axis=AX.X),
                         reads=[sq], writes=[s2])
                    c.op("dve", lambda e: e.scalar_tensor_tensor(s1[:], s1[:], 1.0 / 128, hmask[:], ALU.mult, ALU.mult),
                         reads=[s1, hmask], writes=[s1])
                    c.op("dve", lambda e: e.tensor_scalar(s2[:], s2[:], 1.0 / 128, RMS_EPS, ALU.mult, ALU.add), reads=[s2], writes=[s2])
                    c.op("dve", lambda e: e.tensor_tensor(sq[:, 0:8], s1[:], s1[:], ALU.mult), reads=[s1], writes=[sq])
                    c.op("dve", lambda e: e.tensor_tensor(s2[:], s2[:], sq[:, 0:8], ALU.subtract), reads=[s2, sq], writes=[s2])
                    c.op("act", lambda e: e.activation(out=s2[:], in_=s2[:], func=AF.Sqrt), reads=[s2], writes=[s2])
                    c.op("dve", lambda e: e.reciprocal(out=s2[:], in_=s2[:]), reads=[s2], writes=[s2])
                    for h in range(8):
                        hs = slice(h * 128, (h + 1) * 128)
                        c.op("dve" if h % 2 == 0 else "pool",
                             lambda e: e.tensor_scalar(ot[:, hs], ot[:, hs], s1[:, h:h + 1], s2[:, h:h + 1], ALU.subtract, ALU.mult),
                             reads=[ot, s1, s2], writes=[ot])
                    c.op("pool", lambda e: e.tensor_tensor(ot[:], ot[:], gh[:], ALU.mult), reads=[ot, gh], writes=[ot])
                    c.op("act", lambda e: e.activation(out=sq[:, 0:512], in_=p[:, 1536:2048], func=AF.Silu), reads=[p], writes=[sq])
                    c.op("act", lambda e: e.activation(out=sq[:, 512:1024], in_=p[:, 4096:4608], func=AF.Silu), reads=[p], writes=[sq])
                    c.op("dve", lambda e: e.tensor_tensor(omix[:], ot[:], sq[:], ALU.mult), reads=[ot, sq], writes=[omix])
                    pt = R["pstB"].next()
                    for k in range(8):
                        c.op("pe", lambda e: e.transpose(out=pt[:, k * 128:(k + 1) * 128], in_=omix[:, k * 128:(k + 1) * 128],
                                                         identity=ident[:]), reads=[omix, ident], writes=[pt])
                    c.op("act", lambda e: e.copy(out=oT[:], in_=pt[:].rearrange("p (k t) -> p k t", k=8)), reads=[pt], writes=[oT])
                    for n in range(2):
                        py = R["psA"].next()
                        for k in range(8):
                            c.op("pe", lambda e: e.matmul(py[:], oT[:, k, :], wout[:, k, n * 512:(n + 1) * 512],
                                                          start=(k == 0), stop=(k == 7)), reads=[oT, wout], writes=[py])
                        c.op("dve", lambda e: e.tensor_tensor(xt[:, n * 512:(n + 1) * 512], py[:], xt[:, n * 512:(n + 1) * 512], ALU.add),
                             reads=[py, xt], writes=[xt])
                    c.dma("sp", d["X"][r0:r0 + 128, :], xt[:], reads=[xt], writes=[self.xtb(r0)])
            c.barrier()

    def gdn_tile(self, c, R, RS, h, dr, qT, kT, ktm, vtm, sc, S, Sb, o_tb, add_tb=None):
        M1, M2, M4, MT = R["gm"][dr]
        hs = slice(h * 128, (h + 1) * 128)
        col = slice(h, h + 1)
        identf = R["identf"]
        W = RS["w"]
        ula = RS["ula"]
        c.op("act", lambda e: e.mul(out=ula[:], in_=M1[:], mul=sc["la"][:, col]), reads=[M1, sc["la"]], writes=[ula])
        pg, pgt = R["psH"].next(), R["psH"].next()
        c.op("pe", lambda e: e.matmul(pg[:, 0:128], ula[:], M4[:], start=True, stop=True), reads=[ula, M4], writes=[pg])
        c.op("pe", lambda e: e.matmul(pgt[:, 0:128], M4[:], ula[:], start=True, stop=True), reads=[ula, M4], writes=[pgt])
        dm, dtm = RS["dm"], RS["dtm"]
        c.op("act", lambda e: e.activation(out=dm[:], in_=pg[:, 0:128], func=AF.Exp), reads=[pg], writes=[dm])
        c.op("act", lambda e: e.activation(out=dtm[:], in_=pgt[:, 0:128], func=AF.Exp), reads=[pgt], writes=[dtm])
        c.op("pool", lambda e: e.tensor_tensor(dm[:], dm[:], R["mstrict"][dr][:], ALU.mult), reads=[dm, R["mstrict"][dr]], writes=[dm])
        c.op("pool", lambda e: e.tensor_tensor(dtm[:], dtm[:], MT[:], ALU.mult), reads=[dtm, MT], writes=[dtm])
        yield
        pk = R["psH"].next()
        c.op("pe", lambda e: e.matmul(pk[:, 0:128], kT[:, h, :], kT[:, h, :], start=True, stop=True), reads=[kT], writes=[pk])
        X = W.next()
        c.op("dve", lambda e: e.scalar_tensor_tensor(X[:], pk[:, 0:128], sc["nbeta"][:, col], dm[:], ALU.mult, ALU.mult),
             reads=[pk, sc["nbeta"], dm], writes=[X])
        yield
        py = R["pstB"].next()
        c.op("pe", lambda e: e.transpose(out=py[:, 0:128], in_=X[:], identity=R["ident"][:]), reads=[X, R["ident"]], writes=[py])
        Y = W.next()
        RT = W.next()
        c.op("act", lambda e: e.copy(out=Y[:], in_=py[:, 0:128]), reads=[py], writes=[Y])
        c.op("dve", lambda e: e.tensor_tensor(RT[:], Y[:], R["ident"][:], ALU.add), reads=[Y, R["ident"]], writes=[RT])
        yield
        ttb = RS["ttb"]
        for m in range(5):
            px = R["psH"].next()
            c.op("pe", lambda e: e.matmul(px[:, 0:128], Y[:], X[:], start=True, stop=True), reads=[X, Y], writes=[px])
            if m < 4:
                c.op("pe", lambda e: e.matmul(px[:, 128:256], X[:], Y[:], start=True, stop=True), reads=[X, Y], writes=[px])
            X2 = W.next()
            c.op("act", lambda e: e.copy(out=X2[:], in_=px[:, 0:128]), reads=[px], writes=[X2])
            if m < 4:
                Y2 = W.next()
                c.op("act", lambda e: e.copy(out=Y2[:], in_=px[:, 128:256]), reads=[px], writes=[Y2])
            yield
            pr = R["psH"].next()
            c.op("pe", lambda e: e.matmul(pr[:, 0:128], X2[:], RT[:], start=True, stop=True), reads=[X2, RT], writes=[pr])
            if m < 4:
                RT2 = W.next()
                c.op("dve", lambda e: e.tensor_tensor(RT2[:], pr[:, 0:128], RT[:], ALU.add), reads=[pr, RT], writes=[RT2])
                RT = RT2
                X, Y = X2, Y2
            else:
                c.op("dve", lambda e: e.tensor_tensor(ttb[:], pr[:, 0:128], RT[:], ALU.add), reads=[pr, RT], writes=[ttb])
            yield
        vb_, kbg, kdec = RS["vbeta"], RS["kbg"], RS["kdec"]
        c.op("act", lambda e: e.mul(out=vb_[:], in_=vtm[:, hs], mul=sc["beta"][:, col]), reads=[vtm, sc["beta"]], writes=[vb_])
        c.op("dve", lambda e: e.tensor_scalar(kbg[:], ktm[:, hs], sc["beg"][:, col], None, ALU.mult), reads=[ktm, sc["beg"]], writes=[kbg])
        c.op("act", lambda e: e.mul(out=kdec[:], in_=ktm[:, hs], mul=sc["egd"][:, col]), reads=[ktm, sc["egd"]], writes=[kdec])
        pu_ = R["psH"].next()
        c.op("pe", lambda e: e.matmul(pu_[:, 0:128], ttb[:], vb_[:], start=True, stop=True), reads=[ttb, vb_], writes=[pu_])
        c.op("pe", lambda e: e.matmul(pu_[:, 128:256], kbg[:], ttb[:], start=True, stop=True), reads=[ttb, kbg], writes=[pu_])
        u = RS["u"]
        wT = RS["wT"]
        c.op("act", lambda e: e.copy(out=u[:], in_=pu_[:, 0:128]), reads=[pu_], writes=[u])
        c.op("act", lambda e: e.copy(out=wT[:], in_=pu_[:, 128:256]), reads=[pu_], writes=[wT])
        yield
        pq = R["psH"].next()
        c.op("pe", lambda e: e.matmul(pq[:, 0:128], kT[:, h, :], qT[:, h, :], start=True, stop=True), reads=[kT, qT], writes=[pq])
        qkm = RS["qkm"]
        c.op("dve", lambda e: e.tensor_tensor(qkm[:], pq[:, 0:128], dtm[:], ALU.mult), reads=[pq, dtm], writes=[qkm])
        yield
        vn = RS["vn"]
        tmp = RS["tmp"]
        order = (0, 1) if dr == 0 else (1, 0)
        for ci in order:
            rs = slice(ci * 64, (ci + 1) * 64)
            pa = R["psH"].next()
            c.op("pe", lambda e: e.matmul(pa[:, 0:128], wT[:], Sb[h][:], start=True, stop=True), reads=[wT, Sb[h]], writes=[pa])
            c.op("pe", lambda e: e.matmul(pa[:, 128:256], qT[:, h, :], Sb[h][:], start=True, stop=True), reads=[qT, Sb[h]], writes=[pa])
            c.op("dve", lambda e: e.tensor_tensor(vn[rs, :], u[rs, :], pa[rs, 0:128], ALU.subtract), reads=[u, pa], writes=[vn])
            c.op("dve", lambda e: e.tensor_scalar(tmp[rs, :], pa[rs, 128:256], sc["eg"][rs, col], None, ALU.mult),
                 reads=[pa, sc["eg"]], writes=[tmp])
            yield
            ps_ = R["psH"].next()
            c.op("pe", lambda e: e.matmul(ps_[:, 0:128], kdec[rs, :], vn[rs, :], start=True, stop=True), reads=[kdec, vn], writes=[ps_])
            c.op("dve", lambda e: e.scalar_tensor_tensor(S[h][:], S[h][:], sc["egl"][ci][:, col], ps_[:, 0:128], ALU.mult, ALU.add),
                 reads=[S[h], sc["egl"][ci], ps_], writes=[S[h]])
            c.op("act", lambda e: e.copy(out=Sb[h][:], in_=S[h][:]), reads=[S[h]], writes=[Sb[h]])
            yield
        p2 = R["psH"].next()
        c.op("pe", lambda e: e.matmul(p2[:, 0:128], qkm[:], vn[:], start=True, stop=True), reads=[qkm, vn], writes=[p2])
        if add_tb is None:
            c.op("dve", lambda e: e.tensor_tensor(o_tb[:, hs], p2[:, 0:128], tmp[:], ALU.add), reads=[p2, tmp], writes=[o_tb])
        else:
            c.op("pool", lambda e: e.tensor_tensor(tmp[:], tmp[:], add_tb[:, hs], ALU.add), reads=[tmp, add_tb], writes=[tmp])
            c.op("dve", lambda e: e.tensor_tensor(o_tb[:, hs], p2[:, 0:128], tmp[:], ALU.add), reads=[p2, tmp], writes=[o_tb])

    def gdn_pair(self, c, R, PS, h0, dr, qT, kT, ktm, vtm, sc, S, Sb, o_tb, add_tb=None):
        M1, M2, M4, MT = R["gm"][dr]
        heads = (h0, h0 + 1)
        ident = R["ident"]
        dd = PS["dd"]
        pgb = R["psH"].next()
        for i, h in enumerate(heads):
            ula = PS["ula"][i]
            c.op("act", lambda e: e.mul(out=ula[:], in_=M1[:], mul=sc["la"][:, h:h + 1]), reads=[M1, sc["la"]], writes=[ula])
            c.op("pe", lambda e: e.matmul(pgb[:, (2 * i) * 128:(2 * i + 1) * 128], ula[:], M4[:], start=True, stop=True),
                 reads=[ula, M4], writes=[pgb])
            c.op("pe", lambda e: e.matmul(pgb[:, (2 * i + 1) * 128:(2 * i + 2) * 128], M4[:], ula[:], start=True, stop=True),
                 reads=[ula, M4], writes=[pgb])
        c.op("act", lambda e: e.activation(out=dd[:], in_=pgb[:], func=AF.Exp), reads=[pgb], writes=[dd])
        yield
        c.op("pool", lambda e: e.tensor_tensor(dd[:], dd[:], R["mask4"][dr][:], ALU.mult), reads=[dd, R["mask4"][dr]], writes=[dd])
        yield
        xy = PS["xy"]
        cur = 0
        XY = xy[cur]
        for i, h in enumerate(heads):
            pk = R["psH"].next()
            c.op("pe", lambda e: e.matmul(pk[:, 0:128], kT[:, h, :], kT[:, h, :], start=True, stop=True), reads=[kT], writes=[pk])
            c.op("dve", lambda e: e.scalar_tensor_tensor(XY[:, (2 * i) * 128:(2 * i + 1) * 128], pk[:, 0:128], sc["nbeta"][:, h:h + 1],
                                                         dd[:, (2 * i) * 128:(2 * i + 1) * 128], ALU.mult, ALU.mult),
                 reads=[pk, sc["nbeta"], dd], writes=[XY])
        yield
        py = R["pstB"].next()
        for i in range(2):
            c.op("pe", lambda e: e.transpose(out=py[:, i * 128:(i + 1) * 128], in_=XY[:, (2 * i) * 128:(2 * i + 1) * 128],
                                             identity=ident[:]), reads=[XY, ident], writes=[py])
        XYv = XY[:].rearrange("p (i t x) -> p i t x", i=2, t=2)
        c.op("act", lambda e: e.copy(out=XYv[:, :, 1, :], in_=py[:, 0:256].rearrange("p (i x) -> p i x", i=2)),
             reads=[py], writes=[XY])
        yield
        rt = PS["rt"]
        RT = rt[0]
        for i in range(2):
            c.op("dve", lambda e: e.tensor_tensor(RT[:, i * 128:(i + 1) * 128], XY[:, (2 * i + 1) * 128:(2 * i + 2) * 128],
                                                  ident[:], ALU.add), reads=[XY, ident], writes=[RT])
        yield
        rcur = 0
        px = R["psH"].next()
        for i in range(2):
            Xi = XY[:, (2 * i) * 128:(2 * i + 1) * 128]
            Yi = XY[:, (2 * i + 1) * 128:(2 * i + 2) * 128]
            c.op("pe", lambda e: e.matmul(px[:, (2 * i) * 128:(2 * i + 1) * 128], Yi, Xi, start=True, stop=True), reads=[XY], writes=[px])
            c.op("pe", lambda e: e.matmul(px[:, (2 * i + 1) * 128:(2 * i + 2) * 128], Xi, Yi, start=True, stop=True), reads=[XY], writes=[px])
        XY2 = xy[1 - cur]
        c.op("act", lambda e: e.copy(out=XY2[:], in_=px[:]), reads=[px], writes=[XY2])
        XY = XY2
        cur = 1 - cur
        yield
        for lv in range(5):
            lastlv = lv == 4
            pA = R["psH"].next()
            pB = None if lastlv else R["psH"].next()
            for i in range(2):
                Xi = XY[:, (2 * i) * 128:(2 * i + 1) * 128]
                Yi = XY[:, (2 * i + 1) * 128:(2 * i + 2) * 128]
                RTi = RT[:, i * 128:(i + 1) * 128]
                if lastlv:
                    c.op("pe", lambda e: e.matmul(pA[:, (2 * i + 1) * 128:(2 * i + 2) * 128], Xi, RTi, start=True, stop=True),
                         reads=[XY, RT], writes=[pA])
                else:
                    c.op("pe", lambda e: e.matmul(pA[:, (2 * i) * 128:(2 * i + 1) * 128], Xi, Yi, start=True, stop=True),
                         reads=[XY], writes=[pA])
                    c.op("pe", lambda e: e.matmul(pA[:, (2 * i + 1) * 128:(2 * i + 2) * 128], Xi, RTi, start=True, stop=True),
                         reads=[XY, RT], writes=[pA])
                    c.op("pe", lambda e: e.matmul(pB[:, i * 128:(i + 1) * 128], Yi, Xi, start=True, stop=True), reads=[XY], writes=[pB])
            pAv = pA[:].rearrange("p (i t x) -> p i t x", i=2, t=2)
            RT2 = rt[1 - rcur]
            c.op("dve", lambda e: e.tensor_tensor(RT2[:].rearrange("p (i x) -> p i x", i=2), pAv[:, :, 1, :],
                                                  RT[:].rearrange("p (i x) -> p i x", i=2), ALU.add), reads=[pA, RT], writes=[RT2])
            if not lastlv:
                XY2 = xy[1 - cur]
                XY2v = XY2[:].rearrange("p (i t x) -> p i t x", i=2, t=2)
                c.op("dve", lambda e: e.tensor_copy(out=XY2v[:, :, 1, :], in_=pAv[:, :, 0, :]), reads=[pA], writes=[XY2])
                c.op("act", lambda e: e.copy(out=XY2v[:, :, 0, :], in_=pB[:, 0:256].rearrange("p (i x) -> p i x", i=2)),
                     reads=[pB], writes=[XY2])
                XY = XY2
                cur = 1 - cur
            RT = RT2
            rcur = 1 - rcur
            yield
        H = []
        for i, h in enumerate(heads):
            hs = slice(h * 128, (h + 1) * 128)
            col = slice(h, h + 1)
            B = PS["hd"][i]
            vb_, kbg, kdec = B["vbeta"], B["kbg"], B["kdec"]
            c.op("act", lambda e: e.mul(out=vb_[:], in_=vtm[:, hs], mul=sc["beta"][:, col]), reads=[vtm, sc["beta"]], writes=[vb_])
            c.op("dve", lambda e: e.tensor_scalar(kbg[:], ktm[:, hs], sc["beg"][:, col], None, ALU.mult), reads=[ktm, sc["beg"]], writes=[kbg])
            c.op("act", lambda e: e.mul(out=kdec[:], in_=ktm[:, hs], mul=sc["egd"][:, col]), reads=[ktm, sc["egd"]], writes=[kdec])
            ttb = RT[:, i * 128:(i + 1) * 128]
            pu_ = R["psH"].next()
            c.op("pe", lambda e: e.matmul(pu_[:, 0:128], ttb, vb_[:], start=True, stop=True), reads=[RT, vb_], writes=[pu_])
            c.op("pe", lambda e: e.matmul(pu_[:, 128:256], kbg[:], ttb, start=True, stop=True), reads=[RT, kbg], writes=[pu_])
            u, wT = B["u"], B["wT"]
            c.op("act", lambda e: e.copy(out=u[:], in_=pu_[:, 0:128]), reads=[pu_], writes=[u])
            c.op("act", lambda e: e.copy(out=wT[:], in_=pu_[:, 128:256]), reads=[pu_], writes=[wT])
            H.append((h, hs, col, B))
        yield
        for i, (h, hs, col, B) in enumerate(H):
            pq = R["psH"].next()
            c.op("pe", lambda e: e.matmul(pq[:, 0:128], kT[:, h, :], qT[:, h, :], start=True, stop=True), reads=[kT, qT], writes=[pq])
            c.op("dve", lambda e: e.tensor_tensor(B["qkm"][:], pq[:, 0:128], dd[:, (2 * i + 1) * 128:(2 * i + 2) * 128], ALU.mult),
                 reads=[pq, dd], writes=[B["qkm"]])
        yield
        order = (0, 1) if dr == 0 else (1, 0)
        for ci in order:
            rs = slice(ci * 64, (ci + 1) * 64)
            for i, (h, hs, col, B) in enumerate(H):
                pa = R["psH"].next()
                c.op("pe", lambda e: e.matmul(pa[:, 0:128], B["wT"][:], Sb[h][:], start=True, stop=True), reads=[B["wT"], Sb[h]], writes=[pa])
                c.op("pe", lambda e: e.matmul(pa[:, 128:256], qT[:, h, :], Sb[h][:], start=True, stop=True), reads=[qT, Sb[h]], writes=[pa])
                c.op("dve", lambda e: e.tensor_tensor(B["vn"][rs, :], B["u"][rs, :], pa[rs, 0:128], ALU.subtract),
                     reads=[B["u"], pa], writes=[B["vn"]])
                c.op("dve", lambda e: e.tensor_scalar(B["tmp"][rs, :], pa[rs, 128:256], sc["eg"][rs, col], None, ALU.mult),
                     reads=[pa, sc["eg"]], writes=[B["tmp"]])
            yield
            for i, (h, hs, col, B) in enumerate(H):
                ps_ = R["psH"].next()
                c.op("pe", lambda e: e.matmul(ps_[:, 0:128], B["kdec"][rs, :], B["vn"][rs, :], start=True, stop=True),
                     reads=[B["kdec"], B["vn"]], writes=[ps_])
                c.op("dve", lambda e: e.scalar_tensor_tensor(S[h][:], S[h][:], sc["egl"][ci][:, col], ps_[:, 0:128], ALU.mult, ALU.add),
                     reads=[S[h], sc["egl"][ci], ps_], writes=[S[h]])
            yield
            for i, (h, hs, col, B) in enumerate(H):
                c.op("act", lambda e: e.copy(out=Sb[h][:], in_=S[h][:]), reads=[S[h]], writes=[Sb[h]])
            yield
        for i, (h, hs, col, B) in enumerate(H):
            p2 = R["psH"].next()
            c.op("pe", lambda e: e.matmul(p2[:, 0:128], B["qkm"][:], B["vn"][:], start=True, stop=True), reads=[B["qkm"], B["vn"]], writes=[p2])
            if add_tb is None:
                c.op("dve", lambda e: e.tensor_tensor(o_tb[:, hs], p2[:, 0:128], B["tmp"][:], ALU.add), reads=[p2, B["tmp"]], writes=[o_tb])
            else:
                c.op("pool", lambda e: e.tensor_tensor(B["tmp"][:], B["tmp"][:], add_tb[:, hs], ALU.add), reads=[B["tmp"], add_tb], writes=[B["tmp"]])
                c.op("dve", lambda e: e.tensor_tensor(o_tb[:, hs], p2[:, 0:128], B["tmp"][:], ALU.add), reads=[p2, B["tmp"]], writes=[o_tb])

    def gdn_scalars(self, c, R, dr, ab, sc):
        M1, M2, M4, MT = R["gm"][dr]
        la = ab[:, dr * 8:(dr + 1) * 8]
        be = ab[:, 16 + dr * 8:16 + (dr + 1) * 8]
        ps = R["psA"].next()
        c.op("pe", lambda e: e.matmul(ps[:, 0:8], M1[:], la, start=True, stop=True), reads=[M1, ab], writes=[ps])
        c.op("pe", lambda e: e.matmul(ps[:, 8:16], M4[:], la, start=True, stop=True), reads=[M4, ab], writes=[ps])
        c.op("pe", lambda e: e.matmul(ps[:, 16:24], R["sel"][0][:], la, start=True, stop=True), reads=[R["sel"][0], ab], writes=[ps])
        c.op("pe", lambda e: e.matmul(ps[:, 24:32], R["sel"][1][:], la, start=True, stop=True), reads=[R["sel"][1], ab], writes=[ps])
        ex = sc["ex"]
        c.op("act", lambda e: e.activation(out=ex[:], in_=ps[:, 0:32], func=AF.Exp), reads=[ps], writes=[ex])
        c.op("dve", lambda e: e.tensor_copy(out=sc["la"][:], in_=la), reads=[ab], writes=[sc["la"]])
        c.op("dve", lambda e: e.tensor_copy(out=sc["beta"][:], in_=be), reads=[ab], writes=[sc["beta"]])
        c.op("dve", lambda e: e.tensor_scalar(sc["nbeta"][:], be, -1.0, None, ALU.mult), reads=[ab], writes=[sc["nbeta"]])
        c.op("dve", lambda e: e.tensor_tensor(sc["beg"][:], be, ex[:, 0:8], ALU.mult), reads=[ab, ex], writes=[sc["beg"]])
        sc["eg"] = _View(ex, (slice(None), slice(0, 8)))
        sc["egd"] = _View(ex, (slice(None), slice(8, 16)))
        sc["egl"] = [_View(ex, (slice(None), slice(16, 24))), _View(ex, (slice(None), slice(24, 32)))]

    def phase_odd(self, c, j, layer, x_in):
        nc, d = c.nc, self.d
        Lm = self.Lmax
        with contextlib.ExitStack() as es:
            win = c.sb("win", [128, 8, ODD_IN], BF16, es)
            wout = c.sb("wout", [128, 8, D], BF16, es)
            gmix = c.sb("gmix", [128, D], F32, es)
            gh = c.sb("gh", [128, 128], F32, es)
            ident = c.sb("identb", [128, 128], BF16, es)
            identf = c.sb("identf", [128, 128], F32, es)
            ones = c.sb("onesb", [128, 128], BF16, es)
            gmt = [[c.sb("gm%d%d" % (a, b_), [128, 128], F32, es) for b_ in range(4)] for a in range(2)]
            mstrict = [c.sb("mstr%d" % a, [128, 128], F32, es) for a in range(2)]
            sel = [c.sb("sel%d" % a, [128, 128], F32, es) for a in range(2)]
            cw = c.sb("cw", [128, 24, 5], F32, es)
            dtb = c.sb("dtb", [128, 16], F32, es)
            nea = c.sb("nea", [128, 16], F32, es)
            zer = c.sb("zer", [128, 16], BF16, es)
            ab_all = c.sb("ab_all", [128, Lm // 128, 32], F32, es)
            xts = Rot([c.sb("xt%d" % i, [128, D], F32, es) for i in range(1)])
            ssq = c.sb("ssq", [128, 1], F32, es)
            s2 = c.sb("s2", [128, 8], F32, es)
            hn = c.sb("hn", [128, D], BF16, es)
            hT = c.sb("hT", [128, 8, 128], BF16, es)
            hTb = c.sb("hTb", [128, 8, 260], BF16, es)
            gt = c.sb("gt", [128, D], F32, es)
            NCS = 3
            CSs = [{"raw": c.sb("raw%d" % i, [128, 260], F32, es), "acc": c.sb("acc%d" % i, [128, 256], F32, es),
                    "sqb": c.sb("sqb%d" % i, [128, 256], BF16, es)} for i in range(NCS)]
            for cs_d in CSs:
                cs_d["rst"] = _View(cs_d["raw"], (slice(None), slice(0, 256)))
            qTb = c.sb("qTb", [128, 8, 256], BF16, es)
            kTb = c.sb("kTb", [128, 8, 256], BF16, es)
            vTb = c.sb("vTb", [128, 8, 256], BF16, es)
            qTt = c.sb("qTt", [128, 8, 128], BF16, es)
            kTt = c.sb("kTt", [128, 8, 128], BF16, es)
            vtm = c.sb("vtm", [128, D], BF16, es)
            ktm = c.sb("ktm", [128, D], BF16, es)
            oft = c.sb("oft", [128, D], F32, es)
            sq = oft
            ot = c.sb("ot", [128, D], F32, es)
            omix = c.sb("omix", [128, D], BF16, es)
            junk = omix
            oT = hT
            abt = c.sb("abt", [128, 32], F32, es)
            sc = {k: c.sb("sc_" + k, [128, 8], F32, es) for k in ("la", "beta", "nbeta", "beg")}
            sc["ex"] = c.sb("sc_ex", [128, 32], F32, es)
            R = {
                "gm": gmt, "ident": ident, "identf": identf, "mstrict": mstrict, "sel": sel,
                "psH": Rot([c.ps("psH%d" % i, [128, 512], F32, es) for i in range(6)]),
                "pstB": Rot([c.ps("pstB%d" % i, [128, D], BF16, es) for i in range(2)]),
            }
            R["psA"] = R["psH"]
            NP = 4
            PSs = []
            for sl in range(NP):
                PS = {"ula": [c.sb("ula%d_%d" % (sl, i), [128, 128], F32, es) for i in range(2)],
                      "dd": c.sb("dd%d" % sl, [128, 512], F32, es),
                      "xy": [c.sb("xy%d_%d" % (sl, i), [128, 512], BF16, es) for i in range(2)],
                      "rt": [c.sb("rt%d_%d" % (sl, i), [128, 256], BF16, es) for i in range(2)],
                      "hd": []}
                for i in range(2):
                    B = {}
                    for nm in ("u", "tmp"):
                        B[nm] = c.sb("%s%d_%d" % (nm, sl, i), [128, 128], F32, es)
                    for nm in ("vbeta", "kbg", "kdec", "wT", "qkm", "vn"):
                        B[nm] = c.sb("%s%d_%d" % (nm, sl, i), [128, 128], BF16, es)
                    PS["hd"].append(B)
                PSs.append(PS)
            mask4 = [c.sb("mask4_%d" % a, [128, 512], F32, es) for a in range(2)]
            R["mask4"] = mask4
            for a in range(2):
                for i in range(2):
                    c.dma("sp", mask4[a][:, (2 * i) * 128:(2 * i + 1) * 128], d["c_mstrict"][a], writes=[mask4[a]])
                    c.dma("sp", mask4[a][:, (2 * i + 1) * 128:(2 * i + 2) * 128], d["c_gla"][a * 4 + 3], writes=[mask4[a]])
            S = [c.sb("S%d" % h, [128, 128], F32, es) for h in range(8)]
            Sb = [c.sb("Sb%d" % h, [128, 128], BF16, es) for h in range(8)]

            c.dma("sp", ident[:], d["c_ident_bf"], writes=[ident])
            c.dma("sp", identf[:], d["c_ident_f"], writes=[identf])
            c.dma("sp", ones[:], d["c_ones_bf"], writes=[ones])
            for a in range(2):
                for b_ in range(4):
                    c.dma("sp", gmt[a][b_][:], d["c_gla"][a * 4 + b_], writes=[gmt[a][b_]])
                c.dma("sp", mstrict[a][:], d["c_mstrict"][a], writes=[mstrict[a]])
                c.dma("sp", sel[a][:], d["c_sel"][a], writes=[sel[a]])
            c.dma("sp", gmix[:], d["norm_mix"][layer:layer + 1, :].partition_broadcast(128), writes=[gmix])
            c.dma("sp", gh[:], d["gdn_norm"][j:j + 1, :].partition_broadcast(128), writes=[gh])
            c.dma("sp", dtb[:], d["gdn_dt_bias"][j:j + 1, :].partition_broadcast(128), writes=[dtb])
            c.dma("sp", nea[:], d["gdn_a_log"][j:j + 1, :].partition_broadcast(128), writes=[nea])
            c.op("act", lambda e: e.activation(out=nea[:], in_=nea[:], func=AF.Exp), reads=[nea], writes=[nea])
            c.op("dve", lambda e: e.tensor_scalar(nea[:], nea[:], -1.0, None, ALU.mult), reads=[nea], writes=[nea])
            c.op("dve", lambda e: e.memset(zer[:], 0.0), writes=[zer])
            for w in range(5):
                c.dma("sp", cw[:, :, w], d["gdn_conv"][j, w, :].rearrange("(f p) -> p f", p=128), writes=[cw], slow=True)
            for k in range(8):
                c.dma("pool", win[:, k, :], d["gdn_w_in"][j, k * 128:(k + 1) * 128, :], writes=[win])
                c.dma("pool", wout[:, k, :], d["gdn_w_out"][j, k * 128:(k + 1) * 128, :], writes=[wout])

            HT = d["HT"].rearrange("p (k t) -> p k t", k=8)
            QT = d["QT"].rearrange("p (k t) -> p k t", k=8)
            KT = d["KT"].rearrange("p (k t) -> p k t", k=8)

            for si, L in enumerate(self.seqs):
                nt = L // 128
                base = self.seq_off[si]
                httb = TB(None, "HT")
                gtb = [TB(None, "G%d" % t) for t in range(nt)]
                qktb = [TB(None, "QK%d" % t) for t in range(nt // 2)]
                vmtb = [TB(None, "VM%d" % t) for t in range(nt)]
                oftb = [TB(None, "OF%d" % t) for t in range(nt)]
                c.dma("sp", HT[:, :, 0:2], zer[:, 0:16].rearrange("p (k t) -> p k t", k=8), reads=[zer], writes=[httb])
                c.dma("sp", HT[:, :, L + 2:L + 4], zer[:, 0:16].rearrange("p (k t) -> p k t", k=8), reads=[zer], writes=[httb])
                dbg = getattr(self, "dbg", 99)
                for t in range(nt if dbg >= 1 else 0):
                    r0 = base + t * 128
                    xt = xts.next()
                    c.dma("sp", xt[:], x_in[r0:r0 + 128, :], reads=[self.xtb(r0)], writes=[xt])
                    self.rmsnorm_to_hT(c, xt, gmix, (hT, hT[:]), junk, ssq, hn, R["pstB"].next(), ident)
                    c.dma("sp", HT[:, :, 2 + t * 128:2 + (t + 1) * 128], hT[:], reads=[hT], writes=[httb])
                    for n in range(2):
                        pa = R["psA"].next()
                        for k in range(8):
                            c.op("pe", lambda e: e.matmul(pa[:], hT[:, k, :], win[:, k, 3072 + n * 512:3072 + (n + 1) * 512],
                                                          start=(k == 0), stop=(k == 7)), reads=[hT, win], writes=[pa])
                        c.op("act", lambda e: e.copy(out=gt[:, n * 512:(n + 1) * 512], in_=pa[:]), reads=[pa], writes=[gt])
                    c.dma("sp", d["P"][t * 128:(t + 1) * 128, 0:D], gt[:], reads=[gt], writes=[gtb[t]])
                    pa = R["psA"].next()
                    for k in range(8):
                        c.op("pe", lambda e: e.matmul(pa[:, 0:32], hT[:, k, :], win[:, k, 4096:4128],
                                                      start=(k == 0), stop=(k == 7)), reads=[hT, win], writes=[pa])
                    c.op("dve", lambda e: e.tensor_tensor(abt[:, 0:16], pa[:, 0:16], dtb[:], ALU.add), reads=[pa, dtb], writes=[abt])
                    c.op("act", lambda e: e.activation(out=abt[:, 0:16], in_=abt[:, 0:16], func=AF.Exp), reads=[abt], writes=[abt])
                    c.op("act", lambda e: e.activation(out=abt[:, 0:16], in_=abt[:, 0:16], func=AF.Ln, bias=1.0), reads=[abt], writes=[abt])
                    c.op("dve", lambda e: e.tensor_tensor(ab_all[:, t, 0:16], abt[:, 0:16], nea[:], ALU.mult), reads=[abt, nea], writes=[ab_all])
                    c.op("dve", lambda e: e.tensor_copy(out=abt[:, 16:32], in_=pa[:, 16:32]), reads=[pa], writes=[abt])
                    c.op("act", lambda e: e.activation(out=ab_all[:, t, 16:32], in_=abt[:, 16:32], func=AF.Sigmoid), reads=[abt], writes=[ab_all])
                for h in range(8):
                    c.op("dve", lambda e: e.memset(S[h][:], 0.0), writes=[S[h]])
                    c.op("pool", lambda e: e.memset(Sb[h][:], 0.0), writes=[Sb[h]])
                for blk in range(nt // 2 if dbg >= 2 else 0):
                    b0 = blk * 256
                    c.dma("sp", hTb[:], HT[:, :, b0:b0 + 260], reads=[httb], writes=[hTb])
                    def fb_gen(fb, CS):
                        rw, ac, sqb_, rst_ = CS["raw"], CS["acc"], CS["sqb"], CS["rst"]
                        pa = R["psA"].next()
                        for k in range(8):
                            c.op("pe", lambda e: e.matmul(pa[:, 0:260], win[:, k, fb * 128:(fb + 1) * 128], hTb[:, k, :],
                                                          start=(k == 0), stop=(k == 7)), reads=[hTb, win], writes=[pa])
                        c.op("act", lambda e: e.copy(out=rw[:], in_=pa[:, 0:260]), reads=[pa], writes=[rw])
                        yield
                        c.op("dve", lambda e: e.tensor_scalar(ac[:], rw[:, 0:256], cw[:, fb, 0:1], None, ALU.mult), reads=[rw, cw], writes=[ac])
                        yield
                        for w in range(1, 5):
                            c.op("dve", lambda e: e.scalar_tensor_tensor(ac[:], rw[:, w:w + 256], cw[:, fb, w:w + 1], ac[:], ALU.mult, ALU.add),
                                 reads=[rw, cw, ac], writes=[ac])
                            yield
                        hh = fb % 8
                        if fb >= 16:
                            c.op("act", lambda e: e.activation(out=vTb[:, hh, :], in_=ac[:], func=AF.Silu), reads=[ac], writes=[vTb])
                            return
                        c.op("act", lambda e: e.activation(out=ac[:], in_=ac[:], func=AF.Silu), reads=[ac], writes=[ac])
                        yield
                        c.op("act", lambda e: e.activation(out=sqb_[:], in_=ac[:], func=AF.Square), reads=[ac], writes=[sqb_])
                        yield
                        pn = R["psA"].next()
                        c.op("pe", lambda e: e.matmul(pn[:, 0:256], ones[:], sqb_[:], start=True, stop=True), reads=[ones, sqb_], writes=[pn])
                        c.op("act", lambda e: e.activation(out=rst_[:], in_=pn[:, 0:256], func=AF.Ln, bias=RMS_EPS), reads=[pn], writes=[rst_])
                        yield
                        c.op("act", lambda e: e.activation(out=rst_[:], in_=rst_[:], func=AF.Exp, scale=-0.5), reads=[rst_], writes=[rst_])
                        yield
                        dst = qTb if fb < 8 else kTb
                        if fb < 8:
                            c.op("dve", lambda e: e.scalar_tensor_tensor(dst[:, hh, :], ac[:], 128 ** -0.5, rst_[:], ALU.mult, ALU.mult),
                                 reads=[ac, rst_], writes=[dst])
                        else:
                            c.op("pool", lambda e: e.tensor_tensor(dst[:, hh, :], ac[:], rst_[:], ALU.mult), reads=[ac, rst_], writes=[dst])

                    for fb0 in range(0, 24, NCS):
                        run_interleaved([fb_gen(fb0 + i, CSs[i]) for i in range(NCS)])
                    c.dma("sp", QT[:, :, b0:b0 + 256], qTb[:], reads=[qTb], writes=[qktb[blk]])
                    c.dma("sp", KT[:, :, b0:b0 + 256], kTb[:], reads=[kTb], writes=[qktb[blk]])
                    for tt in range(2):
                        t = blk * 2 + tt
                        cs_ = slice(tt * 128, (tt + 1) * 128)
                        for src, dstm in ((vTb, vtm), (kTb, ktm)):
                            pt = R["pstB"].next()
                            for h in range(8):
                                c.op("pe", lambda e: e.transpose(out=pt[:, h * 128:(h + 1) * 128], in_=src[:, h, cs_], identity=ident[:]),
                                     reads=[src, ident], writes=[pt])
                            c.op("act", lambda e: e.copy(out=dstm[:], in_=pt[:]), reads=[pt], writes=[dstm])
                        c.dma("sp", d["VM"][t * 128:(t + 1) * 128, :], vtm[:], reads=[vtm], writes=[vmtb[t]])
                        c.op("pool", lambda e: e.tensor_copy(out=qTt[:], in_=qTb[:, :, cs_]), reads=[qTb], writes=[qTt])
                        c.op("pool", lambda e: e.tensor_copy(out=kTt[:], in_=kTb[:, :, cs_]), reads=[kTb], writes=[kTt])
                        abv = _View(ab_all, (slice(None), t, slice(None)))
                        self.gdn_scalars(c, R, 0, abv, sc)
                        run_interleaved([self.gdn_pair(c, R, PSs[i], 2 * i, 0, qTt, kTt, ktm, vtm, sc, S, Sb, oft)
                                         for i in range(NP)])
                        c.dma("sp", d["OF"][t * 128:(t + 1) * 128, :], oft[:], reads=[oft], writes=[oftb[t]])
                for h in range(8):
                    c.op("dve", lambda e: e.memset(S[h][:], 0.0), writes=[S[h]])
                    c.op("pool", lambda e: e.memset(Sb[h][:], 0.0), writes=[Sb[h]])
                for t in (range(nt - 1, -1, -1) if dbg >= 4 else []):
                    r0 = base + t * 128
                    xt = xts.next()
                    c.dma("sp", xt[:], x_in[r0:r0 + 128, :], reads=[self.xtb(r0)], writes=[xt])
                    c.dma("sp", qTt[:], QT[:, :, t * 128:(t + 1) * 128], reads=[qktb[t // 2]], writes=[qTt])
                    c.dma("sp", kTt[:], KT[:, :, t * 128:(t + 1) * 128], reads=[qktb[t // 2]], writes=[kTt])
                    c.dma("sp", vtm[:], d["VM"][t * 128:(t + 1) * 128, :], reads=[vmtb[t]], writes=[vtm])
                    c.dma("sp", oft[:], d["OF"][t * 128:(t + 1) * 128, :], reads=[oftb[t]], writes=[oft])
                    c.dma("sp", gt[:], d["P"][t * 128:(t + 1) * 128, 0:D], reads=[gtb[t]], writes=[gt])
                    pt = R["pstB"].next()
                    for h in range(8):
                        c.op("pe", lambda e: e.transpose(out=pt[:, h * 128:(h + 1) * 128], in_=kTt[:, h, :], identity=ident[:]),
                             reads=[kTt, ident], writes=[pt])
                    c.op("act", lambda e: e.copy(out=ktm[:], in_=pt[:]), reads=[pt], writes=[ktm])
                    abv = _View(ab_all, (slice(None), t, slice(None)))
                    self.gdn_scalars(c, R, 1, abv, sc)
                    run_interleaved([self.gdn_pair(c, R, PSs[i], 2 * i, 1, qTt, kTt, ktm, vtm, sc, S, Sb, ot, add_tb=oft)
                                     for i in range(NP)])
                    c.op("act", lambda e: e.activation(out=sq[:], in_=ot[:], func=AF.Square), reads=[ot], writes=[sq])
                    c.op("dve", lambda e: e.reduce_sum(out=s2[:], in_=sq[:].rearrange("p (h x) -> p h x", h=8), axis=AX.X),
                         reads=[sq], writes=[s2])
                    c.op("dve", lambda e: e.tensor_scalar(s2[:], s2[:], 1.0 / 128, RMS_EPS, ALU.mult, ALU.add), reads=[s2], writes=[s2])
                    c.op("act", lambda e: e.activation(out=s2[:], in_=s2[:], func=AF.Sqrt), reads=[s2], writes=[s2])
                    c.op("dve", lambda e: e.reciprocal(out=s2[:], in_=s2[:]), reads=[s2], writes=[s2])
                    for h in range(8):
                        hs = slice(h * 128, (h + 1) * 128)
                        c.op("dve", lambda e: e.scalar_tensor_tensor(ot[:, hs], ot[:, hs], s2[:, h:h + 1], gh[:], ALU.mult, ALU.mult),
                             reads=[ot, s2, gh], writes=[ot])
                    c.op("act", lambda e: e.activation(out=gt[:], in_=gt[:], func=AF.Silu), reads=[gt], writes=[gt])
                    c.op("dve", lambda e: e.tensor_tensor(omix[:], ot[:], gt[:], ALU.mult), reads=[ot, gt], writes=[omix])
                    pt = R["pstB"].next()
                    for k in range(8):
                        c.op("pe", lambda e: e.transpose(out=pt[:, k * 128:(k + 1) * 128], in_=omix[:, k * 128:(k + 1) * 128],
                                                         identity=ident[:]), reads=[omix, ident], writes=[pt])
                    c.op("act", lambda e: e.copy(out=oT[:], in_=pt[:].rearrange("p (k t) -> p k t", k=8)), reads=[pt], writes=[oT])
                    for n in range(2):
                        py = R["psA"].next()
                        for k in range(8):
                            c.op("pe", lambda e: e.matmul(py[:], oT[:, k, :], wout[:, k, n * 512:(n + 1) * 512],
                                                          start=(k == 0), stop=(k == 7)), reads=[oT, wout], writes=[py])
                        c.op("dve", lambda e: e.tensor_tensor(xt[:, n * 512:(n + 1) * 512], py[:], xt[:, n * 512:(n + 1) * 512], ALU.add),
                             reads=[py, xt], writes=[xt])
                    c.dma("sp", d["X"][r0:r0 + 128, :], xt[:], reads=[xt], writes=[self.xtb(r0)])
            c.barrier()

    def phase_copy(self, c, x_in, to_y=False):
        d = self.d
        with contextlib.ExitStack() as es:
            xts = Rot([c.sb("cx%d" % i, [128, D], F32, es) for i in range(4)])
            for r0 in range(0, self.T, 128):
                xt = xts.next()
                c.dma("sp", xt[:], x_in[r0:r0 + 128, :], reads=[self.xtb(r0)], writes=[xt])
                if to_y:
                    c.dma("sp", d["y"][r0:r0 + 128, :], xt[:], reads=[xt], writes=[self.ytb(r0)])
                else:
                    c.dma("sp", d["X"][r0:r0 + 128, :], xt[:], reads=[xt], writes=[self.xtb(r0)])
            c.barrier()

    def build(self):
        nc = bass.Bass("TRN2", target_bir_lowering=False)
        self.declare(nc)
        with contextlib.ExitStack() as es:
            c = Ctx(nc, es)
            self.c = c
            self._xtb = [TB(None, "X%d" % i) for i in range(self.T // 128)]
            self._ytb = [TB(None, "Y%d" % i) for i in range(self.T // 128)]
            first = True
            for layer in range(self.depth):
                last = layer == self.depth - 1
                if self.do_mixer:
                    from_x = self.d["x"] if first else self.d["X"]
                    kind, jj = self.kinds[layer]
                    if kind == "even":
                        self.phase_even(c, jj, layer, from_x)
                    else:
                        self.phase_odd(c, jj, layer, from_x)
                    first = False
                if self.do_xattn:
                    self.phase_xattn(c, layer, self.d["x"] if first else self.d["X"])
                    first = False
                if self.do_ffn:
                    self.phase_ffn(c, layer, self.d["x"] if first else self.d["X"], last)
                    first = False
            c.barrier()
            self.stats = (c.n_inst, c.n_wait)
        return nc


def gla_masks(direction):
    t = np.arange(128)
    same = (t[:, None] // 64) == (t[None, :] // 64)
    if direction == 0:
        le = t[:, None] <= t[None, :]
        mid = 64 * (t // 64) + 31
        lemid = t[:, None] <= mid[None, :]
        gt = t[:, None] > t[None, :]
    else:
        le = t[:, None] >= t[None, :]
        mid = 64 * (t // 64) + 32
        lemid = t[:, None] >= mid[None, :]
        gt = t[:, None] < t[None, :]
    M1 = (same & le).astype(np.float32)
    M2 = (same * (le.astype(np.float32) - lemid.astype(np.float32))).astype(np.float32)
    M4 = (same & gt).astype(np.float32)
    MT = (same & le).astype(np.float32)
    return M1, M2, M4, MT


def host_constants(Lmax):
    cst = {}
    cst["c_ident_bf"] = np.eye(128, dtype=np.float32).astype(ml_dtypes.bfloat16)
    cst["c_ones_bf"] = np.ones((128, 128), np.float32).astype(ml_dtypes.bfloat16)
    cst["c_ident_f"] = np.eye(128, dtype=np.float32)
    gm = np.zeros((2, 4, 128, 128), np.float32)
    for dr in range(2):
        for i, m in enumerate(gla_masks(dr)):
            gm[dr, i] = m
    cst["c_gla"] = gm.reshape(8, 128, 128)
    ind = np.zeros((128, 2), np.float32)
    ind[:64, 0] = 1.0
    ind[64:, 1] = 1.0
    cst["c_ind"] = ind
    inv = (1.0 / (np.float32(10000.0) ** (np.arange(64, dtype=np.float32) / np.float32(64)))).astype(np.float32)
    ang = (np.arange(Lmax, dtype=np.float32)[:, None] * inv[None, :]).astype(np.float32)
    cs = np.zeros((Lmax, 4, 64), np.float32)
    cs[:, 0] = np.cos(ang)
    cs[:, 1] = np.sin(ang)
    cs[:, 2] = np.cos(ang) * np.float32(128 ** -0.5)
    cs[:, 3] = np.sin(ang) * np.float32(128 ** -0.5)
    cst["c_rope"] = cs.reshape(Lmax, 256)
    lg = np.log1p(-np.exp2(-5.0 - np.arange(4, dtype=np.float32))).astype(np.float32)
    cst["c_lgam"] = np.repeat(lg, 128)[None, :].astype(np.float32)
    t = np.arange(128)
    same = (t[:, None] // 64) == (t[None, :] // 64)
    ms = np.zeros((2, 128, 128), np.float32)
    ms[0] = same & (t[None, :] < t[:, None])
    ms[1] = same & (t[None, :] > t[:, None])
    cst["c_mstrict"] = ms
    sl = np.zeros((2, 128, 128), np.float32)
    sl[0, :64, :] = 1.0
    sl[1, 64:, :] = 1.0
    cst["c_sel"] = sl
    hm = np.zeros((1, 8), np.float32)
    hm[0, :4] = 1.0
    cst["c_hmask"] = hm
    return cst


_CACHE = {}


def _get_program(seqs):
    key = tuple(seqs)
    if key not in _CACHE:
        b = Builder(seqs)
        nc = b.build()
        _CACHE[key] = (b, nc)
    return _CACHE[key]


def kernel(**inputs):
    inp = {k: np.asarray(v) for k, v in inputs.items()}
    seqs = [2048, 2048, 4096, 4096]
    b, nc = _get_program(seqs)
    cst = host_constants(max(seqs))
    shared = {}
    for k in ("norm_mix", "norm_xq", "norm_mem", "norm_ffn", "even_w_in", "even_w_out", "hgrn_lb_logits",
              "gdn_w_in", "gdn_conv", "gdn_norm", "gdn_w_out", "xa_w_q", "xa_w_kv", "xa_w_o", "ffn_w_gu", "ffn_w_down"):
        shared[k] = np.ascontiguousarray(inp[k], dtype=np.float32)
    shared["norm_final"] = np.ascontiguousarray(inp["norm_final"].reshape(1, D), dtype=np.float32)
    shared["ret_norm"] = np.ascontiguousarray(inp["ret_norm"].reshape(2, 512), dtype=np.float32)
    shared["hgrn_norm"] = np.ascontiguousarray(inp["hgrn_norm"].reshape(2, 512), dtype=np.float32)
    shared["gdn_a_log"] = np.ascontiguousarray(inp["gdn_a_log"].reshape(2, 16), dtype=np.float32)
    shared["gdn_dt_bias"] = np.ascontiguousarray(inp["gdn_dt_bias"].reshape(2, 16), dtype=np.float32)
    shared.update(cst)
    in_maps = []
    for ci in range(NCORES):
        m = dict(shared)
        xp = inp["x_prompt"][2 * ci:2 * ci + 2].reshape(-1, D)
        xs = inp["x_sample"][2 * ci:2 * ci + 2].reshape(-1, D)
        m["x"] = np.ascontiguousarray(np.concatenate([xp, xs], axis=0), dtype=np.float32)
        mp = inp["mem_prompt"][2 * ci:2 * ci + 2].reshape(-1, D)
        ms = inp["mem_sample"][2 * ci:2 * ci + 2].reshape(-1, D)
        m["mem"] = np.ascontiguousarray(np.concatenate([mp, ms], axis=0), dtype=np.float32)
        in_maps.append(m)
    res = run_bass_kernel_spmd(nc, in_maps, core_ids=list(range(NCORES)))
    yp = np.empty((16, 2048, D), np.float32)
    ys = np.empty((16, 4096, D), np.float32)
    for ci in range(NCORES):
        y = np.asarray(res.results[ci]["y"])
        yp[2 * ci:2 * ci + 2] = y[:4096].reshape(2, 2048, D)
        ys[2 * ci:2 * ci + 2] = y[4096:].reshape(2, 4096, D)
    return (yp, ys)
```
